# Optimizing a Trainium2 kernel written in Bass

```python
import math
import jax
import jax.numpy as jnp
from jax import lax
import numpy as np

D_MODEL = 1024
BATCH = 32
SEQ = 256
DEPTH = 4
DEC_BATCH = 2
DEC_SEQ = 4096
PAST_LEN = 256

GRID_W = 64
D_MIX = D_MODEL
A_HEADS = 4
A_QK = 64
A_V = 2 * A_QK
A_WIDTH = A_HEADS * A_V
B_HEADS = 4
B_DK = 64
B_DV = 64
B_WIDTH = B_HEADS * B_DV
B_QKV = B_HEADS * (2 * B_DK + B_DV)
C_HEADS = 4
C_DK = 64
C_DV = 64
C_KEYS = C_HEADS * C_DK
C_WIDTH = C_HEADS * C_DV
SHORT_CONV = 3
DELTA_CHUNK = 64
HGRN_CHUNK = 16
Q_BLOCK = 128
ROPE_BASE = 10000.0
D_FF = ((8 * D_MODEL // 3 + 255) // 256) * 256
ALPHA = (2 * DEPTH) ** 0.25
BETA_INIT = (8 * DEPTH) ** -0.25
LN_EPS = 1e-5
RMS_EPS = 1e-6
IN_SPLITS = (A_HEADS * 2 * A_QK, A_HEADS * 2 * A_QK, A_WIDTH,
             B_QKV, B_WIDTH, 2 * B_HEADS, 2 * B_HEADS,
             C_KEYS, 2 * C_KEYS, C_WIDTH, C_WIDTH)
D_IN = sum(IN_SPLITS)

kernel_name = 'hybrid_diff_delta_hgrn2_dit_step'


def _split_points():
    return [int(s) for s in np.cumsum(IN_SPLITS)[:-1]]


def layer_norm(x, g, b):
    xf = x.astype(jnp.float32)
    mu = jnp.mean(xf, -1, keepdims=True)
    var = jnp.mean(jnp.square(xf - mu), -1, keepdims=True)
    return ((xf - mu) * lax.rsqrt(var + LN_EPS) * g.astype(jnp.float32) + b.astype(jnp.float32)).astype(x.dtype)


def rms_norm(x, g):
    xf = x.astype(jnp.float32)
    return (xf * lax.rsqrt(jnp.mean(xf * xf, -1, keepdims=True) + RMS_EPS) * g.astype(jnp.float32)).astype(x.dtype)


def l2norm(x):
    return x * lax.rsqrt(jnp.sum(x * x, -1, keepdims=True) + 1e-6)


def short_conv(x, w):
    return lax.conv_general_dilated(x, w[:, None, :].astype(x.dtype), window_strides=(1,),
                                    padding=[(SHORT_CONV // 2, SHORT_CONV // 2)],
                                    dimension_numbers=('NWC', 'WIO', 'NWC'),
                                    feature_group_count=x.shape[-1])


def axial_rope(x):
    L = x.shape[1]
    n_rows = L // GRID_W
    row = jnp.repeat(jnp.arange(n_rows), GRID_W)
    col = jnp.tile(jnp.arange(GRID_W), n_rows)
    half = A_QK // 2
    nf = half // 2
    inv_freq = ROPE_BASE ** (-jnp.arange(nf, dtype=jnp.float32) / nf)
    bshape = (1, L) + (1,) * (x.ndim - 3) + (nf,)
    xf = x.astype(jnp.float32)

    def rotate(xa, pos):
        ang = pos.astype(jnp.float32)[:, None] * inv_freq
        cos = jnp.cos(ang).reshape(bshape)
        sin = jnp.sin(ang).reshape(bshape)
        x1, x2 = xa[..., :nf], xa[..., nf:]
        return jnp.concatenate([x1 * cos - x2 * sin, x2 * cos + x1 * sin], -1)

    return jnp.concatenate([rotate(xf[..., :half], row), rotate(xf[..., half:], col)], -1).astype(x.dtype)


def diff_attention(q1, q2, k1, k2, v, lam):
    B, Lq, H, d = q1.shape
    nb = Lq // Q_BLOCK
    scale = d ** -0.5
    qs = jnp.moveaxis(jnp.stack([q1, q2]).reshape(2, B, nb, Q_BLOCK, H, d), 2, 0)

    def block(qq):
        s1 = jnp.einsum('bqhd,bkhd->bhqk', qq[0], k1, preferred_element_type=jnp.float32) * scale
        s2 = jnp.einsum('bqhd,bkhd->bhqk', qq[1], k2, preferred_element_type=jnp.float32) * scale
        p = jax.nn.softmax(s1, -1) - lam * jax.nn.softmax(s2, -1)
        return jnp.einsum('bhqk,bkhv->bqhv', p.astype(v.dtype), v)

    out = lax.map(block, qs)
    return jnp.moveaxis(out, 0, 1).reshape(B, Lq, H, v.shape[-1])


def gated_delta_chunked(q, k, v, beta, g, s0):
    B, H, L, dk = q.shape
    dv = v.shape[-1]
    C = DELTA_CHUNK
    N = L // C
    q = q.reshape(B, H, N, C, dk)
    k = k.reshape(B, H, N, C, dk)
    v = v.reshape(B, H, N, C, dv)
    beta = beta.reshape(B, H, N, C)
    b = jnp.cumsum(g.reshape(B, H, N, C), -1)
    causal = jnp.tril(jnp.ones((C, C), bool))
    strict = jnp.tril(jnp.ones((C, C), bool), -1)
    decay = jnp.where(causal, jnp.exp(jnp.where(causal, b[..., :, None] - b[..., None, :], 0.0)), 0.0)
    kb = k * beta[..., None]
    m = jnp.where(strict, jnp.einsum('bhntd,bhnsd->bhnts', kb, k) * decay, 0.0)
    a = m + jnp.eye(C, dtype=q.dtype)
    rhs = jnp.concatenate([v * beta[..., None], kb * jnp.exp(b)[..., None]], -1)
    sol = lax.linalg.triangular_solve(a, rhs, left_side=True, lower=True, unit_diagonal=True)
    u, w = sol[..., :dv], sol[..., dv:]
    qk = jnp.where(causal, jnp.einsum('bhntd,bhnsd->bhnts', q, k) * decay, 0.0)
    b_last = b[..., -1:]
    q_dec = q * jnp.exp(b)[..., None]
    k_dec = k * jnp.exp(b_last - b)[..., None]
    g_last = jnp.exp(b_last[..., 0])
    xs = tuple(jnp.moveaxis(t, 2, 0) for t in (u, w, qk, q_dec, k_dec, g_last))

    def step(S, inp):
        u_n, w_n, qk_n, qd_n, kd_n, gl_n = inp
        v_new = u_n - jnp.einsum('bhcd,bhde->bhce', w_n, S)
        o = jnp.einsum('bhcd,bhde->bhce', qd_n, S) + jnp.einsum('bhts,bhse->bhte', qk_n, v_new)
        S = S * gl_n[..., None, None] + jnp.einsum('bhcd,bhce->bhde', kd_n, v_new)
        return S, o

    S, o = lax.scan(step, s0, xs)
    return jnp.moveaxis(o, 0, 2).reshape(B, H, L, dv), S


def gla_chunked(q, k, v, g, s0):
    B, H, L, dk = q.shape
    dv = v.shape[-1]
    C = HGRN_CHUNK
    N = L // C
    q = q.reshape(B, H, N, C, dk)
    k = k.reshape(B, H, N, C, dk)
    g = g.reshape(B, H, N, C, dk)
    v = v.reshape(B, H, N, C, dv)
    b = jnp.cumsum(g, axis=3)
    causal = jnp.tril(jnp.ones((C, C), bool))[:, :, None]
    diff = b[:, :, :, :, None, :] - b[:, :, :, None, :, :]
    dec = jnp.where(causal, jnp.exp(jnp.where(causal, diff, 0.0)), 0.0)
    attn = jnp.einsum('bhntsd,bhnsd->bhnts', q[:, :, :, :, None, :] * dec, k)
    intra = jnp.einsum('bhnts,bhnse->bhnte', attn, v)
    b_last = b[:, :, :, -1:, :]
    q_dec = q * jnp.exp(b)
    k_dec = k * jnp.exp(b_last - b)
    g_last = jnp.exp(b_last[:, :, :, 0, :])
    xs = tuple(jnp.moveaxis(t, 2, 0) for t in (intra, q_dec, k_dec, v, g_last))

    def step(S, inp):
        intra_n, qd_n, kd_n, v_n, gl_n = inp
        o = intra_n + jnp.einsum('bhcd,bhde->bhce', qd_n, S)
        S = S * gl_n[..., None] + jnp.einsum('bhcd,bhce->bhde', kd_n, v_n)
        return S, o

    S, o = lax.scan(step, s0, xs)
    return jnp.moveaxis(o, 0, 2).reshape(B, H, L, dv), S


def _flip(t):
    return jnp.flip(t, axis=2)


def delta_mixer(qkv, gate, beta_raw, a_raw, conv_w, a_log, dt_bias, norm_g, s0):
    B, L, _ = qkv.shape
    f32 = jnp.float32
    qkv_c = jax.nn.silu(short_conv(qkv, conv_w)).astype(f32)
    q, k, v = jnp.split(qkv_c, [B_HEADS * B_DK, 2 * B_HEADS * B_DK], axis=-1)
    q = l2norm(q.reshape(B, L, B_HEADS, B_DK).transpose(0, 2, 1, 3)) * B_DK ** -0.5
    k = l2norm(k.reshape(B, L, B_HEADS, B_DK).transpose(0, 2, 1, 3))
    v = v.reshape(B, L, B_HEADS, B_DV).transpose(0, 2, 1, 3)
    beta = jax.nn.sigmoid(beta_raw.astype(f32)).reshape(B, L, 2, B_HEADS).transpose(2, 0, 3, 1)
    a = a_raw.astype(f32).reshape(B, L, 2, B_HEADS).transpose(2, 0, 3, 1)
    g = -jnp.exp(a_log.astype(f32))[:, None, :, None] * jax.nn.softplus(a + dt_bias.astype(f32)[:, None, :, None])
    s0 = s0.astype(f32)
    o_f, s_f = gated_delta_chunked(q, k, v, beta[0], g[0], s0[:, 0])
    o_b, s_b = gated_delta_chunked(_flip(q), _flip(k), _flip(v), _flip(beta[1]), _flip(g[1]), s0[:, 1])
    o = (o_f + _flip(o_b)).transpose(0, 2, 1, 3)
    o = rms_norm(o, norm_g) * jax.nn.silu(gate.astype(f32).reshape(B, L, B_HEADS, B_DV))
    return o.reshape(B, L, B_WIDTH).astype(qkv.dtype), jnp.stack([s_f, s_b], 1)


def hgrn_mixer(q_raw, f_raw, i_raw, gate, lb, norm_g, s0):
    B, L, _ = q_raw.shape
    f32 = jnp.float32
    q = jax.nn.silu(q_raw.astype(f32)).reshape(B, L, C_HEADS, C_DK).transpose(0, 2, 1, 3)
    v = i_raw.astype(f32).reshape(B, L, C_HEADS, C_DV).transpose(0, 2, 1, 3)
    lbb = lb.astype(f32)[:, None, None, :]
    forget = lbb + (1.0 - lbb) * jax.nn.sigmoid(f_raw.astype(f32).reshape(B, L, 2, C_KEYS).transpose(2, 0, 1, 3))
    key = (1.0 - forget).reshape(2, B, L, C_HEADS, C_DK).transpose(0, 1, 3, 2, 4)
    g = jnp.log(forget).reshape(2, B, L, C_HEADS, C_DK).transpose(0, 1, 3, 2, 4)
    s0 = s0.astype(f32)
    o_f, s_f = gla_chunked(q, key[0], v, g[0], s0[:, 0])
    o_b, s_b = gla_chunked(_flip(q), _flip(key[1]), _flip(v), _flip(g[1]), s0[:, 1])
    o = (o_f + _flip(o_b)).transpose(0, 2, 1, 3)
    o = rms_norm(o, norm_g) * jax.nn.silu(gate.astype(f32).reshape(B, L, C_HEADS, C_DV))
    return o.reshape(B, L, C_WIDTH).astype(q_raw.dtype), jnp.stack([s_f, s_b], 1)


def token_mixers(z, conv_w, delta_a_log, delta_dt_bias, delta_norm, lb, hgrn_norm, lam, lam_scale, diff_norm, ctx):
    B, L, _ = z.shape
    aq, ak, av, bqkv, bg, bbeta, ba, cq, cf, ci, cg = jnp.split(z, _split_points(), axis=-1)
    aq = aq.reshape(B, L, A_HEADS, 2, A_QK)
    ak = ak.reshape(B, L, A_HEADS, 2, A_QK)
    v = av.reshape(B, L, A_HEADS, A_V)
    if ctx is None:
        k_all, v_all = ak, v
        s0_d = jnp.zeros((B, 2, B_HEADS, B_DK, B_DV), jnp.float32)
        s0_h = jnp.zeros((B, 2, C_HEADS, C_DK, C_DV), jnp.float32)
    else:
        ctx_k, ctx_v, s0_d, s0_h = ctx
        aq = axial_rope(aq)
        k_lat = axial_rope(ak)
        k_all = jnp.concatenate([k_lat, ctx_k.reshape(B, -1, A_HEADS, 2, A_QK).astype(ak.dtype)], 1)
        v_all = jnp.concatenate([v, ctx_v.astype(v.dtype)], 1)
    o_a = diff_attention(aq[..., 0, :], aq[..., 1, :], k_all[..., 0, :], k_all[..., 1, :], v_all, lam)
    o_a = (rms_norm(o_a, diff_norm) * lam_scale).reshape(B, L, A_WIDTH)
    o_b, s_d = delta_mixer(bqkv, bg, bbeta, ba, conv_w, delta_a_log, delta_dt_bias, delta_norm, s0_d)
    o_c, s_h = hgrn_mixer(cq, cf, ci, cg, lb, hgrn_norm, s0_h)
    o = jnp.concatenate([o_a.astype(z.dtype), o_b, o_c], -1)
    return o, ak.reshape(B, L, A_HEADS, 2 * A_QK), v, s_d, s_h


def trunk_layer(x, mod, w_in, w_out, conv_w, delta_a_log, delta_dt_bias, delta_norm, lb, hgrn_norm,
                lam, lam_scale, diff_norm, ln_g, ln_b, w_ffn_in, w_ffn_out, ctx):
    shift_m, scale_m, gate_m, shift_f, scale_f, gate_f = jnp.split(mod.astype(x.dtype), 6, axis=-1)
    z = (x * (1 + scale_m) + shift_m) @ w_in
    o, k_c, v_c, s_d, s_h = token_mixers(z, conv_w, delta_a_log, delta_dt_bias, delta_norm, lb, hgrn_norm,
                                         lam, lam_scale, diff_norm, ctx)
    x = layer_norm(ALPHA * x + gate_m * (o @ w_out), ln_g[0], ln_b[0])
    gt, up = jnp.split((x * (1 + scale_f) + shift_f) @ w_ffn_in, 2, axis=-1)
    x = layer_norm(ALPHA * x + gate_f * ((jax.nn.silu(gt) * up) @ w_ffn_out), ln_g[1], ln_b[1])
    return x, k_c, v_c, s_d, s_h


def setup_inputs(seed: int = 0) -> dict:
    key = jax.random.key(seed)
    ks = jax.random.split(key, 24)
    f32 = jnp.float32

    def nrm(k, shape, s):
        return jax.random.normal(k, shape, f32) * s

    dt = jnp.exp(jax.random.uniform(ks[13], (DEPTH, 2, B_HEADS), f32, math.log(1e-3), math.log(1e-1)))
    return {
        'x_prompt': nrm(ks[0], (BATCH, SEQ, D_MODEL), 1.0),
        'x_sample': nrm(ks[1], (DEC_BATCH, DEC_SEQ, D_MODEL), 1.0),
        'cache_attn_k': nrm(ks[2], (DEC_BATCH, DEPTH, PAST_LEN, A_HEADS, 2 * A_QK), 1.0),
        'cache_attn_v': nrm(ks[3], (DEC_BATCH, DEPTH, PAST_LEN, A_HEADS, A_V), 1.0),
        'state_delta': nrm(ks[4], (DEC_BATCH, DEPTH, 2, B_HEADS, B_DK, B_DV), 0.1),
        'state_hgrn': nrm(ks[5], (DEC_BATCH, DEPTH, 2, C_HEADS, C_DK, C_DV), 0.5),
        'c': nrm(ks[6], (DEC_BATCH, D_MODEL), 1.0),
        'c_ctx': nrm(ks[7], (D_MODEL,), 1.0),
        'w_mod': nrm(ks[8], (DEPTH, D_MODEL, 6 * D_MODEL), D_MODEL ** -0.5),
        'b_mod': nrm(ks[9], (DEPTH, 6 * D_MODEL), 0.02),
        'w_in': nrm(ks[10], (DEPTH, D_MODEL, D_IN), D_MODEL ** -0.5),
        'conv_w': nrm(ks[11], (DEPTH, SHORT_CONV, B_QKV), SHORT_CONV ** -0.5),
        'delta_a_log': jnp.log(jax.random.uniform(ks[12], (DEPTH, 2, B_HEADS), f32, 1.0, 16.0)),
        'delta_dt_bias': dt + jnp.log(-jnp.expm1(-dt)),
        'delta_norm': 1.0 + nrm(ks[14], (DEPTH, B_DV), 0.02),
        'hgrn_lb': 1.0 + nrm(ks[15], (2, DEPTH, C_KEYS), 0.1),
        'hgrn_norm': 1.0 + nrm(ks[16], (DEPTH, C_DV), 0.02),
        'diff_lambda': nrm(ks[17], (DEPTH, 4, A_QK), 0.1),
        'diff_norm': 1.0 + nrm(ks[18], (DEPTH, A_V), 0.02),
        'w_out': nrm(ks[19], (DEPTH, D_MIX, D_MODEL), BETA_INIT * D_MIX ** -0.5),
        'ln_g': 1.0 + nrm(ks[20], (DEPTH, 2, D_MODEL), 0.02),
        'ln_b': nrm(ks[21], (DEPTH, 2, D_MODEL), 0.02),
        'w_ffn_in': nrm(ks[22], (DEPTH, D_MODEL, 2 * D_FF), D_MODEL ** -0.5),
        'w_ffn_out': nrm(ks[23], (DEPTH, D_FF, D_MODEL), BETA_INIT * D_FF ** -0.5),
    }


def reference(x_prompt, x_sample, cache_attn_k, cache_attn_v, state_delta, state_hgrn, c, c_ctx,
              w_mod, b_mod, w_in, conv_w, delta_a_log, delta_dt_bias, delta_norm, hgrn_lb, hgrn_norm,
              diff_lambda, diff_norm, w_out, ln_g, ln_b, w_ffn_in, w_ffn_out):
    f32 = jnp.float32
    lb_soft = jax.nn.softmax(hgrn_lb.astype(f32), axis=1)
    lb_all = jnp.cumsum(lb_soft, axis=1) - lb_soft[:, :1]
    c_silu = jax.nn.silu(c)
    cctx_silu = jax.nn.silu(c_ctx)
    xp, xs = x_prompt, x_sample
    new_k, new_v, new_sd, new_sh = [], [], [], []
    for l in range(DEPTH):
        lam_init = 0.8 - 0.6 * math.exp(-0.3 * l)
        dl = diff_lambda[l].astype(f32)
        lam = jnp.exp(jnp.sum(dl[0] * dl[1])) - jnp.exp(jnp.sum(dl[2] * dl[3])) + lam_init
        mod_p = (cctx_silu @ w_mod[l] + b_mod[l])[None, None, :]
        mod_s = (c_silu @ w_mod[l] + b_mod[l])[:, None, :]
        shared = (w_in[l], w_out[l], conv_w[l], delta_a_log[l], delta_dt_bias[l], delta_norm[l], lb_all[:, l],
                  hgrn_norm[l], lam, 1.0 - lam_init, diff_norm[l], ln_g[l], ln_b[l], w_ffn_in[l], w_ffn_out[l])
        xp, k_c, v_c, s_d, s_h = trunk_layer(xp, mod_p, *shared, None)
        xs, _, _, _, _ = trunk_layer(xs, mod_s, *shared,
                                     (cache_attn_k[:, l], cache_attn_v[:, l], state_delta[:, l], state_hgrn[:, l]))
        new_k.append(k_c)
        new_v.append(v_c)
        new_sd.append(s_d)
        new_sh.append(s_h)
    return (xp, xs, jnp.stack(new_k, 1), jnp.stack(new_v, 1), jnp.stack(new_sd, 1), jnp.stack(new_sh, 1))
```

```python
import math
import os
from contextlib import ExitStack
CSTOP = int(os.environ.get('CSTOP', '99'))
CSUB = os.environ.get('CSUB', '')
BSTOP = int(os.environ.get('BSTOP', '99'))

import numpy as np
import concourse.bass as bass
import concourse.mybir as mybir
from concourse.bass_utils import run_bass_kernel_spmd

F32 = mybir.dt.float32
BF16 = mybir.dt.bfloat16
ALU = mybir.AluOpType
AF = mybir.ActivationFunctionType
AX = mybir.AxisListType

D = 1024
NCH = 8
DEPTH = 4
SEQ = 256
DEC_SEQ = 4096
PAST = 256
D_FF = 2816
NF = 22
D_IN = 3856
ALPHA = (2 * DEPTH) ** 0.25
LN_EPS = 1e-5
TOK = 1024
GROUPS = [[0, 1, 2, 3], [4, 5, 6, 7]]

O_AQ, O_AK, O_AV = 0, 512, 1024
O_BQKV, O_BG, O_BBETA, O_BA = 1536, 2304, 2560, 2568
O_CQ, O_CF, O_CI, O_CG = 2576, 2832, 3344, 3600


class Tok:
    __slots__ = ("w", "rs", "excl")

    def __init__(self):
        self.w = None
        self.rs = {}
        self.excl = False


class Opd:
    __slots__ = ("ap", "toks")

    def __init__(self, ap, toks):
        self.ap = ap
        self.toks = toks


class View:
    __slots__ = ("t", "toks")

    def __init__(self, t, toks):
        self.t = t
        self.toks = toks

    def __getitem__(self, idx):
        return Opd(self.t[idx], self.toks)


class Buf:
    def __init__(self, t, ntok=1):
        self.t = t
        self.toks = [Tok() for _ in range(ntok)]

    def at(self, *keys):
        return View(self.t, [self.toks[k] for k in keys])

    def all(self):
        return View(self.t, self.toks)

    def __getitem__(self, idx):
        return Opd(self.t[idx], self.toks)


class Ring:
    def __init__(self, bufs):
        self.bufs = bufs
        self.i = 0

    def next(self):
        b = self.bufs[self.i % len(self.bufs)]
        self.i += 1
        return b


class KB:
    ENGS = ("pe", "dve", "act", "pool", "sp")

    def __init__(self):
        self.nc = bass.Bass("TRN2", target_bir_lowering=False)
        self.es = ExitStack()
        self.streams = {e: [] for e in self.ENGS}
        self.count = {e: 0 for e in self.ENGS}
        self.seen = {e: {} for e in self.ENGS}
        self.latest = {}
        self.sems = {}
        for e in self.ENGS:
            self.sems[e] = self.es.enter_context(self.nc.semaphore("c_" + e))
        self.ndsem = {"sp": 24, "pool": 12, "act": 8}
        self.dsem_i = {q: 0 for q in self.ndsem}
        self.dsem_v = {}
        for q, n in self.ndsem.items():
            for j in range(n):
                k = "d_%s_%d" % (q, j)
                self.sems[k] = self.es.enter_context(self.nc.semaphore(k))
                self.dsem_v[k] = 0
        self.nalloc = 0
        self.pending_barrier = {e: None for e in self.ENGS}

    def sbuf(self, name, shape, dt, ntok=1):
        t = self.es.enter_context(self.nc.sbuf_tensor(name, list(shape), dt))
        return Buf(t, ntok)

    def psum(self, name, shape, dt=F32):
        t = self.es.enter_context(self.nc.psum_tensor(name, list(shape), dt))
        b = Buf(t, 1)
        b.toks[0].excl = True
        return b

    def ring(self, name, shape, dt, n):
        return Ring([self.sbuf("%s%d" % (name, i), shape, dt) for i in range(n)])

    def dram(self, name, shape, dt, kind="Internal"):
        return self.nc.dram_tensor(name, list(shape), dt, kind=kind)

    def _deps(self, eng, reads, writes):
        need = {}

        def add(ref):
            if ref is None:
                return
            k, v = ref
            if need.get(k, 0) < v:
                need[k] = v

        for t in reads:
            add(t.w)
            if t.excl:
                for k2, v in t.rs.items():
                    if k2 != eng:
                        add((k2, v))
        for t in writes:
            add(t.w)
            for k, v in t.rs.items():
                add((k, v))
        pb = self.pending_barrier[eng]
        if pb is not None:
            for k, v in pb.items():
                add((k, v))
            self.pending_barrier[eng] = None
        if eng == "pe":
            need.pop("pe", None)
        seen = self.seen[eng]
        waits = []
        for k, v in need.items():
            if seen.get(k, 0) < v:
                seen[k] = v
                waits.append((k, v))
        return waits

    def op(self, eng, fn, reads=(), writes=()):
        waits = self._deps(eng, reads, writes)
        self.count[eng] += 1
        idx = self.count[eng]
        ref = (eng, idx)
        for t in reads:
            t.rs[eng] = idx
        for t in writes:
            t.w = ref
            t.rs = {}
        self.latest[eng] = idx
        self.streams[eng].append((waits, fn, (eng, 1)))

    def dma(self, q, out_ap, in_ap, reads=(), writes=(), fn=None, selfinc=False):
        waits = self._deps(q, reads, writes)
        j = self.dsem_i[q] % self.ndsem[q]
        self.dsem_i[q] += 1
        k = "d_%s_%d" % (q, j)
        prev = self.dsem_v[k]
        if prev > 0 and self.seen[q].get(k, 0) < prev:
            self.seen[q][k] = prev
            waits.append((k, prev))
        self.dsem_v[k] = prev + 16
        ref = (k, prev + 16)
        for t in reads:
            t.rs[k] = prev + 16
        for t in writes:
            t.w = ref
            t.rs = {}
        self.latest[k] = prev + 16
        if fn is None:
            fn = lambda e, o=out_ap, i=in_ap: e.dma_start(out=o, in_=i)
        self.streams[q].append((waits, fn, ("SELF", k) if selfinc else (k, 16)))

    def cc(self, fn, reads=(), writes=()):
        waits = self._deps("pool", reads, writes)
        if "cc" not in self.sems:
            self.sems["cc"] = self.es.enter_context(self.nc.semaphore("cc"))
            self.ccv = 0
        prev = self.ccv
        if prev > 0 and self.seen["pool"].get("cc", 0) < prev:
            self.seen["pool"]["cc"] = prev
            waits.append(("cc", prev))
        self.ccv = prev + 1
        ref = ("cc", prev + 1)
        for t in reads:
            t.rs["cc"] = prev + 1
        for t in writes:
            t.w = ref
            t.rs = {}
        self.latest["cc"] = prev + 1
        self.streams["pool"].append((waits, fn, ("cc", 1)))

    def barrier(self):
        snap = dict(self.latest)
        for e in self.ENGS:
            pb = self.pending_barrier[e]
            if pb is None:
                self.pending_barrier[e] = dict(snap)
            else:
                for k, v in snap.items():
                    if pb.get(k, 0) < v:
                        pb[k] = v

    def raw(self, eng, fn):
        self.streams[eng].append(([], fn, None))

    def finish(self):
        nc = self.nc
        final = dict(self.latest)
        engmap = {"pe": "tensor", "dve": "vector", "act": "scalar", "pool": "gpsimd", "sp": "sync"}
        with nc.Block() as block:
            for e in self.ENGS:
                stream = self.streams[e]

                def body(h, e=e, stream=stream):
                    sems = self.sems
                    for waits, fn, inc in stream:
                        for k, v in waits:
                            h.wait_ge(sems[k], v)
                        if inc is not None and inc[0] == "SELF":
                            fn(h, sems[inc[1]])
                            continue
                        ins = fn(h)
                        if inc is not None:
                            ins.then_inc(sems[inc[0]], inc[1])
                    for k, v in final.items():
                        if k == e:
                            continue
                        h.wait_ge(sems[k], v)

                getattr(block, engmap[e])(body)
        self.es.close()
        return nc

    @staticmethod
    def _tk(*ops):
        out = []
        for o in ops:
            if isinstance(o, Opd):
                out.extend(o.toks)
        return out

    @staticmethod
    def _ap(o):
        return o.ap if isinstance(o, Opd) else o

    def mm(self, out, lhsT, rhs, start=True, stop=True):
        o, l, r = out.ap, lhsT.ap, rhs.ap
        self.op("pe", lambda e: e.matmul(o, l, r, start=start, stop=stop),
                reads=self._tk(lhsT, rhs), writes=self._tk(out))

    def act(self, out, in_, func, bias=0.0, scale=1.0, accum=None, eng="act"):
        o, i, b, s = out.ap, in_.ap, self._ap(bias), self._ap(scale)
        kw = {}
        if accum is not None:
            kw["accum_out"] = accum.ap
        self.op("act", lambda e: e.activation(o, i, func, bias=b, scale=s, **kw),
                reads=self._tk(in_, bias, scale), writes=self._tk(out, accum))

    def tt(self, eng, out, in0, in1, op):
        o, a, b = out.ap, in0.ap, in1.ap
        self.op(eng, lambda e: e.tensor_tensor(o, a, b, op), reads=self._tk(in0, in1), writes=self._tk(out))

    def ts(self, eng, out, in0, s1, op0, s2=None, op1=None, accum=None):
        o, a, x1, x2 = out.ap, in0.ap, self._ap(s1), self._ap(s2)
        kw = {}
        if accum is not None:
            kw["accum_out"] = accum.ap
        if op1 is None:
            fn = lambda e: e.tensor_scalar(o, a, x1, None, op0, **kw)
        else:
            fn = lambda e: e.tensor_scalar(o, a, x1, x2, op0, op1, **kw)
        self.op(eng, fn, reads=self._tk(in0, s1, s2), writes=self._tk(out, accum))

    def stt(self, out, in0, scalar, in1, op0, op1, eng="dve"):
        o, a, s, b = out.ap, in0.ap, self._ap(scalar), in1.ap
        self.op(eng, lambda e: e.scalar_tensor_tensor(o, a, s, b, op0, op1),
                reads=self._tk(in0, scalar, in1), writes=self._tk(out))

    def copy(self, eng, out, in_):
        o, i = out.ap, in_.ap
        if eng == "act":
            self.op("act", lambda e: e.copy(o, i), reads=self._tk(in_), writes=self._tk(out))
        else:
            self.op(eng, lambda e: e.tensor_copy(o, i), reads=self._tk(in_), writes=self._tk(out))

    def memset(self, eng, out, val):
        o = out.ap
        self.op(eng, lambda e: e.memset(o, val), writes=self._tk(out))

    def recip(self, out, in_):
        o, i = out.ap, in_.ap
        self.op("dve", lambda e: e.reciprocal(o, i), reads=self._tk(in_), writes=self._tk(out))

    def reduce(self, out, in_, op=ALU.add, axis=AX.X):
        o, i = out.ap, in_.ap
        self.op("dve", lambda e: e.tensor_reduce(o, i, axis, op), reads=self._tk(in_), writes=self._tk(out))

    def scan(self, out, d0, d1, initial, op0, op1):
        o, a, b, ini = out.ap, d0.ap, d1.ap, self._ap(initial)
        self.op("dve", lambda e: e.tensor_tensor_scan(o, a, b, ini, op0, op1),
                reads=self._tk(d0, d1, initial), writes=self._tk(out))

    def load(self, out, in_ap, q="sp"):
        self.dma(q, out.ap, in_ap, writes=self._tk(out))

    def store(self, out_ap, in_, q="pool", dram_tok=None):
        w = [dram_tok] if dram_tok is not None else []
        self.dma(q, out_ap, in_.ap, reads=self._tk(in_), writes=w)


(C_ID, C_MEAN, C_ONE, C_M128, C_BLK64, C_TRID_F, C_TRID_B, C_REMD_F, C_REMD_B, C_STR_F, C_STR_B,
 C_INC_F, C_INC_B, C_TRIC_F, C_TRIC_B, C_REMC_F, C_REMC_B, C_ROPE, C_CI16, C_CI64) = range(20)
NCONST = 20


def make_consts():
    c = np.zeros((128, NCONST, 128), np.float32)
    i = np.arange(128)
    P, Q = np.meshgrid(i, i, indexing="ij")
    c[:, C_ID] = (P == Q)
    c[:, C_MEAN] = 1.0 / D
    c[:, C_ONE] = 1.0
    c[:, C_M128] = 1.0 / 128
    c[:, C_BLK64] = (P // 64 == Q // 64) / 64.0
    s64 = (P // 64 == Q // 64)
    s16 = (P // 16 == Q // 16)
    c[:, C_TRID_F] = s64 & (P <= Q)
    c[:, C_TRID_B] = s64 & (P >= Q)
    c[:, C_REMD_F] = s64 & (P > Q)
    c[:, C_REMD_B] = s64 & (P < Q)
    c[:, C_STR_F] = s64 & (Q < P)
    c[:, C_STR_B] = s64 & (Q > P)
    c[:, C_INC_F] = s64 & (Q <= P)
    c[:, C_INC_B] = s64 & (Q >= P)
    c[:, C_TRIC_F] = s16 & (P <= Q)
    c[:, C_TRIC_B] = s16 & (P >= Q)
    c[:, C_REMC_F] = s16 & (P > Q)
    c[:, C_REMC_B] = s16 & (P < Q)
    R = np.zeros((128, 128), np.float32)
    for m in range(128):
        if m % 32 < 16:
            R[m, m + 16] = -1.0
        else:
            R[m, m - 16] = 1.0
    c[:, C_ROPE] = R.T
    c[:, C_CI16, 0:8] = (P[:, 0:8] // 16 == Q[:, 0:8])
    c[:, C_CI64, 0:2] = (P[:, 0:2] // 64 == Q[:, 0:2])
    return c


ARENA_BYTES = 91 * 1024


class Prog:
    def __init__(self, depth=DEPTH, mixers=("A", "B", "C"), dbg=None):
        self.depth = depth
        self.mixers = mixers
        self.dbg = dbg or {}
        self.k = KB()
        self.build()
        self.nc = self.k.finish()

    def declare_io(self):
        nc = self.k.nc
        L = self.depth

        def inp(name, shape, dt=F32):
            return nc.dram_tensor(name, list(shape), dt, kind="ExternalInput").ap()

        def outp(name, shape, dt=F32):
            return nc.dram_tensor(name, list(shape), dt, kind="ExternalOutput").ap()

        io = {}
        io["xT_p"] = inp("xT_p", [D, TOK])
        io["xT_s"] = inp("xT_s", [D, TOK])
        io["cond"] = inp("cond", [128, NCH, 2])
        io["w_mod"] = inp("w_mod", [L, D, 6 * D])
        io["b_mod"] = inp("b_mod", [L, 128, 48])
        io["w_in"] = inp("w_in", [L, D, D_IN])
        io["w_out_p"] = inp("w_out_p", [L, D, D])
        io["w_out_s"] = inp("w_out_s", [L, D, D])
        io["ln_g"] = inp("ln_g", [L, 128, 2, NCH])
        io["ln_b"] = inp("ln_b", [L, 128, 2, NCH])
        io["w_ffn_in"] = inp("w_ffn_in", [L, D, 2 * D_FF])
        io["w_ffn_out"] = inp("w_ffn_out", [L, D_FF, D])
        io["consts"] = inp("consts", [128, NCONST, 128])
        io["w_in_h"] = inp("w_in_h", [L, D, 964])
        io["rope_cs"] = inp("rope_cs", [2, 128, DEC_SEQ])
        io["diff_lambda"] = inp("diff_lambda", [L * 256])
        io["diff_norm"] = inp("diff_norm", [128, L])
        io["ctx_k"] = inp("ctx_k", [L, PAST, 128])
        io["ctx_v"] = inp("ctx_v", [L, PAST, 128])
        io["conv_hm"] = inp("conv_hm", [L, 4, 3 * 192])
        io["conv_h"] = inp("conv_h", [L, 3 * 192])
        io["w_ba_p"] = inp("w_ba_p", [L, D, 4, 4])
        io["dpar"] = inp("dpar", [L * 16])
        io["dpar_h"] = inp("dpar_h", [L * 4])
        io["dnorm"] = inp("dnorm", [L * 64])
        io["s0_d"] = inp("s0_d", [L, 2, 64, 64])
        io["new_sd"] = outp("new_sd", [4, L, 2, 4, 64, 64])
        io["hgrn_lb"] = inp("hgrn_lb", [2 * L * 256])
        io["hgrn_lb_h"] = inp("hgrn_lb_h", [2 * L * 64])
        io["hnorm"] = inp("hnorm", [L, 128])
        io["s0_h"] = inp("s0_h", [L, 2, 64, 64])
        io["new_sh"] = outp("new_sh", [4, L, 2, 4 * 64, 64])
        io["new_k"] = outp("new_k", [4, L, SEQ, 512])
        io["new_v"] = outp("new_v", [4, L, SEQ, 512])
        self.agx_in = [nc.dram_tensor("agx_in%d" % i, [512, TOK], BF16).ap() for i in range(2)]
        self.agx = [nc.dram_tensor("agx%d" % i, [4 * 512, TOK], BF16).ap() for i in range(2)]
        self.ago_in = [nc.dram_tensor("ago_in%d" % i, [128, DEC_SEQ], BF16).ap() for i in range(2)]
        self.ago = [nc.dram_tensor("ago%d" % i, [4 * 128, DEC_SEQ], BF16).ap() for i in range(2)]
        self.t_agx_in, self.t_agx, self.t_ago_in, self.t_ago = Tok(), Tok(), Tok(), Tok()
        io["yT_p"] = outp("yT_p", [D, TOK])
        io["yT_s"] = outp("yT_s", [D, TOK])
        for name, shape in self.dbg.items():
            io[name] = outp(name, shape)
        self.io = io

    def carve(self, off, shape, dt, ntok=1):
        n = 1
        for s in shape[1:]:
            n *= s
        nbytes = n * (2 if dt == BF16 else 4)
        assert off % 4 == 0 and off + nbytes <= ARENA_BYTES, (off, nbytes)
        ap = self.arena_t[:, off // 4:(off + nbytes + 3) // 4]
        if dt == BF16:
            ap = ap.bitcast(BF16)
        if len(shape) == 3:
            ap = ap.rearrange("p (a n) -> p a n", a=shape[1])
        elif len(shape) == 4:
            ap = ap.rearrange("p (a b n) -> p a b n", a=shape[1], b=shape[2])
        return Buf(ap, ntok), off + ((nbytes + 3) // 4) * 4

    def build(self):
        k = self.k
        self.declare_io()
        io = self.io
        L = self.depth
        self.xs = [k.sbuf("xs_p", [128, NCH, TOK], F32, ntok=2), k.sbuf("xs_s", [128, NCH, TOK], F32, ntok=2)]
        self.cst = k.sbuf("cst_sb", [128, NCONST, 128], F32)
        self.oT = k.sbuf("oT", [128, NCH, TOK], BF16, ntok=2)
        self.mod = k.sbuf("mod", [128, L, 48, 2], F32)
        self.lng = k.sbuf("lng", [128, L, 2, NCH], F32)
        self.lnb = k.sbuf("lnb", [128, L, 2, NCH], F32)
        self.ps = Ring([k.psum("ps%d" % i, [128, 512]) for i in range(4)])
        self.acc = [k.psum("acc%d" % i, [128, 512]) for i in range(4)]
        self.wst = k.ring("wst", [128, 2048], F32, 1)
        self.wbf = k.ring("wbf", [128, 2048], BF16, 2)
        self.tmp = k.ring("tmp", [128, 512], F32, 4)
        self.arena_t = self.k.es.enter_context(k.nc.sbuf_tensor("arena", [128, ARENA_BYTES // 4], F32))

        k.load(self.cst[:], io["consts"])
        for g, nm in enumerate(("xT_p", "xT_s")):
            for b in range(2):
                k.load(self.xs[g].at(b)[:, :, b * 512:(b + 1) * 512],
                       io[nm][:, b * 512:(b + 1) * 512].rearrange("(c p) n -> p c n", p=128))
        k.load(self.lng[:], io["ln_g"].rearrange("l p a c -> p l a c"))
        k.load(self.lnb[:], io["ln_b"].rearrange("l p a c -> p l a c"))
        self.preamble_mod()
        self.preamble_small()
        for l in range(L):
            self.layer(l)
        for g, nm in enumerate(("yT_p", "yT_s")):
            for b in range(2):
                k.store(io[nm][:, b * 512:(b + 1) * 512].rearrange("(c p) n -> p c n", p=128),
                        self.xs[g].at(b)[:, :, b * 512:(b + 1) * 512], q="sp")

    def C(self, i, rows=slice(None), cols=slice(None)):
        return self.cst[rows, i, cols]

    def preamble_mod(self):
        k, io = self.k, self.io
        L = self.depth
        k.barrier()
        off = 0
        wblk = []
        for i in range(2):
            b, off = self.carve(off, [128, NCH, 512], F32)
            wblk.append(b)
        cond, off = self.carve(off, [128, NCH, 2], F32)
        csil, off = self.carve(off, [128, NCH, 2], F32)
        bmod, off = self.carve(off, [128, L, 48], F32)
        k.load(cond[:], io["cond"])
        k.load(bmod[:], io["b_mod"].rearrange("l p m -> p l m"))
        k.act(csil[:], cond[:], AF.Silu)
        n = 0
        for l in range(L):
            for cb in range(12):
                w = wblk[n % 2]
                n += 1
                k.load(w[:], io["w_mod"][l, :, cb * 512:(cb + 1) * 512].rearrange("(c p) n -> p c n", p=128))
                ps = self.ps.next()
                for mi in range(4):
                    m = cb * 4 + mi
                    for kc in range(NCH):
                        k.mm(ps[:, 2 * mi:2 * mi + 2], w[:, kc, mi * 128:(mi + 1) * 128], csil[:, kc, :],
                             start=(kc == 0), stop=(kc == NCH - 1))
                k.tt("dve", self.mod[:, l, cb * 4:(cb + 1) * 4, :],
                     Opd(ps.t[:, 0:8].rearrange("p (m j) -> p m j", j=2), ps.toks),
                     Opd(bmod.t[:, l, cb * 4:(cb + 1) * 4].unsqueeze(2).broadcast_to([128, 4, 2]), bmod.toks),
                     ALU.add)
            for a in (8, 32):
                k.ts("dve", self.mod[:, l, a:a + 8, :], self.mod[:, l, a:a + 8, :], 1.0, ALU.add)
            for a in (16, 40):
                k.ts("dve", self.mod[:, l, a:a + 8, :], self.mod[:, l, a:a + 8, :], 1.0 / ALPHA, ALU.mult)
        k.barrier()

    def modv(self, l, which, g, kc):
        return self.mod[:, l, which * 8 + kc, g:g + 1]

    def load_w_bf16(self, dram_ap, rows, cols):
        st = self.wst.next()
        wb = self.wbf.next()
        k = self.k
        if len(dram_ap.shape) == 3:
            a, n = dram_ap.shape[1], dram_ap.shape[2]
            sv = Opd(st.t[:, 0:a * n].rearrange("p (a n) -> p a n", a=a), st.toks)
            wv = View(wb.t[:, 0:a * n].rearrange("p (a n) -> p a n", a=a), wb.toks)
            k.load(sv, dram_ap)
            k.copy("pool", wv[:], sv)
            return wv
        n = dram_ap.shape[1]
        r = dram_ap.shape[0]
        k.load(st[0:r, 0:n], dram_ap)
        k.copy("pool", wb[0:r, 0:n], st[0:r, 0:n])
        return View(wb.t, wb.toks)

    def layer(self, l):
        k = self.k
        self.mixer_phase(l)
        k.barrier()
        off = 0
        self.hT, off = self.carve(off, [128, NF, TOK], BF16, ntok=2)
        self.xm2, off = self.carve(off, [128, NCH, TOK], BF16, ntok=2)
        self.stat = Ring([self.carve(off + i * 2048, [128, 512], F32)[0] for i in range(6)])
        off += 6 * 2048
        self.usq = Ring([self.carve(off + i * 2048, [128, 512], F32)[0] for i in range(3)])
        off += 3 * 2048
        for g in range(2):
            if g == 1:
                self.load_oT_sample()
            self.dense_group(l, g)
        k.barrier()

    def preamble_small(self):
        k, io = self.k, self.io
        L = self.depth
        self.lam = k.sbuf("lam", [128, L], F32)
        self.nlam = k.sbuf("nlam", [128, L], F32)
        self.gA = k.sbuf("gA", [128, L], F32)
        self.onesb = k.sbuf("onesb", [128, 128], BF16)
        k.memset("pool", self.onesb[:, :], 1.0)
        off = 0
        dlb, off = self.carve(off, [128, L, 4, 64], F32)
        pr, off = self.carve(off, [128, L, 2, 64], F32)
        sm, off = self.carve(off, [128, L, 2], F32)
        k.load(dlb[:], io["diff_lambda"].partition_broadcast(128).rearrange("p (l a d) -> p l a d", l=L, a=4))
        k.load(self.gA[:, :], io["diff_norm"])
        for j in range(2):
            k.tt("dve", pr[:, :, j, :], dlb[:, :, 2 * j, :], dlb[:, :, 2 * j + 1, :], ALU.mult)
        k.reduce(sm[:], pr[:])
        k.act(sm[:], sm[:], AF.Exp)
        k.tt("dve", self.lam[:, :], sm[:, :, 0], sm[:, :, 1], ALU.subtract)
        for l in range(L):
            lam_init = 0.8 - 0.6 * math.exp(-0.3 * l)
            k.ts("dve", self.lam[:, l:l + 1], self.lam[:, l:l + 1], lam_init, ALU.add)
            k.ts("dve", self.gA[:, l:l + 1], self.gA[:, l:l + 1], 1.0 - lam_init, ALU.mult)
        k.ts("dve", self.nlam[:, :], self.lam[:, :], -1.0, ALU.mult)
        k.barrier()

    def mixer_phase(self, l):
        k, io = self.k, self.io
        k.barrier()
        off = 0
        self.xmp, off = self.carve(off, [128, NCH, 4, 258], BF16, ntok=4)
        self.slab = []
        for i in range(2):
            b, off = self.carve(off, [128, NCH, 514], BF16)
            self.slab.append(b)
        self.moff = off
        k.memset("pool", self.xmp.all()[:, :, :, 0:1], 0.0)
        k.memset("pool", self.xmp.all()[:, :, :, 257:258], 0.0)
        for sq in range(4):
            for kc in range(NCH):
                k.act(self.xmp.at(sq)[:, kc, sq, 1:257], self.xs[0].at(sq // 2)[:, kc, sq * 256:(sq + 1) * 256],
                      AF.Identity, scale=self.modv(l, 1, 0, kc), bias=self.modv(l, 0, 0, kc))
        for b in range(2):
            for kc in range(NCH):
                k.ts("dve", self.slab[b][:, kc, 0:512], self.xs[1].at(b)[:, kc, b * 512:(b + 1) * 512],
                     self.modv(l, 1, 1, kc), ALU.mult, self.modv(l, 0, 1, kc), ALU.add)
            for hf in range(2):
                k.dma("sp", self.agx_in[hf][:, b * 512:(b + 1) * 512].rearrange("(c p) n -> p c n", p=128),
                      self.slab[b].t[:, 4 * hf:4 * hf + 4, 0:512], reads=self.slab[b].toks, writes=[self.t_agx_in])
        for hf in range(2):
            ain, aout = self.agx_in[hf], self.agx[hf]
            k.cc(lambda e, ain=ain, aout=aout: e.collective_compute("AllGather", ALU.bypass, replica_groups=GROUPS,
                                                                    ins=[ain], outs=[aout]),
                 reads=[self.t_agx_in], writes=[self.t_agx])
        if "A" in self.mixers:
            self.mixA_sample(l)
            k.barrier()
            self.mixA_prompt(l)
            k.barrier()
        else:
            self.zero_o(l, 0, 4, 0, 128)
        if "B" in self.mixers:
            self.mixB(l)
            k.barrier()
        else:
            self.zero_o(l, 4, 6, 128, 192)
        if "C" in self.mixers:
            self.mixC(l)
            k.barrier()
        else:
            self.zero_o(l, 6, 8, 192, 256)
        for hf in range(2):
            gin, gout = self.ago_in[hf], self.ago[hf]
            k.cc(lambda e, gin=gin, gout=gout: e.collective_compute("AllGather", ALU.bypass, replica_groups=GROUPS,
                                                                    ins=[gin], outs=[gout]),
                 reads=[self.t_ago_in], writes=[self.t_ago])

    def zero_o(self, l, c0, c1, r0, r1):
        k = self.k
        z = self.tmp.next()
        k.memset("pool", z[:, :], 0.0)
        zb = Opd(z.t[:, 0:256].bitcast(BF16), z.toks)
        for blk in range(8):
            k.dma("sp", self.ago_in[r0 // 128][r0 % 128:r0 % 128 + (r1 - r0), blk * 512:(blk + 1) * 512], zb.ap[0:r1 - r0, :],
                  reads=z.toks, writes=[self.t_ago_in])
        for b in range(2):
            k.memset("pool", self.oT.at(b)[:, c0:c1, b * 512:(b + 1) * 512], 0.0)

    def load_oT_sample(self):
        k = self.k
        ago = self.ago
        oT = self.oT

        def fn(e, sem, HF):
            core = e.partition_id()
            for c in range(8):
                r = c % 4
                with e.If(core == c):
                    e.dma_start(out=oT.t[:, HF::2, :], in_=ago[HF][:, r * TOK:(r + 1) * TOK].rearrange("(r p) n -> p r n", p=128)).then_inc(sem, 16)
        for HF in range(2):
            k.dma("pool", None, None, reads=[self.t_ago], writes=oT.toks, fn=(lambda e, sem, HF=HF: fn(e, sem, HF)), selfinc=True)

    def load_slab(self, buf, b, halo=False):
        k = self.k
        r, hf = b // 2, b % 2
        agx3 = [a.rearrange("(r f) n -> r f n", r=4) for a in self.agx]
        for fh in range(2):
            k.dma("sp", buf.t[:, 4 * fh:4 * fh + 4, 1:513], agx3[fh][r, :, hf * 512:(hf + 1) * 512].rearrange("(c p) n -> p c n", p=128),
                  reads=[self.t_agx], writes=buf.toks)
        if halo:
            for side, g0, col in ((0, b * 512 - 1, 0), (1, (b + 1) * 512, 513)):
                if g0 < 0 or g0 >= DEC_SEQ:
                    k.memset("pool", buf[:, :, col:col + 1], 0.0)
                else:
                    rr, cc = g0 // TOK, g0 % TOK
                    for fh in range(2):
                        o_ = buf.t[:, 4 * fh:4 * fh + 4, col:col + 1]
                        i_ = agx3[fh][rr, :, cc:cc + 1].rearrange("(c p) n -> p c n", p=128)
                        k.dma("sp", None, None, reads=[self.t_agx], writes=buf.toks,
                              fn=lambda e, o_=o_, i_=i_: e.dma_start(out=o_, in_=i_, allow_slow_non_contiguous=True))

    def wtile(self, dram_ap):
        return self.load_w_bf16(dram_ap.rearrange("(c p) n -> p c n", p=128), 128, dram_ap.shape[1])

    def attn_core(self, l, qT, kT, V, nkt, q0, nq, out_fn, Pt, o0, o1, rr):
        k = self.k
        om = [o0, o1]
        for m in range(2):
            rows = slice(64 * m, 64 * m + 64)
            psO, psR = self.acc[2 * m], self.acc[2 * m + 1]
            for kt in range(nkt):
                psS = self.ps.next()
                k.mm(psS[:, 0:nq], kT[rows, kt * 128:(kt + 1) * 128], qT[rows, q0:q0 + nq])
                P = Pt.next()
                k.act(P[:, 0:nq], psS[:, 0:nq], AF.Exp, scale=0.125)
                k.mm(psO[:, 0:nq], V[:, kt, :], P[:, 0:nq], start=(kt == 0), stop=(kt == nkt - 1))
                k.mm(psR[:, 0:nq], self.onesb[:, :], P[:, 0:nq], start=(kt == 0), stop=(kt == nkt - 1))
            k.recip(rr[:, 0:nq], psR[:, 0:nq])
            k.tt("dve", om[m][:, 0:nq], psO[:, 0:nq], rr[:, 0:nq], ALU.mult)
        k.stt(o0[:, 0:nq], o1[:, 0:nq], self.nlam[:, l:l + 1], o0[:, 0:nq], ALU.mult, ALU.add)
        k.act(o1[:, 0:nq], o0[:, 0:nq], AF.Square)
        psN = self.ps.next()
        k.mm(psN[:, 0:nq], self.C(C_M128), o1[:, 0:nq])
        k.act(rr[:, 0:nq], psN[:, 0:nq], AF.Sqrt, bias=1e-6)
        k.recip(rr[:, 0:nq], rr[:, 0:nq])
        k.tt("dve", o0[:, 0:nq], o0[:, 0:nq], rr[:, 0:nq], ALU.mult)
        out_fn(o0)

    def mixA_sample(self, l):
        k, io = self.k, self.io
        off = self.moff
        qT, off = self.carve(off, [128, DEC_SEQ], BF16)
        kT, off = self.carve(off, [128, DEC_SEQ + PAST], BF16)
        V, off = self.carve(off, [128, 34, 128], BF16)
        Pt = Ring([self.carve(off + i * 1024, [128, 512], BF16)[0] for i in range(3)]); off += 3 * 1024
        xf = Ring([self.carve(off + i * 2048, [128, 512], F32)[0] for i in range(3)]); off += 3 * 2048
        o0, off = self.carve(off, [128, 512], F32)
        o1, off = self.carve(off, [128, 512], F32)
        rr, off = self.carve(off, [128, 512], F32)
        osb = Ring([self.carve(off + i * 1024, [128, 512], BF16)[0] for i in range(2)]); off += 2 * 1024
        cst = Ring([self.carve(off + i * 1024, [128, 2, 128], F32)[0] for i in range(2)]); off += 2 * 1024
        wqk = self.wtile(io["w_in_h"][l, :, 0:256])
        wv = self.wtile(io["w_in_h"][l, :, 256:384])
        for j in range(2):
            c = cst.next()
            k.load(c[:, 0, :], io["ctx_k"][l, j * 128:(j + 1) * 128, :])
            k.load(c[:, 1, :], io["ctx_v"][l, j * 128:(j + 1) * 128, :])
            ps = self.ps.next()
            k.mm(ps[:, 0:128], c[:, 0, :], self.C(C_ID))
            k.copy("act", kT[:, DEC_SEQ + j * 128:DEC_SEQ + (j + 1) * 128], ps[:, 0:128])
            k.copy("dve", V[:, 32 + j, :], c[:, 1, :])
        for b in range(8):
            sb = self.slab[b % 2]
            self.load_slab(sb, b)
            cos = self.tmp.next()
            sin = self.tmp.next()
            k.load(cos[:, :], io["rope_cs"][0, :, b * 512:(b + 1) * 512])
            k.load(sin[:, :], io["rope_cs"][1, :, b * 512:(b + 1) * 512])
            for which, dst in ((0, qT), (1, kT)):
                ps = self.ps.next()
                for kc in range(NCH):
                    k.mm(ps[:, :], wqk[:, kc, which * 128:(which + 1) * 128], sb[:, kc, 1:513], start=(kc == 0), stop=(kc == NCH - 1))
                x = xf.next()
                k.copy("act", x[:, :], ps[:, :])
                ps2 = self.ps.next()
                k.mm(ps2[:, :], self.C(C_ROPE), x[:, :])
                t1 = xf.next()
                k.tt("dve", t1[:, :], x[:, :], cos[:, :], ALU.mult)
                k.tt("dve", x[:, :], ps2[:, :], sin[:, :], ALU.mult)
                k.tt("dve", dst[:, b * 512:(b + 1) * 512], t1[:, :], x[:, :], ALU.add)
            for j in range(4):
                ps = self.ps.next()
                for kc in range(NCH):
                    k.mm(ps[:, 0:128], sb[:, kc, 1 + j * 128:1 + (j + 1) * 128], wv[:, kc, :], start=(kc == 0), stop=(kc == NCH - 1))
                k.copy("act", V[:, b * 4 + j, :], ps[:, 0:128])
        qv, kv, vv = View(qT.t, qT.toks), View(kT.t, kT.toks), View(V.t, V.toks)
        for b in range(8):
            def out_fn(o, b=b):
                ob = osb.next()
                k.ts("dve", ob[:, :], o[:, :], self.gA[:, l:l + 1], ALU.mult)
                k.dma("sp", self.ago_in[0][0:128, b * 512:(b + 1) * 512], ob.t[:, :], reads=ob.toks, writes=[self.t_ago_in])
            self.attn_core(l, qv, kv, vv, 34, b * 512, 512, out_fn, Pt, o0, o1, rr)

    def mixA_prompt(self, l):
        k, io = self.k, self.io
        off = self.moff
        Vp, off = self.carve(off, [128, 8, 512], BF16)
        qk = Ring([self.carve(off + i * 1024, [128, 2, 256], BF16)[0] for i in range(2)]); off += 2 * 1024
        Pt = Ring([self.carve(off + i * 1024, [128, 512], BF16)[0] for i in range(3)]); off += 3 * 1024
        o0, off = self.carve(off, [128, 512], F32)
        o1, off = self.carve(off, [128, 512], F32)
        rr, off = self.carve(off, [128, 512], F32)
        stg = Ring([self.carve(off + i * 1024, [128, 256], F32)[0] for i in range(3)]); off += 3 * 1024
        for cc in range(4):
            w = self.wtile(io["w_in"][l, :, O_AK + cc * 256:O_AK + (cc + 1) * 256])
            for t in range(8):
                sq, i = t // 2, t % 2
                ps = self.ps.next()
                for kc in range(NCH):
                    k.mm(ps[:, 0:256], self.xmp.at(sq)[:, kc, sq, 1 + i * 128:1 + (i + 1) * 128], w[:, kc, :],
                         start=(kc == 0), stop=(kc == NCH - 1))
                sg = stg.next()
                k.copy("act", sg[:, :], ps[:, 0:256])
                dst = io["new_k"] if cc < 2 else io["new_v"]
                c0 = (cc % 2) * 256
                k.dma("sp", dst[sq, l, i * 128:(i + 1) * 128, c0:c0 + 256], sg.t[:, :], reads=sg.toks)
                if cc >= 2:
                    k.copy("dve", Vp[:, t, c0:c0 + 256], sg[:, :])
        for h in range(4):
            w = self.wtile(io["w_in"][l, :, O_AQ + h * 128:O_AQ + (h + 1) * 128])
            w2 = self.wtile(io["w_in"][l, :, O_AK + h * 128:O_AK + (h + 1) * 128])
            for sq in range(4):
                qb = qk.next()
                for which, ww in ((0, w), (1, w2)):
                    ps = self.ps.next()
                    for kc in range(NCH):
                        k.mm(ps[:, 0:256], ww[:, kc, :], self.xmp.at(sq)[:, kc, sq, 1:257], start=(kc == 0), stop=(kc == NCH - 1))
                    k.copy("act", qb[:, which, :], ps[:, 0:256])
                qv = View(qb.t[:, 0, :], qb.toks)
                kv = View(qb.t[:, 1, :], qb.toks)
                vv = View(Vp.t[:, 2 * sq:2 * sq + 2, h * 128:(h + 1) * 128], Vp.toks)

                def out_fn(o, h=h, sq=sq):
                    k.ts("dve", self.oT.at(sq // 2)[:, h, sq * 256:(sq + 1) * 256], o[:, 0:256], self.gA[:, l:l + 1], ALU.mult)
                self.attn_core(l, qv, kv, vv, 2, 0, 256, out_fn, Pt, o0, o1, rr)

    def interleave(self, fixed, queue, nslots=2):
        fixed = list(fixed)
        queue = list(queue)
        slots = []
        while fixed or slots or queue:
            if not slots:
                while len(slots) < nslots and queue:
                    slots.append(queue.pop(0))
            for lst in (fixed, slots):
                for g in list(lst):
                    try:
                        next(g)
                    except StopIteration:
                        lst.remove(g)

    def lockstep(self, gens):
        gens = list(gens)
        while gens:
            for g in list(gens):
                try:
                    next(g)
                except StopIteration:
                    gens.remove(g)

    def src_prompt(self, sq):
        def f(tile, kc, shift=0):
            c0 = 1 + tile * 128 + shift
            return self.xmp.at(sq)[:, kc, sq, c0:c0 + 128]
        return f

    def src_prompt_fm(self, sq):
        return lambda kc: self.xmp.at(sq)[:, kc, sq, 1:257]

    def sample_src(self, order, halo):
        state = {"b": None, "buf": None}
        slab = self.slab_ring

        def f(tile, kc, shift=0):
            b = tile // 4
            if state["b"] != b:
                state["b"] = b
                state["buf"] = slab.next()
                self.load_slab(state["buf"], b, halo=halo)
            c0 = 1 + (tile % 4) * 128 + shift
            return state["buf"][:, kc, c0:c0 + 128]
        return f

    def mixB(self, l):
        k, io = self.k, self.io
        L = self.depth
        off = self.moff
        self.slab_ring = Ring(self.slab)
        dpp, off = self.carve(off, [128, 4, 2, 2], F32)
        dps, off = self.carve(off, [128, 2, 2], F32)
        dnr, off = self.carve(off, [128, 64], F32)
        k.load(dpp[:], io["dpar"][l * 16:(l + 1) * 16].partition_broadcast(128).rearrange("p (h d a) -> p h d a", h=4, d=2))
        k.load(dps[:], io["dpar_h"][l * 4:(l + 1) * 4].partition_broadcast(128).rearrange("p (d a) -> p d a", d=2))
        k.load(dnr[:, :], io["dnorm"][l * 64:(l + 1) * 64].partition_broadcast(128))
        k.act(dpp[:, :, :, 0:1], dpp[:, :, :, 0:1], AF.Exp)
        k.ts("dve", dpp[:, :, :, 0:1], dpp[:, :, :, 0:1], -1.0, ALU.mult)
        k.act(dps[:, :, 0:1], dps[:, :, 0:1], AF.Exp)
        k.ts("dve", dps[:, :, 0:1], dps[:, :, 0:1], -1.0, ALU.mult)
        cwb, off = self.carve(off, [128, 576], F32)
        wset = {}
        for nm in ("p", "s"):
            wc, off = self.carve(off, [128, 3, NCH, 192], BF16)
            wba, off = self.carve(off, [128, NCH, 4], BF16)
            wset[nm] = (wc, wba)

        def fold(wc, wba, qkv_srcs, ba_src, cw_src):
            k.load(cwb[:, :], cw_src.partition_broadcast(128))
            cw = cwb.t
            st = self.wst.next()
            sv = View(st.t[:, 0:NCH * 192].rearrange("p (c n) -> p c n", c=NCH), st.toks)
            for j, src in enumerate(qkv_srcs):
                n = src.shape[1]
                k.load(sv[:, :, j * (192 // len(qkv_srcs)):j * (192 // len(qkv_srcs)) + n], src.rearrange("(c p) n -> p c n", p=128))
            for tap in range(3):
                k.tt("pool", wc[:, tap, :, :], sv[:], Opd(cw[:, tap * 192:(tap + 1) * 192].unsqueeze(1).broadcast_to([128, NCH, 192]), cwb.toks),
                     ALU.mult)
            st2 = self.wst.next()
            sv2 = View(st2.t[:, 0:NCH * 4].rearrange("p (c n) -> p c n", c=NCH), st2.toks)
            k.load(sv2[:], ba_src)
            k.copy("pool", wba[:], sv2[:])

        fold(wset["s"][0], wset["s"][1], [io["w_in_h"][l, :, 384:576]],
             io["w_in_h"][l, :, 640:644].rearrange("(c p) n -> p c n", p=128), io["conv_h"][l])
        wcur = {"h": None}

        def getw(h):
            if wcur["h"] != h:
                wcur["h"] = h
                fold(wset["p"][0], wset["p"][1],
                     [io["w_in"][l, :, O_BQKV + j * 256 + h * 64:O_BQKV + j * 256 + (h + 1) * 64] for j in range(3)],
                     io["w_ba_p"][l, :, h, :].rearrange("(c p) n -> p c n", p=128), io["conv_hm"][l, h])
            return wset["p"]
        Oacc_s, off = self.carve(off, [128, 32, 64], F32, ntok=32)
        Oacc_p, off = self.carve(off, [128, 8, 256], F32, ntok=32)
        self.b_base = off
        R = {}
        for nm, w, n in (("qkv", 192, 2), ("et", 192, 1), ("sq", 128, 1), ("sm", 16, 4), ("kbe", 64, 2), ("vb", 64, 2),
                         ("kd", 64, 2), ("qd", 64, 2), ("gb", 64, 2), ("diag", 128, 1), ("dec", 128, 1),
                         ("dS", 128, 1), ("dI", 128, 1), ("Nm", 128, 2), ("QKm", 128, 2), ("NQT", 256, 2),
                         ("Xr", 128, 3), ("U", 192, 2), ("vnew", 64, 2), ("S", 64, 16)):
            R[nm] = Ring([self.carve(off + i * w * 4, [128, w], F32)[0] for i in range(n)])
            off += n * w * 4
        tb = self.tmp.bufs
        R["T3"] = Ring([tb[0], tb[1]])
        pp3 = Buf(tb[2].t, 1)
        R["PP"] = Ring([Buf(tb[2].t[:, 0:256], 1), Buf(tb[2].t[:, 256:512], 1), Buf(tb[3].t[:, 0:256], 1)])
        ID, ONE = self.C(C_ID), self.C(C_ONE)

        def job(d, src, ntiles, wc, wba, dpar, x0_fn, o_fn, fin_fn):
            order = list(range(ntiles)) if d == 0 else list(range(ntiles - 1, -1, -1))
            tri, rem, cstr, cinc = ((C_TRID_F, C_REMD_F, C_STR_F, C_INC_F) if d == 0 else
                                    (C_TRID_B, C_REMD_B, C_STR_B, C_INC_B))
            S = R["S"].next()
            x0_fn(S)
            for tile in order:
                ps1 = self.ps.next()
                n = 0
                for tap in range(3):
                    for kc in range(NCH):
                        k.mm(ps1[:, 0:192], src(tile, kc, tap - 1), wc[:, tap, kc, :], start=(n == 0), stop=(n == 23))
                        n += 1
                for kc in range(NCH):
                    k.mm(ps1[:, 192:196], src(tile, kc), wba[:, kc, :], start=(kc == 0), stop=(kc == NCH - 1))
                qkv, et, sq, sm = R["qkv"].next(), R["et"].next(), R["sq"].next(), R["sm"].next()
                k.act(et[:, :], ps1[:, 0:192], AF.Exp, scale=-1.0)
                k.ts("dve", et[:, :], et[:, :], 1.0, ALU.add)
                k.recip(et[:, :], et[:, :])
                k.tt("dve", qkv[:, :], ps1[:, 0:192], et[:, :], ALU.mult)
                k.tt("dve", sq[:, :], qkv[:, 0:128], qkv[:, 0:128], ALU.mult)
                k.reduce(sm[:, 0:2], Opd(sq.t[:, :].rearrange("p (a e) -> p a e", a=2), sq.toks))
                k.act(sm[:, 8:10], sm[:, 0:2], AF.Sqrt, bias=1e-6)
                k.recip(sm[:, 8:10], sm[:, 8:10])
                k.ts("dve", qkv[:, 0:64], qkv[:, 0:64], sm[:, 8:9], ALU.mult, 0.125, ALU.mult)
                k.ts("dve", qkv[:, 64:128], qkv[:, 64:128], sm[:, 9:10], ALU.mult)
                k.act(sm[:, 2:3], ps1[:, 192 + d:193 + d], AF.Exp, scale=-1.0)
                k.ts("dve", sm[:, 2:3], sm[:, 2:3], 1.0, ALU.add)
                k.recip(sm[:, 2:3], sm[:, 2:3])
                k.ts("dve", sm[:, 3:4], sm[:, 2:3], -1.0, ALU.mult)
                k.act(sm[:, 4:5], ps1[:, 194 + d:195 + d], AF.Exp, bias=dpar[:, d, 1:2])
                k.act(sm[:, 4:5], sm[:, 4:5], AF.Ln, bias=1.0)
                k.ts("dve", sm[:, 4:5], sm[:, 4:5], dpar[:, d, 0:1], ALU.mult)
                k.copy("dve", sm[:, 5:6], sm[:, 4:5])
                gb = R["gb"].next()
                k.copy("dve", gb[:, :], Opd(sm.t[:, 4:5].broadcast_to([128, 64]), sm.toks))
                psc = self.ps.next()
                k.mm(psc[:, 0:2], self.C(tri), sm[:, 4:6])
                k.mm(psc[:, 2:4], self.C(rem), sm[:, 4:6])
                k.mm(psc[0:64, 4:6], gb[:, :], self.C(C_CI64, cols=slice(0, 2)))
                k.copy("act", sm[:, 6:7], psc[:, 0:1])
                k.act(sm[:, 12:16], psc[:, 0:4], AF.Exp)
                k.act(sm[0:64, 10:12], psc[0:64, 4:6], AF.Exp)
                kbe, vb, kd, qd = R["kbe"].next(), R["vb"].next(), R["kd"].next(), R["qd"].next()
                k.ts("dve", kbe[:, :], qkv[:, 64:128], sm[:, 2:3], ALU.mult, sm[:, 12:13], ALU.mult)
                k.act(vb[:, :], qkv[:, 128:192], AF.Identity, scale=sm[:, 2:3])
                k.act(kd[:, :], qkv[:, 64:128], AF.Identity, scale=sm[:, 14:15])
                k.ts("dve", qd[:, :], qkv[:, 0:64], sm[:, 12:13], ALU.mult)
                pst = self.ps.next()
                k.mm(pst[0:64, 0:128], qkv[:, 64:128], ID)
                k.mm(pst[0:64, 128:256], qkv[:, 0:64], ID)
                k.mm(pst[0:64, 256:384], qd[:, :], ID)
                T3 = R["T3"].next()
                k.copy("act", T3[0:64, 0:384], pst[0:64, 0:384])
                knT, qnT, qdT = View(T3.t[0:64, 0:128], T3.toks), View(T3.t[0:64, 128:256], T3.toks), View(T3.t[0:64, 256:384], T3.toks)
                diag = R["diag"].next()
                k.act(diag[:, :], ID, AF.Identity, scale=sm[:, 6:7])
                psG = self.ps.next()
                k.mm(psG[:, 0:128], knT[:, :], knT[:, :])
                k.mm(psG[:, 128:256], qnT[:, :], knT[:, :])
                k.mm(psG[:, 256:384], ONE, diag[:, :])
                dec, dS, dI, Nm, QKm = (R[x].next() for x in ("dec", "dS", "dI", "Nm", "QKm"))
                k.ts("dve", dec[:, :], psG[:, 256:384], sm[:, 6:7], ALU.subtract, 0.0, ALU.max)
                k.act(dec[:, :], dec[:, :], AF.Exp, scale=-1.0)
                k.tt("pool", dS[:, :], dec[:, :], self.C(cstr), ALU.mult)
                k.tt("pool", dI[:, :], dec[:, :], self.C(cinc), ALU.mult)
                k.stt(Nm[:, :], dS[:, :], sm[:, 3:4], psG[:, 0:128], ALU.mult, ALU.mult)
                k.tt("dve", QKm[:, :], dI[:, :], psG[:, 128:256], ALU.mult)
                psT = self.ps.next()
                k.mm(psT[:, 0:128], Nm[:, :], ID)
                k.mm(psT[:, 128:256], QKm[:, :], ID)
                NQT = R["NQT"].next()
                k.copy("act", NQT[:, :], psT[:, 0:256])
                QKT = View(NQT.t[:, 128:256], NQT.toks)
                P, PT = View(Nm.t, Nm.toks), View(NQT.t[:, 0:128], NQT.toks)
                X = R["Xr"].next()
                k.tt("dve", X[:, :], PT[:, :], ID, ALU.add)
                for j in range(5):
                    psD = self.ps.next()
                    k.mm(psD[:, 0:128], PT[:, :], P[:, :])
                    if j < 4:
                        k.mm(psD[:, 128:256], P[:, :], PT[:, :])
                    PP = R["PP"].next()
                    k.copy("act", PP[:, 0:(256 if j < 4 else 128)], psD[:, 0:(256 if j < 4 else 128)])
                    P, PT = View(PP.t[:, 0:128], PP.toks), View(PP.t[:, 128:256], PP.toks)
                    psX = self.ps.next()
                    k.mm(psX[:, 0:128], ID, X[:, :], start=True, stop=False)
                    k.mm(psX[:, 0:128], P[:, :], X[:, :], start=False, stop=True)
                    X2 = R["Xr"].next()
                    k.copy("dve", X2[:, :], psX[:, 0:128])
                    X = X2
                psU = self.ps.next()
                k.mm(psU[:, 0:64], X[:, :], vb[:, :])
                k.mm(psU[0:64, 64:192], kbe[:, :], X[:, :])
                U = R["U"].next()
                k.copy("act", U[:, 0:64], psU[:, 0:64])
                k.copy("act", U[0:64, 64:192], psU[0:64, 64:192])
                for ci in ((0, 1) if d == 0 else (1, 0)):
                    r = slice(64 * ci, 64 * ci + 64)
                    psa = self.ps.next()
                    psb = self.ps.next()
                    k.mm(psa[r, 0:64], U[0:64, 64 + 64 * ci:128 + 64 * ci], S[0:64, :])
                    k.mm(psa[r, 64:128], qdT[:, 64 * ci:64 * ci + 64], S[0:64, :])
                    vnew = R["vnew"].next()
                    k.tt("dve", vnew[r, :], U[r, 0:64], psa[r, 0:64], ALU.subtract)
                    k.mm(psb[r, 0:64], QKT[r, 64 * ci:64 * ci + 64], vnew[r, :])
                    k.mm(psb[0:64, 64:128], kd[r, :], vnew[r, :])
                    S2 = R["S"].next()
                    k.stt(S2[0:64, :], S[0:64, :], sm[0:64, 10 + ci:11 + ci], psb[0:64, 64:128], ALU.mult, ALU.add)
                    o_fn(tile, r, psa, psb)
                    S = S2
                yield
            if BSTOP >= 4:
                fin_fn(S)

        done_s = set()

        def x0_sample(d):
            return lambda S: k.load(S[0:64, :], io["s0_d"][l, d])

        def o_sample(tile, r, psa, psb):
            dst = Oacc_s.at(tile)[r, tile, :]
            key = (tile, r.start)
            if key in done_s:
                k.tt("dve", dst, psa[r, 64:128], dst, ALU.add)
            else:
                done_s.add(key)
                k.copy("act", dst, psa[r, 64:128])
            k.tt("dve", dst, psb[r, 0:64], dst, ALU.add)

        sj = [job(d, self.sample_src(None, True), 32, wset["s"][0], wset["s"][1], dps, x0_sample(d), o_sample, lambda S: None)
              for d in range(2)]
        done_p = set()

        def mk_prompt(sq, d, h):
            def x0(S):
                k.memset("dve", S[0:64, :], 0.0)

            def o_fn(tile, r, psa, psb):
                t = sq * 2 + tile
                dst = Oacc_p.at(t * 4 + h)[r, t, h * 64:(h + 1) * 64]
                key = (t, h, r.start)
                if key in done_p:
                    k.tt("dve", dst, psa[r, 64:128], dst, ALU.add)
                else:
                    done_p.add(key)
                    k.copy("act", dst, psa[r, 64:128])
                k.tt("dve", dst, psb[r, 0:64], dst, ALU.add)

            def fin(S):
                k.dma("sp", io["new_sd"][sq, l, d, h], S.t[0:64, :], reads=S.toks)

            def gen():
                wc, wba = getw(h)
                yield from job(d, self.src_prompt(sq), 2, wc, wba, View(dpp.t[:, h], dpp.toks), x0, o_fn, fin)
            return gen()

        pj = [mk_prompt(sq, d, h) for h in range(4) for sq in range(4) for d in range(2)]
        self.interleave(sj, pj, nslots=2)

        if BSTOP < 5:
            self.zero_o(l, 4, 6, 128, 192)
            return
        k.barrier()
        off = self.b_base
        P1 = {}
        for nm, w, n in (("sq", 64, 2), ("ss", 4, 2), ("e", 64, 2), ("o", 64, 2)):
            P1[nm] = Ring([self.carve(off + i * w * 4, [128, w], F32)[0] for i in range(n)])
            off += n * w * 4
        osb = Ring([self.carve(off + i * 1024, [128, 512], BF16)[0] for i in range(2)]); off += 2048

        def post(ov, gate_mm, prow):
            sq_, ss, e, o = (P1[x].next() for x in ("sq", "ss", "e", "o"))
            k.act(sq_[:, :], ov, AF.Square, accum=ss[:, 0:1])
            k.act(ss[:, 1:2], ss[:, 0:1], AF.Sqrt, bias=1e-6, scale=1.0 / 64)
            k.recip(ss[:, 1:2], ss[:, 1:2])
            psG = self.ps.next()
            gate_mm(psG)
            k.act(e[:, :], psG[:, 0:64], AF.Exp, scale=-1.0)
            k.ts("dve", e[:, :], e[:, :], 1.0, ALU.add)
            k.recip(e[:, :], e[:, :])
            k.tt("dve", e[:, :], e[:, :], psG[:, 0:64], ALU.mult)
            k.stt(o[:, :], ov, ss[:, 1:2], dnr[:, :], ALU.mult, ALU.mult)
            k.tt("dve", o[:, :], o[:, :], e[:, :], ALU.mult)
            psT = self.ps.next()
            k.mm(psT[prow, 0:128], o[:, :], ID)
            return psT

        for h in range(4):
            wg = self.wtile(io["w_in"][l, :, O_BG + h * 64:O_BG + (h + 1) * 64])
            prow = slice(64 * (h % 2), 64 * (h % 2) + 64)
            for t in range(8):
                sq, tile = t // 2, t % 2
                ov = Oacc_p.at(t * 4 + h)[:, t, h * 64:(h + 1) * 64]

                def gate_mm(ps, sq=sq, tile=tile, wg=wg):
                    sp = self.src_prompt(sq)
                    for kc in range(NCH):
                        k.mm(ps[:, 0:64], sp(tile, kc), wg[:, kc, 0:64], start=(kc == 0), stop=(kc == NCH - 1))
                psT = post(ov, gate_mm, prow)
                k.copy("act", self.oT.at(t // 4)[prow, 4 + h // 2, t * 128:(t + 1) * 128], psT[prow, 0:128])
        wg = self.wtile(io["w_in_h"][l, :, 576:640])
        ssrc = self.sample_src(None, False)
        for b in range(8):
            ob = osb.next()
            for j in range(4):
                tile = b * 4 + j
                ov = Oacc_s.at(tile)[:, tile, :]

                def gate_mm(ps, tile=tile):
                    for kc in range(NCH):
                        k.mm(ps[:, 0:64], ssrc(tile, kc), wg[:, kc, 0:64], start=(kc == 0), stop=(kc == NCH - 1))
                psT = post(ov, gate_mm, slice(0, 64))
                k.copy("act", ob[0:64, j * 128:(j + 1) * 128], psT[0:64, 0:128])
            k.dma("sp", self.ago_in[1][0:64, b * 512:(b + 1) * 512], ob.t[0:64, :], reads=ob.toks, writes=[self.t_ago_in])

    def mixC(self, l):
        k, io = self.k, self.io
        L = self.depth
        off = self.moff
        self.slab_ring = Ring(self.slab)
        lbs = {}
        scr = ARENA_BYTES - (2 * L * 256 * 4 + 2 * 256 * 4)
        for nm, Wd in (("hgrn_lb", 256), ("hgrn_lb_h", 64)):
            e, o2 = self.carve(scr, [128, 2, L, Wd], F32)
            tot, o2 = self.carve(o2, [128, 2, Wd], F32)
            lb, off = self.carve(off, [128, 2, Wd], F32)
            om, off = self.carve(off, [128, 2, Wd], F32)
            k.load(e[:], io[nm].partition_broadcast(128).rearrange("p (d l w) -> p d l w", d=2, l=L))
            k.act(e[:], e[:], AF.Exp)
            k.copy("dve", tot[:], e[:, :, 0, :])
            for j in range(1, L):
                k.tt("dve", tot[:], tot[:], e[:, :, j, :], ALU.add)
            k.recip(tot[:], tot[:])
            if l == 0:
                k.memset("dve", lb[:], 0.0)
            else:
                k.copy("dve", lb[:], e[:, :, 1, :])
                for j in range(2, l + 1):
                    k.tt("dve", lb[:], lb[:], e[:, :, j, :], ALU.add)
                k.tt("dve", lb[:], lb[:], tot[:], ALU.mult)
            k.ts("dve", om[:], lb[:], -1.0, ALU.mult, 1.0, ALU.add)
            lbs[nm] = (lb, om)
            k.barrier()
        hn = self.carve(off, [128, 1], F32)[0]; off += 64
        k.load(hn[:, :], io["hnorm"][l].rearrange("(p o) -> p o", o=1))
        def wS(c0, n):
            b, _ = self.carve(wS.off, [128, NCH, n], BF16)
            wS.off += NCH * n * 2
            st = self.wst.next()
            sv = Opd(st.t[:, 0:NCH * n].rearrange("p (c n) -> p c n", c=NCH), st.toks)
            k.load(sv, c0.rearrange("(c p) n -> p c n", p=128))
            k.copy("pool", b[:], sv)
            return b
        wS.off = off
        ws_q = wS(io["w_in_h"][l, :, 644:708], 64)
        ws_f = [wS(io["w_in_h"][l, :, 708 + 64 * d:772 + 64 * d], 64) for d in range(2)]
        ws_i = wS(io["w_in_h"][l, :, 836:900], 64)
        wpb = [self.carve(wS.off + i * 2048, [128, NCH, 128], BF16)[0] for i in range(4)]
        wS.off += 4 * 2048
        wp_cur = {"pr": None}

        def getw(pr):
            if wp_cur["pr"] != pr:
                wp_cur["pr"] = pr
                srcs = [io["w_in"][l, :, O_CQ + pr * 128:O_CQ + (pr + 1) * 128],
                        io["w_in"][l, :, O_CF + pr * 128:O_CF + (pr + 1) * 128],
                        io["w_in"][l, :, O_CF + 256 + pr * 128:O_CF + 256 + (pr + 1) * 128],
                        io["w_in"][l, :, O_CI + pr * 128:O_CI + (pr + 1) * 128]]
                for b, c0 in zip(wpb, srcs):
                    st = self.wst.next()
                    sv = Opd(st.t[:, 0:NCH * 128].rearrange("p (c n) -> p c n", c=NCH), st.toks)
                    k.load(sv, c0.rearrange("(c p) n -> p c n", p=128))
                    k.copy("pool", b[:], sv)
            return wpb[0], [wpb[1], wpb[2]], wpb[3]
        off = wS.off
        Oacc_s, off = self.carve(off, [128, 2048], F32, ntok=32)
        Oacc_p, off = self.carve(off, [128, 2, 1024], F32, ntok=16)
        self.c_base = off
        R = {}
        for nm, shp, n in (("E1", [128, 256], 2), ("qs", [128, 128], 2), ("f", [128, 128], 2), ("g", [128, 128], 2),
                           ("kk", [128, 128], 2), ("v", [128, 128], 2), ("qd", [128, 128], 2), ("kdi", [128, 128], 2),
                           ("kd", [128, 128], 2), ("qdT", [128, 128], 2), ("kdiT", [128, 128], 2), ("AT", [128, 128], 2),
                           ("gl", [128, 16], 2), ("X", [128, 64], 8), ("fs", [128, 64], 2)):
            sz = shp[1] * 4
            R[nm] = Ring([self.carve(off + i * sz, shp, F32)[0] for i in range(n)])
            off += n * sz
        self.c_off = off

        def job(d, W, src, ntiles, w_q, w_f, w_i, lb, om, lcol, x0_fn, o_fn, fin_fn):
            nh = W // 64
            order = list(range(ntiles)) if d == 0 else list(range(ntiles - 1, -1, -1))
            tri, rem = (C_TRIC_F, C_REMC_F) if d == 0 else (C_TRIC_B, C_REMC_B)
            X = R["X"].next()
            x0_fn(X)
            for tile in order:
                ps1 = self.ps.next()
                for kc in range(NCH):
                    k.mm(ps1[:, 0:W], src(tile, kc), w_q[:, kc, :], start=(kc == 0), stop=(kc == NCH - 1))
                for kc in range(NCH):
                    k.mm(ps1[:, W:2 * W], src(tile, kc), w_f[:, kc, :], start=(kc == 0), stop=(kc == NCH - 1))
                ps2 = self.ps.next()
                for kc in range(NCH):
                    k.mm(ps2[:, 0:W], src(tile, kc), w_i[:, kc, :], start=(kc == 0), stop=(kc == NCH - 1))
                E1, qs, f, g, kk, v = (R[n].next() for n in ("E1", "qs", "f", "g", "kk", "v"))
                k.act(E1[:, 0:2 * W], ps1[:, 0:2 * W], AF.Exp, scale=-1.0)
                k.ts("dve", E1[:, 0:2 * W], E1[:, 0:2 * W], 1.0, ALU.add)
                k.recip(E1[:, 0:2 * W], E1[:, 0:2 * W])
                k.tt("dve", qs[:, 0:W], ps1[:, 0:W], E1[:, 0:W], ALU.mult)
                k.tt("dve", f[:, 0:W], E1[:, W:2 * W], om[:, d, lcol:lcol + W], ALU.mult)
                k.tt("dve", f[:, 0:W], f[:, 0:W], lb[:, d, lcol:lcol + W], ALU.add)
                k.act(g[:, 0:W], f[:, 0:W], AF.Ln)
                k.ts("pool", kk[:, 0:W], f[:, 0:W], -1.0, ALU.mult, 1.0, ALU.add)
                k.copy("act", v[:, 0:W], ps2[:, 0:W])
                yield
                psb = self.ps.next()
                k.mm(psb[:, 0:W], self.C(tri), g[:, 0:W])
                k.mm(psb[:, W:2 * W], self.C(rem), g[:, 0:W])
                k.mm(psb[0:W, 2 * W:2 * W + 8], g[:, 0:W], self.C(C_CI16, cols=slice(0, 8)))
                qd, kdi, kd, gl = (R[n].next() for n in ("qd", "kdi", "kd", "gl"))
                Eb = R["E1"].next()
                k.act(Eb[:, 0:W], psb[:, 0:W], AF.Exp)
                k.tt("pool", qd[:, 0:W], qs[:, 0:W], Eb[:, 0:W], ALU.mult)
                k.act(Eb[:, W:2 * W], psb[:, 0:W], AF.Exp, scale=-1.0)
                k.tt("pool", kdi[:, 0:W], kk[:, 0:W], Eb[:, W:2 * W], ALU.mult)
                Ed = R["f"].next()
                k.act(Ed[:, 0:W], psb[:, W:2 * W], AF.Exp)
                k.tt("pool", kd[:, 0:W], kk[:, 0:W], Ed[:, 0:W], ALU.mult)
                glo = gl.t[0:W, 0:8] if d == 0 else gl.t[0:W, 7::-1]
                k.act(Opd(glo, gl.toks), psb[0:W, 2 * W:2 * W + 8], AF.Exp)
                k.copy("dve", gl[0:W, 8:16], gl[0:W, 0:8])
                k.memset("dve", gl[0:W, 8:9], 0.0)
                yield
                qdT, kdiT = R["qdT"].next(), R["kdiT"].next()
                pst = self.ps.next()
                k.mm(pst[0:W, 0:128], qd[:, 0:W], self.C(C_ID))
                k.mm(pst[0:W, 128:256], kdi[:, 0:W], self.C(C_ID))
                if 'c' not in CSUB:
                    if 'x' not in CSUB:
                        k.copy("act", qdT[0:W, :], pst[0:W, 0:128])
                    if 'y' not in CSUB:
                        k.copy("dve", kdiT[0:W, :], pst[0:W, 128:256])
                psKV = self.ps.next()
                ATs = []
                for h in range(nh):
                    r0 = 64 * h
                    if 'a' in CSUB:
                        continue
                    psA = self.ps.next()
                    k.mm(psA[:, 0:128], kdiT[r0:r0 + 64, :], qdT[r0:r0 + 64, :])
                    AT = R["AT"].next()
                    k.tt("dve", AT[:, :], psA[:, 0:128], self.C(tri), ALU.mult)
                    ATs.append(AT)
                    if 'b' in CSUB:
                        continue
                    vx = self.tmp.next()
                    k.tt("pool", Opd(vx.t[:, :].rearrange("p (c e) -> p c e", c=8), vx.toks),
                         Opd(v.t[:, r0:r0 + 64].unsqueeze(1).broadcast_to([128, 8, 64]), v.toks),
                         Opd(self.cst.t[:, C_CI16, 0:8].unsqueeze(2).broadcast_to([128, 8, 64]), self.cst.toks), ALU.mult)
                    k.mm(psKV[r0:r0 + 64, :], kd[:, r0:r0 + 64], vx[:, :])
                KVs, GLx, XS = self.tmp.next(), self.tmp.next(), self.tmp.next()
                kv3 = KVs.t[0:W, :].rearrange("p (e c) -> p e c", c=8)
                kvo = kv3 if d == 0 else kv3[:, :, ::-1]
                k.copy("act", Opd(kvo, KVs.toks), Opd(psKV.t[0:W, :].rearrange("p (c e) -> p e c", c=8), psKV.toks))
                k.stt(Opd(kv3[:, :, 0], KVs.toks), X[0:W, :], gl[0:W, 0:1], Opd(kv3[:, :, 0], KVs.toks), ALU.mult, ALU.add)
                k.copy("pool", Opd(GLx.t[0:W, :].rearrange("p (e c) -> p e c", c=8), GLx.toks),
                       Opd(gl.t[0:W, 8:16].unsqueeze(1).broadcast_to([W, 64, 8]), gl.toks))
                k.scan(XS[0:W, :], GLx[0:W, :], KVs[0:W, :], 0.0, ALU.mult, ALU.add)
                xs3 = XS.t[0:W, :].rearrange("p (e c) -> p e c", c=8)
                Xn = R["X"].next()
                k.copy("dve", Xn[0:W, :], Opd(xs3[:, :, 7], XS.toks))
                orow = o_fn(tile, None)
                psO = self.ps.next()
                for h in range(nh):
                    r0 = 64 * h
                    ro = orow + r0
                    k.mm(psO[ro:ro + 64, 0:128], v[:, r0:r0 + 64], ATs[h][:, :], start=True, stop=False)
                    for i in range(8):
                        c = i if d == 0 else 7 - i
                        xc = X[r0:r0 + 64, :] if i == 0 else Opd(xs3[r0:r0 + 64, :, i - 1], XS.toks)
                        k.mm(psO[ro:ro + 64, 16 * c:16 * c + 16], xc, qdT[r0:r0 + 64, 16 * c:16 * c + 16],
                             start=False, stop=(i == 7))
                o_fn(tile, psO)
                X = Xn
                yield
            if CSTOP >= 4:
                fin_fn(X)

        def x0_sample(d):
            def f(X):
                k.load(X[0:64, :], io["s0_h"][l, d])
            return f

        done_s = set()

        def o_sample(tile, psO):
            row = 0 if tile < 16 else 64
            if psO is None:
                return row
            col = (tile % 16) * 128
            dst = Oacc_s.at(tile)[row:row + 64, col:col + 128]
            if tile in done_s:
                k.tt("dve", dst, psO[row:row + 64, 0:128], dst, ALU.add)
            else:
                done_s.add(tile)
                k.copy("act", dst, psO[row:row + 64, 0:128])

        sj = [job(d, 64, self.sample_src(None, False), 32, ws_q, ws_f[d], ws_i, lbs["hgrn_lb_h"][0], lbs["hgrn_lb_h"][1], 0,
                  x0_sample(d), o_sample, lambda X: None) for d in range(2)]

        done_p = set()

        def mk_prompt(sq, d, pr):
            def x0(X):
                k.memset("dve", X[:, :], 0.0)

            def o_fn(tile, psO):
                if psO is None:
                    return 0
                key = (sq, pr, tile)
                c0 = sq * 256 + tile * 128
                dst = Oacc_p.at((sq * 2 + tile) * 2 + pr)[:, pr, c0:c0 + 128]
                if key in done_p:
                    k.tt("dve", dst, psO[:, 0:128], dst, ALU.add)
                else:
                    done_p.add(key)
                    k.copy("act", dst, psO[:, 0:128])

            def fin(X):
                fs = R["fs"].next()
                k.copy("act", fs[:, :], X[:, :])
                k.dma("sp", io["new_sh"][sq, l, d, pr * 128:(pr + 1) * 128, :], fs.t[:, :], reads=fs.toks)
            def gen():
                wq_, wf_, wi_ = getw(pr)
                yield from job(d, 128, self.src_prompt(sq), 2, wq_, wf_[d], wi_, lbs["hgrn_lb"][0], lbs["hgrn_lb"][1], pr * 128,
                               x0, o_fn, fin)
            return gen()

        self.lockstep(sj)
        for pr in range(2):
            for sq in range(4):
                self.lockstep([mk_prompt(sq, d, pr) for d in range(2)])

        if CSTOP < 5:
            self.zero_o(l, 6, 8, 192, 256)
            return
        k.barrier()
        off = self.c_base
        sqb = Ring([self.carve(off + i * 2048, [128, 512], F32)[0] for i in range(2)]); off += 4096
        rsb = Ring([self.carve(off + i * 2048, [128, 512], F32)[0] for i in range(2)]); off += 4096
        osb = Ring([self.carve(off + i * 1024, [128, 512], BF16)[0] for i in range(2)]); off += 2048

        def post(ov, rows, n, gate_mm, out_fn):
            sq_ = sqb.next()
            k.act(sq_[rows, 0:n], ov, AF.Square)
            psN = self.ps.next()
            k.mm(psN[rows, 0:n], self.C(C_BLK64, rows=rows, cols=rows), sq_[rows, 0:n])
            rs = rsb.next()
            k.act(rs[rows, 0:n], psN[rows, 0:n], AF.Sqrt, bias=1e-6)
            k.recip(rs[rows, 0:n], rs[rows, 0:n])
            k.tt("dve", rs[rows, 0:n], rs[rows, 0:n], ov, ALU.mult)
            psG = self.ps.next()
            gate_mm(psG)
            e = sqb.next()
            k.act(e[rows, 0:n], psG[rows, 0:n], AF.Exp, scale=-1.0)
            k.ts("dve", e[rows, 0:n], e[rows, 0:n], 1.0, ALU.add)
            k.recip(e[rows, 0:n], e[rows, 0:n])
            k.tt("dve", e[rows, 0:n], e[rows, 0:n], psG[rows, 0:n], ALU.mult)
            out_fn(rs, e)

        for pr in range(2):
            wg = self.wtile(io["w_in"][l, :, O_CG + pr * 128:O_CG + (pr + 1) * 128])
            for sq in range(4):
                toks = [Oacc_p.toks[(sq * 2 + t) * 2 + pr] for t in range(2)]
                ov = Opd(Oacc_p.t[:, pr, sq * 256:(sq + 1) * 256], toks)

                def gate_mm(ps, sq=sq, wg=wg):
                    fm = self.src_prompt_fm(sq)
                    for kc in range(NCH):
                        k.mm(ps[:, 0:256], wg[:, kc, :], fm(kc), start=(kc == 0), stop=(kc == NCH - 1))

                def out_fn(rs, e, sq=sq, pr=pr):
                    k.stt(self.oT.at(sq // 2)[:, 6 + pr, sq * 256:(sq + 1) * 256], rs[:, 0:256], hn[:, 0:1], e[:, 0:256],
                          ALU.mult, ALU.mult)
                post(ov, slice(0, 128), 256, gate_mm, out_fn)
        wg = self.wtile(io["w_in_h"][l, :, 900:964])
        for b in range(8):
            sb = self.slab_ring.next()
            self.load_slab(sb, b)
            rows = slice(0, 64) if b < 4 else slice(64, 128)
            c0 = (b % 4) * 512
            toks = [Oacc_s.toks[b * 4 + t] for t in range(4)]
            ov = Opd(Oacc_s.t[rows, c0:c0 + 512], toks)

            def gate_mm(ps, sb=sb, rows=rows, wg=wg):
                for kc in range(NCH):
                    k.mm(ps[rows, 0:512], wg[:, kc, 0:64], sb[:, kc, 1:513], start=(kc == 0), stop=(kc == NCH - 1))

            def out_fn(rs, e, b=b, rows=rows):
                ob = osb.next()
                k.stt(ob[rows, :], rs[rows, :], hn[rows, 0:1], e[rows, :], ALU.mult, ALU.mult)
                k.dma("sp", self.ago_in[1][64:128, b * 512:(b + 1) * 512], ob.t[rows, :], reads=ob.toks, writes=[self.t_ago_in])
            post(ov, rows, 512, gate_mm, out_fn)

    def layer_norm(self, l, which, g, b):
        k = self.k
        xs = self.xs[g].at(b)
        sl = slice(b * 512, (b + 1) * 512)
        ps_m = self.ps.next()
        ps_q = self.ps.next()
        for kc in range(NCH):
            k.mm(ps_m[:, :], self.C(C_MEAN), xs[:, kc, sl], start=(kc == 0), stop=(kc == NCH - 1))
        for kc in range(NCH):
            sq = self.usq.next()
            k.act(sq[:, :], xs[:, kc, sl], AF.Square)
            k.mm(ps_q[:, :], self.C(C_MEAN), sq[:, :], start=(kc == 0), stop=(kc == NCH - 1))
        mean = self.stat.next()
        rstd = self.stat.next()
        k.copy("act", mean[:, :], ps_m[:, :])
        k.tt("dve", rstd[:, :], mean[:, :], mean[:, :], ALU.mult)
        k.tt("dve", rstd[:, :], ps_q[:, :], rstd[:, :], ALU.subtract)
        k.act(rstd[:, :], rstd[:, :], AF.Sqrt, bias=LN_EPS / ALPHA ** 2)
        k.recip(rstd[:, :], rstd[:, :])
        for kc in range(NCH):
            t = self.tmp.next()
            k.tt("dve", t[:, :], xs[:, kc, sl], mean[:, :], ALU.subtract)
            k.tt("dve", t[:, :], t[:, :], rstd[:, :], ALU.mult)
            k.act(xs[:, kc, sl], t[:, :], AF.Identity, scale=self.lng[:, l, which, kc:kc + 1],
                  bias=self.lnb[:, l, which, kc:kc + 1])

    def dense_group(self, l, g):
        k, io = self.k, self.io
        xs = self.xs[g]
        wname = "w_out_p" if g == 0 else "w_out_s"
        for oc in range(NCH):
            w = self.load_w_bf16(io[wname][l, :, oc * 128:(oc + 1) * 128].rearrange("(c p) n -> p c n", p=128), 128, 128)
            for b in range(2):
                sl = slice(b * 512, (b + 1) * 512)
                ps = self.ps.next()
                for kc in range(NCH):
                    k.mm(ps[:, :], w[:, kc, :], self.oT.at(b)[:, kc, sl], start=(kc == 0), stop=(kc == NCH - 1))
                k.stt(xs.at(b)[:, oc, sl], ps[:, :], self.modv(l, 2, g, oc), xs.at(b)[:, oc, sl], ALU.mult, ALU.add)
        for b in range(2):
            self.layer_norm(l, 0, g, b)
        for b in range(2):
            sl = slice(b * 512, (b + 1) * 512)
            for kc in range(NCH):
                k.act(self.xm2.at(b)[:, kc, sl], xs.at(b)[:, kc, sl], AF.Identity,
                      scale=self.modv(l, 4, g, kc), bias=self.modv(l, 3, g, kc))
        for f in range(NF):
            st = self.wst.next()
            wb = self.wbf.next()
            sv = Opd(st.t[:, 0:2048].rearrange("p (c j n) -> p c j n", c=NCH, j=2), st.toks)
            wv = View(wb.t[:, 0:2048].rearrange("p (c j n) -> p c j n", c=NCH, j=2), wb.toks)
            for j in range(2):
                k.load(Opd(sv.ap[:, :, j, :], st.toks),
                       io["w_ffn_in"][l, :, j * D_FF + f * 128:j * D_FF + (f + 1) * 128].rearrange("(c p) n -> p c n", p=128))
            k.copy("pool", wv[:], sv)
            for b in range(2):
                sl = slice(b * 512, (b + 1) * 512)
                ps_g = self.ps.next()
                ps_u = self.ps.next()
                for kc in range(NCH):
                    k.mm(ps_g[:, :], wv[:, kc, 0, :], self.xm2.at(b)[:, kc, sl], start=(kc == 0), stop=(kc == NCH - 1))
                for kc in range(NCH):
                    k.mm(ps_u[:, :], wv[:, kc, 1, :], self.xm2.at(b)[:, kc, sl], start=(kc == 0), stop=(kc == NCH - 1))
                t = self.tmp.next()
                k.act(t[:, :], ps_g[:, :], AF.Silu)
                k.tt("dve", self.hT.at(b)[:, f, sl], t[:, :], ps_u[:, :], ALU.mult)
        for oc in range(NCH):
            wvs = []
            for hf in range(2):
                st = self.wst.next()
                wb = self.wbf.next()
                sv = Opd(st.t[:, 0:11 * 128].rearrange("p (f n) -> p f n", f=11), st.toks)
                wv = View(wb.t[:, 0:11 * 128].rearrange("p (f n) -> p f n", f=11), wb.toks)
                k.load(sv, io["w_ffn_out"][l, hf * 1408:(hf + 1) * 1408, oc * 128:(oc + 1) * 128]
                       .rearrange("(f p) n -> p f n", p=128))
                k.copy("pool", wv[:], sv)
                wvs.append(wv)
            for b in range(2):
                sl = slice(b * 512, (b + 1) * 512)
                ps = self.ps.next()
                for f in range(NF):
                    k.mm(ps[:, :], wvs[f // 11][:, f % 11, :], self.hT.at(b)[:, f, sl], start=(f == 0), stop=(f == NF - 1))
                k.stt(xs.at(b)[:, oc, sl], ps[:, :], self.modv(l, 5, g, oc), xs.at(b)[:, oc, sl], ALU.mult, ALU.add)
        for b in range(2):
            self.layer_norm(l, 1, g, b)


def head_cols(h):
    r = lambda o, n: list(range(o, o + n))
    cols = []
    cols += r(O_AQ + h * 128, 128) + r(O_AK + h * 128, 128) + r(O_AV + h * 128, 128)
    cols += r(O_BQKV + h * 64, 64) + r(O_BQKV + 256 + h * 64, 64) + r(O_BQKV + 512 + h * 64, 64)
    cols += r(O_BG + h * 64, 64)
    cols += [O_BBETA + h, O_BBETA + 4 + h, O_BA + h, O_BA + 4 + h]
    cols += r(O_CQ + h * 64, 64) + r(O_CF + h * 64, 64) + r(O_CF + 256 + h * 64, 64)
    cols += r(O_CI + h * 64, 64) + r(O_CG + h * 64, 64)
    return cols


def w_out_perm():
    rows = []
    for r in range(4):
        rows += list(range(r * 128, (r + 1) * 128))
        rows += list(range(512 + r * 64, 512 + (r + 1) * 64))
        rows += list(range(768 + r * 64, 768 + (r + 1) * 64))
    return rows


def rope_tables():
    t = np.arange(DEC_SEQ)
    inv = (np.float32(10000.0) ** (-np.arange(16, dtype=np.float32) / np.float32(16))).astype(np.float32)
    out = np.zeros((2, 128, DEC_SEQ), np.float32)
    for p in range(128):
        d = p % 64
        pos = (t // 64) if d < 32 else (t % 64)
        ang = pos.astype(np.float32) * inv[d % 16]
        out[0, p] = np.cos(ang)
        out[1, p] = np.sin(ang)
    return out


def prep_inputs(inp, depth=DEPTH):
    f = lambda a: np.ascontiguousarray(np.asarray(a, dtype=np.float32))
    L = depth
    consts = make_consts()
    shared = {
        "w_mod": f(inp["w_mod"][:L]),
        "b_mod": f(np.asarray(inp["b_mod"])[:L].reshape(L, 48, 128).transpose(0, 2, 1)),
        "w_in": f(inp["w_in"][:L]),
        "w_out_p": f(inp["w_out"][:L]),
        "w_out_s": f(np.asarray(inp["w_out"])[:L][:, w_out_perm(), :]),
        "ln_g": f(np.asarray(inp["ln_g"])[:L].reshape(L, 2, NCH, 128).transpose(0, 3, 1, 2)),
        "ln_b": f(np.asarray(inp["ln_b"])[:L].reshape(L, 2, NCH, 128).transpose(0, 3, 1, 2)),
        "w_ffn_in": f(inp["w_ffn_in"][:L]),
        "w_ffn_out": f(inp["w_ffn_out"][:L]),
        "consts": consts,
        "rope_cs": rope_tables(),
        "diff_lambda": f(np.asarray(inp["diff_lambda"])[:L].reshape(-1)),
        "diff_norm": f(np.asarray(inp["diff_norm"])[:L].T),
        "hgrn_lb": f(np.asarray(inp["hgrn_lb"])[:, :L].reshape(-1)),
        "conv_hm": f(np.asarray(inp["conv_w"])[:L].reshape(L, 3, 3, 4, 64).transpose(0, 3, 1, 2, 4).reshape(L, 4, 576)),
        "w_ba_p": f(np.stack([np.asarray(inp["w_in"])[:L][:, :, [O_BBETA + h, O_BBETA + 4 + h, O_BA + h, O_BA + 4 + h]]
                              for h in range(4)], 2)),
        "dpar": f(np.stack([np.asarray(inp["delta_a_log"])[:L], np.asarray(inp["delta_dt_bias"])[:L]], -1)
                  .transpose(0, 2, 1, 3).reshape(-1)),
        "dnorm": f(np.asarray(inp["delta_norm"])[:L].reshape(-1)),
        "hnorm": f(np.tile(np.asarray(inp["hgrn_norm"])[:L], (1, 2))),
    }
    xp = np.asarray(inp["x_prompt"], np.float32)
    xsm = np.asarray(inp["x_sample"], np.float32)
    maps = []
    for c in range(8):
        s, r = c // 4, c % 4
        m = dict(shared)
        m["xT_p"] = f(xp[4 * c:4 * c + 4].reshape(TOK, D).T)
        m["xT_s"] = f(xsm[s, r * TOK:(r + 1) * TOK].T)
        cond = np.stack([np.asarray(inp["c_ctx"], np.float32), np.asarray(inp["c"], np.float32)[s]], -1)
        m["cond"] = f(cond.reshape(NCH, 128, 2).transpose(1, 0, 2))
        m["w_in_h"] = f(np.asarray(inp["w_in"])[:L][:, :, head_cols(r)])
        m["ctx_k"] = f(np.asarray(inp["cache_attn_k"])[s, :L, :, r, :])
        m["ctx_v"] = f(np.asarray(inp["cache_attn_v"])[s, :L, :, r, :])
        m["hgrn_lb_h"] = f(np.asarray(inp["hgrn_lb"])[:, :L, r * 64:(r + 1) * 64].reshape(-1))
        m["s0_h"] = f(np.asarray(inp["state_hgrn"])[s, :L, :, r])
        m["s0_d"] = f(np.asarray(inp["state_delta"])[s, :L, :, r])
        m["conv_h"] = f(np.asarray(inp["conv_w"])[:L].reshape(L, 3, 3, 4, 64)[:, :, :, r, :].reshape(L, 576))
        m["dpar_h"] = f(np.stack([np.asarray(inp["delta_a_log"])[:L, :, r], np.asarray(inp["delta_dt_bias"])[:L, :, r]], -1).reshape(-1))
        maps.append(m)
    return maps


_PROG = {}


def get_prog(depth=DEPTH, mixers=("A", "B", "C"), dbg=None):
    key = (depth, tuple(mixers), tuple(sorted((dbg or {}).items())))
    if key not in _PROG:
        _PROG[key] = Prog(depth, mixers, dbg)
    return _PROG[key]


def run(inp, depth=DEPTH, mixers=("A", "B", "C"), dbg=None, trace=False):
    prog = get_prog(depth, mixers, dbg)
    maps = prep_inputs(inp, depth)
    res = run_bass_kernel_spmd(prog.nc, maps, core_ids=list(range(8)), trace=trace)
    return res


def assemble(res, depth=DEPTH):
    R = res.results
    y_p = np.concatenate([R[c]["yT_p"].T.reshape(4, SEQ, D) for c in range(8)], 0)
    y_s = np.stack([np.concatenate([R[s * 4 + r]["yT_s"].T for r in range(4)], 0) for s in range(2)], 0)
    L = depth
    nk = np.concatenate([R[c]["new_k"] for c in range(8)], 0).reshape(32, L, SEQ, 4, 128)
    nv = np.concatenate([R[c]["new_v"] for c in range(8)], 0).reshape(32, L, SEQ, 4, 128)
    nsh = np.concatenate([R[c]["new_sh"] for c in range(8)], 0).reshape(32, L, 2, 4, 64, 64)
    nsd = np.concatenate([R[c]["new_sd"] for c in range(8)], 0).reshape(32, L, 2, 4, 64, 64)
    return (y_p.astype(np.float32), y_s.astype(np.float32), nk.astype(np.float32), nv.astype(np.float32),
            nsd.astype(np.float32), nsh.astype(np.float32))


def kernel(**inputs):
    res = run(inputs)
    return assemble(res)
```

```python
import math
import os
from contextlib import ExitStack
CSTOP = int(os.environ.get('CSTOP', '99'))
CSUB = os.environ.get('CSUB', '')
BSTOP = int(os.environ.get('BSTOP', '99'))
BY = int(os.environ.get('BY', '63'))

import numpy as np
import concourse.bass as bass
import concourse.mybir as mybir
from concourse.bass_utils import run_bass_kernel_spmd

F32 = mybir.dt.float32
BF16 = mybir.dt.bfloat16
ALU = mybir.AluOpType
AF = mybir.ActivationFunctionType
AX = mybir.AxisListType

D = 1024
NCH = 8
DEPTH = 4
SEQ = 256
DEC_SEQ = 4096
PAST = 256
D_FF = 2816
NF = 22
D_IN = 3856
ALPHA = (2 * DEPTH) ** 0.25
LN_EPS = 1e-5
TOK = 1024
GROUPS = [[0, 1, 2, 3], [4, 5, 6, 7]]

O_AQ, O_AK, O_AV = 0, 512, 1024
O_BQKV, O_BG, O_BBETA, O_BA = 1536, 2304, 2560, 2568
O_CQ, O_CF, O_CI, O_CG = 2576, 2832, 3344, 3600


class Tok:
    __slots__ = ("w", "rs", "excl")

    def __init__(self):
        self.w = None
        self.rs = {}
        self.excl = False


class Opd:
    __slots__ = ("ap", "toks")

    def __init__(self, ap, toks):
        self.ap = ap
        self.toks = toks


class View:
    __slots__ = ("t", "toks")

    def __init__(self, t, toks):
        self.t = t
        self.toks = toks

    def __getitem__(self, idx):
        return Opd(self.t[idx], self.toks)


class Buf:
    def __init__(self, t, ntok=1):
        self.t = t
        self.toks = [Tok() for _ in range(ntok)]

    def at(self, *keys):
        return View(self.t, [self.toks[k] for k in keys])

    def all(self):
        return View(self.t, self.toks)

    def __getitem__(self, idx):
        return Opd(self.t[idx], self.toks)


class Ring:
    def __init__(self, bufs):
        self.bufs = bufs
        self.i = 0

    def next(self):
        b = self.bufs[self.i % len(self.bufs)]
        self.i += 1
        return b


class KB:
    ENGS = ("pe", "dve", "act", "pool", "sp")

    def __init__(self):
        self.nc = bass.Bass("TRN2", target_bir_lowering=False)
        self.es = ExitStack()
        self.streams = {e: [] for e in self.ENGS}
        self.count = {e: 0 for e in self.ENGS}
        self.seen = {e: {} for e in self.ENGS}
        self.latest = {}
        self.sems = {}
        for e in self.ENGS:
            self.sems[e] = self.es.enter_context(self.nc.semaphore("c_" + e))
        self.ndsem = {"sp": 24, "pool": 12, "act": 8}
        self.dsem_i = {q: 0 for q in self.ndsem}
        self.dsem_v = {}
        for q, n in self.ndsem.items():
            for j in range(n):
                k = "d_%s_%d" % (q, j)
                self.sems[k] = self.es.enter_context(self.nc.semaphore(k))
                self.dsem_v[k] = 0
        self.nalloc = 0
        self.pending_barrier = {e: None for e in self.ENGS}

    def sbuf(self, name, shape, dt, ntok=1):
        t = self.es.enter_context(self.nc.sbuf_tensor(name, list(shape), dt))
        return Buf(t, ntok)

    def psum(self, name, shape, dt=F32):
        t = self.es.enter_context(self.nc.psum_tensor(name, list(shape), dt))
        b = Buf(t, 1)
        b.toks[0].excl = True
        return b

    def ring(self, name, shape, dt, n):
        return Ring([self.sbuf("%s%d" % (name, i), shape, dt) for i in range(n)])

    def dram(self, name, shape, dt, kind="Internal"):
        return self.nc.dram_tensor(name, list(shape), dt, kind=kind)

    def _deps(self, eng, reads, writes):
        need = {}

        def add(ref):
            if ref is None:
                return
            k, v = ref
            if need.get(k, 0) < v:
                need[k] = v

        for t in reads:
            add(t.w)
            if t.excl:
                for k2, v in t.rs.items():
                    if k2 != eng:
                        add((k2, v))
        for t in writes:
            add(t.w)
            for k, v in t.rs.items():
                add((k, v))
        pb = self.pending_barrier[eng]
        if pb is not None:
            for k, v in pb.items():
                add((k, v))
            self.pending_barrier[eng] = None
        if eng == "pe":
            need.pop("pe", None)
        seen = self.seen[eng]
        waits = []
        for k, v in need.items():
            if seen.get(k, 0) < v:
                seen[k] = v
                waits.append((k, v))
        return waits

    def op(self, eng, fn, reads=(), writes=()):
        waits = self._deps(eng, reads, writes)
        self.count[eng] += 1
        idx = self.count[eng]
        ref = (eng, idx)
        for t in reads:
            t.rs[eng] = idx
        for t in writes:
            t.w = ref
            t.rs = {}
        self.latest[eng] = idx
        self.streams[eng].append((waits, fn, (eng, 1)))

    def dma(self, q, out_ap, in_ap, reads=(), writes=(), fn=None, selfinc=False):
        waits = self._deps(q, reads, writes)
        j = self.dsem_i[q] % self.ndsem[q]
        self.dsem_i[q] += 1
        k = "d_%s_%d" % (q, j)
        prev = self.dsem_v[k]
        if prev > 0 and self.seen[q].get(k, 0) < prev:
            self.seen[q][k] = prev
            waits.append((k, prev))
        self.dsem_v[k] = prev + 16
        ref = (k, prev + 16)
        for t in reads:
            t.rs[k] = prev + 16
        for t in writes:
            t.w = ref
            t.rs = {}
        self.latest[k] = prev + 16
        if fn is None:
            fn = lambda e, o=out_ap, i=in_ap: e.dma_start(out=o, in_=i)
        self.streams[q].append((waits, fn, ("SELF", k) if selfinc else (k, 16)))

    def cc(self, fn, reads=(), writes=()):
        waits = self._deps("pool", reads, writes)
        if "cc" not in self.sems:
            self.sems["cc"] = self.es.enter_context(self.nc.semaphore("cc"))
            self.ccv = 0
        prev = self.ccv
        if prev > 0 and self.seen["pool"].get("cc", 0) < prev:
            self.seen["pool"]["cc"] = prev
            waits.append(("cc", prev))
        self.ccv = prev + 1
        ref = ("cc", prev + 1)
        for t in reads:
            t.rs["cc"] = prev + 1
        for t in writes:
            t.w = ref
            t.rs = {}
        self.latest["cc"] = prev + 1
        self.streams["pool"].append((waits, fn, ("cc", 1)))

    def barrier(self):
        snap = dict(self.latest)
        for e in self.ENGS:
            pb = self.pending_barrier[e]
            if pb is None:
                self.pending_barrier[e] = dict(snap)
            else:
                for k, v in snap.items():
                    if pb.get(k, 0) < v:
                        pb[k] = v

    def raw(self, eng, fn):
        self.streams[eng].append(([], fn, None))

    def finish(self):
        nc = self.nc
        final = dict(self.latest)
        engmap = {"pe": "tensor", "dve": "vector", "act": "scalar", "pool": "gpsimd", "sp": "sync"}
        with nc.Block() as block:
            for e in self.ENGS:
                stream = self.streams[e]

                def body(h, e=e, stream=stream):
                    sems = self.sems
                    for waits, fn, inc in stream:
                        for k, v in waits:
                            h.wait_ge(sems[k], v)
                        if inc is not None and inc[0] == "SELF":
                            fn(h, sems[inc[1]])
                            continue
                        ins = fn(h)
                        if inc is not None:
                            ins.then_inc(sems[inc[0]], inc[1])
                    for k, v in final.items():
                        if k == e:
                            continue
                        h.wait_ge(sems[k], v)

                getattr(block, engmap[e])(body)
        self.es.close()
        return nc

    @staticmethod
    def _tk(*ops):
        out = []
        for o in ops:
            if isinstance(o, Opd):
                out.extend(o.toks)
        return out

    @staticmethod
    def _ap(o):
        return o.ap if isinstance(o, Opd) else o

    def mm(self, out, lhsT, rhs, start=True, stop=True):
        o, l, r = out.ap, lhsT.ap, rhs.ap
        self.op("pe", lambda e: e.matmul(o, l, r, start=start, stop=stop),
                reads=self._tk(lhsT, rhs), writes=self._tk(out))

    def act(self, out, in_, func, bias=0.0, scale=1.0, accum=None, eng="act"):
        o, i, b, s = out.ap, in_.ap, self._ap(bias), self._ap(scale)
        kw = {}
        if accum is not None:
            kw["accum_out"] = accum.ap
        self.op("act", lambda e: e.activation(o, i, func, bias=b, scale=s, **kw),
                reads=self._tk(in_, bias, scale), writes=self._tk(out, accum))

    def tt(self, eng, out, in0, in1, op):
        o, a, b = out.ap, in0.ap, in1.ap
        self.op(eng, lambda e: e.tensor_tensor(o, a, b, op), reads=self._tk(in0, in1), writes=self._tk(out))

    def ts(self, eng, out, in0, s1, op0, s2=None, op1=None, accum=None):
        o, a, x1, x2 = out.ap, in0.ap, self._ap(s1), self._ap(s2)
        kw = {}
        if accum is not None:
            kw["accum_out"] = accum.ap
        if op1 is None:
            fn = lambda e: e.tensor_scalar(o, a, x1, None, op0, **kw)
        else:
            fn = lambda e: e.tensor_scalar(o, a, x1, x2, op0, op1, **kw)
        self.op(eng, fn, reads=self._tk(in0, s1, s2), writes=self._tk(out, accum))

    def stt(self, out, in0, scalar, in1, op0, op1, eng="dve"):
        o, a, s, b = out.ap, in0.ap, self._ap(scalar), in1.ap
        self.op(eng, lambda e: e.scalar_tensor_tensor(o, a, s, b, op0, op1),
                reads=self._tk(in0, scalar, in1), writes=self._tk(out))

    def copy(self, eng, out, in_):
        o, i = out.ap, in_.ap
        if eng == "act":
            self.op("act", lambda e: e.copy(o, i), reads=self._tk(in_), writes=self._tk(out))
        else:
            self.op(eng, lambda e: e.tensor_copy(o, i), reads=self._tk(in_), writes=self._tk(out))

    def memset(self, eng, out, val):
        o = out.ap
        self.op(eng, lambda e: e.memset(o, val), writes=self._tk(out))

    def recip(self, out, in_):
        o, i = out.ap, in_.ap
        self.op("dve", lambda e: e.reciprocal(o, i), reads=self._tk(in_), writes=self._tk(out))

    def reduce(self, out, in_, op=ALU.add, axis=AX.X):
        o, i = out.ap, in_.ap
        self.op("dve", lambda e: e.tensor_reduce(o, i, axis, op), reads=self._tk(in_), writes=self._tk(out))

    def scan(self, out, d0, d1, initial, op0, op1):
        o, a, b, ini = out.ap, d0.ap, d1.ap, self._ap(initial)
        self.op("dve", lambda e: e.tensor_tensor_scan(o, a, b, ini, op0, op1),
                reads=self._tk(d0, d1, initial), writes=self._tk(out))

    def load(self, out, in_ap, q="sp"):
        self.dma(q, out.ap, in_ap, writes=self._tk(out))

    def store(self, out_ap, in_, q="pool", dram_tok=None):
        w = [dram_tok] if dram_tok is not None else []
        self.dma(q, out_ap, in_.ap, reads=self._tk(in_), writes=w)


(C_ID, C_MEAN, C_ONE, C_M128, C_BLK64, C_TRID_F, C_TRID_B, C_REMD_F, C_REMD_B, C_STR_F, C_STR_B,
 C_INC_F, C_INC_B, C_TRIC_F, C_TRIC_B, C_REMC_F, C_REMC_B, C_ROPE, C_CI16, C_CI64) = range(20)
NCONST = 20


def make_consts():
    c = np.zeros((128, NCONST, 128), np.float32)
    i = np.arange(128)
    P, Q = np.meshgrid(i, i, indexing="ij")
    c[:, C_ID] = (P == Q)
    c[:, C_MEAN] = 1.0 / D
    c[:, C_ONE] = 1.0
    c[:, C_M128] = 1.0 / 128
    c[:, C_BLK64] = (P // 64 == Q // 64) / 64.0
    s64 = (P // 64 == Q // 64)
    s16 = (P // 16 == Q // 16)
    c[:, C_TRID_F] = s64 & (P <= Q)
    c[:, C_TRID_B] = s64 & (P >= Q)
    c[:, C_REMD_F] = s64 & (P > Q)
    c[:, C_REMD_B] = s64 & (P < Q)
    c[:, C_STR_F] = s64 & (Q < P)
    c[:, C_STR_B] = s64 & (Q > P)
    c[:, C_INC_F] = s64 & (Q <= P)
    c[:, C_INC_B] = s64 & (Q >= P)
    c[:, C_TRIC_F] = s16 & (P <= Q)
    c[:, C_TRIC_B] = s16 & (P >= Q)
    c[:, C_REMC_F] = s16 & (P > Q)
    c[:, C_REMC_B] = s16 & (P < Q)
    R = np.zeros((128, 128), np.float32)
    for m in range(128):
        if m % 32 < 16:
            R[m, m + 16] = -1.0
        else:
            R[m, m - 16] = 1.0
    c[:, C_ROPE] = R.T
    c[:, C_CI16, 0:8] = (P[:, 0:8] // 16 == Q[:, 0:8])
    c[:, C_CI64, 0:2] = (P[:, 0:2] // 64 == Q[:, 0:2])
    return c


ARENA_BYTES = 91 * 1024


class Prog:
    def __init__(self, depth=DEPTH, mixers=("A", "B", "C"), dbg=None):
        self.depth = depth
        self.mixers = mixers
        self.dbg = dbg or {}
        self.k = KB()
        self.build()
        self.nc = self.k.finish()

    def declare_io(self):
        nc = self.k.nc
        L = self.depth

        def inp(name, shape, dt=F32):
            return nc.dram_tensor(name, list(shape), dt, kind="ExternalInput").ap()

        def outp(name, shape, dt=F32):
            return nc.dram_tensor(name, list(shape), dt, kind="ExternalOutput").ap()

        io = {}
        io["xT_p"] = inp("xT_p", [D, TOK])
        io["xT_s"] = inp("xT_s", [D, TOK])
        io["cond"] = inp("cond", [128, NCH, 2])
        io["w_mod"] = inp("w_mod", [L, D, 6 * D])
        io["b_mod"] = inp("b_mod", [L, 128, 48])
        io["w_in"] = inp("w_in", [L, D, D_IN])
        io["w_out_p"] = inp("w_out_p", [L, D, D])
        io["w_out_s"] = inp("w_out_s", [L, D, D])
        io["ln_g"] = inp("ln_g", [L, 128, 2, NCH])
        io["ln_b"] = inp("ln_b", [L, 128, 2, NCH])
        io["w_ffn_in"] = inp("w_ffn_in", [L, D, 2 * D_FF])
        io["w_ffn_out"] = inp("w_ffn_out", [L, D_FF, D])
        io["consts"] = inp("consts", [128, NCONST, 128])
        io["w_in_h"] = inp("w_in_h", [L, D, 964])
        io["rope_cs"] = inp("rope_cs", [2, 128, DEC_SEQ])
        io["diff_lambda"] = inp("diff_lambda", [L * 256])
        io["diff_norm"] = inp("diff_norm", [128, L])
        io["ctx_k"] = inp("ctx_k", [L, PAST, 128])
        io["ctx_v"] = inp("ctx_v", [L, PAST, 128])
        io["conv_hm"] = inp("conv_hm", [L, 4, 3 * 192])
        io["conv_h"] = inp("conv_h", [L, 3 * 192])
        io["w_ba_p"] = inp("w_ba_p", [L, D, 4, 4])
        io["dpar"] = inp("dpar", [L * 16])
        io["dpar_h"] = inp("dpar_h", [L * 4])
        io["dnorm"] = inp("dnorm", [L * 64])
        io["s0_d"] = inp("s0_d", [L, 2, 64, 64])
        io["new_sd"] = outp("new_sd", [4, L, 2, 4, 64, 64])
        io["hgrn_lb"] = inp("hgrn_lb", [2 * L * 256])
        io["hgrn_lb_h"] = inp("hgrn_lb_h", [2 * L * 64])
        io["hnorm"] = inp("hnorm", [L, 128])
        io["s0_h"] = inp("s0_h", [L, 2, 64, 64])
        io["new_sh"] = outp("new_sh", [4, L, 2, 4 * 64, 64])
        io["new_k"] = outp("new_k", [4, L, SEQ, 512])
        io["new_v"] = outp("new_v", [4, L, SEQ, 512])
        self.agx_in = [nc.dram_tensor("agx_in%d" % i, [512, TOK], BF16).ap() for i in range(2)]
        self.agx = [nc.dram_tensor("agx%d" % i, [4 * 512, TOK], BF16).ap() for i in range(2)]
        self.ago_in = [nc.dram_tensor("ago_in%d" % i, [128, DEC_SEQ], BF16).ap() for i in range(2)]
        self.ago = [nc.dram_tensor("ago%d" % i, [4 * 128, DEC_SEQ], BF16).ap() for i in range(2)]
        self.t_agx_in, self.t_agx, self.t_ago_in, self.t_ago = Tok(), Tok(), Tok(), Tok()
        io["yT_p"] = outp("yT_p", [D, TOK])
        io["yT_s"] = outp("yT_s", [D, TOK])
        for name, shape in self.dbg.items():
            io[name] = outp(name, shape)
        self.io = io

    def carve(self, off, shape, dt, ntok=1):
        n = 1
        for s in shape[1:]:
            n *= s
        nbytes = n * (2 if dt == BF16 else 4)
        assert off % 4 == 0 and off + nbytes <= ARENA_BYTES, (off, nbytes)
        ap = self.arena_t[:, off // 4:(off + nbytes + 3) // 4]
        if dt == BF16:
            ap = ap.bitcast(BF16)
        if len(shape) == 3:
            ap = ap.rearrange("p (a n) -> p a n", a=shape[1])
        elif len(shape) == 4:
            ap = ap.rearrange("p (a b n) -> p a b n", a=shape[1], b=shape[2])
        return Buf(ap, ntok), off + ((nbytes + 3) // 4) * 4

    def build(self):
        k = self.k
        self.declare_io()
        io = self.io
        L = self.depth
        self.xs = [k.sbuf("xs_p", [128, NCH, TOK], F32, ntok=2), k.sbuf("xs_s", [128, NCH, TOK], F32, ntok=2)]
        self.cst = k.sbuf("cst_sb", [128, NCONST, 128], F32)
        self.oT = k.sbuf("oT", [128, NCH, TOK], BF16, ntok=2)
        self.mod = k.sbuf("mod", [128, L, 48, 2], F32)
        self.lng = k.sbuf("lng", [128, L, 2, NCH], F32)
        self.lnb = k.sbuf("lnb", [128, L, 2, NCH], F32)
        self.ps = Ring([k.psum("ps%d" % i, [128, 512]) for i in range(4)])
        self.acc = [k.psum("acc%d" % i, [128, 512]) for i in range(4)]
        self.wst = k.ring("wst", [128, 2048], F32, 1)
        self.wbf = k.ring("wbf", [128, 2048], BF16, 2)
        self.tmp = k.ring("tmp", [128, 512], F32, 4)
        self.arena_t = self.k.es.enter_context(k.nc.sbuf_tensor("arena", [128, ARENA_BYTES // 4], F32))

        k.load(self.cst[:], io["consts"])
        for g, nm in enumerate(("xT_p", "xT_s")):
            for b in range(2):
                k.load(self.xs[g].at(b)[:, :, b * 512:(b + 1) * 512],
                       io[nm][:, b * 512:(b + 1) * 512].rearrange("(c p) n -> p c n", p=128))
        k.load(self.lng[:], io["ln_g"].rearrange("l p a c -> p l a c"))
        k.load(self.lnb[:], io["ln_b"].rearrange("l p a c -> p l a c"))
        self.preamble_mod()
        self.preamble_small()
        for l in range(L):
            self.layer(l)
        for g, nm in enumerate(("yT_p", "yT_s")):
            for b in range(2):
                k.store(io[nm][:, b * 512:(b + 1) * 512].rearrange("(c p) n -> p c n", p=128),
                        self.xs[g].at(b)[:, :, b * 512:(b + 1) * 512], q="sp")

    def C(self, i, rows=slice(None), cols=slice(None)):
        return self.cst[rows, i, cols]

    def preamble_mod(self):
        k, io = self.k, self.io
        L = self.depth
        k.barrier()
        off = 0
        wblk = []
        for i in range(2):
            b, off = self.carve(off, [128, NCH, 512], F32)
            wblk.append(b)
        cond, off = self.carve(off, [128, NCH, 2], F32)
        csil, off = self.carve(off, [128, NCH, 2], F32)
        bmod, off = self.carve(off, [128, L, 48], F32)
        k.load(cond[:], io["cond"])
        k.load(bmod[:], io["b_mod"].rearrange("l p m -> p l m"))
        k.act(csil[:], cond[:], AF.Silu)
        n = 0
        for l in range(L):
            for cb in range(12):
                w = wblk[n % 2]
                n += 1
                k.load(w[:], io["w_mod"][l, :, cb * 512:(cb + 1) * 512].rearrange("(c p) n -> p c n", p=128))
                ps = self.ps.next()
                for mi in range(4):
                    m = cb * 4 + mi
                    for kc in range(NCH):
                        k.mm(ps[:, 2 * mi:2 * mi + 2], w[:, kc, mi * 128:(mi + 1) * 128], csil[:, kc, :],
                             start=(kc == 0), stop=(kc == NCH - 1))
                k.tt("dve", self.mod[:, l, cb * 4:(cb + 1) * 4, :],
                     Opd(ps.t[:, 0:8].rearrange("p (m j) -> p m j", j=2), ps.toks),
                     Opd(bmod.t[:, l, cb * 4:(cb + 1) * 4].unsqueeze(2).broadcast_to([128, 4, 2]), bmod.toks),
                     ALU.add)
            for a in (8, 32):
                k.ts("dve", self.mod[:, l, a:a + 8, :], self.mod[:, l, a:a + 8, :], 1.0, ALU.add)
            for a in (16, 40):
                k.ts("dve", self.mod[:, l, a:a + 8, :], self.mod[:, l, a:a + 8, :], 1.0 / ALPHA, ALU.mult)
        k.barrier()

    def modv(self, l, which, g, kc):
        return self.mod[:, l, which * 8 + kc, g:g + 1]

    def load_w_bf16(self, dram_ap, rows, cols):
        st = self.wst.next()
        wb = self.wbf.next()
        k = self.k
        if len(dram_ap.shape) == 3:
            a, n = dram_ap.shape[1], dram_ap.shape[2]
            sv = Opd(st.t[:, 0:a * n].rearrange("p (a n) -> p a n", a=a), st.toks)
            wv = View(wb.t[:, 0:a * n].rearrange("p (a n) -> p a n", a=a), wb.toks)
            k.load(sv, dram_ap)
            k.copy("pool", wv[:], sv)
            return wv
        n = dram_ap.shape[1]
        r = dram_ap.shape[0]
        k.load(st[0:r, 0:n], dram_ap)
        k.copy("pool", wb[0:r, 0:n], st[0:r, 0:n])
        return View(wb.t, wb.toks)

    def layer(self, l):
        k = self.k
        self.mixer_phase(l)
        k.barrier()
        off = 0
        self.hT, off = self.carve(off, [128, NF, TOK], BF16, ntok=2)
        self.xm2, off = self.carve(off, [128, NCH, TOK], BF16, ntok=2)
        self.stat = Ring([self.carve(off + i * 2048, [128, 512], F32)[0] for i in range(6)])
        off += 6 * 2048
        self.usq = Ring([self.carve(off + i * 2048, [128, 512], F32)[0] for i in range(3)])
        off += 3 * 2048
        for g in range(2):
            if g == 1:
                self.load_oT_sample()
            self.dense_group(l, g)
        k.barrier()

    def preamble_small(self):
        k, io = self.k, self.io
        L = self.depth
        self.lam = k.sbuf("lam", [128, L], F32)
        self.nlam = k.sbuf("nlam", [128, L], F32)
        self.gA = k.sbuf("gA", [128, L], F32)
        self.onesb = k.sbuf("onesb", [128, 128], BF16)
        k.memset("pool", self.onesb[:, :], 1.0)
        off = 0
        dlb, off = self.carve(off, [128, L, 4, 64], F32)
        pr, off = self.carve(off, [128, L, 2, 64], F32)
        sm, off = self.carve(off, [128, L, 2], F32)
        k.load(dlb[:], io["diff_lambda"].partition_broadcast(128).rearrange("p (l a d) -> p l a d", l=L, a=4))
        k.load(self.gA[:, :], io["diff_norm"])
        for j in range(2):
            k.tt("dve", pr[:, :, j, :], dlb[:, :, 2 * j, :], dlb[:, :, 2 * j + 1, :], ALU.mult)
        k.reduce(sm[:], pr[:])
        k.act(sm[:], sm[:], AF.Exp)
        k.tt("dve", self.lam[:, :], sm[:, :, 0], sm[:, :, 1], ALU.subtract)
        for l in range(L):
            lam_init = 0.8 - 0.6 * math.exp(-0.3 * l)
            k.ts("dve", self.lam[:, l:l + 1], self.lam[:, l:l + 1], lam_init, ALU.add)
            k.ts("dve", self.gA[:, l:l + 1], self.gA[:, l:l + 1], 1.0 - lam_init, ALU.mult)
        k.ts("dve", self.nlam[:, :], self.lam[:, :], -1.0, ALU.mult)
        k.barrier()

    def mixer_phase(self, l):
        k, io = self.k, self.io
        k.barrier()
        off = 0
        self.xmp, off = self.carve(off, [128, NCH, 4, 258], BF16, ntok=4)
        self.slab = []
        for i in range(2):
            b, off = self.carve(off, [128, NCH, 514], BF16)
            self.slab.append(b)
        self.moff = off
        k.memset("pool", self.xmp.all()[:, :, :, 0:1], 0.0)
        k.memset("pool", self.xmp.all()[:, :, :, 257:258], 0.0)
        for sq in range(4):
            for kc in range(NCH):
                k.act(self.xmp.at(sq)[:, kc, sq, 1:257], self.xs[0].at(sq // 2)[:, kc, sq * 256:(sq + 1) * 256],
                      AF.Identity, scale=self.modv(l, 1, 0, kc), bias=self.modv(l, 0, 0, kc))
        for b in range(2):
            for kc in range(NCH):
                k.ts("dve", self.slab[b][:, kc, 0:512], self.xs[1].at(b)[:, kc, b * 512:(b + 1) * 512],
                     self.modv(l, 1, 1, kc), ALU.mult, self.modv(l, 0, 1, kc), ALU.add)
            for hf in range(2):
                k.dma("sp", self.agx_in[hf][:, b * 512:(b + 1) * 512].rearrange("(c p) n -> p c n", p=128),
                      self.slab[b].t[:, 4 * hf:4 * hf + 4, 0:512], reads=self.slab[b].toks, writes=[self.t_agx_in])
        for hf in range(2):
            ain, aout = self.agx_in[hf], self.agx[hf]
            k.cc(lambda e, ain=ain, aout=aout: e.collective_compute("AllGather", ALU.bypass, replica_groups=GROUPS,
                                                                    ins=[ain], outs=[aout]),
                 reads=[self.t_agx_in], writes=[self.t_agx])
        if "A" in self.mixers:
            self.mixA_sample(l)
            k.barrier()
            self.mixA_prompt(l)
            k.barrier()
        else:
            self.zero_o(l, 0, 4, 0, 128)
        if "B" in self.mixers:
            self.mixB(l)
            k.barrier()
        else:
            self.zero_o(l, 4, 6, 128, 192)
        if "C" in self.mixers:
            self.mixC(l)
            k.barrier()
        else:
            self.zero_o(l, 6, 8, 192, 256)
        for hf in range(2):
            gin, gout = self.ago_in[hf], self.ago[hf]
            k.cc(lambda e, gin=gin, gout=gout: e.collective_compute("AllGather", ALU.bypass, replica_groups=GROUPS,
                                                                    ins=[gin], outs=[gout]),
                 reads=[self.t_ago_in], writes=[self.t_ago])

    def zero_o(self, l, c0, c1, r0, r1):
        k = self.k
        z = self.tmp.next()
        k.memset("pool", z[:, :], 0.0)
        zb = Opd(z.t[:, 0:256].bitcast(BF16), z.toks)
        for blk in range(8):
            k.dma("sp", self.ago_in[r0 // 128][r0 % 128:r0 % 128 + (r1 - r0), blk * 512:(blk + 1) * 512], zb.ap[0:r1 - r0, :],
                  reads=z.toks, writes=[self.t_ago_in])
        for b in range(2):
            k.memset("pool", self.oT.at(b)[:, c0:c1, b * 512:(b + 1) * 512], 0.0)

    def load_oT_sample(self):
        k = self.k
        ago = self.ago
        oT = self.oT

        def fn(e, sem, HF):
            core = e.partition_id()
            for c in range(8):
                r = c % 4
                with e.If(core == c):
                    e.dma_start(out=oT.t[:, HF::2, :], in_=ago[HF][:, r * TOK:(r + 1) * TOK].rearrange("(r p) n -> p r n", p=128)).then_inc(sem, 16)
        for HF in range(2):
            k.dma("pool", None, None, reads=[self.t_ago], writes=oT.toks, fn=(lambda e, sem, HF=HF: fn(e, sem, HF)), selfinc=True)

    def load_slab(self, buf, b, halo=False):
        k = self.k
        r, hf = b // 2, b % 2
        agx3 = [a.rearrange("(r f) n -> r f n", r=4) for a in self.agx]
        for fh in range(2):
            k.dma("sp", buf.t[:, 4 * fh:4 * fh + 4, 1:513], agx3[fh][r, :, hf * 512:(hf + 1) * 512].rearrange("(c p) n -> p c n", p=128),
                  reads=[self.t_agx], writes=buf.toks)
        if halo:
            for side, g0, col in ((0, b * 512 - 1, 0), (1, (b + 1) * 512, 513)):
                if g0 < 0 or g0 >= DEC_SEQ:
                    k.memset("pool", buf[:, :, col:col + 1], 0.0)
                else:
                    rr, cc = g0 // TOK, g0 % TOK
                    for fh in range(2):
                        o_ = buf.t[:, 4 * fh:4 * fh + 4, col:col + 1]
                        i_ = agx3[fh][rr, :, cc:cc + 1].rearrange("(c p) n -> p c n", p=128)
                        k.dma("sp", None, None, reads=[self.t_agx], writes=buf.toks,
                              fn=lambda e, o_=o_, i_=i_: e.dma_start(out=o_, in_=i_, allow_slow_non_contiguous=True))

    def wtile(self, dram_ap):
        return self.load_w_bf16(dram_ap.rearrange("(c p) n -> p c n", p=128), 128, dram_ap.shape[1])

    def attn_core(self, l, qT, kT, V, nkt, q0, nq, out_fn, Pt, o0, o1, rr):
        k = self.k
        om = [o0, o1]
        for m in range(2):
            rows = slice(64 * m, 64 * m + 64)
            psO, psR = self.acc[2 * m], self.acc[2 * m + 1]
            for kt in range(nkt):
                psS = self.ps.next()
                k.mm(psS[:, 0:nq], kT[rows, kt * 128:(kt + 1) * 128], qT[rows, q0:q0 + nq])
                P = Pt.next()
                k.act(P[:, 0:nq], psS[:, 0:nq], AF.Exp, scale=0.125)
                k.mm(psO[:, 0:nq], V[:, kt, :], P[:, 0:nq], start=(kt == 0), stop=(kt == nkt - 1))
                k.mm(psR[:, 0:nq], self.onesb[:, :], P[:, 0:nq], start=(kt == 0), stop=(kt == nkt - 1))
            k.recip(rr[:, 0:nq], psR[:, 0:nq])
            k.tt("dve", om[m][:, 0:nq], psO[:, 0:nq], rr[:, 0:nq], ALU.mult)
        k.stt(o0[:, 0:nq], o1[:, 0:nq], self.nlam[:, l:l + 1], o0[:, 0:nq], ALU.mult, ALU.add)
        k.act(o1[:, 0:nq], o0[:, 0:nq], AF.Square)
        psN = self.ps.next()
        k.mm(psN[:, 0:nq], self.C(C_M128), o1[:, 0:nq])
        k.act(rr[:, 0:nq], psN[:, 0:nq], AF.Sqrt, bias=1e-6)
        k.recip(rr[:, 0:nq], rr[:, 0:nq])
        k.tt("dve", o0[:, 0:nq], o0[:, 0:nq], rr[:, 0:nq], ALU.mult)
        out_fn(o0)

    def mixA_sample(self, l):
        k, io = self.k, self.io
        off = self.moff
        qT, off = self.carve(off, [128, DEC_SEQ], BF16)
        kT, off = self.carve(off, [128, DEC_SEQ + PAST], BF16)
        V, off = self.carve(off, [128, 34, 128], BF16)
        Pt = Ring([self.carve(off + i * 1024, [128, 512], BF16)[0] for i in range(3)]); off += 3 * 1024
        xf = Ring([self.carve(off + i * 2048, [128, 512], F32)[0] for i in range(3)]); off += 3 * 2048
        o0, off = self.carve(off, [128, 512], F32)
        o1, off = self.carve(off, [128, 512], F32)
        rr, off = self.carve(off, [128, 512], F32)
        osb = Ring([self.carve(off + i * 1024, [128, 512], BF16)[0] for i in range(2)]); off += 2 * 1024
        cst = Ring([self.carve(off + i * 1024, [128, 2, 128], F32)[0] for i in range(2)]); off += 2 * 1024
        wqk = self.wtile(io["w_in_h"][l, :, 0:256])
        wv = self.wtile(io["w_in_h"][l, :, 256:384])
        for j in range(2):
            c = cst.next()
            k.load(c[:, 0, :], io["ctx_k"][l, j * 128:(j + 1) * 128, :])
            k.load(c[:, 1, :], io["ctx_v"][l, j * 128:(j + 1) * 128, :])
            ps = self.ps.next()
            k.mm(ps[:, 0:128], c[:, 0, :], self.C(C_ID))
            k.copy("act", kT[:, DEC_SEQ + j * 128:DEC_SEQ + (j + 1) * 128], ps[:, 0:128])
            k.copy("dve", V[:, 32 + j, :], c[:, 1, :])
        for b in range(8):
            sb = self.slab[b % 2]
            self.load_slab(sb, b)
            cos = self.tmp.next()
            sin = self.tmp.next()
            k.load(cos[:, :], io["rope_cs"][0, :, b * 512:(b + 1) * 512])
            k.load(sin[:, :], io["rope_cs"][1, :, b * 512:(b + 1) * 512])
            for which, dst in ((0, qT), (1, kT)):
                ps = self.ps.next()
                for kc in range(NCH):
                    k.mm(ps[:, :], wqk[:, kc, which * 128:(which + 1) * 128], sb[:, kc, 1:513], start=(kc == 0), stop=(kc == NCH - 1))
                x = xf.next()
                k.copy("act", x[:, :], ps[:, :])
                ps2 = self.ps.next()
                k.mm(ps2[:, :], self.C(C_ROPE), x[:, :])
                t1 = xf.next()
                k.tt("dve", t1[:, :], x[:, :], cos[:, :], ALU.mult)
                k.tt("dve", x[:, :], ps2[:, :], sin[:, :], ALU.mult)
                k.tt("dve", dst[:, b * 512:(b + 1) * 512], t1[:, :], x[:, :], ALU.add)
            for j in range(4):
                ps = self.ps.next()
                for kc in range(NCH):
                    k.mm(ps[:, 0:128], sb[:, kc, 1 + j * 128:1 + (j + 1) * 128], wv[:, kc, :], start=(kc == 0), stop=(kc == NCH - 1))
                k.copy("act", V[:, b * 4 + j, :], ps[:, 0:128])
        qv, kv, vv = View(qT.t, qT.toks), View(kT.t, kT.toks), View(V.t, V.toks)
        for b in range(8):
            def out_fn(o, b=b):
                ob = osb.next()
                k.ts("dve", ob[:, :], o[:, :], self.gA[:, l:l + 1], ALU.mult)
                k.dma("sp", self.ago_in[0][0:128, b * 512:(b + 1) * 512], ob.t[:, :], reads=ob.toks, writes=[self.t_ago_in])
            self.attn_core(l, qv, kv, vv, 34, b * 512, 512, out_fn, Pt, o0, o1, rr)

    def mixA_prompt(self, l):
        k, io = self.k, self.io
        off = self.moff
        Vp, off = self.carve(off, [128, 8, 512], BF16)
        qk = Ring([self.carve(off + i * 1024, [128, 2, 256], BF16)[0] for i in range(2)]); off += 2 * 1024
        Pt = Ring([self.carve(off + i * 1024, [128, 512], BF16)[0] for i in range(3)]); off += 3 * 1024
        o0, off = self.carve(off, [128, 512], F32)
        o1, off = self.carve(off, [128, 512], F32)
        rr, off = self.carve(off, [128, 512], F32)
        stg = Ring([self.carve(off + i * 1024, [128, 256], F32)[0] for i in range(3)]); off += 3 * 1024
        for cc in range(4):
            w = self.wtile(io["w_in"][l, :, O_AK + cc * 256:O_AK + (cc + 1) * 256])
            for t in range(8):
                sq, i = t // 2, t % 2
                ps = self.ps.next()
                for kc in range(NCH):
                    k.mm(ps[:, 0:256], self.xmp.at(sq)[:, kc, sq, 1 + i * 128:1 + (i + 1) * 128], w[:, kc, :],
                         start=(kc == 0), stop=(kc == NCH - 1))
                sg = stg.next()
                k.copy("act", sg[:, :], ps[:, 0:256])
                dst = io["new_k"] if cc < 2 else io["new_v"]
                c0 = (cc % 2) * 256
                k.dma("sp", dst[sq, l, i * 128:(i + 1) * 128, c0:c0 + 256], sg.t[:, :], reads=sg.toks)
                if cc >= 2:
                    k.copy("dve", Vp[:, t, c0:c0 + 256], sg[:, :])
        for h in range(4):
            w = self.wtile(io["w_in"][l, :, O_AQ + h * 128:O_AQ + (h + 1) * 128])
            w2 = self.wtile(io["w_in"][l, :, O_AK + h * 128:O_AK + (h + 1) * 128])
            for sq in range(4):
                qb = qk.next()
                for which, ww in ((0, w), (1, w2)):
                    ps = self.ps.next()
                    for kc in range(NCH):
                        k.mm(ps[:, 0:256], ww[:, kc, :], self.xmp.at(sq)[:, kc, sq, 1:257], start=(kc == 0), stop=(kc == NCH - 1))
                    k.copy("act", qb[:, which, :], ps[:, 0:256])
                qv = View(qb.t[:, 0, :], qb.toks)
                kv = View(qb.t[:, 1, :], qb.toks)
                vv = View(Vp.t[:, 2 * sq:2 * sq + 2, h * 128:(h + 1) * 128], Vp.toks)

                def out_fn(o, h=h, sq=sq):
                    k.ts("dve", self.oT.at(sq // 2)[:, h, sq * 256:(sq + 1) * 256], o[:, 0:256], self.gA[:, l:l + 1], ALU.mult)
                self.attn_core(l, qv, kv, vv, 2, 0, 256, out_fn, Pt, o0, o1, rr)

    def interleave(self, fixed, queue, nslots=2):
        fixed = list(fixed)
        queue = list(queue)
        slots = []
        while fixed or slots or queue:
            if not slots:
                while len(slots) < nslots and queue:
                    slots.append(queue.pop(0))
            for lst in (fixed, slots):
                for g in list(lst):
                    try:
                        next(g)
                    except StopIteration:
                        lst.remove(g)

    def lockstep(self, gens):
        gens = list(gens)
        while gens:
            for g in list(gens):
                try:
                    next(g)
                except StopIteration:
                    gens.remove(g)

    def src_prompt(self, sq):
        def f(tile, kc, shift=0):
            c0 = 1 + tile * 128 + shift
            return self.xmp.at(sq)[:, kc, sq, c0:c0 + 128]
        return f

    def src_prompt_fm(self, sq):
        return lambda kc: self.xmp.at(sq)[:, kc, sq, 1:257]

    def sample_src(self, order, halo):
        state = {"b": None, "buf": None}
        slab = self.slab_ring

        def f(tile, kc, shift=0):
            b = tile // 4
            if state["b"] != b:
                state["b"] = b
                state["buf"] = slab.next()
                self.load_slab(state["buf"], b, halo=halo)
            c0 = 1 + (tile % 4) * 128 + shift
            return state["buf"][:, kc, c0:c0 + 128]
        return f

    def mixB(self, l):
        k, io = self.k, self.io
        L = self.depth
        off = self.moff
        self.slab_ring = Ring(self.slab)
        dpp, off = self.carve(off, [128, 4, 2, 2], F32)
        dps, off = self.carve(off, [128, 2, 2], F32)
        dnr, off = self.carve(off, [128, 64], F32)
        k.load(dpp[:], io["dpar"][l * 16:(l + 1) * 16].partition_broadcast(128).rearrange("p (h d a) -> p h d a", h=4, d=2))
        k.load(dps[:], io["dpar_h"][l * 4:(l + 1) * 4].partition_broadcast(128).rearrange("p (d a) -> p d a", d=2))
        k.load(dnr[:, :], io["dnorm"][l * 64:(l + 1) * 64].partition_broadcast(128))
        k.act(dpp[:, :, :, 0:1], dpp[:, :, :, 0:1], AF.Exp)
        k.ts("dve", dpp[:, :, :, 0:1], dpp[:, :, :, 0:1], -1.0, ALU.mult)
        k.act(dps[:, :, 0:1], dps[:, :, 0:1], AF.Exp)
        k.ts("dve", dps[:, :, 0:1], dps[:, :, 0:1], -1.0, ALU.mult)
        cwb, off = self.carve(off, [128, 576], F32)
        wset = {}
        for nm in ("p", "s"):
            wc, off = self.carve(off, [128, 3, NCH, 192], BF16)
            wba, off = self.carve(off, [128, NCH, 4], BF16)
            wset[nm] = (wc, wba)

        def fold(wc, wba, qkv_srcs, ba_src, cw_src):
            k.load(cwb[:, :], cw_src.partition_broadcast(128))
            cw = cwb.t
            st = self.wst.next()
            sv = View(st.t[:, 0:NCH * 192].rearrange("p (c n) -> p c n", c=NCH), st.toks)
            for j, src in enumerate(qkv_srcs):
                n = src.shape[1]
                k.load(sv[:, :, j * (192 // len(qkv_srcs)):j * (192 // len(qkv_srcs)) + n], src.rearrange("(c p) n -> p c n", p=128))
            for tap in range(3):
                k.tt("pool", wc[:, tap, :, :], sv[:], Opd(cw[:, tap * 192:(tap + 1) * 192].unsqueeze(1).broadcast_to([128, NCH, 192]), cwb.toks),
                     ALU.mult)
            st2 = self.wst.next()
            sv2 = View(st2.t[:, 0:NCH * 4].rearrange("p (c n) -> p c n", c=NCH), st2.toks)
            k.load(sv2[:], ba_src)
            k.copy("pool", wba[:], sv2[:])

        fold(wset["s"][0], wset["s"][1], [io["w_in_h"][l, :, 384:576]],
             io["w_in_h"][l, :, 640:644].rearrange("(c p) n -> p c n", p=128), io["conv_h"][l])
        wcur = {"h": None}

        def getw(h):
            if wcur["h"] != h:
                wcur["h"] = h
                fold(wset["p"][0], wset["p"][1],
                     [io["w_in"][l, :, O_BQKV + j * 256 + h * 64:O_BQKV + j * 256 + (h + 1) * 64] for j in range(3)],
                     io["w_ba_p"][l, :, h, :].rearrange("(c p) n -> p c n", p=128), io["conv_hm"][l, h])
            return wset["p"]
        Oacc_s, off = self.carve(off, [128, 32, 64], F32, ntok=32)
        Oacc_p, off = self.carve(off, [128, 8, 256], F32, ntok=32)
        self.b_base = off
        R = {}
        for nm, w, n in (("qkv", 192, 2), ("et", 192, 1), ("sq", 128, 1), ("sm", 16, 4), ("kbe", 64, 2), ("vb", 64, 2),
                         ("kd", 64, 2), ("qd", 64, 2), ("gb", 64, 2), ("diag", 128, 1), ("dec", 128, 1),
                         ("dS", 128, 1), ("dI", 128, 1), ("Nm", 128, 2), ("QKm", 128, 2), ("NQT", 256, 2),
                         ("Xr", 128, 3), ("U", 192, 2), ("vnew", 64, 2), ("S", 64, 16)):
            R[nm] = Ring([self.carve(off + i * w * 4, [128, w], F32)[0] for i in range(n)])
            off += n * w * 4
        tb = self.tmp.bufs
        R["T3"] = Ring([tb[0], tb[1]])
        pp3 = Buf(tb[2].t, 1)
        R["PP"] = Ring([Buf(tb[2].t[:, 0:256], 1), Buf(tb[2].t[:, 256:512], 1), Buf(tb[3].t[:, 0:256], 1)])
        ID, ONE = self.C(C_ID), self.C(C_ONE)

        def job(d, src, ntiles, wc, wba, dpar, x0_fn, o_fn, fin_fn):
            order = list(range(ntiles)) if d == 0 else list(range(ntiles - 1, -1, -1))
            tri, rem, cstr, cinc = ((C_TRID_F, C_REMD_F, C_STR_F, C_INC_F) if d == 0 else
                                    (C_TRID_B, C_REMD_B, C_STR_B, C_INC_B))
            S = R["S"].next()
            x0_fn(S)
            for tile in order:
                ps1 = self.ps.next()
                n = 0
                for tap in range(3):
                    for kc in range(NCH):
                        k.mm(ps1[:, 0:192], src(tile, kc, tap - 1), wc[:, tap, kc, :], start=(n == 0), stop=(n == 23))
                        n += 1
                for kc in range(NCH):
                    k.mm(ps1[:, 192:196], src(tile, kc), wba[:, kc, :], start=(kc == 0), stop=(kc == NCH - 1))
                qkv, et, sq, sm = R["qkv"].next(), R["et"].next(), R["sq"].next(), R["sm"].next()
                k.act(et[:, :], ps1[:, 0:192], AF.Exp, scale=-1.0)
                k.ts("dve", et[:, :], et[:, :], 1.0, ALU.add)
                k.recip(et[:, :], et[:, :])
                k.tt("dve", qkv[:, :], ps1[:, 0:192], et[:, :], ALU.mult)
                k.tt("dve", sq[:, :], qkv[:, 0:128], qkv[:, 0:128], ALU.mult)
                k.reduce(sm[:, 0:2], Opd(sq.t[:, :].rearrange("p (a e) -> p a e", a=2), sq.toks))
                k.act(sm[:, 8:10], sm[:, 0:2], AF.Sqrt, bias=1e-6)
                k.recip(sm[:, 8:10], sm[:, 8:10])
                k.ts("dve", qkv[:, 0:64], qkv[:, 0:64], sm[:, 8:9], ALU.mult, 0.125, ALU.mult)
                k.ts("dve", qkv[:, 64:128], qkv[:, 64:128], sm[:, 9:10], ALU.mult)
                k.act(sm[:, 2:3], ps1[:, 192 + d:193 + d], AF.Exp, scale=-1.0)
                k.ts("dve", sm[:, 2:3], sm[:, 2:3], 1.0, ALU.add)
                k.recip(sm[:, 2:3], sm[:, 2:3])
                k.ts("dve", sm[:, 3:4], sm[:, 2:3], -1.0, ALU.mult)
                k.act(sm[:, 4:5], ps1[:, 194 + d:195 + d], AF.Exp, bias=dpar[:, d, 1:2])
                k.act(sm[:, 4:5], sm[:, 4:5], AF.Ln, bias=1.0)
                k.ts("dve", sm[:, 4:5], sm[:, 4:5], dpar[:, d, 0:1], ALU.mult)
                k.copy("dve", sm[:, 5:6], sm[:, 4:5])
                if BY & 1:
                    yield
                gb = R["gb"].next()
                k.copy("dve", gb[:, :], Opd(sm.t[:, 4:5].broadcast_to([128, 64]), sm.toks))
                psc = self.ps.next()
                k.mm(psc[:, 0:2], self.C(tri), sm[:, 4:6])
                k.mm(psc[:, 2:4], self.C(rem), sm[:, 4:6])
                k.mm(psc[0:64, 4:6], gb[:, :], self.C(C_CI64, cols=slice(0, 2)))
                k.copy("act", sm[:, 6:7], psc[:, 0:1])
                k.act(sm[:, 12:16], psc[:, 0:4], AF.Exp)
                k.act(sm[0:64, 10:12], psc[0:64, 4:6], AF.Exp)
                kbe, vb, kd, qd = R["kbe"].next(), R["vb"].next(), R["kd"].next(), R["qd"].next()
                k.ts("dve", kbe[:, :], qkv[:, 64:128], sm[:, 2:3], ALU.mult, sm[:, 12:13], ALU.mult)
                k.act(vb[:, :], qkv[:, 128:192], AF.Identity, scale=sm[:, 2:3])
                k.act(kd[:, :], qkv[:, 64:128], AF.Identity, scale=sm[:, 14:15])
                k.ts("dve", qd[:, :], qkv[:, 0:64], sm[:, 12:13], ALU.mult)
                pst = self.ps.next()
                k.mm(pst[0:64, 0:128], qkv[:, 64:128], ID)
                k.mm(pst[0:64, 128:256], qkv[:, 0:64], ID)
                k.mm(pst[0:64, 256:384], qd[:, :], ID)
                T3 = R["T3"].next()
                k.copy("act", T3[0:64, 0:384], pst[0:64, 0:384])
                knT, qnT, qdT = View(T3.t[0:64, 0:128], T3.toks), View(T3.t[0:64, 128:256], T3.toks), View(T3.t[0:64, 256:384], T3.toks)
                if BY & 2:
                    yield
                diag = R["diag"].next()
                k.act(diag[:, :], ID, AF.Identity, scale=sm[:, 6:7])
                psG = self.ps.next()
                k.mm(psG[:, 0:128], knT[:, :], knT[:, :])
                k.mm(psG[:, 128:256], qnT[:, :], knT[:, :])
                k.mm(psG[:, 256:384], ONE, diag[:, :])
                dec, dS, dI, Nm, QKm = (R[x].next() for x in ("dec", "dS", "dI", "Nm", "QKm"))
                k.ts("dve", dec[:, :], psG[:, 256:384], sm[:, 6:7], ALU.subtract, 0.0, ALU.max)
                k.act(dec[:, :], dec[:, :], AF.Exp, scale=-1.0)
                k.tt("pool", dS[:, :], dec[:, :], self.C(cstr), ALU.mult)
                k.tt("pool", dI[:, :], dec[:, :], self.C(cinc), ALU.mult)
                k.stt(Nm[:, :], dS[:, :], sm[:, 3:4], psG[:, 0:128], ALU.mult, ALU.mult)
                k.tt("dve", QKm[:, :], dI[:, :], psG[:, 128:256], ALU.mult)
                psT = self.ps.next()
                k.mm(psT[:, 0:128], Nm[:, :], ID)
                k.mm(psT[:, 128:256], QKm[:, :], ID)
                NQT = R["NQT"].next()
                k.copy("act", NQT[:, :], psT[:, 0:256])
                QKT = View(NQT.t[:, 128:256], NQT.toks)
                if BY & 4:
                    yield
                P, PT = View(Nm.t, Nm.toks), View(NQT.t[:, 0:128], NQT.toks)
                X = R["Xr"].next()
                k.tt("dve", X[:, :], PT[:, :], ID, ALU.add)
                for j in range(5):
                    psD = self.ps.next()
                    k.mm(psD[:, 0:128], PT[:, :], P[:, :])
                    if j < 4:
                        k.mm(psD[:, 128:256], P[:, :], PT[:, :])
                    PP = R["PP"].next()
                    k.copy("act", PP[:, 0:(256 if j < 4 else 128)], psD[:, 0:(256 if j < 4 else 128)])
                    P, PT = View(PP.t[:, 0:128], PP.toks), View(PP.t[:, 128:256], PP.toks)
                    psX = self.ps.next()
                    k.mm(psX[:, 0:128], ID, X[:, :], start=True, stop=False)
                    k.mm(psX[:, 0:128], P[:, :], X[:, :], start=False, stop=True)
                    X2 = R["Xr"].next()
                    k.copy("dve", X2[:, :], psX[:, 0:128])
                    X = X2
                    if BY & 8:
                        yield
                psU = self.ps.next()
                k.mm(psU[:, 0:64], X[:, :], vb[:, :])
                k.mm(psU[0:64, 64:192], kbe[:, :], X[:, :])
                U = R["U"].next()
                k.copy("act", U[:, 0:64], psU[:, 0:64])
                k.copy("act", U[0:64, 64:192], psU[0:64, 64:192])
                if BY & 16:
                    yield
                for ci in ((0, 1) if d == 0 else (1, 0)):
                    r = slice(64 * ci, 64 * ci + 64)
                    psa = self.ps.next()
                    psb = self.ps.next()
                    k.mm(psa[r, 0:64], U[0:64, 64 + 64 * ci:128 + 64 * ci], S[0:64, :])
                    k.mm(psa[r, 64:128], qdT[:, 64 * ci:64 * ci + 64], S[0:64, :])
                    vnew = R["vnew"].next()
                    k.tt("dve", vnew[r, :], U[r, 0:64], psa[r, 0:64], ALU.subtract)
                    k.mm(psb[r, 0:64], QKT[r, 64 * ci:64 * ci + 64], vnew[r, :])
                    k.mm(psb[0:64, 64:128], kd[r, :], vnew[r, :])
                    S2 = R["S"].next()
                    k.stt(S2[0:64, :], S[0:64, :], sm[0:64, 10 + ci:11 + ci], psb[0:64, 64:128], ALU.mult, ALU.add)
                    o_fn(tile, r, psa, psb)
                    S = S2
                    if BY & 32:
                        yield
                if not (BY & 32):
                    yield
            if BSTOP >= 4:
                fin_fn(S)

        done_s = set()

        def x0_sample(d):
            return lambda S: k.load(S[0:64, :], io["s0_d"][l, d])

        def o_sample(tile, r, psa, psb):
            dst = Oacc_s.at(tile)[r, tile, :]
            key = (tile, r.start)
            if key in done_s:
                k.tt("dve", dst, psa[r, 64:128], dst, ALU.add)
            else:
                done_s.add(key)
                k.copy("act", dst, psa[r, 64:128])
            k.tt("dve", dst, psb[r, 0:64], dst, ALU.add)

        sj = [job(d, self.sample_src(None, True), 32, wset["s"][0], wset["s"][1], dps, x0_sample(d), o_sample, lambda S: None)
              for d in range(2)]
        done_p = set()

        def mk_prompt(sq, d, h):
            def x0(S):
                k.memset("dve", S[0:64, :], 0.0)

            def o_fn(tile, r, psa, psb):
                t = sq * 2 + tile
                dst = Oacc_p.at(t * 4 + h)[r, t, h * 64:(h + 1) * 64]
                key = (t, h, r.start)
                if key in done_p:
                    k.tt("dve", dst, psa[r, 64:128], dst, ALU.add)
                else:
                    done_p.add(key)
                    k.copy("act", dst, psa[r, 64:128])
                k.tt("dve", dst, psb[r, 0:64], dst, ALU.add)

            def fin(S):
                k.dma("sp", io["new_sd"][sq, l, d, h], S.t[0:64, :], reads=S.toks)

            def gen():
                wc, wba = getw(h)
                yield from job(d, self.src_prompt(sq), 2, wc, wba, View(dpp.t[:, h], dpp.toks), x0, o_fn, fin)
            return gen()

        self.lockstep(sj)
        for h in range(4):
            for sq in range(4):
                self.lockstep([mk_prompt(sq, d, h) for d in range(2)])

        if BSTOP < 5:
            self.zero_o(l, 4, 6, 128, 192)
            return
        k.barrier()
        off = self.b_base
        P1 = {}
        for nm, w, n in (("sq", 64, 2), ("ss", 4, 2), ("e", 64, 2), ("o", 64, 2)):
            P1[nm] = Ring([self.carve(off + i * w * 4, [128, w], F32)[0] for i in range(n)])
            off += n * w * 4
        osb = Ring([self.carve(off + i * 1024, [128, 512], BF16)[0] for i in range(2)]); off += 2048

        def post(ov, gate_mm, prow):
            sq_, ss, e, o = (P1[x].next() for x in ("sq", "ss", "e", "o"))
            k.act(sq_[:, :], ov, AF.Square, accum=ss[:, 0:1])
            k.act(ss[:, 1:2], ss[:, 0:1], AF.Sqrt, bias=1e-6, scale=1.0 / 64)
            k.recip(ss[:, 1:2], ss[:, 1:2])
            psG = self.ps.next()
            gate_mm(psG)
            k.act(e[:, :], psG[:, 0:64], AF.Exp, scale=-1.0)
            k.ts("dve", e[:, :], e[:, :], 1.0, ALU.add)
            k.recip(e[:, :], e[:, :])
            k.tt("dve", e[:, :], e[:, :], psG[:, 0:64], ALU.mult)
            k.stt(o[:, :], ov, ss[:, 1:2], dnr[:, :], ALU.mult, ALU.mult)
            k.tt("dve", o[:, :], o[:, :], e[:, :], ALU.mult)
            psT = self.ps.next()
            k.mm(psT[prow, 0:128], o[:, :], ID)
            return psT

        for h in range(4):
            wg = self.wtile(io["w_in"][l, :, O_BG + h * 64:O_BG + (h + 1) * 64])
            prow = slice(64 * (h % 2), 64 * (h % 2) + 64)
            for t in range(8):
                sq, tile = t // 2, t % 2
                ov = Oacc_p.at(t * 4 + h)[:, t, h * 64:(h + 1) * 64]

                def gate_mm(ps, sq=sq, tile=tile, wg=wg):
                    sp = self.src_prompt(sq)
                    for kc in range(NCH):
                        k.mm(ps[:, 0:64], sp(tile, kc), wg[:, kc, 0:64], start=(kc == 0), stop=(kc == NCH - 1))
                psT = post(ov, gate_mm, prow)
                k.copy("act", self.oT.at(t // 4)[prow, 4 + h // 2, t * 128:(t + 1) * 128], psT[prow, 0:128])
        wg = self.wtile(io["w_in_h"][l, :, 576:640])
        ssrc = self.sample_src(None, False)
        for b in range(8):
            ob = osb.next()
            for j in range(4):
                tile = b * 4 + j
                ov = Oacc_s.at(tile)[:, tile, :]

                def gate_mm(ps, tile=tile):
                    for kc in range(NCH):
                        k.mm(ps[:, 0:64], ssrc(tile, kc), wg[:, kc, 0:64], start=(kc == 0), stop=(kc == NCH - 1))
                psT = post(ov, gate_mm, slice(0, 64))
                k.copy("act", ob[0:64, j * 128:(j + 1) * 128], psT[0:64, 0:128])
            k.dma("sp", self.ago_in[1][0:64, b * 512:(b + 1) * 512], ob.t[0:64, :], reads=ob.toks, writes=[self.t_ago_in])

    def mixC(self, l):
        k, io = self.k, self.io
        L = self.depth
        off = self.moff
        self.slab_ring = Ring(self.slab)
        lbs = {}
        scr = ARENA_BYTES - (2 * L * 256 * 4 + 2 * 256 * 4)
        for nm, Wd in (("hgrn_lb", 256), ("hgrn_lb_h", 64)):
            e, o2 = self.carve(scr, [128, 2, L, Wd], F32)
            tot, o2 = self.carve(o2, [128, 2, Wd], F32)
            lb, off = self.carve(off, [128, 2, Wd], F32)
            om, off = self.carve(off, [128, 2, Wd], F32)
            k.load(e[:], io[nm].partition_broadcast(128).rearrange("p (d l w) -> p d l w", d=2, l=L))
            k.act(e[:], e[:], AF.Exp)
            k.copy("dve", tot[:], e[:, :, 0, :])
            for j in range(1, L):
                k.tt("dve", tot[:], tot[:], e[:, :, j, :], ALU.add)
            k.recip(tot[:], tot[:])
            if l == 0:
                k.memset("dve", lb[:], 0.0)
            else:
                k.copy("dve", lb[:], e[:, :, 1, :])
                for j in range(2, l + 1):
                    k.tt("dve", lb[:], lb[:], e[:, :, j, :], ALU.add)
                k.tt("dve", lb[:], lb[:], tot[:], ALU.mult)
            k.ts("dve", om[:], lb[:], -1.0, ALU.mult, 1.0, ALU.add)
            lbs[nm] = (lb, om)
            k.barrier()
        hn = self.carve(off, [128, 1], F32)[0]; off += 64
        k.load(hn[:, :], io["hnorm"][l].rearrange("(p o) -> p o", o=1))
        def wS(c0, n):
            b, _ = self.carve(wS.off, [128, NCH, n], BF16)
            wS.off += NCH * n * 2
            st = self.wst.next()
            sv = Opd(st.t[:, 0:NCH * n].rearrange("p (c n) -> p c n", c=NCH), st.toks)
            k.load(sv, c0.rearrange("(c p) n -> p c n", p=128))
            k.copy("pool", b[:], sv)
            return b
        wS.off = off
        ws_q = wS(io["w_in_h"][l, :, 644:708], 64)
        ws_f = [wS(io["w_in_h"][l, :, 708 + 64 * d:772 + 64 * d], 64) for d in range(2)]
        ws_i = wS(io["w_in_h"][l, :, 836:900], 64)
        wpb = [self.carve(wS.off + i * 2048, [128, NCH, 128], BF16)[0] for i in range(4)]
        wS.off += 4 * 2048
        wp_cur = {"pr": None}

        def getw(pr):
            if wp_cur["pr"] != pr:
                wp_cur["pr"] = pr
                srcs = [io["w_in"][l, :, O_CQ + pr * 128:O_CQ + (pr + 1) * 128],
                        io["w_in"][l, :, O_CF + pr * 128:O_CF + (pr + 1) * 128],
                        io["w_in"][l, :, O_CF + 256 + pr * 128:O_CF + 256 + (pr + 1) * 128],
                        io["w_in"][l, :, O_CI + pr * 128:O_CI + (pr + 1) * 128]]
                for b, c0 in zip(wpb, srcs):
                    st = self.wst.next()
                    sv = Opd(st.t[:, 0:NCH * 128].rearrange("p (c n) -> p c n", c=NCH), st.toks)
                    k.load(sv, c0.rearrange("(c p) n -> p c n", p=128))
                    k.copy("pool", b[:], sv)
            return wpb[0], [wpb[1], wpb[2]], wpb[3]
        off = wS.off
        Oacc_s, off = self.carve(off, [128, 2048], F32, ntok=32)
        Oacc_p, off = self.carve(off, [128, 2, 1024], F32, ntok=16)
        self.c_base = off
        R = {}
        for nm, shp, n in (("E1", [128, 256], 2), ("qs", [128, 128], 2), ("f", [128, 128], 2), ("g", [128, 128], 2),
                           ("kk", [128, 128], 2), ("v", [128, 128], 2), ("qd", [128, 128], 2), ("kdi", [128, 128], 2),
                           ("kd", [128, 128], 2), ("qdT", [128, 128], 2), ("kdiT", [128, 128], 2), ("AT", [128, 128], 2),
                           ("gl", [128, 16], 2), ("X", [128, 64], 8), ("fs", [128, 64], 2)):
            sz = shp[1] * 4
            R[nm] = Ring([self.carve(off + i * sz, shp, F32)[0] for i in range(n)])
            off += n * sz
        self.c_off = off

        def job(d, W, src, ntiles, w_q, w_f, w_i, lb, om, lcol, x0_fn, o_fn, fin_fn):
            nh = W // 64
            order = list(range(ntiles)) if d == 0 else list(range(ntiles - 1, -1, -1))
            tri, rem = (C_TRIC_F, C_REMC_F) if d == 0 else (C_TRIC_B, C_REMC_B)
            X = R["X"].next()
            x0_fn(X)
            for tile in order:
                ps1 = self.ps.next()
                for kc in range(NCH):
                    k.mm(ps1[:, 0:W], src(tile, kc), w_q[:, kc, :], start=(kc == 0), stop=(kc == NCH - 1))
                for kc in range(NCH):
                    k.mm(ps1[:, W:2 * W], src(tile, kc), w_f[:, kc, :], start=(kc == 0), stop=(kc == NCH - 1))
                ps2 = self.ps.next()
                for kc in range(NCH):
                    k.mm(ps2[:, 0:W], src(tile, kc), w_i[:, kc, :], start=(kc == 0), stop=(kc == NCH - 1))
                E1, qs, f, g, kk, v = (R[n].next() for n in ("E1", "qs", "f", "g", "kk", "v"))
                k.act(E1[:, 0:2 * W], ps1[:, 0:2 * W], AF.Exp, scale=-1.0)
                k.ts("dve", E1[:, 0:2 * W], E1[:, 0:2 * W], 1.0, ALU.add)
                k.recip(E1[:, 0:2 * W], E1[:, 0:2 * W])
                k.tt("dve", qs[:, 0:W], ps1[:, 0:W], E1[:, 0:W], ALU.mult)
                k.tt("dve", f[:, 0:W], E1[:, W:2 * W], om[:, d, lcol:lcol + W], ALU.mult)
                k.tt("dve", f[:, 0:W], f[:, 0:W], lb[:, d, lcol:lcol + W], ALU.add)
                k.act(g[:, 0:W], f[:, 0:W], AF.Ln)
                k.ts("pool", kk[:, 0:W], f[:, 0:W], -1.0, ALU.mult, 1.0, ALU.add)
                k.copy("act", v[:, 0:W], ps2[:, 0:W])
                yield
                psb = self.ps.next()
                k.mm(psb[:, 0:W], self.C(tri), g[:, 0:W])
                k.mm(psb[:, W:2 * W], self.C(rem), g[:, 0:W])
                k.mm(psb[0:W, 2 * W:2 * W + 8], g[:, 0:W], self.C(C_CI16, cols=slice(0, 8)))
                qd, kdi, kd, gl = (R[n].next() for n in ("qd", "kdi", "kd", "gl"))
                Eb = R["E1"].next()
                k.act(Eb[:, 0:W], psb[:, 0:W], AF.Exp)
                k.tt("pool", qd[:, 0:W], qs[:, 0:W], Eb[:, 0:W], ALU.mult)
                k.act(Eb[:, W:2 * W], psb[:, 0:W], AF.Exp, scale=-1.0)
                k.tt("pool", kdi[:, 0:W], kk[:, 0:W], Eb[:, W:2 * W], ALU.mult)
                Ed = R["f"].next()
                k.act(Ed[:, 0:W], psb[:, W:2 * W], AF.Exp)
                k.tt("pool", kd[:, 0:W], kk[:, 0:W], Ed[:, 0:W], ALU.mult)
                glo = gl.t[0:W, 0:8] if d == 0 else gl.t[0:W, 7::-1]
                k.act(Opd(glo, gl.toks), psb[0:W, 2 * W:2 * W + 8], AF.Exp)
                k.copy("dve", gl[0:W, 8:16], gl[0:W, 0:8])
                k.memset("dve", gl[0:W, 8:9], 0.0)
                yield
                qdT, kdiT = R["qdT"].next(), R["kdiT"].next()
                pst = self.ps.next()
                k.mm(pst[0:W, 0:128], qd[:, 0:W], self.C(C_ID))
                k.mm(pst[0:W, 128:256], kdi[:, 0:W], self.C(C_ID))
                if 'c' not in CSUB:
                    if 'x' not in CSUB:
                        k.copy("act", qdT[0:W, :], pst[0:W, 0:128])
                    if 'y' not in CSUB:
                        k.copy("dve", kdiT[0:W, :], pst[0:W, 128:256])
                psKV = self.ps.next()
                ATs = []
                for h in range(nh):
                    r0 = 64 * h
                    if 'a' in CSUB:
                        continue
                    psA = self.ps.next()
                    k.mm(psA[:, 0:128], kdiT[r0:r0 + 64, :], qdT[r0:r0 + 64, :])
                    AT = R["AT"].next()
                    k.tt("dve", AT[:, :], psA[:, 0:128], self.C(tri), ALU.mult)
                    ATs.append(AT)
                    if 'b' in CSUB:
                        continue
                    vx = self.tmp.next()
                    k.tt("pool", Opd(vx.t[:, :].rearrange("p (c e) -> p c e", c=8), vx.toks),
                         Opd(v.t[:, r0:r0 + 64].unsqueeze(1).broadcast_to([128, 8, 64]), v.toks),
                         Opd(self.cst.t[:, C_CI16, 0:8].unsqueeze(2).broadcast_to([128, 8, 64]), self.cst.toks), ALU.mult)
                    k.mm(psKV[r0:r0 + 64, :], kd[:, r0:r0 + 64], vx[:, :])
                KVs, GLx, XS = self.tmp.next(), self.tmp.next(), self.tmp.next()
                kv3 = KVs.t[0:W, :].rearrange("p (e c) -> p e c", c=8)
                kvo = kv3 if d == 0 else kv3[:, :, ::-1]
                k.copy("act", Opd(kvo, KVs.toks), Opd(psKV.t[0:W, :].rearrange("p (c e) -> p e c", c=8), psKV.toks))
                k.stt(Opd(kv3[:, :, 0], KVs.toks), X[0:W, :], gl[0:W, 0:1], Opd(kv3[:, :, 0], KVs.toks), ALU.mult, ALU.add)
                k.copy("pool", Opd(GLx.t[0:W, :].rearrange("p (e c) -> p e c", c=8), GLx.toks),
                       Opd(gl.t[0:W, 8:16].unsqueeze(1).broadcast_to([W, 64, 8]), gl.toks))
                k.scan(XS[0:W, :], GLx[0:W, :], KVs[0:W, :], 0.0, ALU.mult, ALU.add)
                xs3 = XS.t[0:W, :].rearrange("p (e c) -> p e c", c=8)
                Xn = R["X"].next()
                k.copy("dve", Xn[0:W, :], Opd(xs3[:, :, 7], XS.toks))
                orow = o_fn(tile, None)
                psO = self.ps.next()
                for h in range(nh):
                    r0 = 64 * h
                    ro = orow + r0
                    k.mm(psO[ro:ro + 64, 0:128], v[:, r0:r0 + 64], ATs[h][:, :], start=True, stop=False)
                    for i in range(8):
                        c = i if d == 0 else 7 - i
                        xc = X[r0:r0 + 64, :] if i == 0 else Opd(xs3[r0:r0 + 64, :, i - 1], XS.toks)
                        k.mm(psO[ro:ro + 64, 16 * c:16 * c + 16], xc, qdT[r0:r0 + 64, 16 * c:16 * c + 16],
                             start=False, stop=(i == 7))
                o_fn(tile, psO)
                X = Xn
                yield
            if CSTOP >= 4:
                fin_fn(X)

        def x0_sample(d):
            def f(X):
                k.load(X[0:64, :], io["s0_h"][l, d])
            return f

        done_s = set()

        def o_sample(tile, psO):
            row = 0 if tile < 16 else 64
            if psO is None:
                return row
            col = (tile % 16) * 128
            dst = Oacc_s.at(tile)[row:row + 64, col:col + 128]
            if tile in done_s:
                k.tt("dve", dst, psO[row:row + 64, 0:128], dst, ALU.add)
            else:
                done_s.add(tile)
                k.copy("act", dst, psO[row:row + 64, 0:128])

        sj = [job(d, 64, self.sample_src(None, False), 32, ws_q, ws_f[d], ws_i, lbs["hgrn_lb_h"][0], lbs["hgrn_lb_h"][1], 0,
                  x0_sample(d), o_sample, lambda X: None) for d in range(2)]

        done_p = set()

        def mk_prompt(sq, d, pr):
            def x0(X):
                k.memset("dve", X[:, :], 0.0)

            def o_fn(tile, psO):
                if psO is None:
                    return 0
                key = (sq, pr, tile)
                c0 = sq * 256 + tile * 128
                dst = Oacc_p.at((sq * 2 + tile) * 2 + pr)[:, pr, c0:c0 + 128]
                if key in done_p:
                    k.tt("dve", dst, psO[:, 0:128], dst, ALU.add)
                else:
                    done_p.add(key)
                    k.copy("act", dst, psO[:, 0:128])

            def fin(X):
                fs = R["fs"].next()
                k.copy("act", fs[:, :], X[:, :])
                k.dma("sp", io["new_sh"][sq, l, d, pr * 128:(pr + 1) * 128, :], fs.t[:, :], reads=fs.toks)
            def gen():
                wq_, wf_, wi_ = getw(pr)
                yield from job(d, 128, self.src_prompt(sq), 2, wq_, wf_[d], wi_, lbs["hgrn_lb"][0], lbs["hgrn_lb"][1], pr * 128,
                               x0, o_fn, fin)
            return gen()

        self.lockstep(sj)
        for pr in range(2):
            for sq in range(4):
                self.lockstep([mk_prompt(sq, d, pr) for d in range(2)])

        if CSTOP < 5:
            self.zero_o(l, 6, 8, 192, 256)
            return
        k.barrier()
        off = self.c_base
        sqb = Ring([self.carve(off + i * 2048, [128, 512], F32)[0] for i in range(2)]); off += 4096
        rsb = Ring([self.carve(off + i * 2048, [128, 512], F32)[0] for i in range(2)]); off += 4096
        osb = Ring([self.carve(off + i * 1024, [128, 512], BF16)[0] for i in range(2)]); off += 2048

        def post(ov, rows, n, gate_mm, out_fn):
            sq_ = sqb.next()
            k.act(sq_[rows, 0:n], ov, AF.Square)
            psN = self.ps.next()
            k.mm(psN[rows, 0:n], self.C(C_BLK64, rows=rows, cols=rows), sq_[rows, 0:n])
            rs = rsb.next()
            k.act(rs[rows, 0:n], psN[rows, 0:n], AF.Sqrt, bias=1e-6)
            k.recip(rs[rows, 0:n], rs[rows, 0:n])
            k.tt("dve", rs[rows, 0:n], rs[rows, 0:n], ov, ALU.mult)
            psG = self.ps.next()
            gate_mm(psG)
            e = sqb.next()
            k.act(e[rows, 0:n], psG[rows, 0:n], AF.Exp, scale=-1.0)
            k.ts("dve", e[rows, 0:n], e[rows, 0:n], 1.0, ALU.add)
            k.recip(e[rows, 0:n], e[rows, 0:n])
            k.tt("dve", e[rows, 0:n], e[rows, 0:n], psG[rows, 0:n], ALU.mult)
            out_fn(rs, e)

        for pr in range(2):
            wg = self.wtile(io["w_in"][l, :, O_CG + pr * 128:O_CG + (pr + 1) * 128])
            for sq in range(4):
                toks = [Oacc_p.toks[(sq * 2 + t) * 2 + pr] for t in range(2)]
                ov = Opd(Oacc_p.t[:, pr, sq * 256:(sq + 1) * 256], toks)

                def gate_mm(ps, sq=sq, wg=wg):
                    fm = self.src_prompt_fm(sq)
                    for kc in range(NCH):
                        k.mm(ps[:, 0:256], wg[:, kc, :], fm(kc), start=(kc == 0), stop=(kc == NCH - 1))

                def out_fn(rs, e, sq=sq, pr=pr):
                    k.stt(self.oT.at(sq // 2)[:, 6 + pr, sq * 256:(sq + 1) * 256], rs[:, 0:256], hn[:, 0:1], e[:, 0:256],
                          ALU.mult, ALU.mult)
                post(ov, slice(0, 128), 256, gate_mm, out_fn)
        wg = self.wtile(io["w_in_h"][l, :, 900:964])
        for b in range(8):
            sb = self.slab_ring.next()
            self.load_slab(sb, b)
            rows = slice(0, 64) if b < 4 else slice(64, 128)
            c0 = (b % 4) * 512
            toks = [Oacc_s.toks[b * 4 + t] for t in range(4)]
            ov = Opd(Oacc_s.t[rows, c0:c0 + 512], toks)

            def gate_mm(ps, sb=sb, rows=rows, wg=wg):
                for kc in range(NCH):
                    k.mm(ps[rows, 0:512], wg[:, kc, 0:64], sb[:, kc, 1:513], start=(kc == 0), stop=(kc == NCH - 1))

            def out_fn(rs, e, b=b, rows=rows):
                ob = osb.next()
                k.stt(ob[rows, :], rs[rows, :], hn[rows, 0:1], e[rows, :], ALU.mult, ALU.mult)
                k.dma("sp", self.ago_in[1][64:128, b * 512:(b + 1) * 512], ob.t[rows, :], reads=ob.toks, writes=[self.t_ago_in])
            post(ov, rows, 512, gate_mm, out_fn)

    def layer_norm(self, l, which, g, b):
        k = self.k
        xs = self.xs[g].at(b)
        sl = slice(b * 512, (b + 1) * 512)
        ps_m = self.ps.next()
        ps_q = self.ps.next()
        for kc in range(NCH):
            k.mm(ps_m[:, :], self.C(C_MEAN), xs[:, kc, sl], start=(kc == 0), stop=(kc == NCH - 1))
        for kc in range(NCH):
            sq = self.usq.next()
            k.act(sq[:, :], xs[:, kc, sl], AF.Square)
            k.mm(ps_q[:, :], self.C(C_MEAN), sq[:, :], start=(kc == 0), stop=(kc == NCH - 1))
        mean = self.stat.next()
        rstd = self.stat.next()
        k.copy("act", mean[:, :], ps_m[:, :])
        k.tt("dve", rstd[:, :], mean[:, :], mean[:, :], ALU.mult)
        k.tt("dve", rstd[:, :], ps_q[:, :], rstd[:, :], ALU.subtract)
        k.act(rstd[:, :], rstd[:, :], AF.Sqrt, bias=LN_EPS / ALPHA ** 2)
        k.recip(rstd[:, :], rstd[:, :])
        for kc in range(NCH):
            t = self.tmp.next()
            k.tt("dve", t[:, :], xs[:, kc, sl], mean[:, :], ALU.subtract)
            k.tt("dve", t[:, :], t[:, :], rstd[:, :], ALU.mult)
            k.act(xs[:, kc, sl], t[:, :], AF.Identity, scale=self.lng[:, l, which, kc:kc + 1],
                  bias=self.lnb[:, l, which, kc:kc + 1])

    def dense_group(self, l, g):
        k, io = self.k, self.io
        xs = self.xs[g]
        wname = "w_out_p" if g == 0 else "w_out_s"
        for oc in range(NCH):
            w = self.load_w_bf16(io[wname][l, :, oc * 128:(oc + 1) * 128].rearrange("(c p) n -> p c n", p=128), 128, 128)
            for b in range(2):
                sl = slice(b * 512, (b + 1) * 512)
                ps = self.ps.next()
                for kc in range(NCH):
                    k.mm(ps[:, :], w[:, kc, :], self.oT.at(b)[:, kc, sl], start=(kc == 0), stop=(kc == NCH - 1))
                k.stt(xs.at(b)[:, oc, sl], ps[:, :], self.modv(l, 2, g, oc), xs.at(b)[:, oc, sl], ALU.mult, ALU.add)
        for b in range(2):
            self.layer_norm(l, 0, g, b)
        for b in range(2):
            sl = slice(b * 512, (b + 1) * 512)
            for kc in range(NCH):
                k.act(self.xm2.at(b)[:, kc, sl], xs.at(b)[:, kc, sl], AF.Identity,
                      scale=self.modv(l, 4, g, kc), bias=self.modv(l, 3, g, kc))
        for f in range(NF):
            st = self.wst.next()
            wb = self.wbf.next()
            sv = Opd(st.t[:, 0:2048].rearrange("p (c j n) -> p c j n", c=NCH, j=2), st.toks)
            wv = View(wb.t[:, 0:2048].rearrange("p (c j n) -> p c j n", c=NCH, j=2), wb.toks)
            for j in range(2):
                k.load(Opd(sv.ap[:, :, j, :], st.toks),
                       io["w_ffn_in"][l, :, j * D_FF + f * 128:j * D_FF + (f + 1) * 128].rearrange("(c p) n -> p c n", p=128))
            k.copy("pool", wv[:], sv)
            for b in range(2):
                sl = slice(b * 512, (b + 1) * 512)
                ps_g = self.ps.next()
                ps_u = self.ps.next()
                for kc in range(NCH):
                    k.mm(ps_g[:, :], wv[:, kc, 0, :], self.xm2.at(b)[:, kc, sl], start=(kc == 0), stop=(kc == NCH - 1))
                for kc in range(NCH):
                    k.mm(ps_u[:, :], wv[:, kc, 1, :], self.xm2.at(b)[:, kc, sl], start=(kc == 0), stop=(kc == NCH - 1))
                t = self.tmp.next()
                k.act(t[:, :], ps_g[:, :], AF.Silu)
                k.tt("dve", self.hT.at(b)[:, f, sl], t[:, :], ps_u[:, :], ALU.mult)
        for oc in range(NCH):
            wvs = []
            for hf in range(2):
                st = self.wst.next()
                wb = self.wbf.next()
                sv = Opd(st.t[:, 0:11 * 128].rearrange("p (f n) -> p f n", f=11), st.toks)
                wv = View(wb.t[:, 0:11 * 128].rearrange("p (f n) -> p f n", f=11), wb.toks)
                k.load(sv, io["w_ffn_out"][l, hf * 1408:(hf + 1) * 1408, oc * 128:(oc + 1) * 128]
                       .rearrange("(f p) n -> p f n", p=128))
                k.copy("pool", wv[:], sv)
                wvs.append(wv)
            for b in range(2):
                sl = slice(b * 512, (b + 1) * 512)
                ps = self.ps.next()
                for f in range(NF):
                    k.mm(ps[:, :], wvs[f // 11][:, f % 11, :], self.hT.at(b)[:, f, sl], start=(f == 0), stop=(f == NF - 1))
                k.stt(xs.at(b)[:, oc, sl], ps[:, :], self.modv(l, 5, g, oc), xs.at(b)[:, oc, sl], ALU.mult, ALU.add)
        for b in range(2):
            self.layer_norm(l, 1, g, b)


def head_cols(h):
    r = lambda o, n: list(range(o, o + n))
    cols = []
    cols += r(O_AQ + h * 128, 128) + r(O_AK + h * 128, 128) + r(O_AV + h * 128, 128)
    cols += r(O_BQKV + h * 64, 64) + r(O_BQKV + 256 + h * 64, 64) + r(O_BQKV + 512 + h * 64, 64)
    cols += r(O_BG + h * 64, 64)
    cols += [O_BBETA + h, O_BBETA + 4 + h, O_BA + h, O_BA + 4 + h]
    cols += r(O_CQ + h * 64, 64) + r(O_CF + h * 64, 64) + r(O_CF + 256 + h * 64, 64)
    cols += r(O_CI + h * 64, 64) + r(O_CG + h * 64, 64)
    return cols


def w_out_perm():
    rows = []
    for r in range(4):
        rows += list(range(r * 128, (r + 1) * 128))
        rows += list(range(512 + r * 64, 512 + (r + 1) * 64))
        rows += list(range(768 + r * 64, 768 + (r + 1) * 64))
    return rows


def rope_tables():
    t = np.arange(DEC_SEQ)
    inv = (np.float32(10000.0) ** (-np.arange(16, dtype=np.float32) / np.float32(16))).astype(np.float32)
    out = np.zeros((2, 128, DEC_SEQ), np.float32)
    for p in range(128):
        d = p % 64
        pos = (t // 64) if d < 32 else (t % 64)
        ang = pos.astype(np.float32) * inv[d % 16]
        out[0, p] = np.cos(ang)
        out[1, p] = np.sin(ang)
    return out


def prep_inputs(inp, depth=DEPTH):
    f = lambda a: np.ascontiguousarray(np.asarray(a, dtype=np.float32))
    L = depth
    consts = make_consts()
    shared = {
        "w_mod": f(inp["w_mod"][:L]),
        "b_mod": f(np.asarray(inp["b_mod"])[:L].reshape(L, 48, 128).transpose(0, 2, 1)),
        "w_in": f(inp["w_in"][:L]),
        "w_out_p": f(inp["w_out"][:L]),
        "w_out_s": f(np.asarray(inp["w_out"])[:L][:, w_out_perm(), :]),
        "ln_g": f(np.asarray(inp["ln_g"])[:L].reshape(L, 2, NCH, 128).transpose(0, 3, 1, 2)),
        "ln_b": f(np.asarray(inp["ln_b"])[:L].reshape(L, 2, NCH, 128).transpose(0, 3, 1, 2)),
        "w_ffn_in": f(inp["w_ffn_in"][:L]),
        "w_ffn_out": f(inp["w_ffn_out"][:L]),
        "consts": consts,
        "rope_cs": rope_tables(),
        "diff_lambda": f(np.asarray(inp["diff_lambda"])[:L].reshape(-1)),
        "diff_norm": f(np.asarray(inp["diff_norm"])[:L].T),
        "hgrn_lb": f(np.asarray(inp["hgrn_lb"])[:, :L].reshape(-1)),
        "conv_hm": f(np.asarray(inp["conv_w"])[:L].reshape(L, 3, 3, 4, 64).transpose(0, 3, 1, 2, 4).reshape(L, 4, 576)),
        "w_ba_p": f(np.stack([np.asarray(inp["w_in"])[:L][:, :, [O_BBETA + h, O_BBETA + 4 + h, O_BA + h, O_BA + 4 + h]]
                              for h in range(4)], 2)),
        "dpar": f(np.stack([np.asarray(inp["delta_a_log"])[:L], np.asarray(inp["delta_dt_bias"])[:L]], -1)
                  .transpose(0, 2, 1, 3).reshape(-1)),
        "dnorm": f(np.asarray(inp["delta_norm"])[:L].reshape(-1)),
        "hnorm": f(np.tile(np.asarray(inp["hgrn_norm"])[:L], (1, 2))),
    }
    xp = np.asarray(inp["x_prompt"], np.float32)
    xsm = np.asarray(inp["x_sample"], np.float32)
    maps = []
    for c in range(8):
        s, r = c // 4, c % 4
        m = dict(shared)
        m["xT_p"] = f(xp[4 * c:4 * c + 4].reshape(TOK, D).T)
        m["xT_s"] = f(xsm[s, r * TOK:(r + 1) * TOK].T)
        cond = np.stack([np.asarray(inp["c_ctx"], np.float32), np.asarray(inp["c"], np.float32)[s]], -1)
        m["cond"] = f(cond.reshape(NCH, 128, 2).transpose(1, 0, 2))
        m["w_in_h"] = f(np.asarray(inp["w_in"])[:L][:, :, head_cols(r)])
        m["ctx_k"] = f(np.asarray(inp["cache_attn_k"])[s, :L, :, r, :])
        m["ctx_v"] = f(np.asarray(inp["cache_attn_v"])[s, :L, :, r, :])
        m["hgrn_lb_h"] = f(np.asarray(inp["hgrn_lb"])[:, :L, r * 64:(r + 1) * 64].reshape(-1))
        m["s0_h"] = f(np.asarray(inp["state_hgrn"])[s, :L, :, r])
        m["s0_d"] = f(np.asarray(inp["state_delta"])[s, :L, :, r])
        m["conv_h"] = f(np.asarray(inp["conv_w"])[:L].reshape(L, 3, 3, 4, 64)[:, :, :, r, :].reshape(L, 576))
        m["dpar_h"] = f(np.stack([np.asarray(inp["delta_a_log"])[:L, :, r], np.asarray(inp["delta_dt_bias"])[:L, :, r]], -1).reshape(-1))
        maps.append(m)
    return maps


_PROG = {}


def get_prog(depth=DEPTH, mixers=("A", "B", "C"), dbg=None):
    key = (depth, tuple(mixers), tuple(sorted((dbg or {}).items())))
    if key not in _PROG:
        _PROG[key] = Prog(depth, mixers, dbg)
    return _PROG[key]


def run(inp, depth=DEPTH, mixers=("A", "B", "C"), dbg=None, trace=False):
    prog = get_prog(depth, mixers, dbg)
    maps = prep_inputs(inp, depth)
    res = run_bass_kernel_spmd(prog.nc, maps, core_ids=list(range(8)), trace=trace)
    return res


def assemble(res, depth=DEPTH):
    R = res.results
    y_p = np.concatenate([R[c]["yT_p"].T.reshape(4, SEQ, D) for c in range(8)], 0)
    y_s = np.stack([np.concatenate([R[s * 4 + r]["yT_s"].T for r in range(4)], 0) for s in range(2)], 0)
    L = depth
    nk = np.concatenate([R[c]["new_k"] for c in range(8)], 0).reshape(32, L, SEQ, 4, 128)
    nv = np.concatenate([R[c]["new_v"] for c in range(8)], 0).reshape(32, L, SEQ, 4, 128)
    nsh = np.concatenate([R[c]["new_sh"] for c in range(8)], 0).reshape(32, L, 2, 4, 64, 64)
    nsd = np.concatenate([R[c]["new_sd"] for c in range(8)], 0).reshape(32, L, 2, 4, 64, 64)
    return (y_p.astype(np.float32), y_s.astype(np.float32), nk.astype(np.float32), nv.astype(np.float32),
            nsd.astype(np.float32), nsh.astype(np.float32))


def kernel(**inputs):
    res = run(inputs)
    return assemble(res)
```

```python
import math
import os
from contextlib import ExitStack
CSTOP = int(os.environ.get('CSTOP', '99'))
CSUB = os.environ.get('CSUB', '')
BSTOP = int(os.environ.get('BSTOP', '99'))
BY = int(os.environ.get('BY', '63'))

import numpy as np
import concourse.bass as bass
import concourse.mybir as mybir
from concourse.bass_utils import run_bass_kernel_spmd

F32 = mybir.dt.float32
BF16 = mybir.dt.bfloat16
ALU = mybir.AluOpType
AF = mybir.ActivationFunctionType
AX = mybir.AxisListType

D = 1024
NCH = 8
DEPTH = 4
SEQ = 256
DEC_SEQ = 4096
PAST = 256
D_FF = 2816
NF = 22
D_IN = 3856
ALPHA = (2 * DEPTH) ** 0.25
LN_EPS = 1e-5
TOK = 1024
GROUPS = [[0, 1, 2, 3], [4, 5, 6, 7]]

O_AQ, O_AK, O_AV = 0, 512, 1024
O_BQKV, O_BG, O_BBETA, O_BA = 1536, 2304, 2560, 2568
O_CQ, O_CF, O_CI, O_CG = 2576, 2832, 3344, 3600


class Tok:
    __slots__ = ("w", "rs", "excl")

    def __init__(self):
        self.w = None
        self.rs = {}
        self.excl = False


class Opd:
    __slots__ = ("ap", "toks")

    def __init__(self, ap, toks):
        self.ap = ap
        self.toks = toks


class View:
    __slots__ = ("t", "toks")

    def __init__(self, t, toks):
        self.t = t
        self.toks = toks

    def __getitem__(self, idx):
        return Opd(self.t[idx], self.toks)


class Buf:
    def __init__(self, t, ntok=1):
        self.t = t
        self.toks = [Tok() for _ in range(ntok)]

    def at(self, *keys):
        return View(self.t, [self.toks[k] for k in keys])

    def all(self):
        return View(self.t, self.toks)

    def __getitem__(self, idx):
        return Opd(self.t[idx], self.toks)


class Ring:
    def __init__(self, bufs):
        self.bufs = bufs
        self.i = 0

    def next(self):
        b = self.bufs[self.i % len(self.bufs)]
        self.i += 1
        return b


class KB:
    ENGS = ("pe", "dve", "act", "pool", "sp")

    def __init__(self):
        self.nc = bass.Bass("TRN2", target_bir_lowering=False)
        self.es = ExitStack()
        self.streams = {e: [] for e in self.ENGS}
        self.count = {e: 0 for e in self.ENGS}
        self.seen = {e: {} for e in self.ENGS}
        self.latest = {}
        self.sems = {}
        for e in self.ENGS:
            self.sems[e] = self.es.enter_context(self.nc.semaphore("c_" + e))
        self.ndsem = {"sp": 24, "pool": 12, "act": 8}
        self.dsem_i = {q: 0 for q in self.ndsem}
        self.dsem_v = {}
        for q, n in self.ndsem.items():
            for j in range(n):
                k = "d_%s_%d" % (q, j)
                self.sems[k] = self.es.enter_context(self.nc.semaphore(k))
                self.dsem_v[k] = 0
        self.nalloc = 0
        self.pending_barrier = {e: None for e in self.ENGS}

    def sbuf(self, name, shape, dt, ntok=1):
        t = self.es.enter_context(self.nc.sbuf_tensor(name, list(shape), dt))
        return Buf(t, ntok)

    def psum(self, name, shape, dt=F32):
        t = self.es.enter_context(self.nc.psum_tensor(name, list(shape), dt))
        b = Buf(t, 1)
        b.toks[0].excl = True
        return b

    def ring(self, name, shape, dt, n):
        return Ring([self.sbuf("%s%d" % (name, i), shape, dt) for i in range(n)])

    def dram(self, name, shape, dt, kind="Internal"):
        return self.nc.dram_tensor(name, list(shape), dt, kind=kind)

    def _deps(self, eng, reads, writes):
        need = {}

        def add(ref):
            if ref is None:
                return
            k, v = ref
            if need.get(k, 0) < v:
                need[k] = v

        for t in reads:
            add(t.w)
            if t.excl:
                for k2, v in t.rs.items():
                    if k2 != eng:
                        add((k2, v))
        for t in writes:
            add(t.w)
            for k, v in t.rs.items():
                add((k, v))
        pb = self.pending_barrier[eng]
        if pb is not None:
            for k, v in pb.items():
                add((k, v))
            self.pending_barrier[eng] = None
        if eng == "pe":
            need.pop("pe", None)
        seen = self.seen[eng]
        waits = []
        for k, v in need.items():
            if seen.get(k, 0) < v:
                seen[k] = v
                waits.append((k, v))
        return waits

    def op(self, eng, fn, reads=(), writes=()):
        waits = self._deps(eng, reads, writes)
        self.count[eng] += 1
        idx = self.count[eng]
        ref = (eng, idx)
        for t in reads:
            t.rs[eng] = idx
        for t in writes:
            t.w = ref
            t.rs = {}
        self.latest[eng] = idx
        self.streams[eng].append((waits, fn, (eng, 1)))

    def dma(self, q, out_ap, in_ap, reads=(), writes=(), fn=None, selfinc=False):
        waits = self._deps(q, reads, writes)
        j = self.dsem_i[q] % self.ndsem[q]
        self.dsem_i[q] += 1
        k = "d_%s_%d" % (q, j)
        prev = self.dsem_v[k]
        if prev > 0 and self.seen[q].get(k, 0) < prev:
            self.seen[q][k] = prev
            waits.append((k, prev))
        self.dsem_v[k] = prev + 16
        ref = (k, prev + 16)
        for t in reads:
            t.rs[k] = prev + 16
        for t in writes:
            t.w = ref
            t.rs = {}
        self.latest[k] = prev + 16
        if fn is None:
            fn = lambda e, o=out_ap, i=in_ap: e.dma_start(out=o, in_=i)
        self.streams[q].append((waits, fn, ("SELF", k) if selfinc else (k, 16)))

    def cc(self, fn, reads=(), writes=()):
        waits = self._deps("pool", reads, writes)
        if "cc" not in self.sems:
            self.sems["cc"] = self.es.enter_context(self.nc.semaphore("cc"))
            self.ccv = 0
        prev = self.ccv
        if prev > 0 and self.seen["pool"].get("cc", 0) < prev:
            self.seen["pool"]["cc"] = prev
            waits.append(("cc", prev))
        self.ccv = prev + 1
        ref = ("cc", prev + 1)
        for t in reads:
            t.rs["cc"] = prev + 1
        for t in writes:
            t.w = ref
            t.rs = {}
        self.latest["cc"] = prev + 1
        self.streams["pool"].append((waits, fn, ("cc", 1)))

    def barrier(self):
        snap = dict(self.latest)
        for e in self.ENGS:
            pb = self.pending_barrier[e]
            if pb is None:
                self.pending_barrier[e] = dict(snap)
            else:
                for k, v in snap.items():
                    if pb.get(k, 0) < v:
                        pb[k] = v

    def raw(self, eng, fn):
        self.streams[eng].append(([], fn, None))

    def finish(self):
        nc = self.nc
        final = dict(self.latest)
        engmap = {"pe": "tensor", "dve": "vector", "act": "scalar", "pool": "gpsimd", "sp": "sync"}
        with nc.Block() as block:
            for e in self.ENGS:
                stream = self.streams[e]

                def body(h, e=e, stream=stream):
                    sems = self.sems
                    for waits, fn, inc in stream:
                        for k, v in waits:
                            h.wait_ge(sems[k], v)
                        if inc is not None and inc[0] == "SELF":
                            fn(h, sems[inc[1]])
                            continue
                        ins = fn(h)
                        if inc is not None:
                            ins.then_inc(sems[inc[0]], inc[1])
                    for k, v in final.items():
                        if k == e:
                            continue
                        h.wait_ge(sems[k], v)

                getattr(block, engmap[e])(body)
        self.es.close()
        return nc

    @staticmethod
    def _tk(*ops):
        out = []
        for o in ops:
            if isinstance(o, Opd):
                out.extend(o.toks)
        return out

    @staticmethod
    def _ap(o):
        return o.ap if isinstance(o, Opd) else o

    def mm(self, out, lhsT, rhs, start=True, stop=True):
        o, l, r = out.ap, lhsT.ap, rhs.ap
        self.op("pe", lambda e: e.matmul(o, l, r, start=start, stop=stop),
                reads=self._tk(lhsT, rhs), writes=self._tk(out))

    def act(self, out, in_, func, bias=0.0, scale=1.0, accum=None, eng="act"):
        o, i, b, s = out.ap, in_.ap, self._ap(bias), self._ap(scale)
        kw = {}
        if accum is not None:
            kw["accum_out"] = accum.ap
        self.op("act", lambda e: e.activation(o, i, func, bias=b, scale=s, **kw),
                reads=self._tk(in_, bias, scale), writes=self._tk(out, accum))

    def tt(self, eng, out, in0, in1, op):
        o, a, b = out.ap, in0.ap, in1.ap
        self.op(eng, lambda e: e.tensor_tensor(o, a, b, op), reads=self._tk(in0, in1), writes=self._tk(out))

    def ts(self, eng, out, in0, s1, op0, s2=None, op1=None, accum=None):
        o, a, x1, x2 = out.ap, in0.ap, self._ap(s1), self._ap(s2)
        kw = {}
        if accum is not None:
            kw["accum_out"] = accum.ap
        if op1 is None:
            fn = lambda e: e.tensor_scalar(o, a, x1, None, op0, **kw)
        else:
            fn = lambda e: e.tensor_scalar(o, a, x1, x2, op0, op1, **kw)
        self.op(eng, fn, reads=self._tk(in0, s1, s2), writes=self._tk(out, accum))

    def stt(self, out, in0, scalar, in1, op0, op1, eng="dve"):
        o, a, s, b = out.ap, in0.ap, self._ap(scalar), in1.ap
        self.op(eng, lambda e: e.scalar_tensor_tensor(o, a, s, b, op0, op1),
                reads=self._tk(in0, scalar, in1), writes=self._tk(out))

    def copy(self, eng, out, in_):
        o, i = out.ap, in_.ap
        if eng == "act":
            self.op("act", lambda e: e.copy(o, i), reads=self._tk(in_), writes=self._tk(out))
        else:
            self.op(eng, lambda e: e.tensor_copy(o, i), reads=self._tk(in_), writes=self._tk(out))

    def memset(self, eng, out, val):
        o = out.ap
        self.op(eng, lambda e: e.memset(o, val), writes=self._tk(out))

    def recip(self, out, in_):
        o, i = out.ap, in_.ap
        self.op("dve", lambda e: e.reciprocal(o, i), reads=self._tk(in_), writes=self._tk(out))

    def reduce(self, out, in_, op=ALU.add, axis=AX.X):
        o, i = out.ap, in_.ap
        self.op("dve", lambda e: e.tensor_reduce(o, i, axis, op), reads=self._tk(in_), writes=self._tk(out))

    def scan(self, out, d0, d1, initial, op0, op1):
        o, a, b, ini = out.ap, d0.ap, d1.ap, self._ap(initial)
        self.op("dve", lambda e: e.tensor_tensor_scan(o, a, b, ini, op0, op1),
                reads=self._tk(d0, d1, initial), writes=self._tk(out))

    def load(self, out, in_ap, q="sp"):
        self.dma(q, out.ap, in_ap, writes=self._tk(out))

    def store(self, out_ap, in_, q="pool", dram_tok=None):
        w = [dram_tok] if dram_tok is not None else []
        self.dma(q, out_ap, in_.ap, reads=self._tk(in_), writes=w)


(C_ID, C_MEAN, C_ONE, C_M128, C_BLK64, C_TRID_F, C_TRID_B, C_REMD_F, C_REMD_B, C_STR_F, C_STR_B,
 C_INC_F, C_INC_B, C_TRIC_F, C_TRIC_B, C_REMC_F, C_REMC_B, C_ROPE, C_CI16, C_CI64) = range(20)
NCONST = 20


def make_consts():
    c = np.zeros((128, NCONST, 128), np.float32)
    i = np.arange(128)
    P, Q = np.meshgrid(i, i, indexing="ij")
    c[:, C_ID] = (P == Q)
    c[:, C_MEAN] = 1.0 / D
    c[:, C_ONE] = 1.0
    c[:, C_M128] = 1.0 / 128
    c[:, C_BLK64] = (P // 64 == Q // 64) / 64.0
    s64 = (P // 64 == Q // 64)
    s16 = (P // 16 == Q // 16)
    c[:, C_TRID_F] = s64 & (P <= Q)
    c[:, C_TRID_B] = s64 & (P >= Q)
    c[:, C_REMD_F] = s64 & (P > Q)
    c[:, C_REMD_B] = s64 & (P < Q)
    c[:, C_STR_F] = s64 & (Q < P)
    c[:, C_STR_B] = s64 & (Q > P)
    c[:, C_INC_F] = s64 & (Q <= P)
    c[:, C_INC_B] = s64 & (Q >= P)
    c[:, C_TRIC_F] = s16 & (P <= Q)
    c[:, C_TRIC_B] = s16 & (P >= Q)
    c[:, C_REMC_F] = s16 & (P > Q)
    c[:, C_REMC_B] = s16 & (P < Q)
    R = np.zeros((128, 128), np.float32)
    for m in range(128):
        if m % 32 < 16:
            R[m, m + 16] = -1.0
        else:
            R[m, m - 16] = 1.0
    c[:, C_ROPE] = R.T
    c[:, C_CI16, 0:8] = (P[:, 0:8] // 16 == Q[:, 0:8])
    c[:, C_CI64, 0:2] = (P[:, 0:2] // 64 == Q[:, 0:2])
    return c


ARENA_BYTES = 91 * 1024


class Prog:
    def __init__(self, depth=DEPTH, mixers=("A", "B", "C"), dbg=None):
        self.depth = depth
        self.mixers = mixers
        self.dbg = dbg or {}
        self.k = KB()
        self.build()
        self.nc = self.k.finish()

    def declare_io(self):
        nc = self.k.nc
        L = self.depth

        def inp(name, shape, dt=F32):
            return nc.dram_tensor(name, list(shape), dt, kind="ExternalInput").ap()

        def outp(name, shape, dt=F32):
            return nc.dram_tensor(name, list(shape), dt, kind="ExternalOutput").ap()

        io = {}
        io["xT_p"] = inp("xT_p", [D, TOK])
        io["xT_s"] = inp("xT_s", [D, TOK])
        io["cond"] = inp("cond", [128, NCH, 2])
        io["w_mod"] = inp("w_mod", [L, D, 6 * D])
        io["b_mod"] = inp("b_mod", [L, 128, 48])
        io["w_in"] = inp("w_in", [L, D, D_IN])
        io["w_out_p"] = inp("w_out_p", [L, D, D])
        io["w_out_s"] = inp("w_out_s", [L, D, D])
        io["ln_g"] = inp("ln_g", [L, 128, 2, NCH])
        io["ln_b"] = inp("ln_b", [L, 128, 2, NCH])
        io["w_ffn_in"] = inp("w_ffn_in", [L, D, 2 * D_FF])
        io["w_ffn_out"] = inp("w_ffn_out", [L, D_FF, D])
        io["consts"] = inp("consts", [128, NCONST, 128])
        io["w_in_h"] = inp("w_in_h", [L, D, 964])
        io["rope_cs"] = inp("rope_cs", [2, 128, DEC_SEQ])
        io["diff_lambda"] = inp("diff_lambda", [L * 256])
        io["diff_norm"] = inp("diff_norm", [128, L])
        io["ctx_k"] = inp("ctx_k", [L, PAST, 128])
        io["ctx_v"] = inp("ctx_v", [L, PAST, 128])
        io["conv_hm"] = inp("conv_hm", [L, 4, 3 * 192])
        io["conv_h"] = inp("conv_h", [L, 3 * 192])
        io["w_ba_p"] = inp("w_ba_p", [L, D, 4, 4])
        io["dpar"] = inp("dpar", [L * 16])
        io["dpar_h"] = inp("dpar_h", [L * 4])
        io["dnorm"] = inp("dnorm", [L * 64])
        io["s0_d"] = inp("s0_d", [L, 2, 64, 64])
        io["new_sd"] = outp("new_sd", [4, L, 2, 4, 64, 64])
        io["hgrn_lb"] = inp("hgrn_lb", [2 * L * 256])
        io["hgrn_lb_h"] = inp("hgrn_lb_h", [2 * L * 64])
        io["hnorm"] = inp("hnorm", [L, 128])
        io["s0_h"] = inp("s0_h", [L, 2, 64, 64])
        io["new_sh"] = outp("new_sh", [4, L, 2, 4 * 64, 64])
        io["new_k"] = outp("new_k", [4, L, SEQ, 512])
        io["new_v"] = outp("new_v", [4, L, SEQ, 512])
        self.agx_in = [nc.dram_tensor("agx_in%d" % i, [512, TOK], BF16).ap() for i in range(2)]
        self.agx = [nc.dram_tensor("agx%d" % i, [4 * 512, TOK], BF16).ap() for i in range(2)]
        self.ago_in = [nc.dram_tensor("ago_in%d" % i, [128, DEC_SEQ], BF16).ap() for i in range(2)]
        self.ago = [nc.dram_tensor("ago%d" % i, [4 * 128, DEC_SEQ], BF16).ap() for i in range(2)]
        self.t_agx_in, self.t_agx, self.t_ago_in, self.t_ago = Tok(), Tok(), Tok(), Tok()
        io["yT_p"] = outp("yT_p", [D, TOK])
        io["yT_s"] = outp("yT_s", [D, TOK])
        for name, shape in self.dbg.items():
            io[name] = outp(name, shape)
        self.io = io

    def carve(self, off, shape, dt, ntok=1):
        n = 1
        for s in shape[1:]:
            n *= s
        nbytes = n * (2 if dt == BF16 else 4)
        assert off % 4 == 0 and off + nbytes <= ARENA_BYTES, (off, nbytes)
        ap = self.arena_t[:, off // 4:(off + nbytes + 3) // 4]
        if dt == BF16:
            ap = ap.bitcast(BF16)
        if len(shape) == 3:
            ap = ap.rearrange("p (a n) -> p a n", a=shape[1])
        elif len(shape) == 4:
            ap = ap.rearrange("p (a b n) -> p a b n", a=shape[1], b=shape[2])
        return Buf(ap, ntok), off + ((nbytes + 3) // 4) * 4

    def build(self):
        k = self.k
        self.declare_io()
        io = self.io
        L = self.depth
        self.xs = [k.sbuf("xs_p", [128, NCH, TOK], F32, ntok=2), k.sbuf("xs_s", [128, NCH, TOK], F32, ntok=2)]
        self.cst = k.sbuf("cst_sb", [128, NCONST, 128], F32)
        self.oT = k.sbuf("oT", [128, NCH, TOK], BF16, ntok=2)
        self.mod = k.sbuf("mod", [128, L, 48, 2], F32)
        self.lng = k.sbuf("lng", [128, L, 2, NCH], F32)
        self.lnb = k.sbuf("lnb", [128, L, 2, NCH], F32)
        self.ps = Ring([k.psum("ps%d" % i, [128, 512]) for i in range(4)])
        self.acc = [k.psum("acc%d" % i, [128, 512]) for i in range(4)]
        self.wst = k.ring("wst", [128, 2048], F32, 1)
        self.wbf = k.ring("wbf", [128, 2048], BF16, 2)
        self.tmp = k.ring("tmp", [128, 512], F32, 4)
        self.arena_t = self.k.es.enter_context(k.nc.sbuf_tensor("arena", [128, ARENA_BYTES // 4], F32))

        k.load(self.cst[:], io["consts"])
        for g, nm in enumerate(("xT_p", "xT_s")):
            for b in range(2):
                k.load(self.xs[g].at(b)[:, :, b * 512:(b + 1) * 512],
                       io[nm][:, b * 512:(b + 1) * 512].rearrange("(c p) n -> p c n", p=128))
        k.load(self.lng[:], io["ln_g"].rearrange("l p a c -> p l a c"))
        k.load(self.lnb[:], io["ln_b"].rearrange("l p a c -> p l a c"))
        self.preamble_mod()
        self.preamble_small()
        for l in range(L):
            self.layer(l)
        for g, nm in enumerate(("yT_p", "yT_s")):
            for b in range(2):
                k.store(io[nm][:, b * 512:(b + 1) * 512].rearrange("(c p) n -> p c n", p=128),
                        self.xs[g].at(b)[:, :, b * 512:(b + 1) * 512], q="sp")

    def C(self, i, rows=slice(None), cols=slice(None)):
        return self.cst[rows, i, cols]

    def preamble_mod(self):
        k, io = self.k, self.io
        L = self.depth
        k.barrier()
        off = 0
        wblk = []
        for i in range(2):
            b, off = self.carve(off, [128, NCH, 512], F32)
            wblk.append(b)
        cond, off = self.carve(off, [128, NCH, 2], F32)
        csil, off = self.carve(off, [128, NCH, 2], F32)
        bmod, off = self.carve(off, [128, L, 48], F32)
        k.load(cond[:], io["cond"])
        k.load(bmod[:], io["b_mod"].rearrange("l p m -> p l m"))
        k.act(csil[:], cond[:], AF.Silu)
        n = 0
        for l in range(L):
            for cb in range(12):
                w = wblk[n % 2]
                n += 1
                k.load(w[:], io["w_mod"][l, :, cb * 512:(cb + 1) * 512].rearrange("(c p) n -> p c n", p=128))
                ps = self.ps.next()
                for mi in range(4):
                    m = cb * 4 + mi
                    for kc in range(NCH):
                        k.mm(ps[:, 2 * mi:2 * mi + 2], w[:, kc, mi * 128:(mi + 1) * 128], csil[:, kc, :],
                             start=(kc == 0), stop=(kc == NCH - 1))
                k.tt("dve", self.mod[:, l, cb * 4:(cb + 1) * 4, :],
                     Opd(ps.t[:, 0:8].rearrange("p (m j) -> p m j", j=2), ps.toks),
                     Opd(bmod.t[:, l, cb * 4:(cb + 1) * 4].unsqueeze(2).broadcast_to([128, 4, 2]), bmod.toks),
                     ALU.add)
            for a in (8, 32):
                k.ts("dve", self.mod[:, l, a:a + 8, :], self.mod[:, l, a:a + 8, :], 1.0, ALU.add)
            for a in (16, 40):
                k.ts("dve", self.mod[:, l, a:a + 8, :], self.mod[:, l, a:a + 8, :], 1.0 / ALPHA, ALU.mult)
        k.barrier()

    def modv(self, l, which, g, kc):
        return self.mod[:, l, which * 8 + kc, g:g + 1]

    def load_w_bf16(self, dram_ap, rows, cols):
        st = self.wst.next()
        wb = self.wbf.next()
        k = self.k
        if len(dram_ap.shape) == 3:
            a, n = dram_ap.shape[1], dram_ap.shape[2]
            sv = Opd(st.t[:, 0:a * n].rearrange("p (a n) -> p a n", a=a), st.toks)
            wv = View(wb.t[:, 0:a * n].rearrange("p (a n) -> p a n", a=a), wb.toks)
            k.load(sv, dram_ap)
            k.copy("pool", wv[:], sv)
            return wv
        n = dram_ap.shape[1]
        r = dram_ap.shape[0]
        k.load(st[0:r, 0:n], dram_ap)
        k.copy("pool", wb[0:r, 0:n], st[0:r, 0:n])
        return View(wb.t, wb.toks)

    def layer(self, l):
        k = self.k
        self.mixer_phase(l)
        k.barrier()
        off = 0
        self.hT, off = self.carve(off, [128, NF, TOK], BF16, ntok=2)
        self.xm2, off = self.carve(off, [128, NCH, TOK], BF16, ntok=2)
        self.stat = Ring([self.carve(off + i * 2048, [128, 512], F32)[0] for i in range(6)])
        off += 6 * 2048
        self.usq = Ring([self.carve(off + i * 2048, [128, 512], F32)[0] for i in range(3)])
        off += 3 * 2048
        wst0 = self.wst.bufs[0]
        extra, off = self.carve(off, [128, 2048], F32)
        self.wst = Ring([wst0, extra])
        for g in range(2):
            if g == 1:
                self.load_oT_sample()
            self.dense_group(l, g)
        self.wst = Ring([wst0])
        k.barrier()

    def preamble_small(self):
        k, io = self.k, self.io
        L = self.depth
        self.lam = k.sbuf("lam", [128, L], F32)
        self.nlam = k.sbuf("nlam", [128, L], F32)
        self.gA = k.sbuf("gA", [128, L], F32)
        self.onesb = k.sbuf("onesb", [128, 128], BF16)
        k.memset("pool", self.onesb[:, :], 1.0)
        off = 0
        dlb, off = self.carve(off, [128, L, 4, 64], F32)
        pr, off = self.carve(off, [128, L, 2, 64], F32)
        sm, off = self.carve(off, [128, L, 2], F32)
        k.load(dlb[:], io["diff_lambda"].partition_broadcast(128).rearrange("p (l a d) -> p l a d", l=L, a=4))
        k.load(self.gA[:, :], io["diff_norm"])
        for j in range(2):
            k.tt("dve", pr[:, :, j, :], dlb[:, :, 2 * j, :], dlb[:, :, 2 * j + 1, :], ALU.mult)
        k.reduce(sm[:], pr[:])
        k.act(sm[:], sm[:], AF.Exp)
        k.tt("dve", self.lam[:, :], sm[:, :, 0], sm[:, :, 1], ALU.subtract)
        for l in range(L):
            lam_init = 0.8 - 0.6 * math.exp(-0.3 * l)
            k.ts("dve", self.lam[:, l:l + 1], self.lam[:, l:l + 1], lam_init, ALU.add)
            k.ts("dve", self.gA[:, l:l + 1], self.gA[:, l:l + 1], 1.0 - lam_init, ALU.mult)
        k.ts("dve", self.nlam[:, :], self.lam[:, :], -1.0, ALU.mult)
        k.barrier()

    def mixer_phase(self, l):
        k, io = self.k, self.io
        k.barrier()
        off = 0
        self.xmp, off = self.carve(off, [128, NCH, 4, 258], BF16, ntok=4)
        self.slab = []
        for i in range(2):
            b, off = self.carve(off, [128, NCH, 514], BF16)
            self.slab.append(b)
        self.moff = off
        k.memset("pool", self.xmp.all()[:, :, :, 0:1], 0.0)
        k.memset("pool", self.xmp.all()[:, :, :, 257:258], 0.0)
        for sq in range(4):
            for kc in range(NCH):
                k.act(self.xmp.at(sq)[:, kc, sq, 1:257], self.xs[0].at(sq // 2)[:, kc, sq * 256:(sq + 1) * 256],
                      AF.Identity, scale=self.modv(l, 1, 0, kc), bias=self.modv(l, 0, 0, kc))
        for b in range(2):
            for kc in range(NCH):
                k.ts("dve", self.slab[b][:, kc, 0:512], self.xs[1].at(b)[:, kc, b * 512:(b + 1) * 512],
                     self.modv(l, 1, 1, kc), ALU.mult, self.modv(l, 0, 1, kc), ALU.add)
            for hf in range(2):
                k.dma("sp", self.agx_in[hf][:, b * 512:(b + 1) * 512].rearrange("(c p) n -> p c n", p=128),
                      self.slab[b].t[:, 4 * hf:4 * hf + 4, 0:512], reads=self.slab[b].toks, writes=[self.t_agx_in])
        for hf in range(2):
            ain, aout = self.agx_in[hf], self.agx[hf]
            k.cc(lambda e, ain=ain, aout=aout: e.collective_compute("AllGather", ALU.bypass, replica_groups=GROUPS,
                                                                    ins=[ain], outs=[aout]),
                 reads=[self.t_agx_in], writes=[self.t_agx])
        if "A" in self.mixers:
            self.mixA_sample(l)
            k.barrier()
            self.mixA_prompt(l)
            k.barrier()
        else:
            self.zero_o(l, 0, 4, 0, 128)
        if "B" in self.mixers:
            self.mixB(l)
            k.barrier()
        else:
            self.zero_o(l, 4, 6, 128, 192)
        if "C" in self.mixers:
            self.mixC(l)
            k.barrier()
        else:
            self.zero_o(l, 6, 8, 192, 256)
        for hf in range(2):
            gin, gout = self.ago_in[hf], self.ago[hf]
            k.cc(lambda e, gin=gin, gout=gout: e.collective_compute("AllGather", ALU.bypass, replica_groups=GROUPS,
                                                                    ins=[gin], outs=[gout]),
                 reads=[self.t_ago_in], writes=[self.t_ago])

    def zero_o(self, l, c0, c1, r0, r1):
        k = self.k
        z = self.tmp.next()
        k.memset("pool", z[:, :], 0.0)
        zb = Opd(z.t[:, 0:256].bitcast(BF16), z.toks)
        for blk in range(8):
            k.dma("sp", self.ago_in[r0 // 128][r0 % 128:r0 % 128 + (r1 - r0), blk * 512:(blk + 1) * 512], zb.ap[0:r1 - r0, :],
                  reads=z.toks, writes=[self.t_ago_in])
        for b in range(2):
            k.memset("pool", self.oT.at(b)[:, c0:c1, b * 512:(b + 1) * 512], 0.0)

    def load_oT_sample(self):
        k = self.k
        ago = self.ago
        oT = self.oT

        def fn(e, sem, HF):
            core = e.partition_id()
            for c in range(8):
                r = c % 4
                with e.If(core == c):
                    e.dma_start(out=oT.t[:, HF::2, :], in_=ago[HF][:, r * TOK:(r + 1) * TOK].rearrange("(r p) n -> p r n", p=128)).then_inc(sem, 16)
        for HF in range(2):
            k.dma("pool", None, None, reads=[self.t_ago], writes=oT.toks, fn=(lambda e, sem, HF=HF: fn(e, sem, HF)), selfinc=True)

    def load_slab(self, buf, b, halo=False):
        k = self.k
        r, hf = b // 2, b % 2
        agx3 = [a.rearrange("(r f) n -> r f n", r=4) for a in self.agx]
        for fh in range(2):
            k.dma("sp", buf.t[:, 4 * fh:4 * fh + 4, 1:513], agx3[fh][r, :, hf * 512:(hf + 1) * 512].rearrange("(c p) n -> p c n", p=128),
                  reads=[self.t_agx], writes=buf.toks)
        if halo:
            for side, g0, col in ((0, b * 512 - 1, 0), (1, (b + 1) * 512, 513)):
                if g0 < 0 or g0 >= DEC_SEQ:
                    k.memset("pool", buf[:, :, col:col + 1], 0.0)
                else:
                    rr, cc = g0 // TOK, g0 % TOK
                    for fh in range(2):
                        o_ = buf.t[:, 4 * fh:4 * fh + 4, col:col + 1]
                        i_ = agx3[fh][rr, :, cc:cc + 1].rearrange("(c p) n -> p c n", p=128)
                        k.dma("sp", None, None, reads=[self.t_agx], writes=buf.toks,
                              fn=lambda e, o_=o_, i_=i_: e.dma_start(out=o_, in_=i_, allow_slow_non_contiguous=True))

    def wtile(self, dram_ap):
        return self.load_w_bf16(dram_ap.rearrange("(c p) n -> p c n", p=128), 128, dram_ap.shape[1])

    def attn_core(self, l, qT, kT, V, nkt, q0, nq, out_fn, Pt, o0, o1, rr):
        k = self.k
        om = [o0, o1]
        for m in range(2):
            rows = slice(64 * m, 64 * m + 64)
            psO, psR = self.acc[2 * m], self.acc[2 * m + 1]
            psS_next = None
            for kt in range(nkt):
                if kt == 0:
                    psS = self.ps.next()
                    k.mm(psS[:, 0:nq], kT[rows, 0:128], qT[rows, q0:q0 + nq])
                else:
                    psS = psS_next
                if kt + 1 < nkt:
                    psS_next = self.ps.next()
                    k.mm(psS_next[:, 0:nq], kT[rows, (kt + 1) * 128:(kt + 2) * 128], qT[rows, q0:q0 + nq])
                P = Pt.next()
                k.act(P[:, 0:nq], psS[:, 0:nq], AF.Exp, scale=0.125)
                k.mm(psO[:, 0:nq], V[:, kt, :], P[:, 0:nq], start=(kt == 0), stop=(kt == nkt - 1))
                k.mm(psR[:, 0:nq], self.onesb[:, :], P[:, 0:nq], start=(kt == 0), stop=(kt == nkt - 1))
            k.recip(rr[:, 0:nq], psR[:, 0:nq])
            k.tt("dve", om[m][:, 0:nq], psO[:, 0:nq], rr[:, 0:nq], ALU.mult)
        k.stt(o0[:, 0:nq], o1[:, 0:nq], self.nlam[:, l:l + 1], o0[:, 0:nq], ALU.mult, ALU.add)
        k.act(o1[:, 0:nq], o0[:, 0:nq], AF.Square)
        psN = self.ps.next()
        k.mm(psN[:, 0:nq], self.C(C_M128), o1[:, 0:nq])
        k.act(rr[:, 0:nq], psN[:, 0:nq], AF.Sqrt, bias=1e-6)
        k.recip(rr[:, 0:nq], rr[:, 0:nq])
        k.tt("dve", o0[:, 0:nq], o0[:, 0:nq], rr[:, 0:nq], ALU.mult)
        out_fn(o0)

    def mixA_sample(self, l):
        k, io = self.k, self.io
        off = self.moff
        qT, off = self.carve(off, [128, DEC_SEQ], BF16)
        kT, off = self.carve(off, [128, DEC_SEQ + PAST], BF16)
        V, off = self.carve(off, [128, 34, 128], BF16)
        Pt = Ring([self.carve(off + i * 1024, [128, 512], BF16)[0] for i in range(3)]); off += 3 * 1024
        xf = Ring([self.carve(off + i * 2048, [128, 512], F32)[0] for i in range(3)]); off += 3 * 2048
        o0, off = self.carve(off, [128, 512], F32)
        o1, off = self.carve(off, [128, 512], F32)
        rr, off = self.carve(off, [128, 512], F32)
        osb = Ring([self.carve(off + i * 1024, [128, 512], BF16)[0] for i in range(2)]); off += 2 * 1024
        cst = Ring([self.carve(off + i * 1024, [128, 2, 128], F32)[0] for i in range(2)]); off += 2 * 1024
        wqk = self.wtile(io["w_in_h"][l, :, 0:256])
        wv = self.wtile(io["w_in_h"][l, :, 256:384])
        for j in range(2):
            c = cst.next()
            k.load(c[:, 0, :], io["ctx_k"][l, j * 128:(j + 1) * 128, :])
            k.load(c[:, 1, :], io["ctx_v"][l, j * 128:(j + 1) * 128, :])
            ps = self.ps.next()
            k.mm(ps[:, 0:128], c[:, 0, :], self.C(C_ID))
            k.copy("act", kT[:, DEC_SEQ + j * 128:DEC_SEQ + (j + 1) * 128], ps[:, 0:128])
            k.copy("dve", V[:, 32 + j, :], c[:, 1, :])
        for b in range(8):
            sb = self.slab[b % 2]
            self.load_slab(sb, b)
            cos = self.tmp.next()
            sin = self.tmp.next()
            k.load(cos[:, :], io["rope_cs"][0, :, b * 512:(b + 1) * 512])
            k.load(sin[:, :], io["rope_cs"][1, :, b * 512:(b + 1) * 512])
            for which, dst in ((0, qT), (1, kT)):
                ps = self.ps.next()
                for kc in range(NCH):
                    k.mm(ps[:, :], wqk[:, kc, which * 128:(which + 1) * 128], sb[:, kc, 1:513], start=(kc == 0), stop=(kc == NCH - 1))
                x = xf.next()
                k.copy("act", x[:, :], ps[:, :])
                ps2 = self.ps.next()
                k.mm(ps2[:, :], self.C(C_ROPE), x[:, :])
                t1 = xf.next()
                k.tt("dve", t1[:, :], x[:, :], cos[:, :], ALU.mult)
                k.tt("dve", x[:, :], ps2[:, :], sin[:, :], ALU.mult)
                k.tt("dve", dst[:, b * 512:(b + 1) * 512], t1[:, :], x[:, :], ALU.add)
            for j in range(4):
                ps = self.ps.next()
                for kc in range(NCH):
                    k.mm(ps[:, 0:128], sb[:, kc, 1 + j * 128:1 + (j + 1) * 128], wv[:, kc, :], start=(kc == 0), stop=(kc == NCH - 1))
                k.copy("act", V[:, b * 4 + j, :], ps[:, 0:128])
        qv, kv, vv = View(qT.t, qT.toks), View(kT.t, kT.toks), View(V.t, V.toks)
        for b in range(8):
            def out_fn(o, b=b):
                ob = osb.next()
                k.ts("dve", ob[:, :], o[:, :], self.gA[:, l:l + 1], ALU.mult)
                k.dma("sp", self.ago_in[0][0:128, b * 512:(b + 1) * 512], ob.t[:, :], reads=ob.toks, writes=[self.t_ago_in])
            self.attn_core(l, qv, kv, vv, 34, b * 512, 512, out_fn, Pt, o0, o1, rr)

    def mixA_prompt(self, l):
        k, io = self.k, self.io
        off = self.moff
        Vp, off = self.carve(off, [128, 8, 512], BF16)
        qk = Ring([self.carve(off + i * 1024, [128, 2, 256], BF16)[0] for i in range(2)]); off += 2 * 1024
        Pt = Ring([self.carve(off + i * 1024, [128, 512], BF16)[0] for i in range(3)]); off += 3 * 1024
        o0, off = self.carve(off, [128, 512], F32)
        o1, off = self.carve(off, [128, 512], F32)
        rr, off = self.carve(off, [128, 512], F32)
        stg = Ring([self.carve(off + i * 1024, [128, 256], F32)[0] for i in range(3)]); off += 3 * 1024
        for cc in range(4):
            w = self.wtile(io["w_in"][l, :, O_AK + cc * 256:O_AK + (cc + 1) * 256])
            for t in range(8):
                sq, i = t // 2, t % 2
                ps = self.ps.next()
                for kc in range(NCH):
                    k.mm(ps[:, 0:256], self.xmp.at(sq)[:, kc, sq, 1 + i * 128:1 + (i + 1) * 128], w[:, kc, :],
                         start=(kc == 0), stop=(kc == NCH - 1))
                sg = stg.next()
                k.copy("act", sg[:, :], ps[:, 0:256])
                dst = io["new_k"] if cc < 2 else io["new_v"]
                c0 = (cc % 2) * 256
                k.dma("sp", dst[sq, l, i * 128:(i + 1) * 128, c0:c0 + 256], sg.t[:, :], reads=sg.toks)
                if cc >= 2:
                    k.copy("dve", Vp[:, t, c0:c0 + 256], sg[:, :])
        for h in range(4):
            w = self.wtile(io["w_in"][l, :, O_AQ + h * 128:O_AQ + (h + 1) * 128])
            w2 = self.wtile(io["w_in"][l, :, O_AK + h * 128:O_AK + (h + 1) * 128])
            for sq in range(4):
                qb = qk.next()
                for which, ww in ((0, w), (1, w2)):
                    ps = self.ps.next()
                    for kc in range(NCH):
                        k.mm(ps[:, 0:256], ww[:, kc, :], self.xmp.at(sq)[:, kc, sq, 1:257], start=(kc == 0), stop=(kc == NCH - 1))
                    k.copy("act", qb[:, which, :], ps[:, 0:256])
                qv = View(qb.t[:, 0, :], qb.toks)
                kv = View(qb.t[:, 1, :], qb.toks)
                vv = View(Vp.t[:, 2 * sq:2 * sq + 2, h * 128:(h + 1) * 128], Vp.toks)

                def out_fn(o, h=h, sq=sq):
                    k.ts("dve", self.oT.at(sq // 2)[:, h, sq * 256:(sq + 1) * 256], o[:, 0:256], self.gA[:, l:l + 1], ALU.mult)
                self.attn_core(l, qv, kv, vv, 2, 0, 256, out_fn, Pt, o0, o1, rr)

    def interleave(self, fixed, queue, nslots=2):
        fixed = list(fixed)
        queue = list(queue)
        slots = []
        while fixed or slots or queue:
            if not slots:
                while len(slots) < nslots and queue:
                    slots.append(queue.pop(0))
            for lst in (fixed, slots):
                for g in list(lst):
                    try:
                        next(g)
                    except StopIteration:
                        lst.remove(g)

    def lockstep(self, gens):
        gens = list(gens)
        while gens:
            for g in list(gens):
                try:
                    next(g)
                except StopIteration:
                    gens.remove(g)

    def src_prompt(self, sq):
        def f(tile, kc, shift=0):
            c0 = 1 + tile * 128 + shift
            return self.xmp.at(sq)[:, kc, sq, c0:c0 + 128]
        return f

    def src_prompt_fm(self, sq):
        return lambda kc: self.xmp.at(sq)[:, kc, sq, 1:257]

    def sample_src(self, order, halo):
        state = {"b": None, "buf": None}
        slab = self.slab_ring

        def f(tile, kc, shift=0):
            b = tile // 4
            if state["b"] != b:
                state["b"] = b
                state["buf"] = slab.next()
                self.load_slab(state["buf"], b, halo=halo)
            c0 = 1 + (tile % 4) * 128 + shift
            return state["buf"][:, kc, c0:c0 + 128]
        return f

    def mixB(self, l):
        k, io = self.k, self.io
        L = self.depth
        off = self.moff
        self.slab_ring = Ring(self.slab)
        dpp, off = self.carve(off, [128, 4, 2, 2], F32)
        dps, off = self.carve(off, [128, 2, 2], F32)
        dnr, off = self.carve(off, [128, 64], F32)
        k.load(dpp[:], io["dpar"][l * 16:(l + 1) * 16].partition_broadcast(128).rearrange("p (h d a) -> p h d a", h=4, d=2))
        k.load(dps[:], io["dpar_h"][l * 4:(l + 1) * 4].partition_broadcast(128).rearrange("p (d a) -> p d a", d=2))
        k.load(dnr[:, :], io["dnorm"][l * 64:(l + 1) * 64].partition_broadcast(128))
        k.act(dpp[:, :, :, 0:1], dpp[:, :, :, 0:1], AF.Exp)
        k.ts("dve", dpp[:, :, :, 0:1], dpp[:, :, :, 0:1], -1.0, ALU.mult)
        k.act(dps[:, :, 0:1], dps[:, :, 0:1], AF.Exp)
        k.ts("dve", dps[:, :, 0:1], dps[:, :, 0:1], -1.0, ALU.mult)
        cwb, off = self.carve(off, [128, 576], F32)
        wset = {}
        for nm in ("p", "s"):
            wc, off = self.carve(off, [128, 3, NCH, 192], BF16)
            wba, off = self.carve(off, [128, NCH, 4], BF16)
            wset[nm] = (wc, wba)

        def fold(wc, wba, qkv_srcs, ba_src, cw_src):
            k.load(cwb[:, :], cw_src.partition_broadcast(128))
            cw = cwb.t
            st = self.wst.next()
            sv = View(st.t[:, 0:NCH * 192].rearrange("p (c n) -> p c n", c=NCH), st.toks)
            for j, src in enumerate(qkv_srcs):
                n = src.shape[1]
                k.load(sv[:, :, j * (192 // len(qkv_srcs)):j * (192 // len(qkv_srcs)) + n], src.rearrange("(c p) n -> p c n", p=128))
            for tap in range(3):
                k.tt("pool", wc[:, tap, :, :], sv[:], Opd(cw[:, tap * 192:(tap + 1) * 192].unsqueeze(1).broadcast_to([128, NCH, 192]), cwb.toks),
                     ALU.mult)
            st2 = self.wst.next()
            sv2 = View(st2.t[:, 0:NCH * 4].rearrange("p (c n) -> p c n", c=NCH), st2.toks)
            k.load(sv2[:], ba_src)
            k.copy("pool", wba[:], sv2[:])

        fold(wset["s"][0], wset["s"][1], [io["w_in_h"][l, :, 384:576]],
             io["w_in_h"][l, :, 640:644].rearrange("(c p) n -> p c n", p=128), io["conv_h"][l])
        wcur = {"h": None}

        def getw(h):
            if wcur["h"] != h:
                wcur["h"] = h
                fold(wset["p"][0], wset["p"][1],
                     [io["w_in"][l, :, O_BQKV + j * 256 + h * 64:O_BQKV + j * 256 + (h + 1) * 64] for j in range(3)],
                     io["w_ba_p"][l, :, h, :].rearrange("(c p) n -> p c n", p=128), io["conv_hm"][l, h])
            return wset["p"]
        Oacc_s, off = self.carve(off, [128, 32, 64], F32, ntok=32)
        Oacc_p, off = self.carve(off, [128, 8, 256], F32, ntok=32)
        self.b_base = off
        R = {}
        for nm, w, n in (("qkv", 192, 2), ("et", 192, 1), ("sq", 128, 1), ("sm", 16, 4), ("kbe", 64, 2), ("vb", 64, 2),
                         ("kd", 64, 2), ("qd", 64, 2), ("gb", 64, 2), ("diag", 128, 1), ("dec", 128, 1),
                         ("dS", 128, 1), ("dI", 128, 1), ("Nm", 128, 2), ("QKm", 128, 2), ("NQT", 256, 2),
                         ("Xr", 128, 3), ("U", 192, 2), ("vnew", 64, 2), ("S", 64, 16)):
            R[nm] = Ring([self.carve(off + i * w * 4, [128, w], F32)[0] for i in range(n)])
            off += n * w * 4
        tb = self.tmp.bufs
        R["T3"] = Ring([tb[0], tb[1]])
        pp3 = Buf(tb[2].t, 1)
        R["PP"] = Ring([Buf(tb[2].t[:, 0:256], 1), Buf(tb[2].t[:, 256:512], 1), Buf(tb[3].t[:, 0:256], 1)])
        ID, ONE = self.C(C_ID), self.C(C_ONE)

        def job(d, src, ntiles, wc, wba, dpar, x0_fn, o_fn, fin_fn):
            order = list(range(ntiles)) if d == 0 else list(range(ntiles - 1, -1, -1))
            tri, rem, cstr, cinc = ((C_TRID_F, C_REMD_F, C_STR_F, C_INC_F) if d == 0 else
                                    (C_TRID_B, C_REMD_B, C_STR_B, C_INC_B))
            S = R["S"].next()
            x0_fn(S)
            for tile in order:
                ps1 = self.ps.next()
                n = 0
                for tap in range(3):
                    for kc in range(NCH):
                        k.mm(ps1[:, 0:192], src(tile, kc, tap - 1), wc[:, tap, kc, :], start=(n == 0), stop=(n == 23))
                        n += 1
                for kc in range(NCH):
                    k.mm(ps1[:, 192:196], src(tile, kc), wba[:, kc, :], start=(kc == 0), stop=(kc == NCH - 1))
                qkv, et, sq, sm = R["qkv"].next(), R["et"].next(), R["sq"].next(), R["sm"].next()
                k.act(et[:, :], ps1[:, 0:192], AF.Exp, scale=-1.0)
                k.ts("dve", et[:, :], et[:, :], 1.0, ALU.add)
                k.recip(et[:, :], et[:, :])
                k.tt("dve", qkv[:, :], ps1[:, 0:192], et[:, :], ALU.mult)
                k.tt("dve", sq[:, :], qkv[:, 0:128], qkv[:, 0:128], ALU.mult)
                k.reduce(sm[:, 0:2], Opd(sq.t[:, :].rearrange("p (a e) -> p a e", a=2), sq.toks))
                k.act(sm[:, 8:10], sm[:, 0:2], AF.Sqrt, bias=1e-6)
                k.recip(sm[:, 8:10], sm[:, 8:10])
                k.ts("dve", qkv[:, 0:64], qkv[:, 0:64], sm[:, 8:9], ALU.mult, 0.125, ALU.mult)
                k.ts("dve", qkv[:, 64:128], qkv[:, 64:128], sm[:, 9:10], ALU.mult)
                k.act(sm[:, 2:3], ps1[:, 192 + d:193 + d], AF.Exp, scale=-1.0)
                k.ts("dve", sm[:, 2:3], sm[:, 2:3], 1.0, ALU.add)
                k.recip(sm[:, 2:3], sm[:, 2:3])
                k.ts("dve", sm[:, 3:4], sm[:, 2:3], -1.0, ALU.mult)
                k.act(sm[:, 4:5], ps1[:, 194 + d:195 + d], AF.Exp, bias=dpar[:, d, 1:2])
                k.act(sm[:, 4:5], sm[:, 4:5], AF.Ln, bias=1.0)
                k.ts("dve", sm[:, 4:5], sm[:, 4:5], dpar[:, d, 0:1], ALU.mult)
                k.copy("dve", sm[:, 5:6], sm[:, 4:5])
                if BY & 1:
                    yield
                gb = R["gb"].next()
                k.copy("dve", gb[:, :], Opd(sm.t[:, 4:5].broadcast_to([128, 64]), sm.toks))
                psc = self.ps.next()
                k.mm(psc[:, 0:2], self.C(tri), sm[:, 4:6])
                k.mm(psc[:, 2:4], self.C(rem), sm[:, 4:6])
                k.mm(psc[0:64, 4:6], gb[:, :], self.C(C_CI64, cols=slice(0, 2)))
                k.copy("act", sm[:, 6:7], psc[:, 0:1])
                k.act(sm[:, 12:16], psc[:, 0:4], AF.Exp)
                k.act(sm[0:64, 10:12], psc[0:64, 4:6], AF.Exp)
                kbe, vb, kd, qd = R["kbe"].next(), R["vb"].next(), R["kd"].next(), R["qd"].next()
                k.ts("dve", kbe[:, :], qkv[:, 64:128], sm[:, 2:3], ALU.mult, sm[:, 12:13], ALU.mult)
                k.act(vb[:, :], qkv[:, 128:192], AF.Identity, scale=sm[:, 2:3])
                k.act(kd[:, :], qkv[:, 64:128], AF.Identity, scale=sm[:, 14:15])
                k.ts("dve", qd[:, :], qkv[:, 0:64], sm[:, 12:13], ALU.mult)
                pst = self.ps.next()
                k.mm(pst[0:64, 0:128], qkv[:, 64:128], ID)
                k.mm(pst[0:64, 128:256], qkv[:, 0:64], ID)
                k.mm(pst[0:64, 256:384], qd[:, :], ID)
                T3 = R["T3"].next()
                k.copy("act", T3[0:64, 0:384], pst[0:64, 0:384])
                knT, qnT, qdT = View(T3.t[0:64, 0:128], T3.toks), View(T3.t[0:64, 128:256], T3.toks), View(T3.t[0:64, 256:384], T3.toks)
                if BY & 2:
                    yield
                diag = R["diag"].next()
                k.act(diag[:, :], ID, AF.Identity, scale=sm[:, 6:7])
                psG = self.ps.next()
                k.mm(psG[:, 0:128], knT[:, :], knT[:, :])
                k.mm(psG[:, 128:256], qnT[:, :], knT[:, :])
                k.mm(psG[:, 256:384], ONE, diag[:, :])
                dec, dS, dI, Nm, QKm = (R[x].next() for x in ("dec", "dS", "dI", "Nm", "QKm"))
                k.ts("dve", dec[:, :], psG[:, 256:384], sm[:, 6:7], ALU.subtract, 0.0, ALU.max)
                k.act(dec[:, :], dec[:, :], AF.Exp, scale=-1.0)
                k.tt("pool", dS[:, :], dec[:, :], self.C(cstr), ALU.mult)
                k.tt("pool", dI[:, :], dec[:, :], self.C(cinc), ALU.mult)
                k.stt(Nm[:, :], dS[:, :], sm[:, 3:4], psG[:, 0:128], ALU.mult, ALU.mult)
                k.tt("dve", QKm[:, :], dI[:, :], psG[:, 128:256], ALU.mult)
                psT = self.ps.next()
                k.mm(psT[:, 0:128], Nm[:, :], ID)
                k.mm(psT[:, 128:256], QKm[:, :], ID)
                NQT = R["NQT"].next()
                k.copy("act", NQT[:, :], psT[:, 0:256])
                QKT = View(NQT.t[:, 128:256], NQT.toks)
                if BY & 4:
                    yield
                P, PT = View(Nm.t, Nm.toks), View(NQT.t[:, 0:128], NQT.toks)
                X = R["Xr"].next()
                k.tt("dve", X[:, :], PT[:, :], ID, ALU.add)
                for j in range(5):
                    psD = self.ps.next()
                    k.mm(psD[:, 0:128], PT[:, :], P[:, :])
                    if j < 4:
                        k.mm(psD[:, 128:256], P[:, :], PT[:, :])
                    PP = R["PP"].next()
                    k.copy("act", PP[:, 0:(256 if j < 4 else 128)], psD[:, 0:(256 if j < 4 else 128)])
                    P, PT = View(PP.t[:, 0:128], PP.toks), View(PP.t[:, 128:256], PP.toks)
                    psX = self.ps.next()
                    k.mm(psX[:, 0:128], ID, X[:, :], start=True, stop=False)
                    k.mm(psX[:, 0:128], P[:, :], X[:, :], start=False, stop=True)
                    X2 = R["Xr"].next()
                    k.copy("dve", X2[:, :], psX[:, 0:128])
                    X = X2
                    if BY & 8:
                        yield
                psU = self.ps.next()
                k.mm(psU[:, 0:64], X[:, :], vb[:, :])
                k.mm(psU[0:64, 64:192], kbe[:, :], X[:, :])
                U = R["U"].next()
                k.copy("act", U[:, 0:64], psU[:, 0:64])
                k.copy("act", U[0:64, 64:192], psU[0:64, 64:192])
                if BY & 16:
                    yield
                for ci in ((0, 1) if d == 0 else (1, 0)):
                    r = slice(64 * ci, 64 * ci + 64)
                    psa = self.ps.next()
                    psb = self.ps.next()
                    k.mm(psa[r, 0:64], U[0:64, 64 + 64 * ci:128 + 64 * ci], S[0:64, :])
                    k.mm(psa[r, 64:128], qdT[:, 64 * ci:64 * ci + 64], S[0:64, :])
                    vnew = R["vnew"].next()
                    k.tt("dve", vnew[r, :], U[r, 0:64], psa[r, 0:64], ALU.subtract)
                    k.mm(psb[r, 0:64], QKT[r, 64 * ci:64 * ci + 64], vnew[r, :])
                    k.mm(psb[0:64, 64:128], kd[r, :], vnew[r, :])
                    S2 = R["S"].next()
                    k.stt(S2[0:64, :], S[0:64, :], sm[0:64, 10 + ci:11 + ci], psb[0:64, 64:128], ALU.mult, ALU.add)
                    o_fn(tile, r, psa, psb)
                    S = S2
                    if BY & 32:
                        yield
                if not (BY & 32):
                    yield
            if BSTOP >= 4:
                fin_fn(S)

        done_s = set()

        def x0_sample(d):
            return lambda S: k.load(S[0:64, :], io["s0_d"][l, d])

        def o_sample(tile, r, psa, psb):
            dst = Oacc_s.at(tile)[r, tile, :]
            key = (tile, r.start)
            if key in done_s:
                k.tt("dve", dst, psa[r, 64:128], dst, ALU.add)
            else:
                done_s.add(key)
                k.copy("act", dst, psa[r, 64:128])
            k.tt("dve", dst, psb[r, 0:64], dst, ALU.add)

        sj = [job(d, self.sample_src(None, True), 32, wset["s"][0], wset["s"][1], dps, x0_sample(d), o_sample, lambda S: None)
              for d in range(2)]
        done_p = set()

        def mk_prompt(sq, d, h):
            def x0(S):
                k.memset("dve", S[0:64, :], 0.0)

            def o_fn(tile, r, psa, psb):
                t = sq * 2 + tile
                dst = Oacc_p.at(t * 4 + h)[r, t, h * 64:(h + 1) * 64]
                key = (t, h, r.start)
                if key in done_p:
                    k.tt("dve", dst, psa[r, 64:128], dst, ALU.add)
                else:
                    done_p.add(key)
                    k.copy("act", dst, psa[r, 64:128])
                k.tt("dve", dst, psb[r, 0:64], dst, ALU.add)

            def fin(S):
                k.dma("sp", io["new_sd"][sq, l, d, h], S.t[0:64, :], reads=S.toks)

            def gen():
                wc, wba = getw(h)
                yield from job(d, self.src_prompt(sq), 2, wc, wba, View(dpp.t[:, h], dpp.toks), x0, o_fn, fin)
            return gen()

        self.lockstep(sj)
        for h in range(4):
            for sq in range(4):
                self.lockstep([mk_prompt(sq, d, h) for d in range(2)])

        if BSTOP < 5:
            self.zero_o(l, 4, 6, 128, 192)
            return
        k.barrier()
        off = self.b_base
        P1 = {}
        for nm, w, n in (("sq", 64, 2), ("ss", 4, 2), ("e", 64, 2), ("o", 64, 2)):
            P1[nm] = Ring([self.carve(off + i * w * 4, [128, w], F32)[0] for i in range(n)])
            off += n * w * 4
        osb = Ring([self.carve(off + i * 1024, [128, 512], BF16)[0] for i in range(2)]); off += 2048

        def post(ov, gate_mm, prow):
            sq_, ss, e, o = (P1[x].next() for x in ("sq", "ss", "e", "o"))
            k.act(sq_[:, :], ov, AF.Square, accum=ss[:, 0:1])
            k.act(ss[:, 1:2], ss[:, 0:1], AF.Sqrt, bias=1e-6, scale=1.0 / 64)
            k.recip(ss[:, 1:2], ss[:, 1:2])
            psG = self.ps.next()
            gate_mm(psG)
            k.act(e[:, :], psG[:, 0:64], AF.Exp, scale=-1.0)
            k.ts("dve", e[:, :], e[:, :], 1.0, ALU.add)
            k.recip(e[:, :], e[:, :])
            k.tt("dve", e[:, :], e[:, :], psG[:, 0:64], ALU.mult)
            k.stt(o[:, :], ov, ss[:, 1:2], dnr[:, :], ALU.mult, ALU.mult)
            k.tt("dve", o[:, :], o[:, :], e[:, :], ALU.mult)
            psT = self.ps.next()
            k.mm(psT[prow, 0:128], o[:, :], ID)
            return psT

        for h in range(4):
            wg = self.wtile(io["w_in"][l, :, O_BG + h * 64:O_BG + (h + 1) * 64])
            prow = slice(64 * (h % 2), 64 * (h % 2) + 64)
            for t in range(8):
                sq, tile = t // 2, t % 2
                ov = Oacc_p.at(t * 4 + h)[:, t, h * 64:(h + 1) * 64]

                def gate_mm(ps, sq=sq, tile=tile, wg=wg):
                    sp = self.src_prompt(sq)
                    for kc in range(NCH):
                        k.mm(ps[:, 0:64], sp(tile, kc), wg[:, kc, 0:64], start=(kc == 0), stop=(kc == NCH - 1))
                psT = post(ov, gate_mm, prow)
                k.copy("act", self.oT.at(t // 4)[prow, 4 + h // 2, t * 128:(t + 1) * 128], psT[prow, 0:128])
        wg = self.wtile(io["w_in_h"][l, :, 576:640])
        ssrc = self.sample_src(None, False)
        for b in range(8):
            ob = osb.next()
            for j in range(4):
                tile = b * 4 + j
                ov = Oacc_s.at(tile)[:, tile, :]

                def gate_mm(ps, tile=tile):
                    for kc in range(NCH):
                        k.mm(ps[:, 0:64], ssrc(tile, kc), wg[:, kc, 0:64], start=(kc == 0), stop=(kc == NCH - 1))
                psT = post(ov, gate_mm, slice(0, 64))
                k.copy("act", ob[0:64, j * 128:(j + 1) * 128], psT[0:64, 0:128])
            k.dma("sp", self.ago_in[1][0:64, b * 512:(b + 1) * 512], ob.t[0:64, :], reads=ob.toks, writes=[self.t_ago_in])

    def mixC(self, l):
        k, io = self.k, self.io
        L = self.depth
        off = self.moff
        self.slab_ring = Ring(self.slab)
        lbs = {}
        scr = ARENA_BYTES - (2 * L * 256 * 4 + 2 * 256 * 4)
        for nm, Wd in (("hgrn_lb", 256), ("hgrn_lb_h", 64)):
            e, o2 = self.carve(scr, [128, 2, L, Wd], F32)
            tot, o2 = self.carve(o2, [128, 2, Wd], F32)
            lb, off = self.carve(off, [128, 2, Wd], F32)
            om, off = self.carve(off, [128, 2, Wd], F32)
            k.load(e[:], io[nm].partition_broadcast(128).rearrange("p (d l w) -> p d l w", d=2, l=L))
            k.act(e[:], e[:], AF.Exp)
            k.copy("dve", tot[:], e[:, :, 0, :])
            for j in range(1, L):
                k.tt("dve", tot[:], tot[:], e[:, :, j, :], ALU.add)
            k.recip(tot[:], tot[:])
            if l == 0:
                k.memset("dve", lb[:], 0.0)
            else:
                k.copy("dve", lb[:], e[:, :, 1, :])
                for j in range(2, l + 1):
                    k.tt("dve", lb[:], lb[:], e[:, :, j, :], ALU.add)
                k.tt("dve", lb[:], lb[:], tot[:], ALU.mult)
            k.ts("dve", om[:], lb[:], -1.0, ALU.mult, 1.0, ALU.add)
            lbs[nm] = (lb, om)
            k.barrier()
        hn = self.carve(off, [128, 1], F32)[0]; off += 64
        k.load(hn[:, :], io["hnorm"][l].rearrange("(p o) -> p o", o=1))
        def wS(c0, n):
            b, _ = self.carve(wS.off, [128, NCH, n], BF16)
            wS.off += NCH * n * 2
            st = self.wst.next()
            sv = Opd(st.t[:, 0:NCH * n].rearrange("p (c n) -> p c n", c=NCH), st.toks)
            k.load(sv, c0.rearrange("(c p) n -> p c n", p=128))
            k.copy("pool", b[:], sv)
            return b
        wS.off = off
        ws_q = wS(io["w_in_h"][l, :, 644:708], 64)
        ws_f = [wS(io["w_in_h"][l, :, 708 + 64 * d:772 + 64 * d], 64) for d in range(2)]
        ws_i = wS(io["w_in_h"][l, :, 836:900], 64)
        wpb = [self.carve(wS.off + i * 2048, [128, NCH, 128], BF16)[0] for i in range(4)]
        wS.off += 4 * 2048
        wp_cur = {"pr": None}

        def getw(pr):
            if wp_cur["pr"] != pr:
                wp_cur["pr"] = pr
                srcs = [io["w_in"][l, :, O_CQ + pr * 128:O_CQ + (pr + 1) * 128],
                        io["w_in"][l, :, O_CF + pr * 128:O_CF + (pr + 1) * 128],
                        io["w_in"][l, :, O_CF + 256 + pr * 128:O_CF + 256 + (pr + 1) * 128],
                        io["w_in"][l, :, O_CI + pr * 128:O_CI + (pr + 1) * 128]]
                for b, c0 in zip(wpb, srcs):
                    st = self.wst.next()
                    sv = Opd(st.t[:, 0:NCH * 128].rearrange("p (c n) -> p c n", c=NCH), st.toks)
                    k.load(sv, c0.rearrange("(c p) n -> p c n", p=128))
                    k.copy("pool", b[:], sv)
            return wpb[0], [wpb[1], wpb[2]], wpb[3]
        off = wS.off
        Oacc_s, off = self.carve(off, [128, 2048], F32, ntok=32)
        Oacc_p, off = self.carve(off, [128, 2, 1024], F32, ntok=16)
        self.c_base = off
        R = {}
        for nm, shp, n in (("E1", [128, 256], 2), ("qs", [128, 128], 2), ("f", [128, 128], 2), ("g", [128, 128], 2),
                           ("kk", [128, 128], 2), ("v", [128, 128], 2), ("qd", [128, 128], 2), ("kdi", [128, 128], 2),
                           ("kd", [128, 128], 2), ("qdT", [128, 128], 2), ("kdiT", [128, 128], 2), ("AT", [128, 128], 2),
                           ("gl", [128, 16], 2), ("X", [128, 64], 8), ("fs", [128, 64], 2)):
            sz = shp[1] * 4
            R[nm] = Ring([self.carve(off + i * sz, shp, F32)[0] for i in range(n)])
            off += n * sz
        self.c_off = off

        def job(d, W, src, ntiles, w_q, w_f, w_i, lb, om, lcol, x0_fn, o_fn, fin_fn):
            nh = W // 64
            order = list(range(ntiles)) if d == 0 else list(range(ntiles - 1, -1, -1))
            tri, rem = (C_TRIC_F, C_REMC_F) if d == 0 else (C_TRIC_B, C_REMC_B)
            X = R["X"].next()
            x0_fn(X)
            for tile in order:
                ps1 = self.ps.next()
                for kc in range(NCH):
                    k.mm(ps1[:, 0:W], src(tile, kc), w_q[:, kc, :], start=(kc == 0), stop=(kc == NCH - 1))
                for kc in range(NCH):
                    k.mm(ps1[:, W:2 * W], src(tile, kc), w_f[:, kc, :], start=(kc == 0), stop=(kc == NCH - 1))
                ps2 = self.ps.next()
                for kc in range(NCH):
                    k.mm(ps2[:, 0:W], src(tile, kc), w_i[:, kc, :], start=(kc == 0), stop=(kc == NCH - 1))
                E1, qs, f, g, kk, v = (R[n].next() for n in ("E1", "qs", "f", "g", "kk", "v"))
                k.act(E1[:, 0:2 * W], ps1[:, 0:2 * W], AF.Exp, scale=-1.0)
                k.ts("dve", E1[:, 0:2 * W], E1[:, 0:2 * W], 1.0, ALU.add)
                k.recip(E1[:, 0:2 * W], E1[:, 0:2 * W])
                k.tt("dve", qs[:, 0:W], ps1[:, 0:W], E1[:, 0:W], ALU.mult)
                k.tt("dve", f[:, 0:W], E1[:, W:2 * W], om[:, d, lcol:lcol + W], ALU.mult)
                k.tt("dve", f[:, 0:W], f[:, 0:W], lb[:, d, lcol:lcol + W], ALU.add)
                k.act(g[:, 0:W], f[:, 0:W], AF.Ln)
                k.ts("pool", kk[:, 0:W], f[:, 0:W], -1.0, ALU.mult, 1.0, ALU.add)
                k.copy("act", v[:, 0:W], ps2[:, 0:W])
                yield
                psb = self.ps.next()
                k.mm(psb[:, 0:W], self.C(tri), g[:, 0:W])
                k.mm(psb[:, W:2 * W], self.C(rem), g[:, 0:W])
                k.mm(psb[0:W, 2 * W:2 * W + 8], g[:, 0:W], self.C(C_CI16, cols=slice(0, 8)))
                qd, kdi, kd, gl = (R[n].next() for n in ("qd", "kdi", "kd", "gl"))
                Eb = R["E1"].next()
                k.act(Eb[:, 0:W], psb[:, 0:W], AF.Exp)
                k.tt("pool", qd[:, 0:W], qs[:, 0:W], Eb[:, 0:W], ALU.mult)
                k.act(Eb[:, W:2 * W], psb[:, 0:W], AF.Exp, scale=-1.0)
                k.tt("pool", kdi[:, 0:W], kk[:, 0:W], Eb[:, W:2 * W], ALU.mult)
                Ed = R["f"].next()
                k.act(Ed[:, 0:W], psb[:, W:2 * W], AF.Exp)
                k.tt("pool", kd[:, 0:W], kk[:, 0:W], Ed[:, 0:W], ALU.mult)
                glo = gl.t[0:W, 0:8] if d == 0 else gl.t[0:W, 7::-1]
                k.act(Opd(glo, gl.toks), psb[0:W, 2 * W:2 * W + 8], AF.Exp)
                k.copy("dve", gl[0:W, 8:16], gl[0:W, 0:8])
                k.memset("dve", gl[0:W, 8:9], 0.0)
                yield
                qdT, kdiT = R["qdT"].next(), R["kdiT"].next()
                pst = self.ps.next()
                k.mm(pst[0:W, 0:128], qd[:, 0:W], self.C(C_ID))
                k.mm(pst[0:W, 128:256], kdi[:, 0:W], self.C(C_ID))
                if 'c' not in CSUB:
                    if 'x' not in CSUB:
                        k.copy("act", qdT[0:W, :], pst[0:W, 0:128])
                    if 'y' not in CSUB:
                        k.copy("dve", kdiT[0:W, :], pst[0:W, 128:256])
                psKV = self.ps.next()
                ATs = []
                for h in range(nh):
                    r0 = 64 * h
                    if 'a' in CSUB:
                        continue
                    psA = self.ps.next()
                    k.mm(psA[:, 0:128], kdiT[r0:r0 + 64, :], qdT[r0:r0 + 64, :])
                    AT = R["AT"].next()
                    k.tt("dve", AT[:, :], psA[:, 0:128], self.C(tri), ALU.mult)
                    ATs.append(AT)
                    if 'b' in CSUB:
                        continue
                    vx = self.tmp.next()
                    k.tt("pool", Opd(vx.t[:, :].rearrange("p (c e) -> p c e", c=8), vx.toks),
                         Opd(v.t[:, r0:r0 + 64].unsqueeze(1).broadcast_to([128, 8, 64]), v.toks),
                         Opd(self.cst.t[:, C_CI16, 0:8].unsqueeze(2).broadcast_to([128, 8, 64]), self.cst.toks), ALU.mult)
                    k.mm(psKV[r0:r0 + 64, :], kd[:, r0:r0 + 64], vx[:, :])
                KVs, GLx, XS = self.tmp.next(), self.tmp.next(), self.tmp.next()
                kv3 = KVs.t[0:W, :].rearrange("p (e c) -> p e c", c=8)
                kvo = kv3 if d == 0 else kv3[:, :, ::-1]
                k.copy("act", Opd(kvo, KVs.toks), Opd(psKV.t[0:W, :].rearrange("p (c e) -> p e c", c=8), psKV.toks))
                k.stt(Opd(kv3[:, :, 0], KVs.toks), X[0:W, :], gl[0:W, 0:1], Opd(kv3[:, :, 0], KVs.toks), ALU.mult, ALU.add)
                k.copy("pool", Opd(GLx.t[0:W, :].rearrange("p (e c) -> p e c", c=8), GLx.toks),
                       Opd(gl.t[0:W, 8:16].unsqueeze(1).broadcast_to([W, 64, 8]), gl.toks))
                k.scan(XS[0:W, :], GLx[0:W, :], KVs[0:W, :], 0.0, ALU.mult, ALU.add)
                xs3 = XS.t[0:W, :].rearrange("p (e c) -> p e c", c=8)
                Xn = R["X"].next()
                k.copy("dve", Xn[0:W, :], Opd(xs3[:, :, 7], XS.toks))
                orow = o_fn(tile, None)
                psO = self.ps.next()
                for h in range(nh):
                    r0 = 64 * h
                    ro = orow + r0
                    k.mm(psO[ro:ro + 64, 0:128], v[:, r0:r0 + 64], ATs[h][:, :], start=True, stop=False)
                    for i in range(8):
                        c = i if d == 0 else 7 - i
                        xc = X[r0:r0 + 64, :] if i == 0 else Opd(xs3[r0:r0 + 64, :, i - 1], XS.toks)
                        k.mm(psO[ro:ro + 64, 16 * c:16 * c + 16], xc, qdT[r0:r0 + 64, 16 * c:16 * c + 16],
                             start=False, stop=(i == 7))
                o_fn(tile, psO)
                X = Xn
                yield
            if CSTOP >= 4:
                fin_fn(X)

        def x0_sample(d):
            def f(X):
                k.load(X[0:64, :], io["s0_h"][l, d])
            return f

        done_s = set()

        def o_sample(tile, psO):
            row = 0 if tile < 16 else 64
            if psO is None:
                return row
            col = (tile % 16) * 128
            dst = Oacc_s.at(tile)[row:row + 64, col:col + 128]
            if tile in done_s:
                k.tt("dve", dst, psO[row:row + 64, 0:128], dst, ALU.add)
            else:
                done_s.add(tile)
                k.copy("act", dst, psO[row:row + 64, 0:128])

        sj = [job(d, 64, self.sample_src(None, False), 32, ws_q, ws_f[d], ws_i, lbs["hgrn_lb_h"][0], lbs["hgrn_lb_h"][1], 0,
                  x0_sample(d), o_sample, lambda X: None) for d in range(2)]

        done_p = set()

        def mk_prompt(sq, d, pr):
            def x0(X):
                k.memset("dve", X[:, :], 0.0)

            def o_fn(tile, psO):
                if psO is None:
                    return 0
                key = (sq, pr, tile)
                c0 = sq * 256 + tile * 128
                dst = Oacc_p.at((sq * 2 + tile) * 2 + pr)[:, pr, c0:c0 + 128]
                if key in done_p:
                    k.tt("dve", dst, psO[:, 0:128], dst, ALU.add)
                else:
                    done_p.add(key)
                    k.copy("act", dst, psO[:, 0:128])

            def fin(X):
                fs = R["fs"].next()
                k.copy("act", fs[:, :], X[:, :])
                k.dma("sp", io["new_sh"][sq, l, d, pr * 128:(pr + 1) * 128, :], fs.t[:, :], reads=fs.toks)
            def gen():
                wq_, wf_, wi_ = getw(pr)
                yield from job(d, 128, self.src_prompt(sq), 2, wq_, wf_[d], wi_, lbs["hgrn_lb"][0], lbs["hgrn_lb"][1], pr * 128,
                               x0, o_fn, fin)
            return gen()

        self.lockstep(sj)
        for pr in range(2):
            for sq in range(4):
                self.lockstep([mk_prompt(sq, d, pr) for d in range(2)])

        if CSTOP < 5:
            self.zero_o(l, 6, 8, 192, 256)
            return
        k.barrier()
        off = self.c_base
        sqb = Ring([self.carve(off + i * 2048, [128, 512], F32)[0] for i in range(2)]); off += 4096
        rsb = Ring([self.carve(off + i * 2048, [128, 512], F32)[0] for i in range(2)]); off += 4096
        osb = Ring([self.carve(off + i * 1024, [128, 512], BF16)[0] for i in range(2)]); off += 2048

        def post(ov, rows, n, gate_mm, out_fn):
            sq_ = sqb.next()
            k.act(sq_[rows, 0:n], ov, AF.Square)
            psN = self.ps.next()
            k.mm(psN[rows, 0:n], self.C(C_BLK64, rows=rows, cols=rows), sq_[rows, 0:n])
            rs = rsb.next()
            k.act(rs[rows, 0:n], psN[rows, 0:n], AF.Sqrt, bias=1e-6)
            k.recip(rs[rows, 0:n], rs[rows, 0:n])
            k.tt("dve", rs[rows, 0:n], rs[rows, 0:n], ov, ALU.mult)
            psG = self.ps.next()
            gate_mm(psG)
            e = sqb.next()
            k.act(e[rows, 0:n], psG[rows, 0:n], AF.Exp, scale=-1.0)
            k.ts("dve", e[rows, 0:n], e[rows, 0:n], 1.0, ALU.add)
            k.recip(e[rows, 0:n], e[rows, 0:n])
            k.tt("dve", e[rows, 0:n], e[rows, 0:n], psG[rows, 0:n], ALU.mult)
            out_fn(rs, e)

        for pr in range(2):
            wg = self.wtile(io["w_in"][l, :, O_CG + pr * 128:O_CG + (pr + 1) * 128])
            for sq in range(4):
                toks = [Oacc_p.toks[(sq * 2 + t) * 2 + pr] for t in range(2)]
                ov = Opd(Oacc_p.t[:, pr, sq * 256:(sq + 1) * 256], toks)

                def gate_mm(ps, sq=sq, wg=wg):
                    fm = self.src_prompt_fm(sq)
                    for kc in range(NCH):
                        k.mm(ps[:, 0:256], wg[:, kc, :], fm(kc), start=(kc == 0), stop=(kc == NCH - 1))

                def out_fn(rs, e, sq=sq, pr=pr):
                    k.stt(self.oT.at(sq // 2)[:, 6 + pr, sq * 256:(sq + 1) * 256], rs[:, 0:256], hn[:, 0:1], e[:, 0:256],
                          ALU.mult, ALU.mult)
                post(ov, slice(0, 128), 256, gate_mm, out_fn)
        wg = self.wtile(io["w_in_h"][l, :, 900:964])
        for b in range(8):
            sb = self.slab_ring.next()
            self.load_slab(sb, b)
            rows = slice(0, 64) if b < 4 else slice(64, 128)
            c0 = (b % 4) * 512
            toks = [Oacc_s.toks[b * 4 + t] for t in range(4)]
            ov = Opd(Oacc_s.t[rows, c0:c0 + 512], toks)

            def gate_mm(ps, sb=sb, rows=rows, wg=wg):
                for kc in range(NCH):
                    k.mm(ps[rows, 0:512], wg[:, kc, 0:64], sb[:, kc, 1:513], start=(kc == 0), stop=(kc == NCH - 1))

            def out_fn(rs, e, b=b, rows=rows):
                ob = osb.next()
                k.stt(ob[rows, :], rs[rows, :], hn[rows, 0:1], e[rows, :], ALU.mult, ALU.mult)
                k.dma("sp", self.ago_in[1][64:128, b * 512:(b + 1) * 512], ob.t[rows, :], reads=ob.toks, writes=[self.t_ago_in])
            post(ov, rows, 512, gate_mm, out_fn)

    def layer_norm(self, l, which, g, b):
        k = self.k
        xs = self.xs[g].at(b)
        sl = slice(b * 512, (b + 1) * 512)
        ps_m = self.ps.next()
        ps_q = self.ps.next()
        for kc in range(NCH):
            k.mm(ps_m[:, :], self.C(C_MEAN), xs[:, kc, sl], start=(kc == 0), stop=(kc == NCH - 1))
        for kc in range(NCH):
            sq = self.usq.next()
            k.act(sq[:, :], xs[:, kc, sl], AF.Square)
            k.mm(ps_q[:, :], self.C(C_MEAN), sq[:, :], start=(kc == 0), stop=(kc == NCH - 1))
        mean = self.stat.next()
        rstd = self.stat.next()
        k.copy("act", mean[:, :], ps_m[:, :])
        k.tt("dve", rstd[:, :], mean[:, :], mean[:, :], ALU.mult)
        k.tt("dve", rstd[:, :], ps_q[:, :], rstd[:, :], ALU.subtract)
        k.act(rstd[:, :], rstd[:, :], AF.Sqrt, bias=LN_EPS / ALPHA ** 2)
        k.recip(rstd[:, :], rstd[:, :])
        for kc in range(NCH):
            t = self.tmp.next()
            k.tt("dve", t[:, :], xs[:, kc, sl], mean[:, :], ALU.subtract)
            k.tt("dve", t[:, :], t[:, :], rstd[:, :], ALU.mult)
            k.act(xs[:, kc, sl], t[:, :], AF.Identity, scale=self.lng[:, l, which, kc:kc + 1],
                  bias=self.lnb[:, l, which, kc:kc + 1])

    def dense_group(self, l, g):
        k, io = self.k, self.io
        xs = self.xs[g]
        wname = "w_out_p" if g == 0 else "w_out_s"
        for oc in range(NCH):
            w = self.load_w_bf16(io[wname][l, :, oc * 128:(oc + 1) * 128].rearrange("(c p) n -> p c n", p=128), 128, 128)
            for b in range(2):
                sl = slice(b * 512, (b + 1) * 512)
                ps = self.ps.next()
                for kc in range(NCH):
                    k.mm(ps[:, :], w[:, kc, :], self.oT.at(b)[:, kc, sl], start=(kc == 0), stop=(kc == NCH - 1))
                k.stt(xs.at(b)[:, oc, sl], ps[:, :], self.modv(l, 2, g, oc), xs.at(b)[:, oc, sl], ALU.mult, ALU.add)
        for b in range(2):
            self.layer_norm(l, 0, g, b)
        for b in range(2):
            sl = slice(b * 512, (b + 1) * 512)
            for kc in range(NCH):
                k.act(self.xm2.at(b)[:, kc, sl], xs.at(b)[:, kc, sl], AF.Identity,
                      scale=self.modv(l, 4, g, kc), bias=self.modv(l, 3, g, kc))
        for f in range(NF):
            st = self.wst.next()
            wb = self.wbf.next()
            sv = Opd(st.t[:, 0:2048].rearrange("p (c j n) -> p c j n", c=NCH, j=2), st.toks)
            wv = View(wb.t[:, 0:2048].rearrange("p (c j n) -> p c j n", c=NCH, j=2), wb.toks)
            for j in range(2):
                k.load(Opd(sv.ap[:, :, j, :], st.toks),
                       io["w_ffn_in"][l, :, j * D_FF + f * 128:j * D_FF + (f + 1) * 128].rearrange("(c p) n -> p c n", p=128))
            k.copy("pool", wv[:], sv)
            for b in range(2):
                sl = slice(b * 512, (b + 1) * 512)
                ps_g = self.ps.next()
                ps_u = self.ps.next()
                for kc in range(NCH):
                    k.mm(ps_g[:, :], wv[:, kc, 0, :], self.xm2.at(b)[:, kc, sl], start=(kc == 0), stop=(kc == NCH - 1))
                for kc in range(NCH):
                    k.mm(ps_u[:, :], wv[:, kc, 1, :], self.xm2.at(b)[:, kc, sl], start=(kc == 0), stop=(kc == NCH - 1))
                t = self.tmp.next()
                k.act(t[:, :], ps_g[:, :], AF.Silu)
                k.tt("dve", self.hT.at(b)[:, f, sl], t[:, :], ps_u[:, :], ALU.mult)
        for oc in range(NCH):
            wvs = []
            for hf in range(2):
                st = self.wst.next()
                wb = self.wbf.next()
                sv = Opd(st.t[:, 0:11 * 128].rearrange("p (f n) -> p f n", f=11), st.toks)
                wv = View(wb.t[:, 0:11 * 128].rearrange("p (f n) -> p f n", f=11), wb.toks)
                k.load(sv, io["w_ffn_out"][l, hf * 1408:(hf + 1) * 1408, oc * 128:(oc + 1) * 128]
                       .rearrange("(f p) n -> p f n", p=128))
                k.copy("pool", wv[:], sv)
                wvs.append(wv)
            for b in range(2):
                sl = slice(b * 512, (b + 1) * 512)
                ps = self.ps.next()
                for f in range(NF):
                    k.mm(ps[:, :], wvs[f // 11][:, f % 11, :], self.hT.at(b)[:, f, sl], start=(f == 0), stop=(f == NF - 1))
                k.stt(xs.at(b)[:, oc, sl], ps[:, :], self.modv(l, 5, g, oc), xs.at(b)[:, oc, sl], ALU.mult, ALU.add)
        for b in range(2):
            self.layer_norm(l, 1, g, b)


def head_cols(h):
    r = lambda o, n: list(range(o, o + n))
    cols = []
    cols += r(O_AQ + h * 128, 128) + r(O_AK + h * 128, 128) + r(O_AV + h * 128, 128)
    cols += r(O_BQKV + h * 64, 64) + r(O_BQKV + 256 + h * 64, 64) + r(O_BQKV + 512 + h * 64, 64)
    cols += r(O_BG + h * 64, 64)
    cols += [O_BBETA + h, O_BBETA + 4 + h, O_BA + h, O_BA + 4 + h]
    cols += r(O_CQ + h * 64, 64) + r(O_CF + h * 64, 64) + r(O_CF + 256 + h * 64, 64)
    cols += r(O_CI + h * 64, 64) + r(O_CG + h * 64, 64)
    return cols


def w_out_perm():
    rows = []
    for r in range(4):
        rows += list(range(r * 128, (r + 1) * 128))
        rows += list(range(512 + r * 64, 512 + (r + 1) * 64))
        rows += list(range(768 + r * 64, 768 + (r + 1) * 64))
    return rows


def rope_tables():
    t = np.arange(DEC_SEQ)
    inv = (np.float32(10000.0) ** (-np.arange(16, dtype=np.float32) / np.float32(16))).astype(np.float32)
    out = np.zeros((2, 128, DEC_SEQ), np.float32)
    for p in range(128):
        d = p % 64
        pos = (t // 64) if d < 32 else (t % 64)
        ang = pos.astype(np.float32) * inv[d % 16]
        out[0, p] = np.cos(ang)
        out[1, p] = np.sin(ang)
    return out


def prep_inputs(inp, depth=DEPTH):
    f = lambda a: np.ascontiguousarray(np.asarray(a, dtype=np.float32))
    L = depth
    consts = make_consts()
    shared = {
        "w_mod": f(inp["w_mod"][:L]),
        "b_mod": f(np.asarray(inp["b_mod"])[:L].reshape(L, 48, 128).transpose(0, 2, 1)),
        "w_in": f(inp["w_in"][:L]),
        "w_out_p": f(inp["w_out"][:L]),
        "w_out_s": f(np.asarray(inp["w_out"])[:L][:, w_out_perm(), :]),
        "ln_g": f(np.asarray(inp["ln_g"])[:L].reshape(L, 2, NCH, 128).transpose(0, 3, 1, 2)),
        "ln_b": f(np.asarray(inp["ln_b"])[:L].reshape(L, 2, NCH, 128).transpose(0, 3, 1, 2)),
        "w_ffn_in": f(inp["w_ffn_in"][:L]),
        "w_ffn_out": f(inp["w_ffn_out"][:L]),
        "consts": consts,
        "rope_cs": rope_tables(),
        "diff_lambda": f(np.asarray(inp["diff_lambda"])[:L].reshape(-1)),
        "diff_norm": f(np.asarray(inp["diff_norm"])[:L].T),
        "hgrn_lb": f(np.asarray(inp["hgrn_lb"])[:, :L].reshape(-1)),
        "conv_hm": f(np.asarray(inp["conv_w"])[:L].reshape(L, 3, 3, 4, 64).transpose(0, 3, 1, 2, 4).reshape(L, 4, 576)),
        "w_ba_p": f(np.stack([np.asarray(inp["w_in"])[:L][:, :, [O_BBETA + h, O_BBETA + 4 + h, O_BA + h, O_BA + 4 + h]]
                              for h in range(4)], 2)),
        "dpar": f(np.stack([np.asarray(inp["delta_a_log"])[:L], np.asarray(inp["delta_dt_bias"])[:L]], -1)
                  .transpose(0, 2, 1, 3).reshape(-1)),
        "dnorm": f(np.asarray(inp["delta_norm"])[:L].reshape(-1)),
        "hnorm": f(np.tile(np.asarray(inp["hgrn_norm"])[:L], (1, 2))),
    }
    xp = np.asarray(inp["x_prompt"], np.float32)
    xsm = np.asarray(inp["x_sample"], np.float32)
    maps = []
    for c in range(8):
        s, r = c // 4, c % 4
        m = dict(shared)
        m["xT_p"] = f(xp[4 * c:4 * c + 4].reshape(TOK, D).T)
        m["xT_s"] = f(xsm[s, r * TOK:(r + 1) * TOK].T)
        cond = np.stack([np.asarray(inp["c_ctx"], np.float32), np.asarray(inp["c"], np.float32)[s]], -1)
        m["cond"] = f(cond.reshape(NCH, 128, 2).transpose(1, 0, 2))
        m["w_in_h"] = f(np.asarray(inp["w_in"])[:L][:, :, head_cols(r)])
        m["ctx_k"] = f(np.asarray(inp["cache_attn_k"])[s, :L, :, r, :])
        m["ctx_v"] = f(np.asarray(inp["cache_attn_v"])[s, :L, :, r, :])
        m["hgrn_lb_h"] = f(np.asarray(inp["hgrn_lb"])[:, :L, r * 64:(r + 1) * 64].reshape(-1))
        m["s0_h"] = f(np.asarray(inp["state_hgrn"])[s, :L, :, r])
        m["s0_d"] = f(np.asarray(inp["state_delta"])[s, :L, :, r])
        m["conv_h"] = f(np.asarray(inp["conv_w"])[:L].reshape(L, 3, 3, 4, 64)[:, :, :, r, :].reshape(L, 576))
        m["dpar_h"] = f(np.stack([np.asarray(inp["delta_a_log"])[:L, :, r], np.asarray(inp["delta_dt_bias"])[:L, :, r]], -1).reshape(-1))
        maps.append(m)
    return maps


_PROG = {}


def get_prog(depth=DEPTH, mixers=("A", "B", "C"), dbg=None):
    key = (depth, tuple(mixers), tuple(sorted((dbg or {}).items())))
    if key not in _PROG:
        _PROG[key] = Prog(depth, mixers, dbg)
    return _PROG[key]


def run(inp, depth=DEPTH, mixers=("A", "B", "C"), dbg=None, trace=False):
    prog = get_prog(depth, mixers, dbg)
    maps = prep_inputs(inp, depth)
    res = run_bass_kernel_spmd(prog.nc, maps, core_ids=list(range(8)), trace=trace)
    return res


def assemble(res, depth=DEPTH):
    R = res.results
    y_p = np.concatenate([R[c]["yT_p"].T.reshape(4, SEQ, D) for c in range(8)], 0)
    y_s = np.stack([np.concatenate([R[s * 4 + r]["yT_s"].T for r in range(4)], 0) for s in range(2)], 0)
    L = depth
    nk = np.concatenate([R[c]["new_k"] for c in range(8)], 0).reshape(32, L, SEQ, 4, 128)
    nv = np.concatenate([R[c]["new_v"] for c in range(8)], 0).reshape(32, L, SEQ, 4, 128)
    nsh = np.concatenate([R[c]["new_sh"] for c in range(8)], 0).reshape(32, L, 2, 4, 64, 64)
    nsd = np.concatenate([R[c]["new_sd"] for c in range(8)], 0).reshape(32, L, 2, 4, 64, 64)
    return (y_p.astype(np.float32), y_s.astype(np.float32), nk.astype(np.float32), nv.astype(np.float32),
            nsd.astype(np.float32), nsh.astype(np.float32))


def kernel(**inputs):
    res = run(inputs)
    return assemble(res)
```

```python
import math
import os
from contextlib import ExitStack
CSTOP = int(os.environ.get('CSTOP', '99'))
CSUB = os.environ.get('CSUB', '')
BSTOP = int(os.environ.get('BSTOP', '99'))
BY = int(os.environ.get('BY', '63'))

import numpy as np
import concourse.bass as bass
import concourse.mybir as mybir
from concourse.bass_utils import run_bass_kernel_spmd

F32 = mybir.dt.float32
BF16 = mybir.dt.bfloat16
ALU = mybir.AluOpType
AF = mybir.ActivationFunctionType
AX = mybir.AxisListType

D = 1024
NCH = 8
DEPTH = 4
SEQ = 256
DEC_SEQ = 4096
PAST = 256
D_FF = 2816
NF = 22
D_IN = 3856
ALPHA = (2 * DEPTH) ** 0.25
LN_EPS = 1e-5
TOK = 1024
GROUPS = [[0, 1, 2, 3], [4, 5, 6, 7]]

O_AQ, O_AK, O_AV = 0, 512, 1024
O_BQKV, O_BG, O_BBETA, O_BA = 1536, 2304, 2560, 2568
O_CQ, O_CF, O_CI, O_CG = 2576, 2832, 3344, 3600


class Tok:
    __slots__ = ("w", "rs", "excl")

    def __init__(self):
        self.w = None
        self.rs = {}
        self.excl = False


class Opd:
    __slots__ = ("ap", "toks")

    def __init__(self, ap, toks):
        self.ap = ap
        self.toks = toks


class View:
    __slots__ = ("t", "toks")

    def __init__(self, t, toks):
        self.t = t
        self.toks = toks

    def __getitem__(self, idx):
        return Opd(self.t[idx], self.toks)


class Buf:
    def __init__(self, t, ntok=1):
        self.t = t
        self.toks = [Tok() for _ in range(ntok)]

    def at(self, *keys):
        return View(self.t, [self.toks[k] for k in keys])

    def all(self):
        return View(self.t, self.toks)

    def __getitem__(self, idx):
        return Opd(self.t[idx], self.toks)


class Ring:
    def __init__(self, bufs):
        self.bufs = bufs
        self.i = 0

    def next(self):
        b = self.bufs[self.i % len(self.bufs)]
        self.i += 1
        return b


class KB:
    ENGS = ("pe", "dve", "act", "pool", "sp")

    def __init__(self):
        self.nc = bass.Bass("TRN2", target_bir_lowering=False)
        self.es = ExitStack()
        self.streams = {e: [] for e in self.ENGS}
        self.count = {e: 0 for e in self.ENGS}
        self.seen = {e: {} for e in self.ENGS}
        self.latest = {}
        self.sems = {}
        for e in self.ENGS:
            self.sems[e] = self.es.enter_context(self.nc.semaphore("c_" + e))
        self.ndsem = {"sp": 24, "pool": 12, "act": 8}
        self.dsem_i = {q: 0 for q in self.ndsem}
        self.dsem_v = {}
        for q, n in self.ndsem.items():
            for j in range(n):
                k = "d_%s_%d" % (q, j)
                self.sems[k] = self.es.enter_context(self.nc.semaphore(k))
                self.dsem_v[k] = 0
        self.nalloc = 0
        self.pending_barrier = {e: None for e in self.ENGS}

    def sbuf(self, name, shape, dt, ntok=1):
        t = self.es.enter_context(self.nc.sbuf_tensor(name, list(shape), dt))
        return Buf(t, ntok)

    def psum(self, name, shape, dt=F32):
        t = self.es.enter_context(self.nc.psum_tensor(name, list(shape), dt))
        b = Buf(t, 1)
        b.toks[0].excl = True
        return b

    def ring(self, name, shape, dt, n):
        return Ring([self.sbuf("%s%d" % (name, i), shape, dt) for i in range(n)])

    def dram(self, name, shape, dt, kind="Internal"):
        return self.nc.dram_tensor(name, list(shape), dt, kind=kind)

    def _deps(self, eng, reads, writes):
        need = {}

        def add(ref):
            if ref is None:
                return
            k, v = ref
            if need.get(k, 0) < v:
                need[k] = v

        for t in reads:
            add(t.w)
            if t.excl:
                for k2, v in t.rs.items():
                    if k2 != eng:
                        add((k2, v))
        for t in writes:
            add(t.w)
            for k, v in t.rs.items():
                add((k, v))
        pb = self.pending_barrier[eng]
        if pb is not None:
            for k, v in pb.items():
                add((k, v))
            self.pending_barrier[eng] = None
        if eng == "pe":
            need.pop("pe", None)
        seen = self.seen[eng]
        waits = []
        for k, v in need.items():
            if seen.get(k, 0) < v:
                seen[k] = v
                waits.append((k, v))
        return waits

    def op(self, eng, fn, reads=(), writes=()):
        waits = self._deps(eng, reads, writes)
        self.count[eng] += 1
        idx = self.count[eng]
        ref = (eng, idx)
        for t in reads:
            t.rs[eng] = idx
        for t in writes:
            t.w = ref
            t.rs = {}
        self.latest[eng] = idx
        self.streams[eng].append((waits, fn, (eng, 1)))

    def dma(self, q, out_ap, in_ap, reads=(), writes=(), fn=None, selfinc=False):
        waits = self._deps(q, reads, writes)
        j = self.dsem_i[q] % self.ndsem[q]
        self.dsem_i[q] += 1
        k = "d_%s_%d" % (q, j)
        prev = self.dsem_v[k]
        if prev > 0 and self.seen[q].get(k, 0) < prev:
            self.seen[q][k] = prev
            waits.append((k, prev))
        self.dsem_v[k] = prev + 16
        ref = (k, prev + 16)
        for t in reads:
            t.rs[k] = prev + 16
        for t in writes:
            t.w = ref
            t.rs = {}
        self.latest[k] = prev + 16
        if fn is None:
            fn = lambda e, o=out_ap, i=in_ap: e.dma_start(out=o, in_=i)
        self.streams[q].append((waits, fn, ("SELF", k) if selfinc else (k, 16)))

    def cc(self, fn, reads=(), writes=()):
        waits = self._deps("pool", reads, writes)
        if "cc" not in self.sems:
            self.sems["cc"] = self.es.enter_context(self.nc.semaphore("cc"))
            self.ccv = 0
        prev = self.ccv
        if prev > 0 and self.seen["pool"].get("cc", 0) < prev:
            self.seen["pool"]["cc"] = prev
            waits.append(("cc", prev))
        self.ccv = prev + 1
        ref = ("cc", prev + 1)
        for t in reads:
            t.rs["cc"] = prev + 1
        for t in writes:
            t.w = ref
            t.rs = {}
        self.latest["cc"] = prev + 1
        self.streams["pool"].append((waits, fn, ("cc", 1)))

    def barrier(self):
        snap = dict(self.latest)
        for e in self.ENGS:
            pb = self.pending_barrier[e]
            if pb is None:
                self.pending_barrier[e] = dict(snap)
            else:
                for k, v in snap.items():
                    if pb.get(k, 0) < v:
                        pb[k] = v

    def raw(self, eng, fn):
        self.streams[eng].append(([], fn, None))

    def finish(self):
        nc = self.nc
        final = dict(self.latest)
        engmap = {"pe": "tensor", "dve": "vector", "act": "scalar", "pool": "gpsimd", "sp": "sync"}
        with nc.Block() as block:
            for e in self.ENGS:
                stream = self.streams[e]

                def body(h, e=e, stream=stream):
                    sems = self.sems
                    for waits, fn, inc in stream:
                        for k, v in waits:
                            h.wait_ge(sems[k], v)
                        if inc is not None and inc[0] == "SELF":
                            fn(h, sems[inc[1]])
                            continue
                        ins = fn(h)
                        if inc is not None:
                            ins.then_inc(sems[inc[0]], inc[1])
                    for k, v in final.items():
                        if k == e:
                            continue
                        h.wait_ge(sems[k], v)

                getattr(block, engmap[e])(body)
        self.es.close()
        return nc

    @staticmethod
    def _tk(*ops):
        out = []
        for o in ops:
            if isinstance(o, Opd):
                out.extend(o.toks)
        return out

    @staticmethod
    def _ap(o):
        return o.ap if isinstance(o, Opd) else o

    def mm(self, out, lhsT, rhs, start=True, stop=True):
        o, l, r = out.ap, lhsT.ap, rhs.ap
        self.op("pe", lambda e: e.matmul(o, l, r, start=start, stop=stop),
                reads=self._tk(lhsT, rhs), writes=self._tk(out))

    def act(self, out, in_, func, bias=0.0, scale=1.0, accum=None, eng="act"):
        o, i, b, s = out.ap, in_.ap, self._ap(bias), self._ap(scale)
        kw = {}
        if accum is not None:
            kw["accum_out"] = accum.ap
        self.op("act", lambda e: e.activation(o, i, func, bias=b, scale=s, **kw),
                reads=self._tk(in_, bias, scale), writes=self._tk(out, accum))

    def tt(self, eng, out, in0, in1, op):
        o, a, b = out.ap, in0.ap, in1.ap
        self.op(eng, lambda e: e.tensor_tensor(o, a, b, op), reads=self._tk(in0, in1), writes=self._tk(out))

    def ts(self, eng, out, in0, s1, op0, s2=None, op1=None, accum=None):
        o, a, x1, x2 = out.ap, in0.ap, self._ap(s1), self._ap(s2)
        kw = {}
        if accum is not None:
            kw["accum_out"] = accum.ap
        if op1 is None:
            fn = lambda e: e.tensor_scalar(o, a, x1, None, op0, **kw)
        else:
            fn = lambda e: e.tensor_scalar(o, a, x1, x2, op0, op1, **kw)
        self.op(eng, fn, reads=self._tk(in0, s1, s2), writes=self._tk(out, accum))

    def stt(self, out, in0, scalar, in1, op0, op1, eng="dve"):
        o, a, s, b = out.ap, in0.ap, self._ap(scalar), in1.ap
        self.op(eng, lambda e: e.scalar_tensor_tensor(o, a, s, b, op0, op1),
                reads=self._tk(in0, scalar, in1), writes=self._tk(out))

    def copy(self, eng, out, in_):
        o, i = out.ap, in_.ap
        if eng == "act":
            self.op("act", lambda e: e.copy(o, i), reads=self._tk(in_), writes=self._tk(out))
        else:
            self.op(eng, lambda e: e.tensor_copy(o, i), reads=self._tk(in_), writes=self._tk(out))

    def memset(self, eng, out, val):
        o = out.ap
        self.op(eng, lambda e: e.memset(o, val), writes=self._tk(out))

    def recip(self, out, in_):
        o, i = out.ap, in_.ap
        self.op("dve", lambda e: e.reciprocal(o, i), reads=self._tk(in_), writes=self._tk(out))

    def reduce(self, out, in_, op=ALU.add, axis=AX.X):
        o, i = out.ap, in_.ap
        self.op("dve", lambda e: e.tensor_reduce(o, i, axis, op), reads=self._tk(in_), writes=self._tk(out))

    def scan(self, out, d0, d1, initial, op0, op1):
        o, a, b, ini = out.ap, d0.ap, d1.ap, self._ap(initial)
        self.op("dve", lambda e: e.tensor_tensor_scan(o, a, b, ini, op0, op1),
                reads=self._tk(d0, d1, initial), writes=self._tk(out))

    def load(self, out, in_ap, q="sp"):
        self.dma(q, out.ap, in_ap, writes=self._tk(out))

    def store(self, out_ap, in_, q="pool", dram_tok=None):
        w = [dram_tok] if dram_tok is not None else []
        self.dma(q, out_ap, in_.ap, reads=self._tk(in_), writes=w)


(C_ID, C_MEAN, C_ONE, C_M128, C_BLK64, C_TRID_F, C_TRID_B, C_REMD_F, C_REMD_B, C_STR_F, C_STR_B,
 C_INC_F, C_INC_B, C_TRIC_F, C_TRIC_B, C_REMC_F, C_REMC_B, C_ROPE, C_CI16, C_CI64) = range(20)
NCONST = 20


def make_consts():
    c = np.zeros((128, NCONST, 128), np.float32)
    i = np.arange(128)
    P, Q = np.meshgrid(i, i, indexing="ij")
    c[:, C_ID] = (P == Q)
    c[:, C_MEAN] = 1.0 / D
    c[:, C_ONE] = 1.0
    c[:, C_M128] = 1.0 / 128
    c[:, C_BLK64] = (P // 64 == Q // 64) / 64.0
    s64 = (P // 64 == Q // 64)
    s16 = (P // 16 == Q // 16)
    c[:, C_TRID_F] = s64 & (P <= Q)
    c[:, C_TRID_B] = s64 & (P >= Q)
    c[:, C_REMD_F] = s64 & (P > Q)
    c[:, C_REMD_B] = s64 & (P < Q)
    c[:, C_STR_F] = s64 & (Q < P)
    c[:, C_STR_B] = s64 & (Q > P)
    c[:, C_INC_F] = s64 & (Q <= P)
    c[:, C_INC_B] = s64 & (Q >= P)
    c[:, C_TRIC_F] = s16 & (P <= Q)
    c[:, C_TRIC_B] = s16 & (P >= Q)
    c[:, C_REMC_F] = s16 & (P > Q)
    c[:, C_REMC_B] = s16 & (P < Q)
    R = np.zeros((128, 128), np.float32)
    for m in range(128):
        if m % 32 < 16:
            R[m, m + 16] = -1.0
        else:
            R[m, m - 16] = 1.0
    c[:, C_ROPE] = R.T
    c[:, C_CI16, 0:8] = (P[:, 0:8] // 16 == Q[:, 0:8])
    c[:, C_CI64, 0:2] = (P[:, 0:2] // 64 == Q[:, 0:2])
    return c


ARENA_BYTES = 91 * 1024


class Prog:
    def __init__(self, depth=DEPTH, mixers=("A", "B", "C"), dbg=None):
        self.depth = depth
        self.mixers = mixers
        self.dbg = dbg or {}
        self.k = KB()
        self.build()
        self.nc = self.k.finish()

    def declare_io(self):
        nc = self.k.nc
        L = self.depth

        def inp(name, shape, dt=F32):
            return nc.dram_tensor(name, list(shape), dt, kind="ExternalInput").ap()

        def outp(name, shape, dt=F32):
            return nc.dram_tensor(name, list(shape), dt, kind="ExternalOutput").ap()

        io = {}
        io["xT_p"] = inp("xT_p", [D, TOK])
        io["xT_s"] = inp("xT_s", [D, TOK])
        io["cond"] = inp("cond", [128, NCH, 2])
        io["w_mod"] = inp("w_mod", [L, D, 6 * D])
        io["b_mod"] = inp("b_mod", [L, 128, 48])
        io["w_in"] = inp("w_in", [L, D, D_IN])
        io["w_out_p"] = inp("w_out_p", [L, D, D])
        io["w_out_s"] = inp("w_out_s", [L, D, D])
        io["ln_g"] = inp("ln_g", [L, 128, 2, NCH])
        io["ln_b"] = inp("ln_b", [L, 128, 2, NCH])
        io["w_ffn_in"] = inp("w_ffn_in", [L, D, 2 * D_FF])
        io["w_ffn_out"] = inp("w_ffn_out", [L, D_FF, D])
        io["consts"] = inp("consts", [128, NCONST, 128])
        io["w_in_h"] = inp("w_in_h", [L, D, 964])
        io["rope_cs"] = inp("rope_cs", [2, 128, DEC_SEQ])
        io["diff_lambda"] = inp("diff_lambda", [L * 256])
        io["diff_norm"] = inp("diff_norm", [128, L])
        io["ctx_k"] = inp("ctx_k", [L, PAST, 128])
        io["ctx_v"] = inp("ctx_v", [L, PAST, 128])
        io["conv_hm"] = inp("conv_hm", [L, 4, 3 * 192])
        io["conv_h"] = inp("conv_h", [L, 3 * 192])
        io["w_ba_p"] = inp("w_ba_p", [L, D, 4, 4])
        io["dpar"] = inp("dpar", [L * 16])
        io["dpar_h"] = inp("dpar_h", [L * 4])
        io["dnorm"] = inp("dnorm", [L * 64])
        io["s0_d"] = inp("s0_d", [L, 2, 64, 64])
        io["new_sd"] = outp("new_sd", [4, L, 2, 4, 64, 64])
        io["hgrn_lb"] = inp("hgrn_lb", [2 * L * 256])
        io["hgrn_lb_h"] = inp("hgrn_lb_h", [2 * L * 64])
        io["hnorm"] = inp("hnorm", [L, 128])
        io["s0_h"] = inp("s0_h", [L, 2, 64, 64])
        io["new_sh"] = outp("new_sh", [4, L, 2, 4 * 64, 64])
        io["new_k"] = outp("new_k", [4, L, SEQ, 512])
        io["new_v"] = outp("new_v", [4, L, SEQ, 512])
        self.agx_in = [nc.dram_tensor("agx_in%d" % i, [512, TOK], BF16).ap() for i in range(2)]
        self.agx = [nc.dram_tensor("agx%d" % i, [4 * 512, TOK], BF16).ap() for i in range(2)]
        self.ago_in = [nc.dram_tensor("ago_in%d" % i, [128, DEC_SEQ], BF16).ap() for i in range(2)]
        self.ago = [nc.dram_tensor("ago%d" % i, [4 * 128, DEC_SEQ], BF16).ap() for i in range(2)]
        self.t_agx_in, self.t_agx, self.t_ago_in, self.t_ago = Tok(), Tok(), Tok(), Tok()
        io["yT_p"] = outp("yT_p", [D, TOK])
        io["yT_s"] = outp("yT_s", [D, TOK])
        for name, shape in self.dbg.items():
            io[name] = outp(name, shape)
        self.io = io

    def carve(self, off, shape, dt, ntok=1):
        n = 1
        for s in shape[1:]:
            n *= s
        nbytes = n * (2 if dt == BF16 else 4)
        assert off % 4 == 0 and off + nbytes <= ARENA_BYTES, (off, nbytes)
        ap = self.arena_t[:, off // 4:(off + nbytes + 3) // 4]
        if dt == BF16:
            ap = ap.bitcast(BF16)
        if len(shape) == 3:
            ap = ap.rearrange("p (a n) -> p a n", a=shape[1])
        elif len(shape) == 4:
            ap = ap.rearrange("p (a b n) -> p a b n", a=shape[1], b=shape[2])
        return Buf(ap, ntok), off + ((nbytes + 3) // 4) * 4

    def build(self):
        k = self.k
        self.declare_io()
        io = self.io
        L = self.depth
        self.xs = [k.sbuf("xs_p", [128, NCH, TOK], F32, ntok=2), k.sbuf("xs_s", [128, NCH, TOK], F32, ntok=2)]
        self.cst = k.sbuf("cst_sb", [128, NCONST, 128], F32)
        self.oT = k.sbuf("oT", [128, NCH, TOK], BF16, ntok=2)
        self.mod = k.sbuf("mod", [128, L, 48, 2], F32)
        self.lng = k.sbuf("lng", [128, L, 2, NCH], F32)
        self.lnb = k.sbuf("lnb", [128, L, 2, NCH], F32)
        self.ps = Ring([k.psum("ps%d" % i, [128, 512]) for i in range(4)])
        self.acc = [k.psum("acc%d" % i, [128, 512]) for i in range(4)]
        self.wst = k.ring("wst", [128, 2048], F32, 1)
        self.wbf = k.ring("wbf", [128, 2048], BF16, 2)
        self.tmp = k.ring("tmp", [128, 512], F32, 4)
        self.arena_t = self.k.es.enter_context(k.nc.sbuf_tensor("arena", [128, ARENA_BYTES // 4], F32))

        k.load(self.cst[:], io["consts"])
        for g, nm in enumerate(("xT_p", "xT_s")):
            for b in range(2):
                k.load(self.xs[g].at(b)[:, :, b * 512:(b + 1) * 512],
                       io[nm][:, b * 512:(b + 1) * 512].rearrange("(c p) n -> p c n", p=128))
        k.load(self.lng[:], io["ln_g"].rearrange("l p a c -> p l a c"))
        k.load(self.lnb[:], io["ln_b"].rearrange("l p a c -> p l a c"))
        self.preamble_mod()
        self.preamble_small()
        for l in range(L):
            self.layer(l)
        for g, nm in enumerate(("yT_p", "yT_s")):
            for b in range(2):
                k.store(io[nm][:, b * 512:(b + 1) * 512].rearrange("(c p) n -> p c n", p=128),
                        self.xs[g].at(b)[:, :, b * 512:(b + 1) * 512], q="sp")

    def C(self, i, rows=slice(None), cols=slice(None)):
        return self.cst[rows, i, cols]

    def preamble_mod(self):
        k, io = self.k, self.io
        L = self.depth
        k.barrier()
        off = 0
        wblk = []
        for i in range(2):
            b, off = self.carve(off, [128, NCH, 512], F32)
            wblk.append(b)
        cond, off = self.carve(off, [128, NCH, 2], F32)
        csil, off = self.carve(off, [128, NCH, 2], F32)
        bmod, off = self.carve(off, [128, L, 48], F32)
        k.load(cond[:], io["cond"])
        k.load(bmod[:], io["b_mod"].rearrange("l p m -> p l m"))
        k.act(csil[:], cond[:], AF.Silu)
        n = 0
        for l in range(L):
            for cb in range(12):
                w = wblk[n % 2]
                n += 1
                k.load(w[:], io["w_mod"][l, :, cb * 512:(cb + 1) * 512].rearrange("(c p) n -> p c n", p=128))
                ps = self.ps.next()
                for mi in range(4):
                    m = cb * 4 + mi
                    for kc in range(NCH):
                        k.mm(ps[:, 2 * mi:2 * mi + 2], w[:, kc, mi * 128:(mi + 1) * 128], csil[:, kc, :],
                             start=(kc == 0), stop=(kc == NCH - 1))
                k.tt("dve", self.mod[:, l, cb * 4:(cb + 1) * 4, :],
                     Opd(ps.t[:, 0:8].rearrange("p (m j) -> p m j", j=2), ps.toks),
                     Opd(bmod.t[:, l, cb * 4:(cb + 1) * 4].unsqueeze(2).broadcast_to([128, 4, 2]), bmod.toks),
                     ALU.add)
            for a in (8, 32):
                k.ts("dve", self.mod[:, l, a:a + 8, :], self.mod[:, l, a:a + 8, :], 1.0, ALU.add)
            for a in (16, 40):
                k.ts("dve", self.mod[:, l, a:a + 8, :], self.mod[:, l, a:a + 8, :], 1.0 / ALPHA, ALU.mult)
        k.barrier()

    def modv(self, l, which, g, kc):
        return self.mod[:, l, which * 8 + kc, g:g + 1]

    def load_w_bf16(self, dram_ap, rows, cols):
        st = self.wst.next()
        wb = self.wbf.next()
        k = self.k
        if len(dram_ap.shape) == 3:
            a, n = dram_ap.shape[1], dram_ap.shape[2]
            sv = Opd(st.t[:, 0:a * n].rearrange("p (a n) -> p a n", a=a), st.toks)
            wv = View(wb.t[:, 0:a * n].rearrange("p (a n) -> p a n", a=a), wb.toks)
            k.load(sv, dram_ap)
            k.copy("pool", wv[:], sv)
            return wv
        n = dram_ap.shape[1]
        r = dram_ap.shape[0]
        k.load(st[0:r, 0:n], dram_ap)
        k.copy("pool", wb[0:r, 0:n], st[0:r, 0:n])
        return View(wb.t, wb.toks)

    def layer(self, l):
        k = self.k
        self.mixer_phase(l)
        k.barrier()
        off = 0
        self.hT, off = self.carve(off, [128, NF, TOK], BF16, ntok=2)
        self.xm2, off = self.carve(off, [128, NCH, TOK], BF16, ntok=2)
        self.stat = Ring([self.carve(off + i * 2048, [128, 512], F32)[0] for i in range(6)])
        off += 6 * 2048
        self.usq = Ring([self.carve(off + i * 2048, [128, 512], F32)[0] for i in range(3)])
        off += 3 * 2048
        wst0 = self.wst.bufs[0]
        extra, off = self.carve(off, [128, 2048], F32)
        self.wst = Ring([wst0, extra])
        for g in range(2):
            if g == 1:
                self.load_oT_sample()
            self.dense_group(l, g)
        self.wst = Ring([wst0])
        k.barrier()

    def preamble_small(self):
        k, io = self.k, self.io
        L = self.depth
        self.lam = k.sbuf("lam", [128, L], F32)
        self.nlam = k.sbuf("nlam", [128, L], F32)
        self.gA = k.sbuf("gA", [128, L], F32)
        self.onesb = k.sbuf("onesb", [128, 128], BF16)
        k.memset("pool", self.onesb[:, :], 1.0)
        off = 0
        dlb, off = self.carve(off, [128, L, 4, 64], F32)
        pr, off = self.carve(off, [128, L, 2, 64], F32)
        sm, off = self.carve(off, [128, L, 2], F32)
        k.load(dlb[:], io["diff_lambda"].partition_broadcast(128).rearrange("p (l a d) -> p l a d", l=L, a=4))
        k.load(self.gA[:, :], io["diff_norm"])
        for j in range(2):
            k.tt("dve", pr[:, :, j, :], dlb[:, :, 2 * j, :], dlb[:, :, 2 * j + 1, :], ALU.mult)
        k.reduce(sm[:], pr[:])
        k.act(sm[:], sm[:], AF.Exp)
        k.tt("dve", self.lam[:, :], sm[:, :, 0], sm[:, :, 1], ALU.subtract)
        for l in range(L):
            lam_init = 0.8 - 0.6 * math.exp(-0.3 * l)
            k.ts("dve", self.lam[:, l:l + 1], self.lam[:, l:l + 1], lam_init, ALU.add)
            k.ts("dve", self.gA[:, l:l + 1], self.gA[:, l:l + 1], 1.0 - lam_init, ALU.mult)
        k.ts("dve", self.nlam[:, :], self.lam[:, :], -1.0, ALU.mult)
        k.barrier()

    def mixer_phase(self, l):
        k, io = self.k, self.io
        k.barrier()
        off = 0
        self.xmp, off = self.carve(off, [128, NCH, 4, 258], BF16, ntok=4)
        self.slab = []
        for i in range(2):
            b, off = self.carve(off, [128, NCH, 514], BF16)
            self.slab.append(b)
        self.moff = off
        k.memset("pool", self.xmp.all()[:, :, :, 0:1], 0.0)
        k.memset("pool", self.xmp.all()[:, :, :, 257:258], 0.0)
        for sq in range(4):
            for kc in range(NCH):
                k.act(self.xmp.at(sq)[:, kc, sq, 1:257], self.xs[0].at(sq // 2)[:, kc, sq * 256:(sq + 1) * 256],
                      AF.Identity, scale=self.modv(l, 1, 0, kc), bias=self.modv(l, 0, 0, kc))
        for b in range(2):
            for kc in range(NCH):
                k.ts("dve", self.slab[b][:, kc, 0:512], self.xs[1].at(b)[:, kc, b * 512:(b + 1) * 512],
                     self.modv(l, 1, 1, kc), ALU.mult, self.modv(l, 0, 1, kc), ALU.add)
            for hf in range(2):
                k.dma("sp", self.agx_in[hf][:, b * 512:(b + 1) * 512].rearrange("(c p) n -> p c n", p=128),
                      self.slab[b].t[:, 4 * hf:4 * hf + 4, 0:512], reads=self.slab[b].toks, writes=[self.t_agx_in])
        for hf in range(2):
            ain, aout = self.agx_in[hf], self.agx[hf]
            k.cc(lambda e, ain=ain, aout=aout: e.collective_compute("AllGather", ALU.bypass, replica_groups=GROUPS,
                                                                    ins=[ain], outs=[aout]),
                 reads=[self.t_agx_in], writes=[self.t_agx])
        if "A" in self.mixers:
            self.mixA_sample(l)
            k.barrier()
            self.mixA_prompt(l)
            k.barrier()
        else:
            self.zero_o(l, 0, 4, 0, 128)
        if "B" in self.mixers:
            self.mixB(l)
            k.barrier()
        else:
            self.zero_o(l, 4, 6, 128, 192)
        if "C" in self.mixers:
            self.mixC(l)
            k.barrier()
        else:
            self.zero_o(l, 6, 8, 192, 256)
        for hf in range(2):
            gin, gout = self.ago_in[hf], self.ago[hf]
            k.cc(lambda e, gin=gin, gout=gout: e.collective_compute("AllGather", ALU.bypass, replica_groups=GROUPS,
                                                                    ins=[gin], outs=[gout]),
                 reads=[self.t_ago_in], writes=[self.t_ago])

    def zero_o(self, l, c0, c1, r0, r1):
        k = self.k
        z = self.tmp.next()
        k.memset("pool", z[:, :], 0.0)
        zb = Opd(z.t[:, 0:256].bitcast(BF16), z.toks)
        for blk in range(8):
            k.dma("sp", self.ago_in[r0 // 128][r0 % 128:r0 % 128 + (r1 - r0), blk * 512:(blk + 1) * 512], zb.ap[0:r1 - r0, :],
                  reads=z.toks, writes=[self.t_ago_in])
        for b in range(2):
            k.memset("pool", self.oT.at(b)[:, c0:c1, b * 512:(b + 1) * 512], 0.0)

    def load_oT_sample(self):
        k = self.k
        ago = self.ago
        oT = self.oT

        def fn(e, sem, HF):
            core = e.partition_id()
            for c in range(8):
                r = c % 4
                with e.If(core == c):
                    e.dma_start(out=oT.t[:, HF::2, :], in_=ago[HF][:, r * TOK:(r + 1) * TOK].rearrange("(r p) n -> p r n", p=128)).then_inc(sem, 16)
        for HF in range(2):
            k.dma("pool", None, None, reads=[self.t_ago], writes=oT.toks, fn=(lambda e, sem, HF=HF: fn(e, sem, HF)), selfinc=True)

    def load_slab(self, buf, b, halo=False):
        k = self.k
        r, hf = b // 2, b % 2
        agx3 = [a.rearrange("(r f) n -> r f n", r=4) for a in self.agx]
        for fh in range(2):
            k.dma("sp", buf.t[:, 4 * fh:4 * fh + 4, 1:513], agx3[fh][r, :, hf * 512:(hf + 1) * 512].rearrange("(c p) n -> p c n", p=128),
                  reads=[self.t_agx], writes=buf.toks)
        if halo:
            for side, g0, col in ((0, b * 512 - 1, 0), (1, (b + 1) * 512, 513)):
                if g0 < 0 or g0 >= DEC_SEQ:
                    k.memset("pool", buf[:, :, col:col + 1], 0.0)
                else:
                    rr, cc = g0 // TOK, g0 % TOK
                    for fh in range(2):
                        o_ = buf.t[:, 4 * fh:4 * fh + 4, col:col + 1]
                        i_ = agx3[fh][rr, :, cc:cc + 1].rearrange("(c p) n -> p c n", p=128)
                        k.dma("sp", None, None, reads=[self.t_agx], writes=buf.toks,
                              fn=lambda e, o_=o_, i_=i_: e.dma_start(out=o_, in_=i_, allow_slow_non_contiguous=True))

    def wtile(self, dram_ap):
        return self.load_w_bf16(dram_ap.rearrange("(c p) n -> p c n", p=128), 128, dram_ap.shape[1])

    def attn_core(self, l, qT, kT, V, nkt, q0, nq, out_fn, Pt, o0, o1, rr):
        k = self.k
        om = [o0, o1]
        for m in range(2):
            rows = slice(64 * m, 64 * m + 64)
            psO, psR = self.acc[2 * m], self.acc[2 * m + 1]
            psS_next = None
            for kt in range(nkt):
                if kt == 0:
                    psS = self.ps.next()
                    k.mm(psS[:, 0:nq], kT[rows, 0:128], qT[rows, q0:q0 + nq])
                else:
                    psS = psS_next
                if kt + 1 < nkt:
                    psS_next = self.ps.next()
                    k.mm(psS_next[:, 0:nq], kT[rows, (kt + 1) * 128:(kt + 2) * 128], qT[rows, q0:q0 + nq])
                P = Pt.next()
                k.act(P[:, 0:nq], psS[:, 0:nq], AF.Exp, scale=0.125)
                k.mm(psO[:, 0:nq], V[:, kt, :], P[:, 0:nq], start=(kt == 0), stop=(kt == nkt - 1))
                k.mm(psR[:, 0:nq], self.onesb[:, :], P[:, 0:nq], start=(kt == 0), stop=(kt == nkt - 1))
            k.recip(rr[:, 0:nq], psR[:, 0:nq])
            k.tt("dve", om[m][:, 0:nq], psO[:, 0:nq], rr[:, 0:nq], ALU.mult)
        k.stt(o0[:, 0:nq], o1[:, 0:nq], self.nlam[:, l:l + 1], o0[:, 0:nq], ALU.mult, ALU.add)
        k.act(o1[:, 0:nq], o0[:, 0:nq], AF.Square)
        psN = self.ps.next()
        k.mm(psN[:, 0:nq], self.C(C_M128), o1[:, 0:nq])
        k.act(rr[:, 0:nq], psN[:, 0:nq], AF.Sqrt, bias=1e-6)
        k.recip(rr[:, 0:nq], rr[:, 0:nq])
        k.tt("dve", o0[:, 0:nq], o0[:, 0:nq], rr[:, 0:nq], ALU.mult)
        out_fn(o0)

    def mixA_sample(self, l):
        k, io = self.k, self.io
        off = self.moff
        qT, off = self.carve(off, [128, DEC_SEQ], BF16)
        kT, off = self.carve(off, [128, DEC_SEQ + PAST], BF16)
        V, off = self.carve(off, [128, 34, 128], BF16)
        Pt = Ring([self.carve(off + i * 1024, [128, 512], BF16)[0] for i in range(3)]); off += 3 * 1024
        xf = Ring([self.carve(off + i * 2048, [128, 512], F32)[0] for i in range(3)]); off += 3 * 2048
        o0, off = self.carve(off, [128, 512], F32)
        o1, off = self.carve(off, [128, 512], F32)
        rr, off = self.carve(off, [128, 512], F32)
        osb = Ring([self.carve(off + i * 1024, [128, 512], BF16)[0] for i in range(2)]); off += 2 * 1024
        cst = Ring([self.carve(off + i * 1024, [128, 2, 128], F32)[0] for i in range(2)]); off += 2 * 1024
        wqk = self.wtile(io["w_in_h"][l, :, 0:256])
        wv = self.wtile(io["w_in_h"][l, :, 256:384])
        for j in range(2):
            c = cst.next()
            k.load(c[:, 0, :], io["ctx_k"][l, j * 128:(j + 1) * 128, :])
            k.load(c[:, 1, :], io["ctx_v"][l, j * 128:(j + 1) * 128, :])
            ps = self.ps.next()
            k.mm(ps[:, 0:128], c[:, 0, :], self.C(C_ID))
            k.copy("act", kT[:, DEC_SEQ + j * 128:DEC_SEQ + (j + 1) * 128], ps[:, 0:128])
            k.copy("dve", V[:, 32 + j, :], c[:, 1, :])
        for b in range(8):
            sb = self.slab[b % 2]
            self.load_slab(sb, b)
            cos = self.tmp.next()
            sin = self.tmp.next()
            k.load(cos[:, :], io["rope_cs"][0, :, b * 512:(b + 1) * 512])
            k.load(sin[:, :], io["rope_cs"][1, :, b * 512:(b + 1) * 512])
            for which, dst in ((0, qT), (1, kT)):
                ps = self.ps.next()
                for kc in range(NCH):
                    k.mm(ps[:, :], wqk[:, kc, which * 128:(which + 1) * 128], sb[:, kc, 1:513], start=(kc == 0), stop=(kc == NCH - 1))
                x = xf.next()
                k.copy("act", x[:, :], ps[:, :])
                ps2 = self.ps.next()
                k.mm(ps2[:, :], self.C(C_ROPE), x[:, :])
                t1 = xf.next()
                k.tt("dve", t1[:, :], x[:, :], cos[:, :], ALU.mult)
                k.tt("dve", x[:, :], ps2[:, :], sin[:, :], ALU.mult)
                k.tt("dve", dst[:, b * 512:(b + 1) * 512], t1[:, :], x[:, :], ALU.add)
            for j in range(4):
                ps = self.ps.next()
                for kc in range(NCH):
                    k.mm(ps[:, 0:128], sb[:, kc, 1 + j * 128:1 + (j + 1) * 128], wv[:, kc, :], start=(kc == 0), stop=(kc == NCH - 1))
                k.copy("act", V[:, b * 4 + j, :], ps[:, 0:128])
        qv, kv, vv = View(qT.t, qT.toks), View(kT.t, kT.toks), View(V.t, V.toks)
        for b in range(8):
            def out_fn(o, b=b):
                ob = osb.next()
                k.ts("dve", ob[:, :], o[:, :], self.gA[:, l:l + 1], ALU.mult)
                k.dma("sp", self.ago_in[0][0:128, b * 512:(b + 1) * 512], ob.t[:, :], reads=ob.toks, writes=[self.t_ago_in])
            self.attn_core(l, qv, kv, vv, 34, b * 512, 512, out_fn, Pt, o0, o1, rr)

    def mixA_prompt(self, l):
        k, io = self.k, self.io
        off = self.moff
        Vp, off = self.carve(off, [128, 8, 512], BF16)
        qk = Ring([self.carve(off + i * 1024, [128, 2, 256], BF16)[0] for i in range(2)]); off += 2 * 1024
        Pt = Ring([self.carve(off + i * 1024, [128, 512], BF16)[0] for i in range(3)]); off += 3 * 1024
        o0, off = self.carve(off, [128, 512], F32)
        o1, off = self.carve(off, [128, 512], F32)
        rr, off = self.carve(off, [128, 512], F32)
        stg = Ring([self.carve(off + i * 1024, [128, 256], F32)[0] for i in range(3)]); off += 3 * 1024
        for cc in range(4):
            w = self.wtile(io["w_in"][l, :, O_AK + cc * 256:O_AK + (cc + 1) * 256])
            for t in range(8):
                sq, i = t // 2, t % 2
                ps = self.ps.next()
                for kc in range(NCH):
                    k.mm(ps[:, 0:256], self.xmp.at(sq)[:, kc, sq, 1 + i * 128:1 + (i + 1) * 128], w[:, kc, :],
                         start=(kc == 0), stop=(kc == NCH - 1))
                sg = stg.next()
                k.copy("act", sg[:, :], ps[:, 0:256])
                dst = io["new_k"] if cc < 2 else io["new_v"]
                c0 = (cc % 2) * 256
                k.dma("sp", dst[sq, l, i * 128:(i + 1) * 128, c0:c0 + 256], sg.t[:, :], reads=sg.toks)
                if cc >= 2:
                    k.copy("dve", Vp[:, t, c0:c0 + 256], sg[:, :])
        for h in range(4):
            w = self.wtile(io["w_in"][l, :, O_AQ + h * 128:O_AQ + (h + 1) * 128])
            w2 = self.wtile(io["w_in"][l, :, O_AK + h * 128:O_AK + (h + 1) * 128])
            for sq in range(4):
                qb = qk.next()
                for which, ww in ((0, w), (1, w2)):
                    ps = self.ps.next()
                    for kc in range(NCH):
                        k.mm(ps[:, 0:256], ww[:, kc, :], self.xmp.at(sq)[:, kc, sq, 1:257], start=(kc == 0), stop=(kc == NCH - 1))
                    k.copy("act", qb[:, which, :], ps[:, 0:256])
                qv = View(qb.t[:, 0, :], qb.toks)
                kv = View(qb.t[:, 1, :], qb.toks)
                vv = View(Vp.t[:, 2 * sq:2 * sq + 2, h * 128:(h + 1) * 128], Vp.toks)

                def out_fn(o, h=h, sq=sq):
                    k.ts("dve", self.oT.at(sq // 2)[:, h, sq * 256:(sq + 1) * 256], o[:, 0:256], self.gA[:, l:l + 1], ALU.mult)
                self.attn_core(l, qv, kv, vv, 2, 0, 256, out_fn, Pt, o0, o1, rr)

    def interleave(self, fixed, queue, nslots=2):
        fixed = list(fixed)
        queue = list(queue)
        slots = []
        while fixed or slots or queue:
            if not slots:
                while len(slots) < nslots and queue:
                    slots.append(queue.pop(0))
            for lst in (fixed, slots):
                for g in list(lst):
                    try:
                        next(g)
                    except StopIteration:
                        lst.remove(g)

    def lockstep(self, gens):
        gens = list(gens)
        while gens:
            for g in list(gens):
                try:
                    next(g)
                except StopIteration:
                    gens.remove(g)

    def src_prompt(self, sq):
        def f(tile, kc, shift=0):
            c0 = 1 + tile * 128 + shift
            return self.xmp.at(sq)[:, kc, sq, c0:c0 + 128]
        return f

    def src_prompt_fm(self, sq):
        return lambda kc: self.xmp.at(sq)[:, kc, sq, 1:257]

    def sample_src(self, order, halo):
        state = {"b": None, "buf": None}
        slab = self.slab_ring

        def f(tile, kc, shift=0):
            b = tile // 4
            if state["b"] != b:
                state["b"] = b
                state["buf"] = slab.next()
                self.load_slab(state["buf"], b, halo=halo)
            c0 = 1 + (tile % 4) * 128 + shift
            return state["buf"][:, kc, c0:c0 + 128]
        return f

    def mixB(self, l):
        k, io = self.k, self.io
        L = self.depth
        off = self.moff
        self.slab_ring = Ring(self.slab)
        dpp, off = self.carve(off, [128, 4, 2, 2], F32)
        dps, off = self.carve(off, [128, 2, 2], F32)
        dnr, off = self.carve(off, [128, 64], F32)
        k.load(dpp[:], io["dpar"][l * 16:(l + 1) * 16].partition_broadcast(128).rearrange("p (h d a) -> p h d a", h=4, d=2))
        k.load(dps[:], io["dpar_h"][l * 4:(l + 1) * 4].partition_broadcast(128).rearrange("p (d a) -> p d a", d=2))
        k.load(dnr[:, :], io["dnorm"][l * 64:(l + 1) * 64].partition_broadcast(128))
        k.act(dpp[:, :, :, 0:1], dpp[:, :, :, 0:1], AF.Exp)
        k.ts("dve", dpp[:, :, :, 0:1], dpp[:, :, :, 0:1], -1.0, ALU.mult)
        k.act(dps[:, :, 0:1], dps[:, :, 0:1], AF.Exp)
        k.ts("dve", dps[:, :, 0:1], dps[:, :, 0:1], -1.0, ALU.mult)
        cwb, off = self.carve(off, [128, 576], F32)
        wset = {}
        for nm in ("p", "s"):
            wc, off = self.carve(off, [128, 3, NCH, 196], BF16)
            k.memset("pool", wc[:, :, :, 192:196], 0.0)
            wset[nm] = (wc, None)

        def fold(wc, wba, qkv_srcs, ba_src, cw_src):
            k.load(cwb[:, :], cw_src.partition_broadcast(128))
            cw = cwb.t
            st = self.wst.next()
            sv = View(st.t[:, 0:NCH * 192].rearrange("p (c n) -> p c n", c=NCH), st.toks)
            for j, src in enumerate(qkv_srcs):
                n = src.shape[1]
                k.load(sv[:, :, j * (192 // len(qkv_srcs)):j * (192 // len(qkv_srcs)) + n], src.rearrange("(c p) n -> p c n", p=128))
            for tap in range(3):
                k.tt("pool", wc[:, tap, :, 0:192], sv[:], Opd(cw[:, tap * 192:(tap + 1) * 192].unsqueeze(1).broadcast_to([128, NCH, 192]), cwb.toks),
                     ALU.mult)
            st2 = self.wst.next()
            sv2 = View(st2.t[:, 0:NCH * 4].rearrange("p (c n) -> p c n", c=NCH), st2.toks)
            k.load(sv2[:], ba_src)
            k.copy("pool", wc[:, 1, :, 192:196], sv2[:])

        fold(wset["s"][0], wset["s"][1], [io["w_in_h"][l, :, 384:576]],
             io["w_in_h"][l, :, 640:644].rearrange("(c p) n -> p c n", p=128), io["conv_h"][l])
        wcur = {"h": None}

        def getw(h):
            if wcur["h"] != h:
                wcur["h"] = h
                fold(wset["p"][0], wset["p"][1],
                     [io["w_in"][l, :, O_BQKV + j * 256 + h * 64:O_BQKV + j * 256 + (h + 1) * 64] for j in range(3)],
                     io["w_ba_p"][l, :, h, :].rearrange("(c p) n -> p c n", p=128), io["conv_hm"][l, h])
            return wset["p"]
        Oacc_s, off = self.carve(off, [128, 32, 64], F32, ntok=32)
        Oacc_p, off = self.carve(off, [128, 8, 256], F32, ntok=32)
        self.b_base = off
        R = {}
        for nm, w, n in (("qkv", 192, 2), ("et", 192, 1), ("sq", 128, 1), ("sm", 16, 4), ("kbe", 64, 2), ("vb", 64, 2),
                         ("kd", 64, 2), ("qd", 64, 2), ("gb", 64, 2), ("diag", 128, 1), ("dec", 128, 1),
                         ("dS", 128, 1), ("dI", 128, 1), ("Nm", 128, 2), ("QKm", 128, 2), ("NQT", 256, 2),
                         ("Xr", 128, 3), ("U", 192, 2), ("vnew", 64, 2), ("S", 64, 16)):
            R[nm] = Ring([self.carve(off + i * w * 4, [128, w], F32)[0] for i in range(n)])
            off += n * w * 4
        tb = self.tmp.bufs
        R["T3"] = Ring([tb[0], tb[1]])
        pp3 = Buf(tb[2].t, 1)
        R["PP"] = Ring([Buf(tb[2].t[:, 0:256], 1), Buf(tb[2].t[:, 256:512], 1), Buf(tb[3].t[:, 0:256], 1)])
        ID, ONE = self.C(C_ID), self.C(C_ONE)

        def job(d, src, ntiles, wc, wba, dpar, x0_fn, o_fn, fin_fn):
            order = list(range(ntiles)) if d == 0 else list(range(ntiles - 1, -1, -1))
            tri, rem, cstr, cinc = ((C_TRID_F, C_REMD_F, C_STR_F, C_INC_F) if d == 0 else
                                    (C_TRID_B, C_REMD_B, C_STR_B, C_INC_B))
            S = R["S"].next()
            x0_fn(S)
            for tile in order:
                ps1 = self.ps.next()
                n = 0
                for tap in range(3):
                    for kc in range(NCH):
                        k.mm(ps1[:, 0:196], src(tile, kc, tap - 1), wc[:, tap, kc, :], start=(n == 0), stop=(n == 23))
                        n += 1
                qkv, et, sq, sm = R["qkv"].next(), R["et"].next(), R["sq"].next(), R["sm"].next()
                k.act(et[:, :], ps1[:, 0:192], AF.Exp, scale=-1.0)
                k.act(sm[:, 2:3], ps1[:, 192 + d:193 + d], AF.Exp, scale=-1.0)
                k.act(sm[:, 4:5], ps1[:, 194 + d:195 + d], AF.Exp, bias=dpar[:, d, 1:2])
                k.ts("dve", et[:, :], et[:, :], 1.0, ALU.add)
                k.recip(et[:, :], et[:, :])
                k.tt("dve", qkv[:, :], ps1[:, 0:192], et[:, :], ALU.mult)
                k.tt("dve", sq[:, :], qkv[:, 0:128], qkv[:, 0:128], ALU.mult)
                k.reduce(sm[:, 0:2], Opd(sq.t[:, :].rearrange("p (a e) -> p a e", a=2), sq.toks))
                k.act(sm[:, 8:10], sm[:, 0:2], AF.Sqrt, bias=1e-6)
                k.recip(sm[:, 8:10], sm[:, 8:10])
                k.ts("dve", qkv[:, 0:64], qkv[:, 0:64], sm[:, 8:9], ALU.mult, 0.125, ALU.mult)
                k.ts("dve", qkv[:, 64:128], qkv[:, 64:128], sm[:, 9:10], ALU.mult)
                k.ts("dve", sm[:, 2:3], sm[:, 2:3], 1.0, ALU.add)
                k.recip(sm[:, 2:3], sm[:, 2:3])
                k.ts("dve", sm[:, 3:4], sm[:, 2:3], -1.0, ALU.mult)
                k.act(sm[:, 4:5], sm[:, 4:5], AF.Ln, bias=1.0)
                k.ts("dve", sm[:, 4:5], sm[:, 4:5], dpar[:, d, 0:1], ALU.mult)
                k.copy("dve", sm[:, 5:6], sm[:, 4:5])
                if BY & 1:
                    yield
                gb = R["gb"].next()
                k.copy("dve", gb[:, :], Opd(sm.t[:, 4:5].broadcast_to([128, 64]), sm.toks))
                psc = self.ps.next()
                k.mm(psc[:, 0:2], self.C(tri), sm[:, 4:6])
                k.mm(psc[:, 2:4], self.C(rem), sm[:, 4:6])
                k.mm(psc[0:64, 4:6], gb[:, :], self.C(C_CI64, cols=slice(0, 2)))
                k.copy("act", sm[:, 6:7], psc[:, 0:1])
                k.act(sm[:, 12:16], psc[:, 0:4], AF.Exp)
                k.act(sm[0:64, 10:12], psc[0:64, 4:6], AF.Exp)
                kbe, vb, kd, qd = R["kbe"].next(), R["vb"].next(), R["kd"].next(), R["qd"].next()
                k.ts("dve", kbe[:, :], qkv[:, 64:128], sm[:, 2:3], ALU.mult, sm[:, 12:13], ALU.mult)
                k.act(vb[:, :], qkv[:, 128:192], AF.Identity, scale=sm[:, 2:3])
                k.act(kd[:, :], qkv[:, 64:128], AF.Identity, scale=sm[:, 14:15])
                k.ts("dve", qd[:, :], qkv[:, 0:64], sm[:, 12:13], ALU.mult)
                pst = self.ps.next()
                k.mm(pst[0:64, 0:128], qkv[:, 64:128], ID)
                k.mm(pst[0:64, 128:256], qkv[:, 0:64], ID)
                k.mm(pst[0:64, 256:384], qd[:, :], ID)
                T3 = R["T3"].next()
                k.copy("act", T3[0:64, 0:384], pst[0:64, 0:384])
                knT, qnT, qdT = View(T3.t[0:64, 0:128], T3.toks), View(T3.t[0:64, 128:256], T3.toks), View(T3.t[0:64, 256:384], T3.toks)
                if BY & 2:
                    yield
                diag = R["diag"].next()
                k.act(diag[:, :], ID, AF.Identity, scale=sm[:, 6:7])
                psG = self.ps.next()
                k.mm(psG[:, 0:128], knT[:, :], knT[:, :])
                k.mm(psG[:, 128:256], qnT[:, :], knT[:, :])
                k.mm(psG[:, 256:384], ONE, diag[:, :])
                dec, dS, dI, Nm, QKm = (R[x].next() for x in ("dec", "dS", "dI", "Nm", "QKm"))
                k.ts("dve", dec[:, :], psG[:, 256:384], sm[:, 6:7], ALU.subtract, 0.0, ALU.max)
                k.act(dec[:, :], dec[:, :], AF.Exp, scale=-1.0)
                k.tt("pool", dS[:, :], dec[:, :], self.C(cstr), ALU.mult)
                k.tt("pool", dI[:, :], dec[:, :], self.C(cinc), ALU.mult)
                k.stt(Nm[:, :], dS[:, :], sm[:, 3:4], psG[:, 0:128], ALU.mult, ALU.mult)
                k.tt("dve", QKm[:, :], dI[:, :], psG[:, 128:256], ALU.mult)
                psT = self.ps.next()
                k.mm(psT[:, 0:128], Nm[:, :], ID)
                k.mm(psT[:, 128:256], QKm[:, :], ID)
                NQT = R["NQT"].next()
                k.copy("act", NQT[:, :], psT[:, 0:256])
                QKT = View(NQT.t[:, 128:256], NQT.toks)
                if BY & 4:
                    yield
                P, PT = View(Nm.t, Nm.toks), View(NQT.t[:, 0:128], NQT.toks)
                X = R["Xr"].next()
                k.tt("dve", X[:, :], PT[:, :], ID, ALU.add)
                for j in range(5):
                    psD = self.ps.next()
                    k.mm(psD[:, 0:128], PT[:, :], P[:, :])
                    if j < 4:
                        k.mm(psD[:, 128:256], P[:, :], PT[:, :])
                    PP = R["PP"].next()
                    k.copy("act", PP[:, 0:(256 if j < 4 else 128)], psD[:, 0:(256 if j < 4 else 128)])
                    P, PT = View(PP.t[:, 0:128], PP.toks), View(PP.t[:, 128:256], PP.toks)
                    psX = self.ps.next()
                    k.mm(psX[:, 0:128], ID, X[:, :], start=True, stop=False)
                    k.mm(psX[:, 0:128], P[:, :], X[:, :], start=False, stop=True)
                    X2 = R["Xr"].next()
                    k.copy("dve", X2[:, :], psX[:, 0:128])
                    X = X2
                    if BY & 8:
                        yield
                psU = self.ps.next()
                k.mm(psU[:, 0:64], X[:, :], vb[:, :])
                k.mm(psU[0:64, 64:192], kbe[:, :], X[:, :])
                U = R["U"].next()
                k.copy("act", U[:, 0:64], psU[:, 0:64])
                k.copy("act", U[0:64, 64:192], psU[0:64, 64:192])
                if BY & 16:
                    yield
                for ci in ((0, 1) if d == 0 else (1, 0)):
                    r = slice(64 * ci, 64 * ci + 64)
                    psa = self.ps.next()
                    psb = self.ps.next()
                    k.mm(psa[r, 0:64], U[0:64, 64 + 64 * ci:128 + 64 * ci], S[0:64, :])
                    k.mm(psa[r, 64:128], qdT[:, 64 * ci:64 * ci + 64], S[0:64, :])
                    vnew = R["vnew"].next()
                    k.tt("dve", vnew[r, :], U[r, 0:64], psa[r, 0:64], ALU.subtract)
                    k.mm(psb[r, 0:64], QKT[r, 64 * ci:64 * ci + 64], vnew[r, :])
                    k.mm(psb[0:64, 64:128], kd[r, :], vnew[r, :])
                    S2 = R["S"].next()
                    k.stt(S2[0:64, :], S[0:64, :], sm[0:64, 10 + ci:11 + ci], psb[0:64, 64:128], ALU.mult, ALU.add)
                    o_fn(tile, r, psa, psb)
                    S = S2
                    if BY & 32:
                        yield
                if not (BY & 32):
                    yield
            if BSTOP >= 4:
                fin_fn(S)

        done_s = set()

        def x0_sample(d):
            return lambda S: k.load(S[0:64, :], io["s0_d"][l, d])

        def o_sample(tile, r, psa, psb):
            dst = Oacc_s.at(tile)[r, tile, :]
            key = (tile, r.start)
            if key in done_s:
                k.tt("dve", dst, psa[r, 64:128], dst, ALU.add)
            else:
                done_s.add(key)
                k.copy("act", dst, psa[r, 64:128])
            k.tt("dve", dst, psb[r, 0:64], dst, ALU.add)

        sj = [job(d, self.sample_src(None, True), 32, wset["s"][0], wset["s"][1], dps, x0_sample(d), o_sample, lambda S: None)
              for d in range(2)]
        done_p = set()

        def mk_prompt(sq, d, h):
            def x0(S):
                k.memset("dve", S[0:64, :], 0.0)

            def o_fn(tile, r, psa, psb):
                t = sq * 2 + tile
                dst = Oacc_p.at(t * 4 + h)[r, t, h * 64:(h + 1) * 64]
                key = (t, h, r.start)
                if key in done_p:
                    k.tt("dve", dst, psa[r, 64:128], dst, ALU.add)
                else:
                    done_p.add(key)
                    k.copy("act", dst, psa[r, 64:128])
                k.tt("dve", dst, psb[r, 0:64], dst, ALU.add)

            def fin(S):
                k.dma("sp", io["new_sd"][sq, l, d, h], S.t[0:64, :], reads=S.toks)

            def gen():
                wc, wba = getw(h)
                yield from job(d, self.src_prompt(sq), 2, wc, wba, View(dpp.t[:, h], dpp.toks), x0, o_fn, fin)
            return gen()

        self.lockstep(sj)
        for h in range(4):
            for sq in range(4):
                self.lockstep([mk_prompt(sq, d, h) for d in range(2)])

        if BSTOP < 5:
            self.zero_o(l, 4, 6, 128, 192)
            return
        k.barrier()
        off = self.b_base
        P1 = {}
        for nm, w, n in (("sq", 64, 2), ("ss", 4, 2), ("e", 64, 2), ("o", 64, 2)):
            P1[nm] = Ring([self.carve(off + i * w * 4, [128, w], F32)[0] for i in range(n)])
            off += n * w * 4
        osb = Ring([self.carve(off + i * 1024, [128, 512], BF16)[0] for i in range(2)]); off += 2048

        def post(ov, gate_mm, prow):
            sq_, ss, e, o = (P1[x].next() for x in ("sq", "ss", "e", "o"))
            k.act(sq_[:, :], ov, AF.Square, accum=ss[:, 0:1])
            k.act(ss[:, 1:2], ss[:, 0:1], AF.Sqrt, bias=1e-6, scale=1.0 / 64)
            k.recip(ss[:, 1:2], ss[:, 1:2])
            psG = self.ps.next()
            gate_mm(psG)
            k.act(e[:, :], psG[:, 0:64], AF.Exp, scale=-1.0)
            k.ts("dve", e[:, :], e[:, :], 1.0, ALU.add)
            k.recip(e[:, :], e[:, :])
            k.tt("dve", e[:, :], e[:, :], psG[:, 0:64], ALU.mult)
            k.stt(o[:, :], ov, ss[:, 1:2], dnr[:, :], ALU.mult, ALU.mult)
            k.tt("dve", o[:, :], o[:, :], e[:, :], ALU.mult)
            psT = self.ps.next()
            k.mm(psT[prow, 0:128], o[:, :], ID)
            return psT

        for h in range(4):
            wg = self.wtile(io["w_in"][l, :, O_BG + h * 64:O_BG + (h + 1) * 64])
            prow = slice(64 * (h % 2), 64 * (h % 2) + 64)
            for t in range(8):
                sq, tile = t // 2, t % 2
                ov = Oacc_p.at(t * 4 + h)[:, t, h * 64:(h + 1) * 64]

                def gate_mm(ps, sq=sq, tile=tile, wg=wg):
                    sp = self.src_prompt(sq)
                    for kc in range(NCH):
                        k.mm(ps[:, 0:64], sp(tile, kc), wg[:, kc, 0:64], start=(kc == 0), stop=(kc == NCH - 1))
                psT = post(ov, gate_mm, prow)
                k.copy("act", self.oT.at(t // 4)[prow, 4 + h // 2, t * 128:(t + 1) * 128], psT[prow, 0:128])
        wg = self.wtile(io["w_in_h"][l, :, 576:640])
        ssrc = self.sample_src(None, False)
        for b in range(8):
            ob = osb.next()
            for j in range(4):
                tile = b * 4 + j
                ov = Oacc_s.at(tile)[:, tile, :]

                def gate_mm(ps, tile=tile):
                    for kc in range(NCH):
                        k.mm(ps[:, 0:64], ssrc(tile, kc), wg[:, kc, 0:64], start=(kc == 0), stop=(kc == NCH - 1))
                psT = post(ov, gate_mm, slice(0, 64))
                k.copy("act", ob[0:64, j * 128:(j + 1) * 128], psT[0:64, 0:128])
            k.dma("sp", self.ago_in[1][0:64, b * 512:(b + 1) * 512], ob.t[0:64, :], reads=ob.toks, writes=[self.t_ago_in])

    def mixC(self, l):
        k, io = self.k, self.io
        L = self.depth
        off = self.moff
        self.slab_ring = Ring(self.slab)
        lbs = {}
        scr = ARENA_BYTES - (2 * L * 256 * 4 + 2 * 256 * 4)
        for nm, Wd in (("hgrn_lb", 256), ("hgrn_lb_h", 64)):
            e, o2 = self.carve(scr, [128, 2, L, Wd], F32)
            tot, o2 = self.carve(o2, [128, 2, Wd], F32)
            lb, off = self.carve(off, [128, 2, Wd], F32)
            om, off = self.carve(off, [128, 2, Wd], F32)
            k.load(e[:], io[nm].partition_broadcast(128).rearrange("p (d l w) -> p d l w", d=2, l=L))
            k.act(e[:], e[:], AF.Exp)
            k.copy("dve", tot[:], e[:, :, 0, :])
            for j in range(1, L):
                k.tt("dve", tot[:], tot[:], e[:, :, j, :], ALU.add)
            k.recip(tot[:], tot[:])
            if l == 0:
                k.memset("dve", lb[:], 0.0)
            else:
                k.copy("dve", lb[:], e[:, :, 1, :])
                for j in range(2, l + 1):
                    k.tt("dve", lb[:], lb[:], e[:, :, j, :], ALU.add)
                k.tt("dve", lb[:], lb[:], tot[:], ALU.mult)
            k.ts("dve", om[:], lb[:], -1.0, ALU.mult, 1.0, ALU.add)
            lbs[nm] = (lb, om)
            k.barrier()
        hn = self.carve(off, [128, 1], F32)[0]; off += 64
        k.load(hn[:, :], io["hnorm"][l].rearrange("(p o) -> p o", o=1))
        def wS(c0, n):
            b, _ = self.carve(wS.off, [128, NCH, n], BF16)
            wS.off += NCH * n * 2
            st = self.wst.next()
            sv = Opd(st.t[:, 0:NCH * n].rearrange("p (c n) -> p c n", c=NCH), st.toks)
            k.load(sv, c0.rearrange("(c p) n -> p c n", p=128))
            k.copy("pool", b[:], sv)
            return b
        wS.off = off
        ws_q = wS(io["w_in_h"][l, :, 644:708], 64)
        ws_f = [wS(io["w_in_h"][l, :, 708 + 64 * d:772 + 64 * d], 64) for d in range(2)]
        ws_i = wS(io["w_in_h"][l, :, 836:900], 64)
        wpb = [self.carve(wS.off + i * 2048, [128, NCH, 128], BF16)[0] for i in range(4)]
        wS.off += 4 * 2048
        wp_cur = {"pr": None}

        def getw(pr):
            if wp_cur["pr"] != pr:
                wp_cur["pr"] = pr
                srcs = [io["w_in"][l, :, O_CQ + pr * 128:O_CQ + (pr + 1) * 128],
                        io["w_in"][l, :, O_CF + pr * 128:O_CF + (pr + 1) * 128],
                        io["w_in"][l, :, O_CF + 256 + pr * 128:O_CF + 256 + (pr + 1) * 128],
                        io["w_in"][l, :, O_CI + pr * 128:O_CI + (pr + 1) * 128]]
                for b, c0 in zip(wpb, srcs):
                    st = self.wst.next()
                    sv = Opd(st.t[:, 0:NCH * 128].rearrange("p (c n) -> p c n", c=NCH), st.toks)
                    k.load(sv, c0.rearrange("(c p) n -> p c n", p=128))
                    k.copy("pool", b[:], sv)
            return wpb[0], [wpb[1], wpb[2]], wpb[3]
        off = wS.off
        Oacc_s, off = self.carve(off, [128, 2048], F32, ntok=32)
        Oacc_p, off = self.carve(off, [128, 2, 1024], F32, ntok=16)
        self.c_base = off
        R = {}
        for nm, shp, n in (("E1", [128, 256], 2), ("qs", [128, 128], 2), ("f", [128, 128], 2), ("g", [128, 128], 2),
                           ("kk", [128, 128], 2), ("v", [128, 128], 2), ("qd", [128, 128], 2), ("kdi", [128, 128], 2),
                           ("kd", [128, 128], 2), ("qdT", [128, 128], 2), ("kdiT", [128, 128], 2), ("AT", [128, 128], 2),
                           ("gl", [128, 16], 2), ("X", [128, 64], 8), ("fs", [128, 64], 2)):
            sz = shp[1] * 4
            R[nm] = Ring([self.carve(off + i * sz, shp, F32)[0] for i in range(n)])
            off += n * sz
        self.c_off = off

        def job(d, W, src, ntiles, w_q, w_f, w_i, lb, om, lcol, x0_fn, o_fn, fin_fn):
            nh = W // 64
            order = list(range(ntiles)) if d == 0 else list(range(ntiles - 1, -1, -1))
            tri, rem = (C_TRIC_F, C_REMC_F) if d == 0 else (C_TRIC_B, C_REMC_B)
            X = R["X"].next()
            x0_fn(X)
            for tile in order:
                ps1 = self.ps.next()
                for kc in range(NCH):
                    k.mm(ps1[:, 0:W], src(tile, kc), w_q[:, kc, :], start=(kc == 0), stop=(kc == NCH - 1))
                for kc in range(NCH):
                    k.mm(ps1[:, W:2 * W], src(tile, kc), w_f[:, kc, :], start=(kc == 0), stop=(kc == NCH - 1))
                ps2 = self.ps.next()
                for kc in range(NCH):
                    k.mm(ps2[:, 0:W], src(tile, kc), w_i[:, kc, :], start=(kc == 0), stop=(kc == NCH - 1))
                E1, qs, f, g, kk, v = (R[n].next() for n in ("E1", "qs", "f", "g", "kk", "v"))
                k.act(E1[:, 0:2 * W], ps1[:, 0:2 * W], AF.Exp, scale=-1.0)
                k.ts("dve", E1[:, 0:2 * W], E1[:, 0:2 * W], 1.0, ALU.add)
                k.recip(E1[:, 0:2 * W], E1[:, 0:2 * W])
                k.tt("dve", qs[:, 0:W], ps1[:, 0:W], E1[:, 0:W], ALU.mult)
                k.tt("dve", f[:, 0:W], E1[:, W:2 * W], om[:, d, lcol:lcol + W], ALU.mult)
                k.tt("dve", f[:, 0:W], f[:, 0:W], lb[:, d, lcol:lcol + W], ALU.add)
                k.act(g[:, 0:W], f[:, 0:W], AF.Ln)
                k.ts("pool", kk[:, 0:W], f[:, 0:W], -1.0, ALU.mult, 1.0, ALU.add)
                k.copy("act", v[:, 0:W], ps2[:, 0:W])
                yield
                psb = self.ps.next()
                k.mm(psb[:, 0:W], self.C(tri), g[:, 0:W])
                k.mm(psb[:, W:2 * W], self.C(rem), g[:, 0:W])
                k.mm(psb[0:W, 2 * W:2 * W + 8], g[:, 0:W], self.C(C_CI16, cols=slice(0, 8)))
                qd, kdi, kd, gl = (R[n].next() for n in ("qd", "kdi", "kd", "gl"))
                Eb = R["E1"].next()
                k.act(Eb[:, 0:W], psb[:, 0:W], AF.Exp)
                k.tt("pool", qd[:, 0:W], qs[:, 0:W], Eb[:, 0:W], ALU.mult)
                k.act(Eb[:, W:2 * W], psb[:, 0:W], AF.Exp, scale=-1.0)
                k.tt("pool", kdi[:, 0:W], kk[:, 0:W], Eb[:, W:2 * W], ALU.mult)
                Ed = R["f"].next()
                k.act(Ed[:, 0:W], psb[:, W:2 * W], AF.Exp)
                k.tt("pool", kd[:, 0:W], kk[:, 0:W], Ed[:, 0:W], ALU.mult)
                glo = gl.t[0:W, 0:8] if d == 0 else gl.t[0:W, 7::-1]
                k.act(Opd(glo, gl.toks), psb[0:W, 2 * W:2 * W + 8], AF.Exp)
                k.copy("dve", gl[0:W, 8:16], gl[0:W, 0:8])
                k.memset("dve", gl[0:W, 8:9], 0.0)
                yield
                qdT, kdiT = R["qdT"].next(), R["kdiT"].next()
                pst = self.ps.next()
                k.mm(pst[0:W, 0:128], qd[:, 0:W], self.C(C_ID))
                k.mm(pst[0:W, 128:256], kdi[:, 0:W], self.C(C_ID))
                if 'c' not in CSUB:
                    if 'x' not in CSUB:
                        k.copy("act", qdT[0:W, :], pst[0:W, 0:128])
                    if 'y' not in CSUB:
                        k.copy("dve", kdiT[0:W, :], pst[0:W, 128:256])
                psKV = self.ps.next()
                ATs = []
                for h in range(nh):
                    r0 = 64 * h
                    if 'a' in CSUB:
                        continue
                    psA = self.ps.next()
                    k.mm(psA[:, 0:128], kdiT[r0:r0 + 64, :], qdT[r0:r0 + 64, :])
                    AT = R["AT"].next()
                    k.tt("dve", AT[:, :], psA[:, 0:128], self.C(tri), ALU.mult)
                    ATs.append(AT)
                    if 'b' in CSUB:
                        continue
                    vx = self.tmp.next()
                    k.tt("pool", Opd(vx.t[:, :].rearrange("p (c e) -> p c e", c=8), vx.toks),
                         Opd(v.t[:, r0:r0 + 64].unsqueeze(1).broadcast_to([128, 8, 64]), v.toks),
                         Opd(self.cst.t[:, C_CI16, 0:8].unsqueeze(2).broadcast_to([128, 8, 64]), self.cst.toks), ALU.mult)
                    k.mm(psKV[r0:r0 + 64, :], kd[:, r0:r0 + 64], vx[:, :])
                KVs, GLx, XS = self.tmp.next(), self.tmp.next(), self.tmp.next()
                kv3 = KVs.t[0:W, :].rearrange("p (e c) -> p e c", c=8)
                kvo = kv3 if d == 0 else kv3[:, :, ::-1]
                k.copy("act", Opd(kvo, KVs.toks), Opd(psKV.t[0:W, :].rearrange("p (c e) -> p e c", c=8), psKV.toks))
                k.stt(Opd(kv3[:, :, 0], KVs.toks), X[0:W, :], gl[0:W, 0:1], Opd(kv3[:, :, 0], KVs.toks), ALU.mult, ALU.add)
                k.copy("pool", Opd(GLx.t[0:W, :].rearrange("p (e c) -> p e c", c=8), GLx.toks),
                       Opd(gl.t[0:W, 8:16].unsqueeze(1).broadcast_to([W, 64, 8]), gl.toks))
                k.scan(XS[0:W, :], GLx[0:W, :], KVs[0:W, :], 0.0, ALU.mult, ALU.add)
                xs3 = XS.t[0:W, :].rearrange("p (e c) -> p e c", c=8)
                Xn = R["X"].next()
                k.copy("dve", Xn[0:W, :], Opd(xs3[:, :, 7], XS.toks))
                orow = o_fn(tile, None)
                psO = self.ps.next()
                for h in range(nh):
                    r0 = 64 * h
                    ro = orow + r0
                    k.mm(psO[ro:ro + 64, 0:128], v[:, r0:r0 + 64], ATs[h][:, :], start=True, stop=False)
                    for i in range(8):
                        c = i if d == 0 else 7 - i
                        xc = X[r0:r0 + 64, :] if i == 0 else Opd(xs3[r0:r0 + 64, :, i - 1], XS.toks)
                        k.mm(psO[ro:ro + 64, 16 * c:16 * c + 16], xc, qdT[r0:r0 + 64, 16 * c:16 * c + 16],
                             start=False, stop=(i == 7))
                o_fn(tile, psO)
                X = Xn
                yield
            if CSTOP >= 4:
                fin_fn(X)

        def x0_sample(d):
            def f(X):
                k.load(X[0:64, :], io["s0_h"][l, d])
            return f

        done_s = set()

        def o_sample(tile, psO):
            row = 0 if tile < 16 else 64
            if psO is None:
                return row
            col = (tile % 16) * 128
            dst = Oacc_s.at(tile)[row:row + 64, col:col + 128]
            if tile in done_s:
                k.tt("dve", dst, psO[row:row + 64, 0:128], dst, ALU.add)
            else:
                done_s.add(tile)
                k.copy("act", dst, psO[row:row + 64, 0:128])

        sj = [job(d, 64, self.sample_src(None, False), 32, ws_q, ws_f[d], ws_i, lbs["hgrn_lb_h"][0], lbs["hgrn_lb_h"][1], 0,
                  x0_sample(d), o_sample, lambda X: None) for d in range(2)]

        done_p = set()

        def mk_prompt(sq, d, pr):
            def x0(X):
                k.memset("dve", X[:, :], 0.0)

            def o_fn(tile, psO):
                if psO is None:
                    return 0
                key = (sq, pr, tile)
                c0 = sq * 256 + tile * 128
                dst = Oacc_p.at((sq * 2 + tile) * 2 + pr)[:, pr, c0:c0 + 128]
                if key in done_p:
                    k.tt("dve", dst, psO[:, 0:128], dst, ALU.add)
                else:
                    done_p.add(key)
                    k.copy("act", dst, psO[:, 0:128])

            def fin(X):
                fs = R["fs"].next()
                k.copy("act", fs[:, :], X[:, :])
                k.dma("sp", io["new_sh"][sq, l, d, pr * 128:(pr + 1) * 128, :], fs.t[:, :], reads=fs.toks)
            def gen():
                wq_, wf_, wi_ = getw(pr)
                yield from job(d, 128, self.src_prompt(sq), 2, wq_, wf_[d], wi_, lbs["hgrn_lb"][0], lbs["hgrn_lb"][1], pr * 128,
                               x0, o_fn, fin)
            return gen()

        self.lockstep(sj)
        for pr in range(2):
            for sq in range(4):
                self.lockstep([mk_prompt(sq, d, pr) for d in range(2)])

        if CSTOP < 5:
            self.zero_o(l, 6, 8, 192, 256)
            return
        k.barrier()
        off = self.c_base
        sqb = Ring([self.carve(off + i * 2048, [128, 512], F32)[0] for i in range(2)]); off += 4096
        rsb = Ring([self.carve(off + i * 2048, [128, 512], F32)[0] for i in range(2)]); off += 4096
        osb = Ring([self.carve(off + i * 1024, [128, 512], BF16)[0] for i in range(2)]); off += 2048

        def post(ov, rows, n, gate_mm, out_fn):
            sq_ = sqb.next()
            k.act(sq_[rows, 0:n], ov, AF.Square)
            psN = self.ps.next()
            k.mm(psN[rows, 0:n], self.C(C_BLK64, rows=rows, cols=rows), sq_[rows, 0:n])
            rs = rsb.next()
            k.act(rs[rows, 0:n], psN[rows, 0:n], AF.Sqrt, bias=1e-6)
            k.recip(rs[rows, 0:n], rs[rows, 0:n])
            k.tt("dve", rs[rows, 0:n], rs[rows, 0:n], ov, ALU.mult)
            psG = self.ps.next()
            gate_mm(psG)
            e = sqb.next()
            k.act(e[rows, 0:n], psG[rows, 0:n], AF.Exp, scale=-1.0)
            k.ts("dve", e[rows, 0:n], e[rows, 0:n], 1.0, ALU.add)
            k.recip(e[rows, 0:n], e[rows, 0:n])
            k.tt("dve", e[rows, 0:n], e[rows, 0:n], psG[rows, 0:n], ALU.mult)
            out_fn(rs, e)

        for pr in range(2):
            wg = self.wtile(io["w_in"][l, :, O_CG + pr * 128:O_CG + (pr + 1) * 128])
            for sq in range(4):
                toks = [Oacc_p.toks[(sq * 2 + t) * 2 + pr] for t in range(2)]
                ov = Opd(Oacc_p.t[:, pr, sq * 256:(sq + 1) * 256], toks)

                def gate_mm(ps, sq=sq, wg=wg):
                    fm = self.src_prompt_fm(sq)
                    for kc in range(NCH):
                        k.mm(ps[:, 0:256], wg[:, kc, :], fm(kc), start=(kc == 0), stop=(kc == NCH - 1))

                def out_fn(rs, e, sq=sq, pr=pr):
                    k.stt(self.oT.at(sq // 2)[:, 6 + pr, sq * 256:(sq + 1) * 256], rs[:, 0:256], hn[:, 0:1], e[:, 0:256],
                          ALU.mult, ALU.mult)
                post(ov, slice(0, 128), 256, gate_mm, out_fn)
        wg = self.wtile(io["w_in_h"][l, :, 900:964])
        for b in range(8):
            sb = self.slab_ring.next()
            self.load_slab(sb, b)
            rows = slice(0, 64) if b < 4 else slice(64, 128)
            c0 = (b % 4) * 512
            toks = [Oacc_s.toks[b * 4 + t] for t in range(4)]
            ov = Opd(Oacc_s.t[rows, c0:c0 + 512], toks)

            def gate_mm(ps, sb=sb, rows=rows, wg=wg):
                for kc in range(NCH):
                    k.mm(ps[rows, 0:512], wg[:, kc, 0:64], sb[:, kc, 1:513], start=(kc == 0), stop=(kc == NCH - 1))

            def out_fn(rs, e, b=b, rows=rows):
                ob = osb.next()
                k.stt(ob[rows, :], rs[rows, :], hn[rows, 0:1], e[rows, :], ALU.mult, ALU.mult)
                k.dma("sp", self.ago_in[1][64:128, b * 512:(b + 1) * 512], ob.t[rows, :], reads=ob.toks, writes=[self.t_ago_in])
            post(ov, rows, 512, gate_mm, out_fn)

    def layer_norm(self, l, which, g, b):
        k = self.k
        xs = self.xs[g].at(b)
        sl = slice(b * 512, (b + 1) * 512)
        ps_m = self.ps.next()
        ps_q = self.ps.next()
        for kc in range(NCH):
            k.mm(ps_m[:, :], self.C(C_MEAN), xs[:, kc, sl], start=(kc == 0), stop=(kc == NCH - 1))
        for kc in range(NCH):
            sq = self.usq.next()
            k.act(sq[:, :], xs[:, kc, sl], AF.Square)
            k.mm(ps_q[:, :], self.C(C_MEAN), sq[:, :], start=(kc == 0), stop=(kc == NCH - 1))
        mean = self.stat.next()
        rstd = self.stat.next()
        k.copy("act", mean[:, :], ps_m[:, :])
        k.tt("dve", rstd[:, :], mean[:, :], mean[:, :], ALU.mult)
        k.tt("dve", rstd[:, :], ps_q[:, :], rstd[:, :], ALU.subtract)
        k.act(rstd[:, :], rstd[:, :], AF.Sqrt, bias=LN_EPS / ALPHA ** 2)
        k.recip(rstd[:, :], rstd[:, :])
        for kc in range(NCH):
            t = self.tmp.next()
            k.tt("dve", t[:, :], xs[:, kc, sl], mean[:, :], ALU.subtract)
            k.tt("dve", t[:, :], t[:, :], rstd[:, :], ALU.mult)
            k.act(xs[:, kc, sl], t[:, :], AF.Identity, scale=self.lng[:, l, which, kc:kc + 1],
                  bias=self.lnb[:, l, which, kc:kc + 1])

    def dense_group(self, l, g):
        k, io = self.k, self.io
        xs = self.xs[g]
        wname = "w_out_p" if g == 0 else "w_out_s"
        for oc in range(NCH):
            w = self.load_w_bf16(io[wname][l, :, oc * 128:(oc + 1) * 128].rearrange("(c p) n -> p c n", p=128), 128, 128)
            for b in range(2):
                sl = slice(b * 512, (b + 1) * 512)
                ps = self.ps.next()
                for kc in range(NCH):
                    k.mm(ps[:, :], w[:, kc, :], self.oT.at(b)[:, kc, sl], start=(kc == 0), stop=(kc == NCH - 1))
                k.stt(xs.at(b)[:, oc, sl], ps[:, :], self.modv(l, 2, g, oc), xs.at(b)[:, oc, sl], ALU.mult, ALU.add)
        for b in range(2):
            self.layer_norm(l, 0, g, b)
        for b in range(2):
            sl = slice(b * 512, (b + 1) * 512)
            for kc in range(NCH):
                k.act(self.xm2.at(b)[:, kc, sl], xs.at(b)[:, kc, sl], AF.Identity,
                      scale=self.modv(l, 4, g, kc), bias=self.modv(l, 3, g, kc))
        for f in range(NF):
            st = self.wst.next()
            wb = self.wbf.next()
            sv = Opd(st.t[:, 0:2048].rearrange("p (c j n) -> p c j n", c=NCH, j=2), st.toks)
            wv = View(wb.t[:, 0:2048].rearrange("p (c j n) -> p c j n", c=NCH, j=2), wb.toks)
            for j in range(2):
                k.load(Opd(sv.ap[:, :, j, :], st.toks),
                       io["w_ffn_in"][l, :, j * D_FF + f * 128:j * D_FF + (f + 1) * 128].rearrange("(c p) n -> p c n", p=128))
            k.copy("pool", wv[:], sv)
            for b in range(2):
                sl = slice(b * 512, (b + 1) * 512)
                ps_g = self.ps.next()
                ps_u = self.ps.next()
                for kc in range(NCH):
                    k.mm(ps_g[:, :], wv[:, kc, 0, :], self.xm2.at(b)[:, kc, sl], start=(kc == 0), stop=(kc == NCH - 1))
                for kc in range(NCH):
                    k.mm(ps_u[:, :], wv[:, kc, 1, :], self.xm2.at(b)[:, kc, sl], start=(kc == 0), stop=(kc == NCH - 1))
                t = self.tmp.next()
                k.act(t[:, :], ps_g[:, :], AF.Silu)
                k.tt("dve", self.hT.at(b)[:, f, sl], t[:, :], ps_u[:, :], ALU.mult)
        for oc in range(NCH):
            wvs = []
            for hf in range(2):
                st = self.wst.next()
                wb = self.wbf.next()
                sv = Opd(st.t[:, 0:11 * 128].rearrange("p (f n) -> p f n", f=11), st.toks)
                wv = View(wb.t[:, 0:11 * 128].rearrange("p (f n) -> p f n", f=11), wb.toks)
                k.load(sv, io["w_ffn_out"][l, hf * 1408:(hf + 1) * 1408, oc * 128:(oc + 1) * 128]
                       .rearrange("(f p) n -> p f n", p=128))
                k.copy("pool", wv[:], sv)
                wvs.append(wv)
            for b in range(2):
                sl = slice(b * 512, (b + 1) * 512)
                ps = self.ps.next()
                for f in range(NF):
                    k.mm(ps[:, :], wvs[f // 11][:, f % 11, :], self.hT.at(b)[:, f, sl], start=(f == 0), stop=(f == NF - 1))
                k.stt(xs.at(b)[:, oc, sl], ps[:, :], self.modv(l, 5, g, oc), xs.at(b)[:, oc, sl], ALU.mult, ALU.add)
        for b in range(2):
            self.layer_norm(l, 1, g, b)


def head_cols(h):
    r = lambda o, n: list(range(o, o + n))
    cols = []
    cols += r(O_AQ + h * 128, 128) + r(O_AK + h * 128, 128) + r(O_AV + h * 128, 128)
    cols += r(O_BQKV + h * 64, 64) + r(O_BQKV + 256 + h * 64, 64) + r(O_BQKV + 512 + h * 64, 64)
    cols += r(O_BG + h * 64, 64)
    cols += [O_BBETA + h, O_BBETA + 4 + h, O_BA + h, O_BA + 4 + h]
    cols += r(O_CQ + h * 64, 64) + r(O_CF + h * 64, 64) + r(O_CF + 256 + h * 64, 64)
    cols += r(O_CI + h * 64, 64) + r(O_CG + h * 64, 64)
    return cols


def w_out_perm():
    rows = []
    for r in range(4):
        rows += list(range(r * 128, (r + 1) * 128))
        rows += list(range(512 + r * 64, 512 + (r + 1) * 64))
        rows += list(range(768 + r * 64, 768 + (r + 1) * 64))
    return rows


def rope_tables():
    t = np.arange(DEC_SEQ)
    inv = (np.float32(10000.0) ** (-np.arange(16, dtype=np.float32) / np.float32(16))).astype(np.float32)
    out = np.zeros((2, 128, DEC_SEQ), np.float32)
    for p in range(128):
        d = p % 64
        pos = (t // 64) if d < 32 else (t % 64)
        ang = pos.astype(np.float32) * inv[d % 16]
        out[0, p] = np.cos(ang)
        out[1, p] = np.sin(ang)
    return out


def prep_inputs(inp, depth=DEPTH):
    f = lambda a: np.ascontiguousarray(np.asarray(a, dtype=np.float32))
    L = depth
    consts = make_consts()
    shared = {
        "w_mod": f(inp["w_mod"][:L]),
        "b_mod": f(np.asarray(inp["b_mod"])[:L].reshape(L, 48, 128).transpose(0, 2, 1)),
        "w_in": f(inp["w_in"][:L]),
        "w_out_p": f(inp["w_out"][:L]),
        "w_out_s": f(np.asarray(inp["w_out"])[:L][:, w_out_perm(), :]),
        "ln_g": f(np.asarray(inp["ln_g"])[:L].reshape(L, 2, NCH, 128).transpose(0, 3, 1, 2)),
        "ln_b": f(np.asarray(inp["ln_b"])[:L].reshape(L, 2, NCH, 128).transpose(0, 3, 1, 2)),
        "w_ffn_in": f(inp["w_ffn_in"][:L]),
        "w_ffn_out": f(inp["w_ffn_out"][:L]),
        "consts": consts,
        "rope_cs": rope_tables(),
        "diff_lambda": f(np.asarray(inp["diff_lambda"])[:L].reshape(-1)),
        "diff_norm": f(np.asarray(inp["diff_norm"])[:L].T),
        "hgrn_lb": f(np.asarray(inp["hgrn_lb"])[:, :L].reshape(-1)),
        "conv_hm": f(np.asarray(inp["conv_w"])[:L].reshape(L, 3, 3, 4, 64).transpose(0, 3, 1, 2, 4).reshape(L, 4, 576)),
        "w_ba_p": f(np.stack([np.asarray(inp["w_in"])[:L][:, :, [O_BBETA + h, O_BBETA + 4 + h, O_BA + h, O_BA + 4 + h]]
                              for h in range(4)], 2)),
        "dpar": f(np.stack([np.asarray(inp["delta_a_log"])[:L], np.asarray(inp["delta_dt_bias"])[:L]], -1)
                  .transpose(0, 2, 1, 3).reshape(-1)),
        "dnorm": f(np.asarray(inp["delta_norm"])[:L].reshape(-1)),
        "hnorm": f(np.tile(np.asarray(inp["hgrn_norm"])[:L], (1, 2))),
    }
    xp = np.asarray(inp["x_prompt"], np.float32)
    xsm = np.asarray(inp["x_sample"], np.float32)
    maps = []
    for c in range(8):
        s, r = c // 4, c % 4
        m = dict(shared)
        m["xT_p"] = f(xp[4 * c:4 * c + 4].reshape(TOK, D).T)
        m["xT_s"] = f(xsm[s, r * TOK:(r + 1) * TOK].T)
        cond = np.stack([np.asarray(inp["c_ctx"], np.float32), np.asarray(inp["c"], np.float32)[s]], -1)
        m["cond"] = f(cond.reshape(NCH, 128, 2).transpose(1, 0, 2))
        m["w_in_h"] = f(np.asarray(inp["w_in"])[:L][:, :, head_cols(r)])
        m["ctx_k"] = f(np.asarray(inp["cache_attn_k"])[s, :L, :, r, :])
        m["ctx_v"] = f(np.asarray(inp["cache_attn_v"])[s, :L, :, r, :])
        m["hgrn_lb_h"] = f(np.asarray(inp["hgrn_lb"])[:, :L, r * 64:(r + 1) * 64].reshape(-1))
        m["s0_h"] = f(np.asarray(inp["state_hgrn"])[s, :L, :, r])
        m["s0_d"] = f(np.asarray(inp["state_delta"])[s, :L, :, r])
        m["conv_h"] = f(np.asarray(inp["conv_w"])[:L].reshape(L, 3, 3, 4, 64)[:, :, :, r, :].reshape(L, 576))
        m["dpar_h"] = f(np.stack([np.asarray(inp["delta_a_log"])[:L, :, r], np.asarray(inp["delta_dt_bias"])[:L, :, r]], -1).reshape(-1))
        maps.append(m)
    return maps


_PROG = {}


def get_prog(depth=DEPTH, mixers=("A", "B", "C"), dbg=None):
    key = (depth, tuple(mixers), tuple(sorted((dbg or {}).items())))
    if key not in _PROG:
        _PROG[key] = Prog(depth, mixers, dbg)
    return _PROG[key]


def run(inp, depth=DEPTH, mixers=("A", "B", "C"), dbg=None, trace=False):
    prog = get_prog(depth, mixers, dbg)
    maps = prep_inputs(inp, depth)
    res = run_bass_kernel_spmd(prog.nc, maps, core_ids=list(range(8)), trace=trace)
    return res


def assemble(res, depth=DEPTH):
    R = res.results
    y_p = np.concatenate([R[c]["yT_p"].T.reshape(4, SEQ, D) for c in range(8)], 0)
    y_s = np.stack([np.concatenate([R[s * 4 + r]["yT_s"].T for r in range(4)], 0) for s in range(2)], 0)
    L = depth
    nk = np.concatenate([R[c]["new_k"] for c in range(8)], 0).reshape(32, L, SEQ, 4, 128)
    nv = np.concatenate([R[c]["new_v"] for c in range(8)], 0).reshape(32, L, SEQ, 4, 128)
    nsh = np.concatenate([R[c]["new_sh"] for c in range(8)], 0).reshape(32, L, 2, 4, 64, 64)
    nsd = np.concatenate([R[c]["new_sd"] for c in range(8)], 0).reshape(32, L, 2, 4, 64, 64)
    return (y_p.astype(np.float32), y_s.astype(np.float32), nk.astype(np.float32), nv.astype(np.float32),
            nsd.astype(np.float32), nsh.astype(np.float32))


def kernel(**inputs):
    res = run(inputs)
    return assemble(res)
```

```python
import math
import os
from contextlib import ExitStack
CSTOP = int(os.environ.get('CSTOP', '99'))
CSUB = os.environ.get('CSUB', '')
BSTOP = int(os.environ.get('BSTOP', '99'))
BY = int(os.environ.get('BY', '63'))

import numpy as np
import concourse.bass as bass
import concourse.mybir as mybir
from concourse.bass_utils import run_bass_kernel_spmd

F32 = mybir.dt.float32
BF16 = mybir.dt.bfloat16
ALU = mybir.AluOpType
AF = mybir.ActivationFunctionType
AX = mybir.AxisListType

D = 1024
NCH = 8
DEPTH = 4
SEQ = 256
DEC_SEQ = 4096
PAST = 256
D_FF = 2816
NF = 22
D_IN = 3856
ALPHA = (2 * DEPTH) ** 0.25
LN_EPS = 1e-5
TOK = 1024
GROUPS = [[0, 1, 2, 3], [4, 5, 6, 7]]

O_AQ, O_AK, O_AV = 0, 512, 1024
O_BQKV, O_BG, O_BBETA, O_BA = 1536, 2304, 2560, 2568
O_CQ, O_CF, O_CI, O_CG = 2576, 2832, 3344, 3600


class Tok:
    __slots__ = ("w", "rs", "excl")

    def __init__(self):
        self.w = None
        self.rs = {}
        self.excl = False


class Opd:
    __slots__ = ("ap", "toks")

    def __init__(self, ap, toks):
        self.ap = ap
        self.toks = toks


class View:
    __slots__ = ("t", "toks")

    def __init__(self, t, toks):
        self.t = t
        self.toks = toks

    def __getitem__(self, idx):
        return Opd(self.t[idx], self.toks)


class Buf:
    def __init__(self, t, ntok=1):
        self.t = t
        self.toks = [Tok() for _ in range(ntok)]

    def at(self, *keys):
        return View(self.t, [self.toks[k] for k in keys])

    def all(self):
        return View(self.t, self.toks)

    def __getitem__(self, idx):
        return Opd(self.t[idx], self.toks)


class Ring:
    def __init__(self, bufs):
        self.bufs = bufs
        self.i = 0

    def next(self):
        b = self.bufs[self.i % len(self.bufs)]
        self.i += 1
        return b


class KB:
    ENGS = ("pe", "dve", "act", "pool", "sp")

    def __init__(self):
        self.nc = bass.Bass("TRN2", target_bir_lowering=False)
        self.es = ExitStack()
        self.streams = {e: [] for e in self.ENGS}
        self.count = {e: 0 for e in self.ENGS}
        self.seen = {e: {} for e in self.ENGS}
        self.latest = {}
        self.sems = {}
        for e in self.ENGS:
            self.sems[e] = self.es.enter_context(self.nc.semaphore("c_" + e))
        self.ndsem = {"sp": 24, "pool": 12, "act": 8}
        self.dsem_i = {q: 0 for q in self.ndsem}
        self.dsem_v = {}
        for q, n in self.ndsem.items():
            for j in range(n):
                k = "d_%s_%d" % (q, j)
                self.sems[k] = self.es.enter_context(self.nc.semaphore(k))
                self.dsem_v[k] = 0
        self.nalloc = 0
        self.pending_barrier = {e: None for e in self.ENGS}

    def sbuf(self, name, shape, dt, ntok=1):
        t = self.es.enter_context(self.nc.sbuf_tensor(name, list(shape), dt))
        return Buf(t, ntok)

    def psum(self, name, shape, dt=F32):
        t = self.es.enter_context(self.nc.psum_tensor(name, list(shape), dt))
        b = Buf(t, 1)
        b.toks[0].excl = True
        return b

    def ring(self, name, shape, dt, n):
        return Ring([self.sbuf("%s%d" % (name, i), shape, dt) for i in range(n)])

    def dram(self, name, shape, dt, kind="Internal"):
        return self.nc.dram_tensor(name, list(shape), dt, kind=kind)

    def _deps(self, eng, reads, writes):
        need = {}

        def add(ref):
            if ref is None:
                return
            k, v = ref
            if need.get(k, 0) < v:
                need[k] = v

        for t in reads:
            add(t.w)
            if t.excl:
                for k2, v in t.rs.items():
                    if k2 != eng:
                        add((k2, v))
        for t in writes:
            add(t.w)
            for k, v in t.rs.items():
                add((k, v))
        pb = self.pending_barrier[eng]
        if pb is not None:
            for k, v in pb.items():
                add((k, v))
            self.pending_barrier[eng] = None
        if eng == "pe":
            need.pop("pe", None)
        seen = self.seen[eng]
        waits = []
        for k, v in need.items():
            if seen.get(k, 0) < v:
                seen[k] = v
                waits.append((k, v))
        return waits

    def op(self, eng, fn, reads=(), writes=()):
        waits = self._deps(eng, reads, writes)
        self.count[eng] += 1
        idx = self.count[eng]
        ref = (eng, idx)
        for t in reads:
            t.rs[eng] = idx
        for t in writes:
            t.w = ref
            t.rs = {}
        self.latest[eng] = idx
        self.streams[eng].append((waits, fn, (eng, 1)))

    def dma(self, q, out_ap, in_ap, reads=(), writes=(), fn=None, selfinc=False):
        waits = self._deps(q, reads, writes)
        j = self.dsem_i[q] % self.ndsem[q]
        self.dsem_i[q] += 1
        k = "d_%s_%d" % (q, j)
        prev = self.dsem_v[k]
        if prev > 0 and self.seen[q].get(k, 0) < prev:
            self.seen[q][k] = prev
            waits.append((k, prev))
        self.dsem_v[k] = prev + 16
        ref = (k, prev + 16)
        for t in reads:
            t.rs[k] = prev + 16
        for t in writes:
            t.w = ref
            t.rs = {}
        self.latest[k] = prev + 16
        if fn is None:
            fn = lambda e, o=out_ap, i=in_ap: e.dma_start(out=o, in_=i)
        self.streams[q].append((waits, fn, ("SELF", k) if selfinc else (k, 16)))

    def cc(self, fn, reads=(), writes=()):
        waits = self._deps("pool", reads, writes)
        if "cc" not in self.sems:
            self.sems["cc"] = self.es.enter_context(self.nc.semaphore("cc"))
            self.ccv = 0
        prev = self.ccv
        if prev > 0 and self.seen["pool"].get("cc", 0) < prev:
            self.seen["pool"]["cc"] = prev
            waits.append(("cc", prev))
        self.ccv = prev + 1
        ref = ("cc", prev + 1)
        for t in reads:
            t.rs["cc"] = prev + 1
        for t in writes:
            t.w = ref
            t.rs = {}
        self.latest["cc"] = prev + 1
        self.streams["pool"].append((waits, fn, ("cc", 1)))

    def barrier(self):
        snap = dict(self.latest)
        for e in self.ENGS:
            pb = self.pending_barrier[e]
            if pb is None:
                self.pending_barrier[e] = dict(snap)
            else:
                for k, v in snap.items():
                    if pb.get(k, 0) < v:
                        pb[k] = v

    def raw(self, eng, fn):
        self.streams[eng].append(([], fn, None))

    def finish(self):
        nc = self.nc
        final = dict(self.latest)
        engmap = {"pe": "tensor", "dve": "vector", "act": "scalar", "pool": "gpsimd", "sp": "sync"}
        with nc.Block() as block:
            for e in self.ENGS:
                stream = self.streams[e]

                def body(h, e=e, stream=stream):
                    sems = self.sems
                    for waits, fn, inc in stream:
                        for k, v in waits:
                            h.wait_ge(sems[k], v)
                        if inc is not None and inc[0] == "SELF":
                            fn(h, sems[inc[1]])
                            continue
                        ins = fn(h)
                        if inc is not None:
                            ins.then_inc(sems[inc[0]], inc[1])
                    for k, v in final.items():
                        if k == e:
                            continue
                        h.wait_ge(sems[k], v)

                getattr(block, engmap[e])(body)
        self.es.close()
        return nc

    @staticmethod
    def _tk(*ops):
        out = []
        for o in ops:
            if isinstance(o, Opd):
                out.extend(o.toks)
        return out

    @staticmethod
    def _ap(o):
        return o.ap if isinstance(o, Opd) else o

    def mm(self, out, lhsT, rhs, start=True, stop=True):
        o, l, r = out.ap, lhsT.ap, rhs.ap
        self.op("pe", lambda e: e.matmul(o, l, r, start=start, stop=stop),
                reads=self._tk(lhsT, rhs), writes=self._tk(out))

    def act(self, out, in_, func, bias=0.0, scale=1.0, accum=None, eng="act"):
        o, i, b, s = out.ap, in_.ap, self._ap(bias), self._ap(scale)
        kw = {}
        if accum is not None:
            kw["accum_out"] = accum.ap
        self.op("act", lambda e: e.activation(o, i, func, bias=b, scale=s, **kw),
                reads=self._tk(in_, bias, scale), writes=self._tk(out, accum))

    def tt(self, eng, out, in0, in1, op):
        o, a, b = out.ap, in0.ap, in1.ap
        self.op(eng, lambda e: e.tensor_tensor(o, a, b, op), reads=self._tk(in0, in1), writes=self._tk(out))

    def ts(self, eng, out, in0, s1, op0, s2=None, op1=None, accum=None):
        o, a, x1, x2 = out.ap, in0.ap, self._ap(s1), self._ap(s2)
        kw = {}
        if accum is not None:
            kw["accum_out"] = accum.ap
        if op1 is None:
            fn = lambda e: e.tensor_scalar(o, a, x1, None, op0, **kw)
        else:
            fn = lambda e: e.tensor_scalar(o, a, x1, x2, op0, op1, **kw)
        self.op(eng, fn, reads=self._tk(in0, s1, s2), writes=self._tk(out, accum))

    def stt(self, out, in0, scalar, in1, op0, op1, eng="dve"):
        o, a, s, b = out.ap, in0.ap, self._ap(scalar), in1.ap
        self.op(eng, lambda e: e.scalar_tensor_tensor(o, a, s, b, op0, op1),
                reads=self._tk(in0, scalar, in1), writes=self._tk(out))

    def copy(self, eng, out, in_):
        o, i = out.ap, in_.ap
        if eng == "act":
            self.op("act", lambda e: e.copy(o, i), reads=self._tk(in_), writes=self._tk(out))
        else:
            self.op(eng, lambda e: e.tensor_copy(o, i), reads=self._tk(in_), writes=self._tk(out))

    def memset(self, eng, out, val):
        o = out.ap
        self.op(eng, lambda e: e.memset(o, val), writes=self._tk(out))

    def recip(self, out, in_):
        o, i = out.ap, in_.ap
        self.op("dve", lambda e: e.reciprocal(o, i), reads=self._tk(in_), writes=self._tk(out))

    def reduce(self, out, in_, op=ALU.add, axis=AX.X):
        o, i = out.ap, in_.ap
        self.op("dve", lambda e: e.tensor_reduce(o, i, axis, op), reads=self._tk(in_), writes=self._tk(out))

    def scan(self, out, d0, d1, initial, op0, op1):
        o, a, b, ini = out.ap, d0.ap, d1.ap, self._ap(initial)
        self.op("dve", lambda e: e.tensor_tensor_scan(o, a, b, ini, op0, op1),
                reads=self._tk(d0, d1, initial), writes=self._tk(out))

    def load(self, out, in_ap, q="sp"):
        self.dma(q, out.ap, in_ap, writes=self._tk(out))

    def store(self, out_ap, in_, q="pool", dram_tok=None):
        w = [dram_tok] if dram_tok is not None else []
        self.dma(q, out_ap, in_.ap, reads=self._tk(in_), writes=w)


(C_ID, C_MEAN, C_ONE, C_M128, C_BLK64, C_TRID_F, C_TRID_B, C_REMD_F, C_REMD_B, C_STR_F, C_STR_B,
 C_INC_F, C_INC_B, C_TRIC_F, C_TRIC_B, C_REMC_F, C_REMC_B, C_ROPE, C_CI16, C_CI64) = range(20)
NCONST = 20


def make_consts():
    c = np.zeros((128, NCONST, 128), np.float32)
    i = np.arange(128)
    P, Q = np.meshgrid(i, i, indexing="ij")
    c[:, C_ID] = (P == Q)
    c[:, C_MEAN] = 1.0 / D
    c[:, C_ONE] = 1.0
    c[:, C_M128] = 1.0 / 128
    c[:, C_BLK64] = (P // 64 == Q // 64) / 64.0
    s64 = (P // 64 == Q // 64)
    s16 = (P // 16 == Q // 16)
    c[:, C_TRID_F] = s64 & (P <= Q)
    c[:, C_TRID_B] = s64 & (P >= Q)
    c[:, C_REMD_F] = s64 & (P > Q)
    c[:, C_REMD_B] = s64 & (P < Q)
    c[:, C_STR_F] = s64 & (Q < P)
    c[:, C_STR_B] = s64 & (Q > P)
    c[:, C_INC_F] = s64 & (Q <= P)
    c[:, C_INC_B] = s64 & (Q >= P)
    c[:, C_TRIC_F] = s16 & (P <= Q)
    c[:, C_TRIC_B] = s16 & (P >= Q)
    c[:, C_REMC_F] = s16 & (P > Q)
    c[:, C_REMC_B] = s16 & (P < Q)
    R = np.zeros((128, 128), np.float32)
    for m in range(128):
        if m % 32 < 16:
            R[m, m + 16] = -1.0
        else:
            R[m, m - 16] = 1.0
    c[:, C_ROPE] = R.T
    c[:, C_CI16, 0:8] = (P[:, 0:8] // 16 == Q[:, 0:8])
    c[:, C_CI64, 0:2] = (P[:, 0:2] // 64 == Q[:, 0:2])
    return c


ARENA_BYTES = 91 * 1024


class Prog:
    def __init__(self, depth=DEPTH, mixers=("A", "B", "C"), dbg=None):
        self.depth = depth
        self.mixers = mixers
        self.dbg = dbg or {}
        self.k = KB()
        self.build()
        self.nc = self.k.finish()

    def declare_io(self):
        nc = self.k.nc
        L = self.depth

        def inp(name, shape, dt=F32):
            return nc.dram_tensor(name, list(shape), dt, kind="ExternalInput").ap()

        def outp(name, shape, dt=F32):
            return nc.dram_tensor(name, list(shape), dt, kind="ExternalOutput").ap()

        io = {}
        io["xT_p"] = inp("xT_p", [D, TOK])
        io["xT_s"] = inp("xT_s", [D, TOK])
        io["cond"] = inp("cond", [128, NCH, 2])
        io["w_mod"] = inp("w_mod", [L, D, 6 * D])
        io["b_mod"] = inp("b_mod", [L, 128, 48])
        io["w_in"] = inp("w_in", [L, D, D_IN])
        io["w_out_p"] = inp("w_out_p", [L, D, D])
        io["w_out_s"] = inp("w_out_s", [L, D, D])
        io["ln_g"] = inp("ln_g", [L, 128, 2, NCH])
        io["ln_b"] = inp("ln_b", [L, 128, 2, NCH])
        io["w_ffn_in"] = inp("w_ffn_in", [L, D, 2 * D_FF])
        io["w_ffn_out"] = inp("w_ffn_out", [L, D_FF, D])
        io["consts"] = inp("consts", [128, NCONST, 128])
        io["w_in_h"] = inp("w_in_h", [L, D, 964])
        io["rope_cs"] = inp("rope_cs", [2, 128, DEC_SEQ])
        io["diff_lambda"] = inp("diff_lambda", [L * 256])
        io["diff_norm"] = inp("diff_norm", [128, L])
        io["ctx_k"] = inp("ctx_k", [L, PAST, 128])
        io["ctx_v"] = inp("ctx_v", [L, PAST, 128])
        io["conv_hm"] = inp("conv_hm", [L, 4, 3 * 192])
        io["conv_h"] = inp("conv_h", [L, 3 * 192])
        io["w_ba_p"] = inp("w_ba_p", [L, D, 4, 4])
        io["dpar"] = inp("dpar", [L * 16])
        io["dpar_h"] = inp("dpar_h", [L * 4])
        io["dnorm"] = inp("dnorm", [L * 64])
        io["s0_d"] = inp("s0_d", [L, 2, 64, 64])
        io["new_sd"] = outp("new_sd", [4, L, 2, 4, 64, 64])
        io["hgrn_lb"] = inp("hgrn_lb", [2 * L * 256])
        io["hgrn_lb_h"] = inp("hgrn_lb_h", [2 * L * 64])
        io["hnorm"] = inp("hnorm", [L, 128])
        io["s0_h"] = inp("s0_h", [L, 2, 64, 64])
        io["new_sh"] = outp("new_sh", [4, L, 2, 4 * 64, 64])
        io["new_k"] = outp("new_k", [4, L, SEQ, 512])
        io["new_v"] = outp("new_v", [4, L, SEQ, 512])
        self.agx_in = [nc.dram_tensor("agx_in%d" % i, [512, TOK], BF16).ap() for i in range(2)]
        self.agx = [nc.dram_tensor("agx%d" % i, [4 * 512, TOK], BF16).ap() for i in range(2)]
        self.ago_in = [nc.dram_tensor("ago_in%d" % i, [128, DEC_SEQ], BF16).ap() for i in range(2)]
        self.ago = [nc.dram_tensor("ago%d" % i, [4 * 128, DEC_SEQ], BF16).ap() for i in range(2)]
        self.t_agx_in, self.t_agx, self.t_ago_in, self.t_ago = Tok(), Tok(), Tok(), Tok()
        io["yT_p"] = outp("yT_p", [D, TOK])
        io["yT_s"] = outp("yT_s", [D, TOK])
        for name, shape in self.dbg.items():
            io[name] = outp(name, shape)
        self.io = io

    def carve(self, off, shape, dt, ntok=1):
        n = 1
        for s in shape[1:]:
            n *= s
        nbytes = n * (2 if dt == BF16 else 4)
        assert off % 4 == 0 and off + nbytes <= ARENA_BYTES, (off, nbytes)
        ap = self.arena_t[:, off // 4:(off + nbytes + 3) // 4]
        if dt == BF16:
            ap = ap.bitcast(BF16)
        if len(shape) == 3:
            ap = ap.rearrange("p (a n) -> p a n", a=shape[1])
        elif len(shape) == 4:
            ap = ap.rearrange("p (a b n) -> p a b n", a=shape[1], b=shape[2])
        return Buf(ap, ntok), off + ((nbytes + 3) // 4) * 4

    def build(self):
        k = self.k
        self.declare_io()
        io = self.io
        L = self.depth
        self.xs = [k.sbuf("xs_p", [128, NCH, TOK], F32, ntok=2), k.sbuf("xs_s", [128, NCH, TOK], F32, ntok=2)]
        self.cst = k.sbuf("cst_sb", [128, NCONST, 128], F32)
        self.oT = k.sbuf("oT", [128, NCH, TOK], BF16, ntok=2)
        self.mod = k.sbuf("mod", [128, L, 48, 2], F32)
        self.lng = k.sbuf("lng", [128, L, 2, NCH], F32)
        self.lnb = k.sbuf("lnb", [128, L, 2, NCH], F32)
        self.ps = Ring([k.psum("ps%d" % i, [128, 512]) for i in range(4)])
        self.acc = [k.psum("acc%d" % i, [128, 512]) for i in range(4)]
        self.wst = k.ring("wst", [128, 2048], F32, 1)
        self.wbf = k.ring("wbf", [128, 2048], BF16, 2)
        self.tmp = k.ring("tmp", [128, 512], F32, 4)
        self.arena_t = self.k.es.enter_context(k.nc.sbuf_tensor("arena", [128, ARENA_BYTES // 4], F32))

        k.load(self.cst[:], io["consts"])
        for g, nm in enumerate(("xT_p", "xT_s")):
            for b in range(2):
                k.load(self.xs[g].at(b)[:, :, b * 512:(b + 1) * 512],
                       io[nm][:, b * 512:(b + 1) * 512].rearrange("(c p) n -> p c n", p=128))
        k.load(self.lng[:], io["ln_g"].rearrange("l p a c -> p l a c"))
        k.load(self.lnb[:], io["ln_b"].rearrange("l p a c -> p l a c"))
        self.preamble_mod()
        self.preamble_small()
        for l in range(L):
            self.layer(l)
        for g, nm in enumerate(("yT_p", "yT_s")):
            for b in range(2):
                k.store(io[nm][:, b * 512:(b + 1) * 512].rearrange("(c p) n -> p c n", p=128),
                        self.xs[g].at(b)[:, :, b * 512:(b + 1) * 512], q="sp")

    def C(self, i, rows=slice(None), cols=slice(None)):
        return self.cst[rows, i, cols]

    def preamble_mod(self):
        k, io = self.k, self.io
        L = self.depth
        k.barrier()
        off = 0
        wblk = []
        for i in range(2):
            b, off = self.carve(off, [128, NCH, 512], F32)
            wblk.append(b)
        cond, off = self.carve(off, [128, NCH, 2], F32)
        csil, off = self.carve(off, [128, NCH, 2], F32)
        bmod, off = self.carve(off, [128, L, 48], F32)
        k.load(cond[:], io["cond"])
        k.load(bmod[:], io["b_mod"].rearrange("l p m -> p l m"))
        k.act(csil[:], cond[:], AF.Silu)
        n = 0
        for l in range(L):
            for cb in range(12):
                w = wblk[n % 2]
                n += 1
                k.load(w[:], io["w_mod"][l, :, cb * 512:(cb + 1) * 512].rearrange("(c p) n -> p c n", p=128))
                ps = self.ps.next()
                for mi in range(4):
                    m = cb * 4 + mi
                    for kc in range(NCH):
                        k.mm(ps[:, 2 * mi:2 * mi + 2], w[:, kc, mi * 128:(mi + 1) * 128], csil[:, kc, :],
                             start=(kc == 0), stop=(kc == NCH - 1))
                k.tt("dve", self.mod[:, l, cb * 4:(cb + 1) * 4, :],
                     Opd(ps.t[:, 0:8].rearrange("p (m j) -> p m j", j=2), ps.toks),
                     Opd(bmod.t[:, l, cb * 4:(cb + 1) * 4].unsqueeze(2).broadcast_to([128, 4, 2]), bmod.toks),
                     ALU.add)
            for a in (8, 32):
                k.ts("dve", self.mod[:, l, a:a + 8, :], self.mod[:, l, a:a + 8, :], 1.0, ALU.add)
            for a in (16, 40):
                k.ts("dve", self.mod[:, l, a:a + 8, :], self.mod[:, l, a:a + 8, :], 1.0 / ALPHA, ALU.mult)
        k.barrier()

    def modv(self, l, which, g, kc):
        return self.mod[:, l, which * 8 + kc, g:g + 1]

    def load_w_bf16(self, dram_ap, rows, cols):
        st = self.wst.next()
        wb = self.wbf.next()
        k = self.k
        if len(dram_ap.shape) == 3:
            a, n = dram_ap.shape[1], dram_ap.shape[2]
            sv = Opd(st.t[:, 0:a * n].rearrange("p (a n) -> p a n", a=a), st.toks)
            wv = View(wb.t[:, 0:a * n].rearrange("p (a n) -> p a n", a=a), wb.toks)
            k.load(sv, dram_ap)
            k.copy("pool", wv[:], sv)
            return wv
        n = dram_ap.shape[1]
        r = dram_ap.shape[0]
        k.load(st[0:r, 0:n], dram_ap)
        k.copy("pool", wb[0:r, 0:n], st[0:r, 0:n])
        return View(wb.t, wb.toks)

    def layer(self, l):
        k = self.k
        self.mixer_phase(l)
        k.barrier()
        off = 0
        self.hT, off = self.carve(off, [128, NF, TOK], BF16, ntok=2)
        self.xm2, off = self.carve(off, [128, NCH, TOK], BF16, ntok=2)
        self.stat = Ring([self.carve(off + i * 2048, [128, 512], F32)[0] for i in range(6)])
        off += 6 * 2048
        self.usq = Ring([self.carve(off + i * 2048, [128, 512], F32)[0] for i in range(3)])
        off += 3 * 2048
        wst0 = self.wst.bufs[0]
        extra, off = self.carve(off, [128, 2048], F32)
        self.wst = Ring([wst0, extra])
        for g in range(2):
            if g == 1:
                self.load_oT_sample()
            self.dense_group(l, g)
        self.wst = Ring([wst0])
        k.barrier()

    def preamble_small(self):
        k, io = self.k, self.io
        L = self.depth
        self.lam = k.sbuf("lam", [128, L], F32)
        self.nlam = k.sbuf("nlam", [128, L], F32)
        self.gA = k.sbuf("gA", [128, L], F32)
        self.onesb = k.sbuf("onesb", [128, 128], BF16)
        k.memset("pool", self.onesb[:, :], 1.0)
        off = 0
        dlb, off = self.carve(off, [128, L, 4, 64], F32)
        pr, off = self.carve(off, [128, L, 2, 64], F32)
        sm, off = self.carve(off, [128, L, 2], F32)
        k.load(dlb[:], io["diff_lambda"].partition_broadcast(128).rearrange("p (l a d) -> p l a d", l=L, a=4))
        k.load(self.gA[:, :], io["diff_norm"])
        for j in range(2):
            k.tt("dve", pr[:, :, j, :], dlb[:, :, 2 * j, :], dlb[:, :, 2 * j + 1, :], ALU.mult)
        k.reduce(sm[:], pr[:])
        k.act(sm[:], sm[:], AF.Exp)
        k.tt("dve", self.lam[:, :], sm[:, :, 0], sm[:, :, 1], ALU.subtract)
        for l in range(L):
            lam_init = 0.8 - 0.6 * math.exp(-0.3 * l)
            k.ts("dve", self.lam[:, l:l + 1], self.lam[:, l:l + 1], lam_init, ALU.add)
            k.ts("dve", self.gA[:, l:l + 1], self.gA[:, l:l + 1], 1.0 - lam_init, ALU.mult)
        k.ts("dve", self.nlam[:, :], self.lam[:, :], -1.0, ALU.mult)
        k.barrier()

    def mixer_phase(self, l):
        k, io = self.k, self.io
        k.barrier()
        off = 0
        self.xmp, off = self.carve(off, [128, NCH, 4, 258], BF16, ntok=4)
        self.slab = []
        for i in range(2):
            b, off = self.carve(off, [128, NCH, 514], BF16)
            self.slab.append(b)
        self.moff = off
        k.memset("pool", self.xmp.all()[:, :, :, 0:1], 0.0)
        k.memset("pool", self.xmp.all()[:, :, :, 257:258], 0.0)
        for sq in range(4):
            for kc in range(NCH):
                k.act(self.xmp.at(sq)[:, kc, sq, 1:257], self.xs[0].at(sq // 2)[:, kc, sq * 256:(sq + 1) * 256],
                      AF.Identity, scale=self.modv(l, 1, 0, kc), bias=self.modv(l, 0, 0, kc))
        for b in range(2):
            for kc in range(NCH):
                k.ts("dve", self.slab[b][:, kc, 0:512], self.xs[1].at(b)[:, kc, b * 512:(b + 1) * 512],
                     self.modv(l, 1, 1, kc), ALU.mult, self.modv(l, 0, 1, kc), ALU.add)
            for hf in range(2):
                k.dma("sp", self.agx_in[hf][:, b * 512:(b + 1) * 512].rearrange("(c p) n -> p c n", p=128),
                      self.slab[b].t[:, 4 * hf:4 * hf + 4, 0:512], reads=self.slab[b].toks, writes=[self.t_agx_in])
        for hf in range(2):
            ain, aout = self.agx_in[hf], self.agx[hf]
            k.cc(lambda e, ain=ain, aout=aout: e.collective_compute("AllGather", ALU.bypass, replica_groups=GROUPS,
                                                                    ins=[ain], outs=[aout]),
                 reads=[self.t_agx_in], writes=[self.t_agx])
        if "A" in self.mixers:
            self.mixA_sample(l)
            k.barrier()
            self.mixA_prompt(l)
            k.barrier()
        else:
            self.zero_o(l, 0, 4, 0, 128)
        if "B" in self.mixers:
            self.mixB(l)
            k.barrier()
        else:
            self.zero_o(l, 4, 6, 128, 192)
        if "C" in self.mixers:
            self.mixC(l)
            k.barrier()
        else:
            self.zero_o(l, 6, 8, 192, 256)
        for hf in range(2):
            gin, gout = self.ago_in[hf], self.ago[hf]
            k.cc(lambda e, gin=gin, gout=gout: e.collective_compute("AllGather", ALU.bypass, replica_groups=GROUPS,
                                                                    ins=[gin], outs=[gout]),
                 reads=[self.t_ago_in], writes=[self.t_ago])

    def zero_o(self, l, c0, c1, r0, r1):
        k = self.k
        z = self.tmp.next()
        k.memset("pool", z[:, :], 0.0)
        zb = Opd(z.t[:, 0:256].bitcast(BF16), z.toks)
        for blk in range(8):
            k.dma("sp", self.ago_in[r0 // 128][r0 % 128:r0 % 128 + (r1 - r0), blk * 512:(blk + 1) * 512], zb.ap[0:r1 - r0, :],
                  reads=z.toks, writes=[self.t_ago_in])
        for b in range(2):
            k.memset("pool", self.oT.at(b)[:, c0:c1, b * 512:(b + 1) * 512], 0.0)

    def load_oT_sample(self):
        k = self.k
        ago = self.ago
        oT = self.oT

        def fn(e, sem, HF):
            core = e.partition_id()
            for c in range(8):
                r = c % 4
                with e.If(core == c):
                    e.dma_start(out=oT.t[:, HF::2, :], in_=ago[HF][:, r * TOK:(r + 1) * TOK].rearrange("(r p) n -> p r n", p=128)).then_inc(sem, 16)
        for HF in range(2):
            k.dma("pool", None, None, reads=[self.t_ago], writes=oT.toks, fn=(lambda e, sem, HF=HF: fn(e, sem, HF)), selfinc=True)

    def load_slab(self, buf, b, halo=False):
        k = self.k
        r, hf = b // 2, b % 2
        agx3 = [a.rearrange("(r f) n -> r f n", r=4) for a in self.agx]
        for fh in range(2):
            k.dma("sp", buf.t[:, 4 * fh:4 * fh + 4, 1:513], agx3[fh][r, :, hf * 512:(hf + 1) * 512].rearrange("(c p) n -> p c n", p=128),
                  reads=[self.t_agx], writes=buf.toks)
        if halo:
            for side, g0, col in ((0, b * 512 - 1, 0), (1, (b + 1) * 512, 513)):
                if g0 < 0 or g0 >= DEC_SEQ:
                    k.memset("pool", buf[:, :, col:col + 1], 0.0)
                else:
                    rr, cc = g0 // TOK, g0 % TOK
                    for fh in range(2):
                        o_ = buf.t[:, 4 * fh:4 * fh + 4, col:col + 1]
                        i_ = agx3[fh][rr, :, cc:cc + 1].rearrange("(c p) n -> p c n", p=128)
                        k.dma("sp", None, None, reads=[self.t_agx], writes=buf.toks,
                              fn=lambda e, o_=o_, i_=i_: e.dma_start(out=o_, in_=i_, allow_slow_non_contiguous=True))

    def wtile(self, dram_ap):
        return self.load_w_bf16(dram_ap.rearrange("(c p) n -> p c n", p=128), 128, dram_ap.shape[1])

    def attn_core(self, l, qT, kT, V, nkt, q0, nq, out_fn, Pt, o0, o1, rr):
        k = self.k
        om = [o0, o1]
        for m in range(2):
            rows = slice(64 * m, 64 * m + 64)
            psO, psR = self.acc[2 * m], self.acc[2 * m + 1]
            psS_next = None
            for kt in range(nkt):
                if kt == 0:
                    psS = self.ps.next()
                    k.mm(psS[:, 0:nq], kT[rows, 0:128], qT[rows, q0:q0 + nq])
                else:
                    psS = psS_next
                if kt + 1 < nkt:
                    psS_next = self.ps.next()
                    k.mm(psS_next[:, 0:nq], kT[rows, (kt + 1) * 128:(kt + 2) * 128], qT[rows, q0:q0 + nq])
                P = Pt.next()
                k.act(P[:, 0:nq], psS[:, 0:nq], AF.Exp, scale=0.125)
                k.mm(psO[:, 0:nq], V[:, kt, :], P[:, 0:nq], start=(kt == 0), stop=(kt == nkt - 1))
                k.mm(psR[:, 0:nq], self.onesb[:, :], P[:, 0:nq], start=(kt == 0), stop=(kt == nkt - 1))
            k.recip(rr[:, 0:nq], psR[:, 0:nq])
            k.tt("dve", om[m][:, 0:nq], psO[:, 0:nq], rr[:, 0:nq], ALU.mult)
        k.stt(o0[:, 0:nq], o1[:, 0:nq], self.nlam[:, l:l + 1], o0[:, 0:nq], ALU.mult, ALU.add)
        k.act(o1[:, 0:nq], o0[:, 0:nq], AF.Square)
        psN = self.ps.next()
        k.mm(psN[:, 0:nq], self.C(C_M128), o1[:, 0:nq])
        k.act(rr[:, 0:nq], psN[:, 0:nq], AF.Sqrt, bias=1e-6)
        k.recip(rr[:, 0:nq], rr[:, 0:nq])
        k.tt("dve", o0[:, 0:nq], o0[:, 0:nq], rr[:, 0:nq], ALU.mult)
        out_fn(o0)

    def mixA_sample(self, l):
        k, io = self.k, self.io
        off = self.moff
        qT, off = self.carve(off, [128, DEC_SEQ], BF16)
        kT, off = self.carve(off, [128, DEC_SEQ + PAST], BF16)
        V, off = self.carve(off, [128, 34, 128], BF16)
        Pt = Ring([self.carve(off + i * 1024, [128, 512], BF16)[0] for i in range(3)]); off += 3 * 1024
        xf = Ring([self.carve(off + i * 2048, [128, 512], F32)[0] for i in range(3)]); off += 3 * 2048
        o0, off = self.carve(off, [128, 512], F32)
        o1, off = self.carve(off, [128, 512], F32)
        rr, off = self.carve(off, [128, 512], F32)
        osb = Ring([self.carve(off + i * 1024, [128, 512], BF16)[0] for i in range(2)]); off += 2 * 1024
        cst = Ring([self.carve(off + i * 1024, [128, 2, 128], F32)[0] for i in range(2)]); off += 2 * 1024
        wqk = self.wtile(io["w_in_h"][l, :, 0:256])
        wv = self.wtile(io["w_in_h"][l, :, 256:384])
        for j in range(2):
            c = cst.next()
            k.load(c[:, 0, :], io["ctx_k"][l, j * 128:(j + 1) * 128, :])
            k.load(c[:, 1, :], io["ctx_v"][l, j * 128:(j + 1) * 128, :])
            ps = self.ps.next()
            k.mm(ps[:, 0:128], c[:, 0, :], self.C(C_ID))
            k.copy("act", kT[:, DEC_SEQ + j * 128:DEC_SEQ + (j + 1) * 128], ps[:, 0:128])
            k.copy("dve", V[:, 32 + j, :], c[:, 1, :])
        for b in range(8):
            sb = self.slab[b % 2]
            self.load_slab(sb, b)
            cos = self.tmp.next()
            sin = self.tmp.next()
            k.load(cos[:, :], io["rope_cs"][0, :, b * 512:(b + 1) * 512])
            k.load(sin[:, :], io["rope_cs"][1, :, b * 512:(b + 1) * 512])
            for which, dst in ((0, qT), (1, kT)):
                ps = self.ps.next()
                for kc in range(NCH):
                    k.mm(ps[:, :], wqk[:, kc, which * 128:(which + 1) * 128], sb[:, kc, 1:513], start=(kc == 0), stop=(kc == NCH - 1))
                x = xf.next()
                k.copy("act", x[:, :], ps[:, :])
                ps2 = self.ps.next()
                k.mm(ps2[:, :], self.C(C_ROPE), x[:, :])
                t1 = xf.next()
                k.tt("dve", t1[:, :], x[:, :], cos[:, :], ALU.mult)
                k.tt("dve", x[:, :], ps2[:, :], sin[:, :], ALU.mult)
                k.tt("dve", dst[:, b * 512:(b + 1) * 512], t1[:, :], x[:, :], ALU.add)
            for j in range(4):
                ps = self.ps.next()
                for kc in range(NCH):
                    k.mm(ps[:, 0:128], sb[:, kc, 1 + j * 128:1 + (j + 1) * 128], wv[:, kc, :], start=(kc == 0), stop=(kc == NCH - 1))
                k.copy("act", V[:, b * 4 + j, :], ps[:, 0:128])
        qv, kv, vv = View(qT.t, qT.toks), View(kT.t, kT.toks), View(V.t, V.toks)
        for b in range(8):
            def out_fn(o, b=b):
                ob = osb.next()
                k.ts("dve", ob[:, :], o[:, :], self.gA[:, l:l + 1], ALU.mult)
                k.dma("sp", self.ago_in[0][0:128, b * 512:(b + 1) * 512], ob.t[:, :], reads=ob.toks, writes=[self.t_ago_in])
            self.attn_core(l, qv, kv, vv, 34, b * 512, 512, out_fn, Pt, o0, o1, rr)

    def mixA_prompt(self, l):
        k, io = self.k, self.io
        off = self.moff
        Vp, off = self.carve(off, [128, 8, 512], BF16)
        qk = Ring([self.carve(off + i * 1024, [128, 2, 256], BF16)[0] for i in range(2)]); off += 2 * 1024
        Pt = Ring([self.carve(off + i * 1024, [128, 512], BF16)[0] for i in range(3)]); off += 3 * 1024
        o0, off = self.carve(off, [128, 512], F32)
        o1, off = self.carve(off, [128, 512], F32)
        rr, off = self.carve(off, [128, 512], F32)
        stg = Ring([self.carve(off + i * 1024, [128, 256], F32)[0] for i in range(3)]); off += 3 * 1024
        for cc in range(4):
            w = self.wtile(io["w_in"][l, :, O_AK + cc * 256:O_AK + (cc + 1) * 256])
            for t in range(8):
                sq, i = t // 2, t % 2
                ps = self.ps.next()
                for kc in range(NCH):
                    k.mm(ps[:, 0:256], self.xmp.at(sq)[:, kc, sq, 1 + i * 128:1 + (i + 1) * 128], w[:, kc, :],
                         start=(kc == 0), stop=(kc == NCH - 1))
                sg = stg.next()
                k.copy("act", sg[:, :], ps[:, 0:256])
                dst = io["new_k"] if cc < 2 else io["new_v"]
                c0 = (cc % 2) * 256
                k.dma("sp", dst[sq, l, i * 128:(i + 1) * 128, c0:c0 + 256], sg.t[:, :], reads=sg.toks)
                if cc >= 2:
                    k.copy("dve", Vp[:, t, c0:c0 + 256], sg[:, :])
        for h in range(4):
            w = self.wtile(io["w_in"][l, :, O_AQ + h * 128:O_AQ + (h + 1) * 128])
            w2 = self.wtile(io["w_in"][l, :, O_AK + h * 128:O_AK + (h + 1) * 128])
            for sq in range(4):
                qb = qk.next()
                for which, ww in ((0, w), (1, w2)):
                    ps = self.ps.next()
                    for kc in range(NCH):
                        k.mm(ps[:, 0:256], ww[:, kc, :], self.xmp.at(sq)[:, kc, sq, 1:257], start=(kc == 0), stop=(kc == NCH - 1))
                    k.copy("act", qb[:, which, :], ps[:, 0:256])
                qv = View(qb.t[:, 0, :], qb.toks)
                kv = View(qb.t[:, 1, :], qb.toks)
                vv = View(Vp.t[:, 2 * sq:2 * sq + 2, h * 128:(h + 1) * 128], Vp.toks)

                def out_fn(o, h=h, sq=sq):
                    k.ts("dve", self.oT.at(sq // 2)[:, h, sq * 256:(sq + 1) * 256], o[:, 0:256], self.gA[:, l:l + 1], ALU.mult)
                self.attn_core(l, qv, kv, vv, 2, 0, 256, out_fn, Pt, o0, o1, rr)

    def interleave(self, fixed, queue, nslots=2):
        fixed = list(fixed)
        queue = list(queue)
        slots = []
        while fixed or slots or queue:
            if not slots:
                while len(slots) < nslots and queue:
                    slots.append(queue.pop(0))
            for lst in (fixed, slots):
                for g in list(lst):
                    try:
                        next(g)
                    except StopIteration:
                        lst.remove(g)

    def lockstep(self, gens):
        gens = list(gens)
        while gens:
            for g in list(gens):
                try:
                    next(g)
                except StopIteration:
                    gens.remove(g)

    def src_prompt(self, sq):
        def f(tile, kc, shift=0):
            c0 = 1 + tile * 128 + shift
            return self.xmp.at(sq)[:, kc, sq, c0:c0 + 128]
        return f

    def src_prompt_fm(self, sq):
        return lambda kc: self.xmp.at(sq)[:, kc, sq, 1:257]

    def sample_src(self, order, halo):
        state = {"b": None, "buf": None}
        slab = self.slab_ring

        def f(tile, kc, shift=0):
            b = tile // 4
            if state["b"] != b:
                state["b"] = b
                state["buf"] = slab.next()
                self.load_slab(state["buf"], b, halo=halo)
            c0 = 1 + (tile % 4) * 128 + shift
            return state["buf"][:, kc, c0:c0 + 128]
        return f

    def mixB(self, l):
        k, io = self.k, self.io
        L = self.depth
        off = self.moff
        self.slab_ring = Ring(self.slab)
        dpp, off = self.carve(off, [128, 4, 2, 2], F32)
        dps, off = self.carve(off, [128, 2, 2], F32)
        dnr, off = self.carve(off, [128, 64], F32)
        k.load(dpp[:], io["dpar"][l * 16:(l + 1) * 16].partition_broadcast(128).rearrange("p (h d a) -> p h d a", h=4, d=2))
        k.load(dps[:], io["dpar_h"][l * 4:(l + 1) * 4].partition_broadcast(128).rearrange("p (d a) -> p d a", d=2))
        k.load(dnr[:, :], io["dnorm"][l * 64:(l + 1) * 64].partition_broadcast(128))
        k.act(dpp[:, :, :, 0:1], dpp[:, :, :, 0:1], AF.Exp)
        k.ts("dve", dpp[:, :, :, 0:1], dpp[:, :, :, 0:1], -1.0, ALU.mult)
        k.act(dps[:, :, 0:1], dps[:, :, 0:1], AF.Exp)
        k.ts("dve", dps[:, :, 0:1], dps[:, :, 0:1], -1.0, ALU.mult)
        cwb, off = self.carve(off, [128, 576], F32)
        wset = {}
        for nm in ("p", "s"):
            wc, off = self.carve(off, [128, 3, NCH, 196], BF16)
            k.memset("pool", wc[:, :, :, 192:196], 0.0)
            wset[nm] = (wc, None)

        def fold(wc, wba, qkv_srcs, ba_src, cw_src):
            k.load(cwb[:, :], cw_src.partition_broadcast(128))
            cw = cwb.t
            st = self.wst.next()
            sv = View(st.t[:, 0:NCH * 192].rearrange("p (c n) -> p c n", c=NCH), st.toks)
            for j, src in enumerate(qkv_srcs):
                n = src.shape[1]
                k.load(sv[:, :, j * (192 // len(qkv_srcs)):j * (192 // len(qkv_srcs)) + n], src.rearrange("(c p) n -> p c n", p=128))
            for tap in range(3):
                k.tt("pool", wc[:, tap, :, 0:192], sv[:], Opd(cw[:, tap * 192:(tap + 1) * 192].unsqueeze(1).broadcast_to([128, NCH, 192]), cwb.toks),
                     ALU.mult)
            st2 = self.wst.next()
            sv2 = View(st2.t[:, 0:NCH * 4].rearrange("p (c n) -> p c n", c=NCH), st2.toks)
            k.load(sv2[:], ba_src)
            k.copy("pool", wc[:, 1, :, 192:196], sv2[:])

        fold(wset["s"][0], wset["s"][1], [io["w_in_h"][l, :, 384:576]],
             io["w_in_h"][l, :, 640:644].rearrange("(c p) n -> p c n", p=128), io["conv_h"][l])
        wcur = {"h": None}

        def getw(h):
            if wcur["h"] != h:
                wcur["h"] = h
                fold(wset["p"][0], wset["p"][1],
                     [io["w_in"][l, :, O_BQKV + j * 256 + h * 64:O_BQKV + j * 256 + (h + 1) * 64] for j in range(3)],
                     io["w_ba_p"][l, :, h, :].rearrange("(c p) n -> p c n", p=128), io["conv_hm"][l, h])
            return wset["p"]
        Oacc_s, off = self.carve(off, [128, 32, 64], F32, ntok=32)
        Oacc_p, off = self.carve(off, [128, 8, 256], F32, ntok=32)
        self.b_base = off
        R = {}
        for nm, w, n in (("qkv", 192, 2), ("et", 192, 1), ("sq", 128, 1), ("sm", 16, 4), ("kbe", 64, 2), ("vb", 64, 2),
                         ("kd", 64, 2), ("qd", 64, 2), ("gb", 64, 2), ("diag", 128, 1), ("dec", 128, 1),
                         ("dS", 128, 1), ("dI", 128, 1), ("Nm", 128, 2), ("QKm", 128, 2), ("NQT", 256, 2),
                         ("Xr", 128, 3), ("U", 192, 2), ("vnew", 64, 2), ("S", 64, 16)):
            R[nm] = Ring([self.carve(off + i * w * 4, [128, w], F32)[0] for i in range(n)])
            off += n * w * 4
        tb = self.tmp.bufs
        R["T3"] = Ring([tb[0], tb[1]])
        pp3 = Buf(tb[2].t, 1)
        R["PP"] = Ring([Buf(tb[2].t[:, 0:256], 1), Buf(tb[2].t[:, 256:512], 1), Buf(tb[3].t[:, 0:256], 1)])
        ID, ONE = self.C(C_ID), self.C(C_ONE)

        def job(d, src, ntiles, wc, wba, dpar, x0_fn, o_fn, fin_fn):
            order = list(range(ntiles)) if d == 0 else list(range(ntiles - 1, -1, -1))
            tri, rem, cstr, cinc = ((C_TRID_F, C_REMD_F, C_STR_F, C_INC_F) if d == 0 else
                                    (C_TRID_B, C_REMD_B, C_STR_B, C_INC_B))
            S = R["S"].next()
            x0_fn(S)
            for tile in order:
                ps1 = self.ps.next()
                n = 0
                for tap in range(3):
                    for kc in range(NCH):
                        k.mm(ps1[:, 0:196], src(tile, kc, tap - 1), wc[:, tap, kc, :], start=(n == 0), stop=(n == 23))
                        n += 1
                qkv, et, sq, sm = R["qkv"].next(), R["et"].next(), R["sq"].next(), R["sm"].next()
                k.act(et[:, :], ps1[:, 0:192], AF.Exp, scale=-1.0)
                k.act(sm[:, 2:3], ps1[:, 192 + d:193 + d], AF.Exp, scale=-1.0)
                k.act(sm[:, 4:5], ps1[:, 194 + d:195 + d], AF.Exp, bias=dpar[:, d, 1:2])
                k.ts("dve", et[:, :], et[:, :], 1.0, ALU.add)
                k.recip(et[:, :], et[:, :])
                k.tt("dve", qkv[:, :], ps1[:, 0:192], et[:, :], ALU.mult)
                k.tt("dve", sq[:, :], qkv[:, 0:128], qkv[:, 0:128], ALU.mult)
                k.reduce(sm[:, 0:2], Opd(sq.t[:, :].rearrange("p (a e) -> p a e", a=2), sq.toks))
                k.act(sm[:, 8:10], sm[:, 0:2], AF.Ln, bias=1e-6)
                k.act(sm[:, 8:10], sm[:, 8:10], AF.Exp, scale=-0.5)
                k.ts("dve", qkv[:, 0:64], qkv[:, 0:64], sm[:, 8:9], ALU.mult, 0.125, ALU.mult)
                k.ts("dve", qkv[:, 64:128], qkv[:, 64:128], sm[:, 9:10], ALU.mult)
                k.ts("dve", sm[:, 2:3], sm[:, 2:3], 1.0, ALU.add)
                k.recip(sm[:, 2:3], sm[:, 2:3])
                k.ts("dve", sm[:, 3:4], sm[:, 2:3], -1.0, ALU.mult)
                k.act(sm[:, 4:5], sm[:, 4:5], AF.Ln, bias=1.0)
                k.ts("dve", sm[:, 4:5], sm[:, 4:5], dpar[:, d, 0:1], ALU.mult)
                k.copy("dve", sm[:, 5:6], sm[:, 4:5])
                if BY & 1:
                    yield
                gb = R["gb"].next()
                k.copy("dve", gb[:, :], Opd(sm.t[:, 4:5].broadcast_to([128, 64]), sm.toks))
                psc = self.ps.next()
                k.mm(psc[:, 0:2], self.C(tri), sm[:, 4:6])
                k.mm(psc[:, 2:4], self.C(rem), sm[:, 4:6])
                k.mm(psc[0:64, 4:6], gb[:, :], self.C(C_CI64, cols=slice(0, 2)))
                k.copy("act", sm[:, 6:7], psc[:, 0:1])
                k.act(sm[:, 12:16], psc[:, 0:4], AF.Exp)
                k.act(sm[0:64, 10:12], psc[0:64, 4:6], AF.Exp)
                kbe, vb, kd, qd = R["kbe"].next(), R["vb"].next(), R["kd"].next(), R["qd"].next()
                k.ts("dve", kbe[:, :], qkv[:, 64:128], sm[:, 2:3], ALU.mult, sm[:, 12:13], ALU.mult)
                k.act(vb[:, :], qkv[:, 128:192], AF.Identity, scale=sm[:, 2:3])
                k.act(kd[:, :], qkv[:, 64:128], AF.Identity, scale=sm[:, 14:15])
                k.ts("dve", qd[:, :], qkv[:, 0:64], sm[:, 12:13], ALU.mult)
                pst = self.ps.next()
                k.mm(pst[0:64, 0:128], qkv[:, 64:128], ID)
                k.mm(pst[0:64, 128:256], qkv[:, 0:64], ID)
                k.mm(pst[0:64, 256:384], qd[:, :], ID)
                T3 = R["T3"].next()
                k.copy("act", T3[0:64, 0:384], pst[0:64, 0:384])
                knT, qnT, qdT = View(T3.t[0:64, 0:128], T3.toks), View(T3.t[0:64, 128:256], T3.toks), View(T3.t[0:64, 256:384], T3.toks)
                if BY & 2:
                    yield
                diag = R["diag"].next()
                k.act(diag[:, :], ID, AF.Identity, scale=sm[:, 6:7])
                psG = self.ps.next()
                k.mm(psG[:, 0:128], knT[:, :], knT[:, :])
                k.mm(psG[:, 128:256], qnT[:, :], knT[:, :])
                k.mm(psG[:, 256:384], ONE, diag[:, :])
                dec, dS, dI, Nm, QKm = (R[x].next() for x in ("dec", "dS", "dI", "Nm", "QKm"))
                k.ts("dve", dec[:, :], psG[:, 256:384], sm[:, 6:7], ALU.subtract, 0.0, ALU.max)
                k.act(dec[:, :], dec[:, :], AF.Exp, scale=-1.0)
                k.tt("pool", dS[:, :], dec[:, :], self.C(cstr), ALU.mult)
                k.tt("pool", dI[:, :], dec[:, :], self.C(cinc), ALU.mult)
                k.stt(Nm[:, :], dS[:, :], sm[:, 3:4], psG[:, 0:128], ALU.mult, ALU.mult)
                k.tt("dve", QKm[:, :], dI[:, :], psG[:, 128:256], ALU.mult)
                psT = self.ps.next()
                k.mm(psT[:, 0:128], Nm[:, :], ID)
                k.mm(psT[:, 128:256], QKm[:, :], ID)
                NQT = R["NQT"].next()
                k.copy("act", NQT[:, :], psT[:, 0:256])
                QKT = View(NQT.t[:, 128:256], NQT.toks)
                if BY & 4:
                    yield
                P, PT = View(Nm.t, Nm.toks), View(NQT.t[:, 0:128], NQT.toks)
                X = R["Xr"].next()
                k.tt("dve", X[:, :], PT[:, :], ID, ALU.add)
                for j in range(5):
                    psD = self.ps.next()
                    k.mm(psD[:, 0:128], PT[:, :], P[:, :])
                    if j < 4:
                        k.mm(psD[:, 128:256], P[:, :], PT[:, :])
                    PP = R["PP"].next()
                    k.copy("act", PP[:, 0:(256 if j < 4 else 128)], psD[:, 0:(256 if j < 4 else 128)])
                    P, PT = View(PP.t[:, 0:128], PP.toks), View(PP.t[:, 128:256], PP.toks)
                    psX = self.ps.next()
                    k.mm(psX[:, 0:128], ID, X[:, :], start=True, stop=False)
                    k.mm(psX[:, 0:128], P[:, :], X[:, :], start=False, stop=True)
                    X2 = R["Xr"].next()
                    k.copy("dve", X2[:, :], psX[:, 0:128])
                    X = X2
                    if BY & 8:
                        yield
                psU = self.ps.next()
                k.mm(psU[:, 0:64], X[:, :], vb[:, :])
                k.mm(psU[0:64, 64:192], kbe[:, :], X[:, :])
                U = R["U"].next()
                k.copy("act", U[:, 0:64], psU[:, 0:64])
                k.copy("act", U[0:64, 64:192], psU[0:64, 64:192])
                if BY & 16:
                    yield
                for ci in ((0, 1) if d == 0 else (1, 0)):
                    r = slice(64 * ci, 64 * ci + 64)
                    psa = self.ps.next()
                    psb = self.ps.next()
                    k.mm(psa[r, 0:64], U[0:64, 64 + 64 * ci:128 + 64 * ci], S[0:64, :])
                    k.mm(psa[r, 64:128], qdT[:, 64 * ci:64 * ci + 64], S[0:64, :])
                    vnew = R["vnew"].next()
                    k.tt("dve", vnew[r, :], U[r, 0:64], psa[r, 0:64], ALU.subtract)
                    k.mm(psb[r, 0:64], QKT[r, 64 * ci:64 * ci + 64], vnew[r, :])
                    k.mm(psb[0:64, 64:128], kd[r, :], vnew[r, :])
                    S2 = R["S"].next()
                    k.stt(S2[0:64, :], S[0:64, :], sm[0:64, 10 + ci:11 + ci], psb[0:64, 64:128], ALU.mult, ALU.add)
                    o_fn(tile, r, psa, psb)
                    S = S2
                    if BY & 32:
                        yield
                if not (BY & 32):
                    yield
            if BSTOP >= 4:
                fin_fn(S)

        done_s = set()

        def x0_sample(d):
            return lambda S: k.load(S[0:64, :], io["s0_d"][l, d])

        def o_sample(tile, r, psa, psb):
            dst = Oacc_s.at(tile)[r, tile, :]
            key = (tile, r.start)
            if key in done_s:
                k.tt("dve", dst, psa[r, 64:128], dst, ALU.add)
            else:
                done_s.add(key)
                k.copy("act", dst, psa[r, 64:128])
            k.tt("dve", dst, psb[r, 0:64], dst, ALU.add)

        sj = [job(d, self.sample_src(None, True), 32, wset["s"][0], wset["s"][1], dps, x0_sample(d), o_sample, lambda S: None)
              for d in range(2)]
        done_p = set()

        def mk_prompt(sq, d, h):
            def x0(S):
                k.memset("dve", S[0:64, :], 0.0)

            def o_fn(tile, r, psa, psb):
                t = sq * 2 + tile
                dst = Oacc_p.at(t * 4 + h)[r, t, h * 64:(h + 1) * 64]
                key = (t, h, r.start)
                if key in done_p:
                    k.tt("dve", dst, psa[r, 64:128], dst, ALU.add)
                else:
                    done_p.add(key)
                    k.copy("act", dst, psa[r, 64:128])
                k.tt("dve", dst, psb[r, 0:64], dst, ALU.add)

            def fin(S):
                k.dma("sp", io["new_sd"][sq, l, d, h], S.t[0:64, :], reads=S.toks)

            def gen():
                wc, wba = getw(h)
                yield from job(d, self.src_prompt(sq), 2, wc, wba, View(dpp.t[:, h], dpp.toks), x0, o_fn, fin)
            return gen()

        self.lockstep(sj)
        for h in range(4):
            for sq in range(4):
                self.lockstep([mk_prompt(sq, d, h) for d in range(2)])

        if BSTOP < 5:
            self.zero_o(l, 4, 6, 128, 192)
            return
        k.barrier()
        off = self.b_base
        P1 = {}
        for nm, w, n in (("sq", 64, 2), ("ss", 4, 2), ("e", 64, 2), ("o", 64, 2)):
            P1[nm] = Ring([self.carve(off + i * w * 4, [128, w], F32)[0] for i in range(n)])
            off += n * w * 4
        osb = Ring([self.carve(off + i * 1024, [128, 512], BF16)[0] for i in range(2)]); off += 2048

        def post(ov, gate_mm, prow):
            sq_, ss, e, o = (P1[x].next() for x in ("sq", "ss", "e", "o"))
            k.act(sq_[:, :], ov, AF.Square, accum=ss[:, 0:1])
            k.act(ss[:, 1:2], ss[:, 0:1], AF.Ln, bias=1e-6, scale=1.0 / 64)
            k.act(ss[:, 1:2], ss[:, 1:2], AF.Exp, scale=-0.5)
            psG = self.ps.next()
            gate_mm(psG)
            k.act(e[:, :], psG[:, 0:64], AF.Exp, scale=-1.0)
            k.ts("dve", e[:, :], e[:, :], 1.0, ALU.add)
            k.recip(e[:, :], e[:, :])
            k.tt("dve", e[:, :], e[:, :], psG[:, 0:64], ALU.mult)
            k.stt(o[:, :], ov, ss[:, 1:2], dnr[:, :], ALU.mult, ALU.mult)
            k.tt("dve", o[:, :], o[:, :], e[:, :], ALU.mult)
            psT = self.ps.next()
            k.mm(psT[prow, 0:128], o[:, :], ID)
            return psT

        for h in range(4):
            wg = self.wtile(io["w_in"][l, :, O_BG + h * 64:O_BG + (h + 1) * 64])
            prow = slice(64 * (h % 2), 64 * (h % 2) + 64)
            for t in range(8):
                sq, tile = t // 2, t % 2
                ov = Oacc_p.at(t * 4 + h)[:, t, h * 64:(h + 1) * 64]

                def gate_mm(ps, sq=sq, tile=tile, wg=wg):
                    sp = self.src_prompt(sq)
                    for kc in range(NCH):
                        k.mm(ps[:, 0:64], sp(tile, kc), wg[:, kc, 0:64], start=(kc == 0), stop=(kc == NCH - 1))
                psT = post(ov, gate_mm, prow)
                k.copy("act", self.oT.at(t // 4)[prow, 4 + h // 2, t * 128:(t + 1) * 128], psT[prow, 0:128])
        wg = self.wtile(io["w_in_h"][l, :, 576:640])
        ssrc = self.sample_src(None, False)
        for b in range(8):
            ob = osb.next()
            for j in range(4):
                tile = b * 4 + j
                ov = Oacc_s.at(tile)[:, tile, :]

                def gate_mm(ps, tile=tile):
                    for kc in range(NCH):
                        k.mm(ps[:, 0:64], ssrc(tile, kc), wg[:, kc, 0:64], start=(kc == 0), stop=(kc == NCH - 1))
                psT = post(ov, gate_mm, slice(0, 64))
                k.copy("act", ob[0:64, j * 128:(j + 1) * 128], psT[0:64, 0:128])
            k.dma("sp", self.ago_in[1][0:64, b * 512:(b + 1) * 512], ob.t[0:64, :], reads=ob.toks, writes=[self.t_ago_in])

    def mixC(self, l):
        k, io = self.k, self.io
        L = self.depth
        off = self.moff
        self.slab_ring = Ring(self.slab)
        lbs = {}
        scr = ARENA_BYTES - (2 * L * 256 * 4 + 2 * 256 * 4)
        for nm, Wd in (("hgrn_lb", 256), ("hgrn_lb_h", 64)):
            e, o2 = self.carve(scr, [128, 2, L, Wd], F32)
            tot, o2 = self.carve(o2, [128, 2, Wd], F32)
            lb, off = self.carve(off, [128, 2, Wd], F32)
            om, off = self.carve(off, [128, 2, Wd], F32)
            k.load(e[:], io[nm].partition_broadcast(128).rearrange("p (d l w) -> p d l w", d=2, l=L))
            k.act(e[:], e[:], AF.Exp)
            k.copy("dve", tot[:], e[:, :, 0, :])
            for j in range(1, L):
                k.tt("dve", tot[:], tot[:], e[:, :, j, :], ALU.add)
            k.recip(tot[:], tot[:])
            if l == 0:
                k.memset("dve", lb[:], 0.0)
            else:
                k.copy("dve", lb[:], e[:, :, 1, :])
                for j in range(2, l + 1):
                    k.tt("dve", lb[:], lb[:], e[:, :, j, :], ALU.add)
                k.tt("dve", lb[:], lb[:], tot[:], ALU.mult)
            k.ts("dve", om[:], lb[:], -1.0, ALU.mult, 1.0, ALU.add)
            lbs[nm] = (lb, om)
            k.barrier()
        hn = self.carve(off, [128, 1], F32)[0]; off += 64
        k.load(hn[:, :], io["hnorm"][l].rearrange("(p o) -> p o", o=1))
        def wS(c0, n):
            b, _ = self.carve(wS.off, [128, NCH, n], BF16)
            wS.off += NCH * n * 2
            st = self.wst.next()
            sv = Opd(st.t[:, 0:NCH * n].rearrange("p (c n) -> p c n", c=NCH), st.toks)
            k.load(sv, c0.rearrange("(c p) n -> p c n", p=128))
            k.copy("pool", b[:], sv)
            return b
        wS.off = off
        ws_q = wS(io["w_in_h"][l, :, 644:708], 64)
        ws_f = [wS(io["w_in_h"][l, :, 708 + 64 * d:772 + 64 * d], 64) for d in range(2)]
        ws_i = wS(io["w_in_h"][l, :, 836:900], 64)
        wpb = [self.carve(wS.off + i * 2048, [128, NCH, 128], BF16)[0] for i in range(4)]
        wS.off += 4 * 2048
        wp_cur = {"pr": None}

        def getw(pr):
            if wp_cur["pr"] != pr:
                wp_cur["pr"] = pr
                srcs = [io["w_in"][l, :, O_CQ + pr * 128:O_CQ + (pr + 1) * 128],
                        io["w_in"][l, :, O_CF + pr * 128:O_CF + (pr + 1) * 128],
                        io["w_in"][l, :, O_CF + 256 + pr * 128:O_CF + 256 + (pr + 1) * 128],
                        io["w_in"][l, :, O_CI + pr * 128:O_CI + (pr + 1) * 128]]
                for b, c0 in zip(wpb, srcs):
                    st = self.wst.next()
                    sv = Opd(st.t[:, 0:NCH * 128].rearrange("p (c n) -> p c n", c=NCH), st.toks)
                    k.load(sv, c0.rearrange("(c p) n -> p c n", p=128))
                    k.copy("pool", b[:], sv)
            return wpb[0], [wpb[1], wpb[2]], wpb[3]
        off = wS.off
        Oacc_s, off = self.carve(off, [128, 2048], F32, ntok=32)
        Oacc_p, off = self.carve(off, [128, 2, 1024], F32, ntok=16)
        self.c_base = off
        R = {}
        for nm, shp, n in (("E1", [128, 256], 2), ("qs", [128, 128], 2), ("f", [128, 128], 2), ("g", [128, 128], 2),
                           ("kk", [128, 128], 2), ("v", [128, 128], 2), ("qd", [128, 128], 2), ("kdi", [128, 128], 2),
                           ("kd", [128, 128], 2), ("qdT", [128, 128], 2), ("kdiT", [128, 128], 2), ("AT", [128, 128], 2),
                           ("gl", [128, 16], 2), ("X", [128, 64], 8), ("fs", [128, 64], 2)):
            sz = shp[1] * 4
            R[nm] = Ring([self.carve(off + i * sz, shp, F32)[0] for i in range(n)])
            off += n * sz
        self.c_off = off

        def job(d, W, src, ntiles, w_q, w_f, w_i, lb, om, lcol, x0_fn, o_fn, fin_fn):
            nh = W // 64
            order = list(range(ntiles)) if d == 0 else list(range(ntiles - 1, -1, -1))
            tri, rem = (C_TRIC_F, C_REMC_F) if d == 0 else (C_TRIC_B, C_REMC_B)
            X = R["X"].next()
            x0_fn(X)
            for tile in order:
                ps1 = self.ps.next()
                for kc in range(NCH):
                    k.mm(ps1[:, 0:W], src(tile, kc), w_q[:, kc, :], start=(kc == 0), stop=(kc == NCH - 1))
                for kc in range(NCH):
                    k.mm(ps1[:, W:2 * W], src(tile, kc), w_f[:, kc, :], start=(kc == 0), stop=(kc == NCH - 1))
                ps2 = self.ps.next()
                for kc in range(NCH):
                    k.mm(ps2[:, 0:W], src(tile, kc), w_i[:, kc, :], start=(kc == 0), stop=(kc == NCH - 1))
                E1, qs, f, g, kk, v = (R[n].next() for n in ("E1", "qs", "f", "g", "kk", "v"))
                k.act(E1[:, 0:2 * W], ps1[:, 0:2 * W], AF.Exp, scale=-1.0)
                k.ts("dve", E1[:, 0:2 * W], E1[:, 0:2 * W], 1.0, ALU.add)
                k.recip(E1[:, 0:2 * W], E1[:, 0:2 * W])
                k.tt("dve", qs[:, 0:W], ps1[:, 0:W], E1[:, 0:W], ALU.mult)
                k.tt("dve", f[:, 0:W], E1[:, W:2 * W], om[:, d, lcol:lcol + W], ALU.mult)
                k.tt("dve", f[:, 0:W], f[:, 0:W], lb[:, d, lcol:lcol + W], ALU.add)
                k.act(g[:, 0:W], f[:, 0:W], AF.Ln)
                k.ts("pool", kk[:, 0:W], f[:, 0:W], -1.0, ALU.mult, 1.0, ALU.add)
                k.copy("act", v[:, 0:W], ps2[:, 0:W])
                yield
                psb = self.ps.next()
                k.mm(psb[:, 0:W], self.C(tri), g[:, 0:W])
                k.mm(psb[:, W:2 * W], self.C(rem), g[:, 0:W])
                k.mm(psb[0:W, 2 * W:2 * W + 8], g[:, 0:W], self.C(C_CI16, cols=slice(0, 8)))
                qd, kdi, kd, gl = (R[n].next() for n in ("qd", "kdi", "kd", "gl"))
                Eb = R["E1"].next()
                k.act(Eb[:, 0:W], psb[:, 0:W], AF.Exp)
                k.tt("pool", qd[:, 0:W], qs[:, 0:W], Eb[:, 0:W], ALU.mult)
                k.act(Eb[:, W:2 * W], psb[:, 0:W], AF.Exp, scale=-1.0)
                k.tt("pool", kdi[:, 0:W], kk[:, 0:W], Eb[:, W:2 * W], ALU.mult)
                Ed = R["f"].next()
                k.act(Ed[:, 0:W], psb[:, W:2 * W], AF.Exp)
                k.tt("pool", kd[:, 0:W], kk[:, 0:W], Ed[:, 0:W], ALU.mult)
                glo = gl.t[0:W, 0:8] if d == 0 else gl.t[0:W, 7::-1]
                k.act(Opd(glo, gl.toks), psb[0:W, 2 * W:2 * W + 8], AF.Exp)
                k.copy("dve", gl[0:W, 8:16], gl[0:W, 0:8])
                k.memset("dve", gl[0:W, 8:9], 0.0)
                yield
                qdT, kdiT = R["qdT"].next(), R["kdiT"].next()
                pst = self.ps.next()
                k.mm(pst[0:W, 0:128], qd[:, 0:W], self.C(C_ID))
                k.mm(pst[0:W, 128:256], kdi[:, 0:W], self.C(C_ID))
                if 'c' not in CSUB:
                    if 'x' not in CSUB:
                        k.copy("act", qdT[0:W, :], pst[0:W, 0:128])
                    if 'y' not in CSUB:
                        k.copy("dve", kdiT[0:W, :], pst[0:W, 128:256])
                psKV = self.ps.next()
                ATs = []
                for h in range(nh):
                    r0 = 64 * h
                    if 'a' in CSUB:
                        continue
                    psA = self.ps.next()
                    k.mm(psA[:, 0:128], kdiT[r0:r0 + 64, :], qdT[r0:r0 + 64, :])
                    AT = R["AT"].next()
                    k.tt("dve", AT[:, :], psA[:, 0:128], self.C(tri), ALU.mult)
                    ATs.append(AT)
                    if 'b' in CSUB:
                        continue
                    vx = self.tmp.next()
                    k.tt("pool", Opd(vx.t[:, :].rearrange("p (c e) -> p c e", c=8), vx.toks),
                         Opd(v.t[:, r0:r0 + 64].unsqueeze(1).broadcast_to([128, 8, 64]), v.toks),
                         Opd(self.cst.t[:, C_CI16, 0:8].unsqueeze(2).broadcast_to([128, 8, 64]), self.cst.toks), ALU.mult)
                    k.mm(psKV[r0:r0 + 64, :], kd[:, r0:r0 + 64], vx[:, :])
                KVs, GLx, XS = self.tmp.next(), self.tmp.next(), self.tmp.next()
                kv3 = KVs.t[0:W, :].rearrange("p (e c) -> p e c", c=8)
                kvo = kv3 if d == 0 else kv3[:, :, ::-1]
                k.copy("act", Opd(kvo, KVs.toks), Opd(psKV.t[0:W, :].rearrange("p (c e) -> p e c", c=8), psKV.toks))
                k.stt(Opd(kv3[:, :, 0], KVs.toks), X[0:W, :], gl[0:W, 0:1], Opd(kv3[:, :, 0], KVs.toks), ALU.mult, ALU.add)
                k.copy("pool", Opd(GLx.t[0:W, :].rearrange("p (e c) -> p e c", c=8), GLx.toks),
                       Opd(gl.t[0:W, 8:16].unsqueeze(1).broadcast_to([W, 64, 8]), gl.toks))
                k.scan(XS[0:W, :], GLx[0:W, :], KVs[0:W, :], 0.0, ALU.mult, ALU.add)
                xs3 = XS.t[0:W, :].rearrange("p (e c) -> p e c", c=8)
                Xn = R["X"].next()
                k.copy("dve", Xn[0:W, :], Opd(xs3[:, :, 7], XS.toks))
                orow = o_fn(tile, None)
                psO = self.ps.next()
                for h in range(nh):
                    r0 = 64 * h
                    ro = orow + r0
                    k.mm(psO[ro:ro + 64, 0:128], v[:, r0:r0 + 64], ATs[h][:, :], start=True, stop=False)
                    for i in range(8):
                        c = i if d == 0 else 7 - i
                        xc = X[r0:r0 + 64, :] if i == 0 else Opd(xs3[r0:r0 + 64, :, i - 1], XS.toks)
                        k.mm(psO[ro:ro + 64, 16 * c:16 * c + 16], xc, qdT[r0:r0 + 64, 16 * c:16 * c + 16],
                             start=False, stop=(i == 7))
                o_fn(tile, psO)
                X = Xn
                yield
            if CSTOP >= 4:
                fin_fn(X)

        def x0_sample(d):
            def f(X):
                k.load(X[0:64, :], io["s0_h"][l, d])
            return f

        done_s = set()

        def o_sample(tile, psO):
            row = 0 if tile < 16 else 64
            if psO is None:
                return row
            col = (tile % 16) * 128
            dst = Oacc_s.at(tile)[row:row + 64, col:col + 128]
            if tile in done_s:
                k.tt("dve", dst, psO[row:row + 64, 0:128], dst, ALU.add)
            else:
                done_s.add(tile)
                k.copy("act", dst, psO[row:row + 64, 0:128])

        sj = [job(d, 64, self.sample_src(None, False), 32, ws_q, ws_f[d], ws_i, lbs["hgrn_lb_h"][0], lbs["hgrn_lb_h"][1], 0,
                  x0_sample(d), o_sample, lambda X: None) for d in range(2)]

        done_p = set()

        def mk_prompt(sq, d, pr):
            def x0(X):
                k.memset("dve", X[:, :], 0.0)

            def o_fn(tile, psO):
                if psO is None:
                    return 0
                key = (sq, pr, tile)
                c0 = sq * 256 + tile * 128
                dst = Oacc_p.at((sq * 2 + tile) * 2 + pr)[:, pr, c0:c0 + 128]
                if key in done_p:
                    k.tt("dve", dst, psO[:, 0:128], dst, ALU.add)
                else:
                    done_p.add(key)
                    k.copy("act", dst, psO[:, 0:128])

            def fin(X):
                fs = R["fs"].next()
                k.copy("act", fs[:, :], X[:, :])
                k.dma("sp", io["new_sh"][sq, l, d, pr * 128:(pr + 1) * 128, :], fs.t[:, :], reads=fs.toks)
            def gen():
                wq_, wf_, wi_ = getw(pr)
                yield from job(d, 128, self.src_prompt(sq), 2, wq_, wf_[d], wi_, lbs["hgrn_lb"][0], lbs["hgrn_lb"][1], pr * 128,
                               x0, o_fn, fin)
            return gen()

        self.lockstep(sj)
        for pr in range(2):
            for sq in range(4):
                self.lockstep([mk_prompt(sq, d, pr) for d in range(2)])

        if CSTOP < 5:
            self.zero_o(l, 6, 8, 192, 256)
            return
        k.barrier()
        off = self.c_base
        sqb = Ring([self.carve(off + i * 2048, [128, 512], F32)[0] for i in range(2)]); off += 4096
        rsb = Ring([self.carve(off + i * 2048, [128, 512], F32)[0] for i in range(2)]); off += 4096
        osb = Ring([self.carve(off + i * 1024, [128, 512], BF16)[0] for i in range(2)]); off += 2048

        def post(ov, rows, n, gate_mm, out_fn):
            sq_ = sqb.next()
            k.act(sq_[rows, 0:n], ov, AF.Square)
            psN = self.ps.next()
            k.mm(psN[rows, 0:n], self.C(C_BLK64, rows=rows, cols=rows), sq_[rows, 0:n])
            rs = rsb.next()
            k.act(rs[rows, 0:n], psN[rows, 0:n], AF.Sqrt, bias=1e-6)
            k.recip(rs[rows, 0:n], rs[rows, 0:n])
            k.tt("dve", rs[rows, 0:n], rs[rows, 0:n], ov, ALU.mult)
            psG = self.ps.next()
            gate_mm(psG)
            e = sqb.next()
            k.act(e[rows, 0:n], psG[rows, 0:n], AF.Exp, scale=-1.0)
            k.ts("dve", e[rows, 0:n], e[rows, 0:n], 1.0, ALU.add)
            k.recip(e[rows, 0:n], e[rows, 0:n])
            k.tt("dve", e[rows, 0:n], e[rows, 0:n], psG[rows, 0:n], ALU.mult)
            out_fn(rs, e)

        for pr in range(2):
            wg = self.wtile(io["w_in"][l, :, O_CG + pr * 128:O_CG + (pr + 1) * 128])
            for sq in range(4):
                toks = [Oacc_p.toks[(sq * 2 + t) * 2 + pr] for t in range(2)]
                ov = Opd(Oacc_p.t[:, pr, sq * 256:(sq + 1) * 256], toks)

                def gate_mm(ps, sq=sq, wg=wg):
                    fm = self.src_prompt_fm(sq)
                    for kc in range(NCH):
                        k.mm(ps[:, 0:256], wg[:, kc, :], fm(kc), start=(kc == 0), stop=(kc == NCH - 1))

                def out_fn(rs, e, sq=sq, pr=pr):
                    k.stt(self.oT.at(sq // 2)[:, 6 + pr, sq * 256:(sq + 1) * 256], rs[:, 0:256], hn[:, 0:1], e[:, 0:256],
                          ALU.mult, ALU.mult)
                post(ov, slice(0, 128), 256, gate_mm, out_fn)
        wg = self.wtile(io["w_in_h"][l, :, 900:964])
        for b in range(8):
            sb = self.slab_ring.next()
            self.load_slab(sb, b)
            rows = slice(0, 64) if b < 4 else slice(64, 128)
            c0 = (b % 4) * 512
            toks = [Oacc_s.toks[b * 4 + t] for t in range(4)]
            ov = Opd(Oacc_s.t[rows, c0:c0 + 512], toks)

            def gate_mm(ps, sb=sb, rows=rows, wg=wg):
                for kc in range(NCH):
                    k.mm(ps[rows, 0:512], wg[:, kc, 0:64], sb[:, kc, 1:513], start=(kc == 0), stop=(kc == NCH - 1))

            def out_fn(rs, e, b=b, rows=rows):
                ob = osb.next()
                k.stt(ob[rows, :], rs[rows, :], hn[rows, 0:1], e[rows, :], ALU.mult, ALU.mult)
                k.dma("sp", self.ago_in[1][64:128, b * 512:(b + 1) * 512], ob.t[rows, :], reads=ob.toks, writes=[self.t_ago_in])
            post(ov, rows, 512, gate_mm, out_fn)

    def layer_norm(self, l, which, g, b):
        k = self.k
        xs = self.xs[g].at(b)
        sl = slice(b * 512, (b + 1) * 512)
        ps_m = self.ps.next()
        ps_q = self.ps.next()
        for kc in range(NCH):
            k.mm(ps_m[:, :], self.C(C_MEAN), xs[:, kc, sl], start=(kc == 0), stop=(kc == NCH - 1))
        for kc in range(NCH):
            sq = self.usq.next()
            k.act(sq[:, :], xs[:, kc, sl], AF.Square)
            k.mm(ps_q[:, :], self.C(C_MEAN), sq[:, :], start=(kc == 0), stop=(kc == NCH - 1))
        mean = self.stat.next()
        rstd = self.stat.next()
        k.copy("act", mean[:, :], ps_m[:, :])
        k.tt("dve", rstd[:, :], mean[:, :], mean[:, :], ALU.mult)
        k.tt("dve", rstd[:, :], ps_q[:, :], rstd[:, :], ALU.subtract)
        k.act(rstd[:, :], rstd[:, :], AF.Sqrt, bias=LN_EPS / ALPHA ** 2)
        k.recip(rstd[:, :], rstd[:, :])
        for kc in range(NCH):
            t = self.tmp.next()
            k.tt("dve", t[:, :], xs[:, kc, sl], mean[:, :], ALU.subtract)
            k.tt("dve", t[:, :], t[:, :], rstd[:, :], ALU.mult)
            k.act(xs[:, kc, sl], t[:, :], AF.Identity, scale=self.lng[:, l, which, kc:kc + 1],
                  bias=self.lnb[:, l, which, kc:kc + 1])

    def dense_group(self, l, g):
        k, io = self.k, self.io
        xs = self.xs[g]
        wname = "w_out_p" if g == 0 else "w_out_s"
        for oc in range(NCH):
            w = self.load_w_bf16(io[wname][l, :, oc * 128:(oc + 1) * 128].rearrange("(c p) n -> p c n", p=128), 128, 128)
            for b in range(2):
                sl = slice(b * 512, (b + 1) * 512)
                ps = self.ps.next()
                for kc in range(NCH):
                    k.mm(ps[:, :], w[:, kc, :], self.oT.at(b)[:, kc, sl], start=(kc == 0), stop=(kc == NCH - 1))
                k.stt(xs.at(b)[:, oc, sl], ps[:, :], self.modv(l, 2, g, oc), xs.at(b)[:, oc, sl], ALU.mult, ALU.add)
        for b in range(2):
            self.layer_norm(l, 0, g, b)
        for b in range(2):
            sl = slice(b * 512, (b + 1) * 512)
            for kc in range(NCH):
                k.act(self.xm2.at(b)[:, kc, sl], xs.at(b)[:, kc, sl], AF.Identity,
                      scale=self.modv(l, 4, g, kc), bias=self.modv(l, 3, g, kc))
        for f in range(NF):
            st = self.wst.next()
            wb = self.wbf.next()
            sv = Opd(st.t[:, 0:2048].rearrange("p (c j n) -> p c j n", c=NCH, j=2), st.toks)
            wv = View(wb.t[:, 0:2048].rearrange("p (c j n) -> p c j n", c=NCH, j=2), wb.toks)
            for j in range(2):
                k.load(Opd(sv.ap[:, :, j, :], st.toks),
                       io["w_ffn_in"][l, :, j * D_FF + f * 128:j * D_FF + (f + 1) * 128].rearrange("(c p) n -> p c n", p=128))
            k.copy("pool", wv[:], sv)
            for b in range(2):
                sl = slice(b * 512, (b + 1) * 512)
                ps_g = self.ps.next()
                ps_u = self.ps.next()
                for kc in range(NCH):
                    k.mm(ps_g[:, :], wv[:, kc, 0, :], self.xm2.at(b)[:, kc, sl], start=(kc == 0), stop=(kc == NCH - 1))
                for kc in range(NCH):
                    k.mm(ps_u[:, :], wv[:, kc, 1, :], self.xm2.at(b)[:, kc, sl], start=(kc == 0), stop=(kc == NCH - 1))
                t = self.tmp.next()
                k.act(t[:, :], ps_g[:, :], AF.Silu)
                k.tt("dve", self.hT.at(b)[:, f, sl], t[:, :], ps_u[:, :], ALU.mult)
        for oc in range(NCH):
            wvs = []
            for hf in range(2):
                st = self.wst.next()
                wb = self.wbf.next()
                sv = Opd(st.t[:, 0:11 * 128].rearrange("p (f n) -> p f n", f=11), st.toks)
                wv = View(wb.t[:, 0:11 * 128].rearrange("p (f n) -> p f n", f=11), wb.toks)
                k.load(sv, io["w_ffn_out"][l, hf * 1408:(hf + 1) * 1408, oc * 128:(oc + 1) * 128]
                       .rearrange("(f p) n -> p f n", p=128))
                k.copy("pool", wv[:], sv)
                wvs.append(wv)
            for b in range(2):
                sl = slice(b * 512, (b + 1) * 512)
                ps = self.ps.next()
                for f in range(NF):
                    k.mm(ps[:, :], wvs[f // 11][:, f % 11, :], self.hT.at(b)[:, f, sl], start=(f == 0), stop=(f == NF - 1))
                k.stt(xs.at(b)[:, oc, sl], ps[:, :], self.modv(l, 5, g, oc), xs.at(b)[:, oc, sl], ALU.mult, ALU.add)
        for b in range(2):
            self.layer_norm(l, 1, g, b)


def head_cols(h):
    r = lambda o, n: list(range(o, o + n))
    cols = []
    cols += r(O_AQ + h * 128, 128) + r(O_AK + h * 128, 128) + r(O_AV + h * 128, 128)
    cols += r(O_BQKV + h * 64, 64) + r(O_BQKV + 256 + h * 64, 64) + r(O_BQKV + 512 + h * 64, 64)
    cols += r(O_BG + h * 64, 64)
    cols += [O_BBETA + h, O_BBETA + 4 + h, O_BA + h, O_BA + 4 + h]
    cols += r(O_CQ + h * 64, 64) + r(O_CF + h * 64, 64) + r(O_CF + 256 + h * 64, 64)
    cols += r(O_CI + h * 64, 64) + r(O_CG + h * 64, 64)
    return cols


def w_out_perm():
    rows = []
    for r in range(4):
        rows += list(range(r * 128, (r + 1) * 128))
        rows += list(range(512 + r * 64, 512 + (r + 1) * 64))
        rows += list(range(768 + r * 64, 768 + (r + 1) * 64))
    return rows


def rope_tables():
    t = np.arange(DEC_SEQ)
    inv = (np.float32(10000.0) ** (-np.arange(16, dtype=np.float32) / np.float32(16))).astype(np.float32)
    out = np.zeros((2, 128, DEC_SEQ), np.float32)
    for p in range(128):
        d = p % 64
        pos = (t // 64) if d < 32 else (t % 64)
        ang = pos.astype(np.float32) * inv[d % 16]
        out[0, p] = np.cos(ang)
        out[1, p] = np.sin(ang)
    return out


def prep_inputs(inp, depth=DEPTH):
    f = lambda a: np.ascontiguousarray(np.asarray(a, dtype=np.float32))
    L = depth
    consts = make_consts()
    shared = {
        "w_mod": f(inp["w_mod"][:L]),
        "b_mod": f(np.asarray(inp["b_mod"])[:L].reshape(L, 48, 128).transpose(0, 2, 1)),
        "w_in": f(inp["w_in"][:L]),
        "w_out_p": f(inp["w_out"][:L]),
        "w_out_s": f(np.asarray(inp["w_out"])[:L][:, w_out_perm(), :]),
        "ln_g": f(np.asarray(inp["ln_g"])[:L].reshape(L, 2, NCH, 128).transpose(0, 3, 1, 2)),
        "ln_b": f(np.asarray(inp["ln_b"])[:L].reshape(L, 2, NCH, 128).transpose(0, 3, 1, 2)),
        "w_ffn_in": f(inp["w_ffn_in"][:L]),
        "w_ffn_out": f(inp["w_ffn_out"][:L]),
        "consts": consts,
        "rope_cs": rope_tables(),
        "diff_lambda": f(np.asarray(inp["diff_lambda"])[:L].reshape(-1)),
        "diff_norm": f(np.asarray(inp["diff_norm"])[:L].T),
        "hgrn_lb": f(np.asarray(inp["hgrn_lb"])[:, :L].reshape(-1)),
        "conv_hm": f(np.asarray(inp["conv_w"])[:L].reshape(L, 3, 3, 4, 64).transpose(0, 3, 1, 2, 4).reshape(L, 4, 576)),
        "w_ba_p": f(np.stack([np.asarray(inp["w_in"])[:L][:, :, [O_BBETA + h, O_BBETA + 4 + h, O_BA + h, O_BA + 4 + h]]
                              for h in range(4)], 2)),
        "dpar": f(np.stack([np.asarray(inp["delta_a_log"])[:L], np.asarray(inp["delta_dt_bias"])[:L]], -1)
                  .transpose(0, 2, 1, 3).reshape(-1)),
        "dnorm": f(np.asarray(inp["delta_norm"])[:L].reshape(-1)),
        "hnorm": f(np.tile(np.asarray(inp["hgrn_norm"])[:L], (1, 2))),
    }
    xp = np.asarray(inp["x_prompt"], np.float32)
    xsm = np.asarray(inp["x_sample"], np.float32)
    maps = []
    for c in range(8):
        s, r = c // 4, c % 4
        m = dict(shared)
        m["xT_p"] = f(xp[4 * c:4 * c + 4].reshape(TOK, D).T)
        m["xT_s"] = f(xsm[s, r * TOK:(r + 1) * TOK].T)
        cond = np.stack([np.asarray(inp["c_ctx"], np.float32), np.asarray(inp["c"], np.float32)[s]], -1)
        m["cond"] = f(cond.reshape(NCH, 128, 2).transpose(1, 0, 2))
        m["w_in_h"] = f(np.asarray(inp["w_in"])[:L][:, :, head_cols(r)])
        m["ctx_k"] = f(np.asarray(inp["cache_attn_k"])[s, :L, :, r, :])
        m["ctx_v"] = f(np.asarray(inp["cache_attn_v"])[s, :L, :, r, :])
        m["hgrn_lb_h"] = f(np.asarray(inp["hgrn_lb"])[:, :L, r * 64:(r + 1) * 64].reshape(-1))
        m["s0_h"] = f(np.asarray(inp["state_hgrn"])[s, :L, :, r])
        m["s0_d"] = f(np.asarray(inp["state_delta"])[s, :L, :, r])
        m["conv_h"] = f(np.asarray(inp["conv_w"])[:L].reshape(L, 3, 3, 4, 64)[:, :, :, r, :].reshape(L, 576))
        m["dpar_h"] = f(np.stack([np.asarray(inp["delta_a_log"])[:L, :, r], np.asarray(inp["delta_dt_bias"])[:L, :, r]], -1).reshape(-1))
        maps.append(m)
    return maps


_PROG = {}


def get_prog(depth=DEPTH, mixers=("A", "B", "C"), dbg=None):
    key = (depth, tuple(mixers), tuple(sorted((dbg or {}).items())))
    if key not in _PROG:
        _PROG[key] = Prog(depth, mixers, dbg)
    return _PROG[key]


def run(inp, depth=DEPTH, mixers=("A", "B", "C"), dbg=None, trace=False):
    prog = get_prog(depth, mixers, dbg)
    maps = prep_inputs(inp, depth)
    res = run_bass_kernel_spmd(prog.nc, maps, core_ids=list(range(8)), trace=trace)
    return res


def assemble(res, depth=DEPTH):
    R = res.results
    y_p = np.concatenate([R[c]["yT_p"].T.reshape(4, SEQ, D) for c in range(8)], 0)
    y_s = np.stack([np.concatenate([R[s * 4 + r]["yT_s"].T for r in range(4)], 0) for s in range(2)], 0)
    L = depth
    nk = np.concatenate([R[c]["new_k"] for c in range(8)], 0).reshape(32, L, SEQ, 4, 128)
    nv = np.concatenate([R[c]["new_v"] for c in range(8)], 0).reshape(32, L, SEQ, 4, 128)
    nsh = np.concatenate([R[c]["new_sh"] for c in range(8)], 0).reshape(32, L, 2, 4, 64, 64)
    nsd = np.concatenate([R[c]["new_sd"] for c in range(8)], 0).reshape(32, L, 2, 4, 64, 64)
    return (y_p.astype(np.float32), y_s.astype(np.float32), nk.astype(np.float32), nv.astype(np.float32),
            nsd.astype(np.float32), nsh.astype(np.float32))


def kernel(**inputs):
    res = run(inputs)
    return assemble(res)
```
